# Optimizing a Trainium2 kernel written in Bass

```python
import math
import jax, jax.numpy as jnp
from jax import lax
import numpy as np

D_MODEL = 1024
BATCH = 8
SEQ = 4096
DEPTH = 4

GRID_W = 64
CTX_LEN = 256
D_MIX = 1024
D_CONV = 512
N_HEADS_DN = 4
HEAD_DK = 128
HEAD_DV = 128
D_DN = N_HEADS_DN * HEAD_DV
CONV_W = 3
CHUNK = 64
EPS = 1e-6

PROJ_SIZES = (D_CONV, D_CONV, D_CONV, D_CONV,
              N_HEADS_DN * HEAD_DK, N_HEADS_DN * HEAD_DK, D_DN, D_DN,
              2 * N_HEADS_DN, 2 * N_HEADS_DN)
PROJ_SPLITS = tuple(int(s) for s in np.cumsum(PROJ_SIZES)[:-1])
D_PROJ = sum(PROJ_SIZES)
D_QKV = 2 * N_HEADS_DN * HEAD_DK + D_DN

kernel_name = "hybrid_conv_deltanet_dit_block"


def rms_norm(x, w):
    x32 = x.astype(jnp.float32)
    y = x32 * lax.rsqrt(jnp.mean(x32 * x32, axis=-1, keepdims=True) + EPS)
    return (y * w.astype(jnp.float32)).astype(x.dtype)


def l2_normalize(t):
    t32 = t.astype(jnp.float32)
    return t32 * lax.rsqrt(jnp.sum(t32 * t32, axis=-1, keepdims=True) + EPS)


def to_scan_order(x, col_major):
    if not col_major:
        return x
    b, n, f = x.shape
    rows = n // GRID_W
    return x.reshape(b, rows, GRID_W, f).transpose(0, 2, 1, 3).reshape(b, n, f)


def from_scan_order(x, col_major):
    if not col_major:
        return x
    b, n, f = x.shape
    rows = n // GRID_W
    return x.reshape(b, GRID_W, rows, f).transpose(0, 2, 1, 3).reshape(b, n, f)


def segment_conv(x, w, seg):
    b, n, ch = x.shape
    pad = CONV_W // 2
    xs = jnp.pad(x.reshape(b, n // seg, seg, ch), ((0, 0), (0, 0), (pad, pad), (0, 0)))
    y = w[0] * xs[:, :, 0:seg]
    for j in range(1, CONV_W):
        y = y + w[j] * xs[:, :, j:j + seg]
    return y.reshape(b, n, ch)


def split_proj(p):
    return jnp.split(p, PROJ_SPLITS, axis=-1)


def gated_delta_chunked(q, k, v, beta, g, s0, with_output):
    b, l, h, dk = q.shape
    dv = v.shape[-1]
    n = l // CHUNK

    def chunks(t):
        t = t.astype(jnp.float32).reshape((b, n, CHUNK) + t.shape[2:])
        return jnp.swapaxes(t, 2, 3)

    qc, kc, vc, bc, gch = (chunks(t) for t in (q, k, v, beta, g))
    gcum = jnp.cumsum(gch, axis=-1)
    idx = jnp.arange(CHUNK)
    incl = idx[:, None] >= idx[None, :]
    strict = idx[:, None] > idx[None, :]
    decay = jnp.exp(jnp.where(incl, gcum[..., :, None] - gcum[..., None, :], -jnp.inf))
    kk = jnp.einsum('bnhtd,bnhsd->bnhts', kc, kc)
    a_mat = jnp.where(strict, bc[..., :, None] * kk * decay, 0.0)
    rhs = jnp.concatenate([vc * bc[..., None], kc * (bc * jnp.exp(gcum))[..., None]], axis=-1)
    sol = lax.linalg.triangular_solve(a_mat + jnp.eye(CHUNK, dtype=jnp.float32), rhs,
                                      left_side=True, lower=True, unit_diagonal=True)
    u0, w = sol[..., :dv], sol[..., dv:]
    g_last = gcum[..., -1]
    k_dec = kc * jnp.exp(g_last[..., None] - gcum)[..., None]
    xs = [u0, w, k_dec, g_last]
    if with_output:
        p_intra = jnp.einsum('bnhtd,bnhsd->bnhts', qc, kc) * decay
        q_dec = qc * jnp.exp(gcum)[..., None]
        xs = xs + [p_intra, q_dec]
    xs = tuple(jnp.moveaxis(t, 1, 0) for t in xs)

    def step(s, inp):
        u0_i, w_i, kd_i, gl_i = inp[:4]
        u = u0_i - jnp.einsum('bhtk,bhkv->bhtv', w_i, s)
        s_new = s * jnp.exp(gl_i)[..., None, None] + jnp.einsum('bhtk,bhtv->bhkv', kd_i, u)
        if with_output:
            p_i, qd_i = inp[4:]
            o = jnp.einsum('bhtk,bhkv->bhtv', qd_i, s) + jnp.einsum('bhts,bhsv->bhtv', p_i, u)
            return s_new, o
        return s_new, None

    s_fin, o = lax.scan(step, s0.astype(jnp.float32), xs)
    if with_output:
        o = jnp.transpose(o, (1, 0, 3, 2, 4)).reshape(b, l, h, dv)
    return o, s_fin


def deltanet_inputs(q, k, v, beta_logit, alpha_in, conv_qkv, a_log, dt_bias, seg):
    b, n, _ = q.shape
    qkv = jax.nn.silu(segment_conv(jnp.concatenate([q, k, v], axis=-1), conv_qkv, seg))
    q, k, v = jnp.split(qkv, [N_HEADS_DN * HEAD_DK, 2 * N_HEADS_DN * HEAD_DK], axis=-1)
    q = l2_normalize(q.reshape(b, n, N_HEADS_DN, HEAD_DK)) * (HEAD_DK ** -0.5)
    k = l2_normalize(k.reshape(b, n, N_HEADS_DN, HEAD_DK))
    v = v.reshape(b, n, N_HEADS_DN, HEAD_DV).astype(jnp.float32)
    beta = jax.nn.sigmoid(beta_logit.reshape(b, n, 2, N_HEADS_DN).astype(jnp.float32))
    g = -jnp.exp(a_log.astype(jnp.float32)) * jax.nn.softplus(
        alpha_in.reshape(b, n, 2, N_HEADS_DN).astype(jnp.float32) + dt_bias.astype(jnp.float32))
    return q, k, v, beta, g


def bidirectional_gdn(dl, dc, ctx_out):
    ql, kl, vl, bl, gl = dl
    qc, kc, vc, bc, gc = dc
    s0 = jnp.zeros((ql.shape[0], N_HEADS_DN, HEAD_DK, HEAD_DV), jnp.float32)

    def flip(t):
        return jnp.flip(t, axis=1)

    oc_f, sc_f = gated_delta_chunked(qc, kc, vc, bc[:, :, 0], gc[:, :, 0], s0, ctx_out)
    ol_f, _ = gated_delta_chunked(ql, kl, vl, bl[:, :, 0], gl[:, :, 0], sc_f, True)
    oc_b, sc_b = gated_delta_chunked(flip(qc), flip(kc), flip(vc), flip(bc[:, :, 1]),
                                     flip(gc[:, :, 1]), s0, ctx_out)
    ol_b, _ = gated_delta_chunked(flip(ql), flip(kl), flip(vl), flip(bl[:, :, 1]),
                                  flip(gl[:, :, 1]), sc_b, True)
    ol = ol_f + flip(ol_b)
    oc = oc_f + flip(oc_b) if ctx_out else None
    return ol, oc


def branch_outputs(p, o_dn, conv_a, gdn_norm, seg):
    xa, bg, cg, za = p[0], p[1], p[2], p[3]
    zb = p[7]
    b, n, _ = za.shape
    ya = bg * segment_conv(cg * xa, conv_a, seg) * jax.nn.silu(za)
    yb = rms_norm(o_dn, gdn_norm) * jax.nn.silu(
        zb.reshape(b, n, N_HEADS_DN, HEAD_DV).astype(jnp.float32))
    return jnp.concatenate([ya, yb.reshape(b, n, D_DN).astype(ya.dtype)], axis=-1)


def hybrid_layer(xl, xc, c, c_ctx, norm_w, w_mod, b_mod, w_in, conv_a, conv_qkv,
                 a_log, dt_bias, gdn_norm, w_out, col_major, ctx_out):
    n = xl.shape[1]
    rows = n // GRID_W
    seg_l = rows if col_major else GRID_W
    seg_c = xc.shape[1]
    mod_l = jax.nn.silu(c) @ w_mod + b_mod
    shift_l, scale_l, gate_l = jnp.split(mod_l[:, None, :], 3, axis=-1)
    mod_c = jax.nn.silu(c_ctx) @ w_mod + b_mod
    shift_c, scale_c, gate_c = jnp.split(mod_c, 3, axis=-1)
    hl = rms_norm(xl, norm_w) * (1.0 + scale_l) + shift_l
    hc = rms_norm(xc, norm_w) * (1.0 + scale_c) + shift_c
    pl = split_proj(to_scan_order(hl, col_major) @ w_in)
    pc = split_proj(hc @ w_in)
    dl = deltanet_inputs(pl[4], pl[5], pl[6], pl[8], pl[9], conv_qkv, a_log, dt_bias, seg_l)
    dc = deltanet_inputs(pc[4], pc[5], pc[6], pc[8], pc[9], conv_qkv, a_log, dt_bias, seg_c)
    ol, oc = bidirectional_gdn(dl, dc, ctx_out)
    yl = from_scan_order(branch_outputs(pl, ol, conv_a, gdn_norm, seg_l), col_major)
    xl = xl + gate_l * (yl @ w_out)
    if ctx_out:
        yc = branch_outputs(pc, oc, conv_a, gdn_norm, seg_c)
        xc = xc + gate_c * (yc @ w_out)
    return xl, xc


def setup_inputs(seed: int = 0) -> dict:
    key = jax.random.key(seed)
    ks = jax.random.split(key, 16)
    f32 = jnp.float32
    x = jax.random.normal(ks[0], (BATCH, SEQ, D_MODEL), f32)
    c = jax.random.normal(ks[1], (BATCH, D_MODEL), f32)
    ctx = jax.random.normal(ks[2], (BATCH, CTX_LEN, D_MODEL), f32)
    c_ctx = jax.random.normal(ks[3], (D_MODEL,), f32)
    norm_w = 1.0 + 0.05 * jax.random.normal(ks[4], (DEPTH, D_MODEL), f32)
    w_mod = 0.5 * D_MODEL ** -0.5 * jax.random.normal(ks[5], (DEPTH, D_MODEL, 3 * D_MODEL), f32)
    b_mod = 0.02 * jax.random.normal(ks[6], (DEPTH, 3 * D_MODEL), f32)
    w_in = D_MODEL ** -0.5 * jax.random.normal(ks[7], (DEPTH, D_MODEL, D_PROJ), f32)
    conv_a = CONV_W ** -0.5 * jax.random.normal(ks[8], (DEPTH, CONV_W, D_CONV), f32)
    conv_qkv = CONV_W ** -0.5 * jax.random.normal(ks[9], (DEPTH, CONV_W, D_QKV), f32)
    a_log = jnp.log(jax.random.uniform(ks[10], (DEPTH, 2, N_HEADS_DN), f32, 1.0, 16.0))
    dt = jnp.exp(jax.random.uniform(ks[11], (DEPTH, 2, N_HEADS_DN), f32)
                 * (math.log(0.1) - math.log(0.001)) + math.log(0.001))
    dt_bias = dt + jnp.log(-jnp.expm1(-dt))
    gdn_norm = 1.0 + 0.05 * jax.random.normal(ks[12], (DEPTH, HEAD_DV), f32)
    w_out = D_MIX ** -0.5 * jax.random.normal(ks[13], (DEPTH, D_MIX, D_MODEL), f32)
    final_norm = 1.0 + 0.05 * jax.random.normal(ks[14], (D_MODEL,), f32)
    return {"x": x, "c": c, "ctx": ctx, "c_ctx": c_ctx, "norm_w": norm_w,
            "w_mod": w_mod, "b_mod": b_mod, "w_in": w_in, "conv_a": conv_a,
            "conv_qkv": conv_qkv, "a_log": a_log, "dt_bias": dt_bias,
            "gdn_norm": gdn_norm, "w_out": w_out, "final_norm": final_norm}


def reference(x, c, ctx, c_ctx, norm_w, w_mod, b_mod, w_in, conv_a, conv_qkv,
              a_log, dt_bias, gdn_norm, w_out, final_norm):
    xl, xc = x, ctx
    for i in range(DEPTH):
        xl, xc = hybrid_layer(xl, xc, c, c_ctx, norm_w[i], w_mod[i], b_mod[i], w_in[i],
                              conv_a[i], conv_qkv[i], a_log[i], dt_bias[i], gdn_norm[i],
                              w_out[i], col_major=(i % 2 == 1), ctx_out=(i < DEPTH - 1))
    return rms_norm(xl, final_norm)
```

```python
import numpy as np
from contextlib import ExitStack
import concourse.bass as bass
import concourse.mybir as mybir
from concourse.bass_utils import run_bass_kernel_spmd

F32 = mybir.dt.float32
BF16 = mybir.dt.bfloat16
AF = mybir.ActivationFunctionType
ALU = mybir.AluOpType

D = 1024
NCTX = 256
NLAT = 4096
NTOK = NCTX + NLAT
DPROJ = 4112
DEPTH = 4
EPS = 1e-6
NBLK = 9
BIGNEG = -30000.0
PSUM_KEYS = {f"B{i}" for i in range(8)}


def blk_range(b):
    if b == 0:
        return 0, NCTX
    return NCTX + (b - 1) * 512, 512


class _Stop(Exception):
    pass


class Prog:
    def __init__(self, nc, es):
        self.nc = nc
        self.es = es
        self.eng = {"pe": nc.tensor, "act": nc.scalar, "dve": nc.vector, "pool": nc.gpsimd, "sp": nc.sync}
        self.sem = {}
        self.cnt = {}
        self.epoch = 0
        self.state = {}
        self.waited = {}
        self.dsem = {}
        self.dcnt = {}
        self.all_sems = {}
        self.nops = 0
        self.stop_at = None
        self.new_epoch()

    def new_epoch(self):
        self.epoch += 1
        for e in ("pe", "act", "dve", "pool"):
            s = self.es.enter_context(self.nc.semaphore(f"s_{e}_{self.epoch}"))
            self.sem[e] = s
            self.cnt[e] = 0
            self.all_sems[id(s)] = s

    def _collect(self, engine, reads, writes):
        need = {}

        def add(ev):
            s, v, e = ev
            k = id(s)
            if k not in need or need[k][1] < v:
                need[k] = (s, v, e)

        for k in reads:
            st = self.state.get(k)
            if st is not None:
                for ev in st["w"].values():
                    add(ev)
                if k in PSUM_KEYS:
                    for ev in st["r"].values():
                        if ev[2] != engine:
                            add(ev)
        for k in writes:
            st = self.state.get(k)
            if st is not None:
                for ev in st["w"].values():
                    if ev[2] != engine:
                        add(ev)
                for ev in st["r"].values():
                    if ev[2] != engine:
                        add(ev)
        out = []
        wd = self.waited.setdefault(engine, {})
        for k, (s, v, e) in need.items():
            if wd.get(k, 0) >= v:
                continue
            wd[k] = v
            out.append((s, v))
        return out

    def _record(self, ev, reads, writes):
        for k in reads:
            st = self.state.setdefault(k, {"w": {}, "r": {}})
            st["r"][id(ev[0])] = ev
        for k in writes:
            st = self.state.setdefault(k, {"w": {}, "r": {}})
            st["w"][id(ev[0])] = ev

    def op(self, engine, fn, reads=(), writes=()):
        e = self.eng[engine]
        for s, v in self._collect(engine, reads, writes):
            e.wait_ge(s, v)
        inst = fn(e)
        self.cnt[engine] += 1
        inst.then_inc(self.sem[engine], 1)
        self._record((self.sem[engine], self.cnt[engine], engine), reads, writes)
        self.nops += 1
        if self.stop_at is not None and self.nops == self.stop_at:
            self.barrier()
            raise _Stop()

    def dma(self, queue, out, in_, reads, writes, slot):
        e = self.eng[queue]
        for s, v in self._collect("q_" + queue, reads, writes):
            e.wait_ge(s, v)
        if slot not in self.dsem:
            self.dsem[slot] = self.es.enter_context(self.nc.semaphore(f"d_{len(self.dsem)}"))
            self.dcnt[slot] = 0
        self.dcnt[slot] += 16
        e.dma_start(out=out, in_=in_).then_inc(self.dsem[slot], 16)
        self._record((self.dsem[slot], self.dcnt[slot], "dma_" + str(slot)), reads, writes)

    def barrier(self):
        evs = [(self.sem[e], self.cnt[e]) for e in ("pe", "act", "dve", "pool") if self.cnt[e] > 0]
        evs += [(self.dsem[s], self.dcnt[s]) for s in self.dsem]
        for en in ("pe", "act", "dve", "pool", "sp"):
            key = en if en != "sp" else "q_sp"
            wd = self.waited.setdefault(key, {})
            for s, v in evs:
                if en in self.sem and s is self.sem.get(en):
                    continue
                if wd.get(id(s), 0) >= v:
                    continue
                wd[id(s)] = v
                self.eng[en].wait_ge(s, v)
        wd = self.waited.setdefault("q_pool", {})
        for s, v in evs:
            wd[id(s)] = max(wd.get(id(s), 0), v)
        self.state = {}


def build(nlayers=DEPTH, stop=None):
    def ck(name):
        if stop == name:
            print("ck", name, "nops", P.nops)
            raise _Stop()
    nc = bass.Bass("TRN2", target_bir_lowering=False)
    dt_in = lambda n, shp, dt=F32: nc.dram_tensor(n, list(shp), dt, kind="ExternalInput").ap()
    xT = dt_in("xT", [D, NTOK])
    cc = dt_in("cc", [128, 16])
    w_mod = dt_in("w_mod", [DEPTH, D, 3 * D])
    bmod = dt_in("bmod", [DEPTH, 128, 24])
    normw = dt_in("normw", [DEPTH, 128, 8])
    w_in = dt_in("w_in", [DEPTH, D, DPROJ])
    conva = dt_in("conva", [DEPTH, 128, 12])
    convq = dt_in("convq", [DEPTH, 128, 36])
    alog = dt_in("alog", [DEPTH, 8, 1])
    dtb = dt_in("dtb", [DEPTH, 8, 1])
    gnorm = dt_in("gnorm", [DEPTH, 128, 1])
    w_out = dt_in("w_out", [DEPTH, D, D])
    fnorm = dt_in("fnorm", [128, 8])
    cmask = dt_in("cmask", [128, 9 * 512])
    cident = dt_in("cident", [128, 128])
    csel = dt_in("csel", [8, 2 * 1024 + 128])
    outT = nc.dram_tensor("outT", [D, NLAT], F32, kind="ExternalOutput").ap()
    xres = nc.dram_tensor("xres", [D, NTOK], F32).ap()
    kqv_s = nc.dram_tensor("kqv_s", [NBLK, 128, 12, 512], BF16).ap()
    ya_s = nc.dram_tensor("ya_s", [NBLK, 128, 4, 512], BF16).ap()
    szb_s = nc.dram_tensor("szb_s", [NBLK, 128, 4, 512], BF16).ap()
    rows_s = nc.dram_tensor("rows_s", [NBLK, 8, 2, 512], F32).ap()

    es = ExitStack()
    try:
      with es:
          P = Prog(nc, es)
          if isinstance(stop, int):
              P.stop_at = stop
          _uniq = [0]

          def sb(n, shp, dt=F32, st=es):
              _uniq[0] += 1
              return st.enter_context(nc.sbuf_tensor(f"{n}_{_uniq[0]}", list(shp), dt))
          BIG = sb("BIG", [128, 8, NTOK], BF16)
          MASK = sb("MASK", [128, 9, 512], BF16)
          IDF = sb("IDF", [128, 128], F32)
          IDB = sb("IDB", [128, 128], BF16)
          ONESB = sb("ONESB", [128, 128], BF16)
          SEL = sb("SEL", [8, 2 * 1024 + 128], F32)
          MODS = sb("MODS", [128, DEPTH, 2, 24], F32)
          CC = sb("CC", [128, 16], F32)
          FN = sb("FN", [128, 8], F32)
          LW = sb("LW", [128, 64], F32)
          L8 = sb("L8", [8, 4], F32)
          DUM = sb("DUM", [128, 2], F32)
          EPST = sb("EPST", [128, 1], F32)
          EPSC = EPST[:, 0:1]
          banks = [es.enter_context(nc.psum_tensor(f"B{i}", [128, 512], F32)) for i in range(8)]
          B = [b[:] for b in banks]
          Bbf = [b[:].bitcast(BF16) for b in banks]

          NEGI = [MASK[:, 0, :], MASK[:, 1, :]]
          SM = [MASK[:, 2, :], MASK[:, 3, :]]
          BLK16 = MASK[:, 4, :]
          OFF = {32: MASK[:, 5, :], 64: MASK[:, 6, :], 128: MASK[:, 7, :]}
          ID4 = MASK[:, 8, :]
          I8 = IDF[0:8, 0:8]
          ONES8 = SEL[:, 2048:2176]

          def h4(ap):
              return ap.rearrange("p (h t) -> p h t", h=4)

          with ExitStack() as st0:
              MST = sb("MST", [128, 9 * 512], F32, st0)
              WST = sb("WST", [128, 2, 4096], F32, st0)
              SC = sb("SC", [128, 16], F32, st0)
              BM = sb("BM", [128, 24], F32, st0)
              NW = sb("NW", [128, 8], F32, st0)
              P.dma("sp", MST[:], cmask, [], ["MST"], "c0")
              P.dma("sp", IDF[:], cident, [], ["IDF"], "c1")
              P.dma("sp", SEL[:], csel, [], ["SEL"], "c2")
              P.dma("sp", CC[:], cc, [], ["CC"], "c3")
              P.dma("sp", FN[:], fnorm, [], ["FN"], "c4")
              P.op("dve", lambda e: e.tensor_copy(MASK[:].rearrange("p a b -> p (a b)"), MST[:]), ["MST"], ["MASK"])
              P.op("dve", lambda e: e.tensor_copy(IDB[:], IDF[:]), ["IDF"], ["IDB"])
              P.op("dve", lambda e: e.memset(ONESB[:], 1.0), [], ["ONESB"])
              P.op("dve", lambda e: e.memset(DUM[:], 0.0), [], ["DUM"])
              P.op("dve", lambda e: e.memset(EPST[:], EPS), [], ["EPST"])
              P.op("act", lambda e: e.activation(SC[:], CC[:], AF.Silu), ["CC"], ["SC"])
              for l in range(nlayers):
                  P.dma("sp", BM[:], bmod[l], [], ["BM"], "c5")
                  P.dma("sp", NW[:], normw[l], [], ["NW"], "c6")
                  for jg in range(6):
                      s = jg % 2
                      P.dma("sp", WST[:, s, :].rearrange("p (k n) -> p k n", k=8),
                            w_mod[l, :, jg * 512:(jg + 1) * 512].rearrange("(k p) n -> p k n", p=128), [], [("WST", s)], ("wst", s))
                      for jj in range(4):
                          j = jg * 4 + jj
                          for k in range(8):
                              P.op("pe", lambda e, j=j, jj=jj, k=k, s=s: e.matmul(
                                  B[0][:, 2 * j:2 * j + 2], WST[:, s, k * 512 + jj * 128:k * 512 + (jj + 1) * 128],
                                  SC[:].rearrange("p (k v) -> p k v", v=2)[:, k, :], start=(k == 0), stop=(k == 7)),
                                  [("WST", s), "SC"], ["B0"])
                  for v in range(2):
                      P.op("dve", lambda e, v=v, l=l: e.tensor_tensor(
                          MODS[:, l, v, :], B[0][:, 0:48].rearrange("p (j v) -> p j v", v=2)[:, :, v], BM[:], ALU.add),
                          ["B0", "BM"], [("MODS", l)])
                      P.op("dve", lambda e, v=v, l=l: e.scalar_tensor_tensor(
                          MODS[:, l, v, 8:16], MODS[:, l, v, 8:16], 1.0, NW[:], ALU.add, ALU.mult),
                          [("MODS", l), "NW"], [("MODS", l)])
              P.barrier()
          ck("s0")

          for l in range(nlayers):
              P.new_epoch()
              colmaj = (l % 2 == 1)
              last = (l == nlayers - 1)
              xsrc = xT if l == 0 else xres
              xsrc_v = xsrc.rearrange("(k p) t -> p k t", p=128)
              xres_v = xres.rearrange("(k p) t -> p k t", p=128)
              out_v = outT.rearrange("(k p) t -> p k t", p=128)

              def mcol(v, j, l=l):
                  return MODS[:, l, v, j:j + 1]

              P.dma("sp", LW[:, 0:12], conva[l], [], ["LWa"], "c7")
              P.dma("sp", LW[:, 12:48], convq[l], [], ["LWq"], "c8")
              P.dma("sp", LW[:, 48:49], gnorm[l], [], ["LWg"], "c9")
              P.dma("sp", L8[:, 0:1], alog[l], [], ["L8a"], "c10")
              P.dma("sp", L8[:, 1:2], dtb[l], [], ["L8"], "c11")
              P.op("act", lambda e: e.activation(L8[:, 2:3], L8[:, 0:1], AF.Exp), ["L8a"], ["L8"])
              P.op("dve", lambda e: e.tensor_scalar(L8[:, 2:3], L8[:, 2:3], -1.0, None, ALU.mult), ["L8"], ["L8"])
              CA = lambda j, tap: LW[:, j * 3 + tap: j * 3 + tap + 1]
              CQ = lambda j, tap: LW[:, 12 + j * 3 + tap: 12 + j * 3 + tap + 1]
              GN = LW[:, 48:49]

              def big_keys(b, permuted):
                  if b == 0 or not permuted:
                      return [("BIG", b)]
                  return [("BIG", i) for i in range(1, 9)]

              def big_view(kc, b, permuted, rowmajor_of_colscan):
                  t0, nt = blk_range(b)
                  if b == 0 or not permuted:
                      return BIG[:, kc, t0:t0 + nt]
                  lat = BIG[:, kc, NCTX:NTOK]
                  if rowmajor_of_colscan:
                      v = lat.rearrange("p (c r) -> p r c", r=64)
                  else:
                      v = lat.rearrange("p (r c) -> p c r", c=64)
                  return v[:, (b - 1) * 8:(b - 1) * 8 + 8, :]

              def pview(ap, b, permuted):
                  t0, nt = blk_range(b)
                  if b == 0 or not permuted:
                      return ap[:, 0:nt]
                  return ap.rearrange("p (a b) -> p a b", b=64)

              with ExitStack() as s1:
                  WIN = sb("WIN", [128, 8, DPROJ], BF16, s1)
                  XSF = sb("XS", [128, 4112], F32, s1)
                  XS = XSF[:, 0:4096].rearrange("p (k t) -> p k t", k=8)
                  T0 = sb("T0", [128, 512], F32, s1)
                  T1 = sb("T1", [128, 512], F32, s1)
                  T2 = sb("T2", [128, 512], F32, s1)
                  SQQ = sb("SQQ", [128, 512], BF16, s1)
                  KQVB = sb("KQVB", [128, 12, 512], BF16, s1)
                  SQB = KQVB[:, 0:8, :]
                  YAB = sb("YAB", [128, 4, 512], BF16, s1)
                  SZBB = sb("SZBB", [128, 4, 512], BF16, s1)
                  ROWB = sb("ROWB", [8, 2, 512], F32, s1)
                  R8 = sb("R8", [8, 512], F32, s1)
                  XSW = XSF[:]
                  i = 0
                  for k in range(8):
                      for hf in range(2):
                          s = i % 2
                          c0 = hf * 2056
                          P.dma("sp", XSW[:, s * 2056:(s + 1) * 2056], w_in[l, k * 128:(k + 1) * 128, c0:c0 + 2056],
                                [], [("XS", s)], ("xs", s))
                          eng = ("dve", "pool", "act")[i % 3]
                          if eng == "act":
                              P.op("act", lambda e, k=k, s=s, c0=c0: e.copy(WIN[:, k, c0:c0 + 2056], XSW[:, s * 2056:(s + 1) * 2056]),
                                   [("XS", s)], ["WIN"])
                          else:
                              P.op(eng, lambda e, k=k, s=s, c0=c0: e.tensor_copy(WIN[:, k, c0:c0 + 2056], XSW[:, s * 2056:(s + 1) * 2056]),
                                   [("XS", s)], ["WIN"])
                          i += 1
                  if stop == "s1w":
                      P.barrier()
                      ck("s1w")
                  for nb in range(NBLK):
                      t0, nt = blk_range(nb)
                      v = 1 if nb == 0 else 0
                      P.dma("sp", XS[:, :, 0:nt], xsrc_v[:, :, t0:t0 + nt], [("x", nb)], [("XS", 0), ("XS", 1)], ("xs", 0))
                      P.op("act", lambda e, nt=nt: e.activation(SQB[:, :, 0:nt], XS[:, :, 0:nt], AF.Square),
                           [("XS", 0), ("XS", 1)], ["KQVB"])
                      for k in range(8):
                          P.op("pe", lambda e, k=k, nt=nt: e.matmul(B[2][:, 0:nt], ONESB[:], SQB[:, k, 0:nt], start=(k == 0), stop=(k == 7)),
                               ["KQVB", "ONESB"], ["B2"])
                      P.op("act", lambda e, nt=nt: e.activation(T0[:, 0:nt], B[2][:, 0:nt], AF.Sqrt, bias=EPSC, scale=1.0 / D), ["B2"], ["T0"])
                      P.op("dve", lambda e, nt=nt: e.reciprocal(T0[:, 0:nt], T0[:, 0:nt]), ["T0"], ["T0"])
                      for k in range(8):
                          eng = "dve" if k % 2 == 0 else "pool"
                          P.op(eng, lambda e, k=k, nt=nt: e.tensor_tensor(XS[:, k, 0:nt], XS[:, k, 0:nt], T0[:, 0:nt], ALU.mult),
                               [("XS", 0), ("XS", 1), "T0", "KQVB"], [("XSn", k)])
                          P.op("act", lambda e, k=k, nt=nt, t0=t0, v=v: e.activation(
                              BIG[:, k, t0:t0 + nt], XS[:, k, 0:nt], AF.Identity, bias=mcol(v, k), scale=mcol(v, 8 + k)),
                              [("XSn", k)], [("BIG", nb)])
                      P.op("act", lambda e: e.activation(DUM[:, 0:1], DUM[:, 1:2], AF.Copy), [("BIG", nb)] + [("XSn", k) for k in range(8)], [("XS", 0), ("XS", 1)])

                  if stop == "s1p":
                      P.barrier()
                      ck("s1p")
                  pp = [0]

                  def proj(b, c0, m):
                      bi = pp[0] % 2
                      pp[0] += 1
                      t0, nt = blk_range(b)
                      for k in range(8):
                          P.op("pe", lambda e, k=k, bi=bi: e.matmul(
                              pview(B[bi][0:m, :], b, colmaj), WIN[:, k, c0:c0 + m], big_view(k, b, colmaj, False),
                              start=(k == 0), stop=(k == 7)), ["WIN"] + big_keys(b, colmaj), [f"B{bi}"])
                      return bi

                  def conv(dst, dkey, src, skey, w, nt, seg):
                      P.op("dve", lambda e: e.tensor_scalar(dst[:, 0:nt], src[:, 0:nt], w(1), None, ALU.mult), [skey, "LWa", "LWq"], [dkey])
                      dv = dst[:, 0:nt].rearrange("p (a b) -> p a b", b=seg)
                      sv = src[:, 0:nt].rearrange("p (a b) -> p a b", b=seg)
                      P.op("dve", lambda e: e.scalar_tensor_tensor(dv[:, :, 1:seg], sv[:, :, 0:seg - 1], w(0), dv[:, :, 1:seg], ALU.mult, ALU.add),
                           [skey, dkey, "LWa", "LWq"], [dkey])
                      P.op("dve", lambda e: e.scalar_tensor_tensor(dv[:, :, 0:seg - 1], sv[:, :, 1:seg], w(2), dv[:, :, 0:seg - 1], ALU.mult, ALU.add),
                           [skey, dkey, "LWa", "LWq"], [dkey])

                  for b in range(NBLK):
                      t0, nt = blk_range(b)
                      seg = NCTX if b == 0 else 64
                      for jj in range(4):
                          bi = proj(b, jj * 128, 128)
                          P.op("act", lambda e, bi=bi: e.copy(T0[:, 0:nt], B[bi][:, 0:nt]), [f"B{bi}"], ["T0"])
                          bi = proj(b, 1024 + jj * 128, 128)
                          P.op("dve", lambda e, bi=bi: e.tensor_tensor(T0[:, 0:nt], B[bi][:, 0:nt], T0[:, 0:nt], ALU.mult), [f"B{bi}", "T0"], ["T0"])
                          conv(T1, "T1", T0, "T0", lambda tap, jj=jj: CA(jj, tap), nt, seg)
                          bi = proj(b, 512 + jj * 128, 128)
                          P.op("dve", lambda e, bi=bi: e.tensor_tensor(T1[:, 0:nt], B[bi][:, 0:nt], T1[:, 0:nt], ALU.mult), [f"B{bi}", "T1"], ["T1"])
                          bi = proj(b, 1536 + jj * 128, 128)
                          P.op("act", lambda e, bi=bi: e.activation(T2[:, 0:nt], B[bi][:, 0:nt], AF.Silu), [f"B{bi}"], ["T2"])
                          P.op("pool", lambda e, jj=jj: e.tensor_tensor(YAB[:, jj, 0:nt], T1[:, 0:nt], T2[:, 0:nt], ALU.mult), ["T1", "T2"], ["YAB"])
                      for idx in range(12):
                          bi = proj(b, 2048 + idx * 128, 128)
                          conv(T1, "T1", B[bi], f"B{bi}", lambda tap, idx=idx: CQ(idx, tap), nt, seg)
                          if idx >= 8:
                              P.op("act", lambda e, idx=idx: e.activation(KQVB[:, idx, 0:nt], T1[:, 0:nt], AF.Silu), ["T1"], ["KQVB"])
                              continue
                          P.op("act", lambda e: e.activation(T2[:, 0:nt], T1[:, 0:nt], AF.Silu), ["T1"], ["T2"])
                          P.op("act", lambda e: e.activation(SQQ[:, 0:nt], T2[:, 0:nt], AF.Square), ["T2"], ["SQQ"])
                          P.op("pe", lambda e: e.matmul(B[2][:, 0:nt], ONESB[:], SQQ[:, 0:nt], start=True, stop=True), ["SQQ", "ONESB"], ["B2"])
                          P.op("act", lambda e: e.activation(T0[:, 0:nt], B[2][:, 0:nt], AF.Sqrt, bias=EPSC, scale=1.0), ["B2"], ["T0"])
                          P.op("dve", lambda e: e.reciprocal(T0[:, 0:nt], T0[:, 0:nt]), ["T0"], ["T0"])
                          sc = (128.0 ** -0.5) if idx < 4 else 1.0
                          P.op("dve", lambda e, idx=idx, sc=sc: e.scalar_tensor_tensor(KQVB[:, idx, 0:nt], T2[:, 0:nt], sc, T0[:, 0:nt], ALU.mult, ALU.mult),
                               ["T2", "T0"], ["KQVB"])
                      for h in range(4):
                          bi = proj(b, 3584 + h * 128, 128)
                          P.op("act", lambda e, bi=bi, h=h: e.activation(SZBB[:, h, 0:nt], B[bi][:, 0:nt], AF.Silu), [f"B{bi}"], ["SZBB"])
                      bi = proj(b, 4096, 8)
                      P.op("act", lambda e, bi=bi: e.activation(ROWB[:, 1, 0:nt], B[bi][0:8, 0:nt], AF.Sigmoid), [f"B{bi}"], ["ROWB"])
                      bi = proj(b, 4104, 8)
                      P.op("act", lambda e, bi=bi: e.activation(R8[:, 0:nt], B[bi][0:8, 0:nt], AF.Exp, bias=L8[:, 1:2]), [f"B{bi}", "L8"], ["R8"])
                      P.op("act", lambda e: e.activation(R8[:, 0:nt], R8[:, 0:nt], AF.Ln, bias=1.0), ["R8"], ["R8"])
                      P.op("dve", lambda e: e.tensor_scalar(ROWB[:, 0, 0:nt], R8[:, 0:nt], L8[:, 2:3], None, ALU.mult), ["R8", "L8"], ["ROWB"])
                      P.dma("pool", kqv_s[b, :, :, 0:nt], KQVB[:, :, 0:nt], ["KQVB"], [("kqv", b)], "sp0")
                      P.dma("pool", ya_s[b, :, :, 0:nt], YAB[:, :, 0:nt], ["YAB"], [("ya", b)], "sp1")
                      P.dma("pool", szb_s[b, :, :, 0:nt], SZBB[:, :, 0:nt], ["SZBB"], [("szb", b)], "sp2")
                      P.dma("pool", rows_s[b, :, :, 0:nt], ROWB[:, :, 0:nt], ["ROWB"], [("rows", b)], "sp3")
                  P.barrier()

              ck("s1")
              with ExitStack() as s2:
                  OF = sb("OF", [128, 4, NTOK], BF16, s2)
                  KQ = [sb(f"KQ{i}", [128, 12, 512], BF16, s2) for i in range(2)]
                  RW = [sb(f"RW{i}", [8, 2, 512], F32, s2) for i in range(2)]
                  SZ = [sb(f"SZ{i}", [128, 4, 512], BF16, s2) for i in range(2)]
                  GC = sb("GC", [8, 128], F32, s2)
                  CS = sb("CS", [8, 128], F32, s2)
                  DG = sb("DG", [8, 8], F32, s2)
                  COLS = sb("COLS", [128, 24], F32, s2)
                  CF = sb("CF", [128, 32], F32, s2)
                  W0 = sb("W0", [128, 512], F32, s2)
                  W1 = sb("W1", [128, 512], F32, s2)
                  W2 = sb("W2", [128, 512], F32, s2)
                  W3 = sb("W3", [128, 512], F32, s2)
                  PTB = sb("PTB", [128, 512], BF16, s2)
                  ATB = sb("ATB", [128, 512], BF16, s2)
                  AB = sb("AB", [128, 512], BF16, s2)
                  PN = [sb(f"PN{i}", [128, 512], BF16, s2) for i in range(2)]
                  PTN = [sb(f"PTN{i}", [128, 512], BF16, s2) for i in range(2)]
                  RB = sb("RB", [128, 512], BF16, s2)
                  RTB = sb("RTB", [128, 512], BF16, s2)
                  ZM = sb("ZM", [128, 512], BF16, s2)
                  YM = sb("YM", [128, 512], BF16, s2)
                  KBE = sb("KBE", [128, 512], BF16, s2)
                  KDEC = sb("KDEC", [128, 512], BF16, s2)
                  VB = sb("VB", [128, 512], BF16, s2)
                  U0 = sb("U0", [128, 512], F32, s2)
                  WTB = sb("WTB", [128, 512], BF16, s2)
                  QDT = sb("QDT", [128, 512], BF16, s2)
                  UB = sb("UB", [128, 512], BF16, s2)
                  S32 = sb("S32", [128, 512], F32, s2)
                  STMP = sb("STMP", [128, 512], F32, s2)
                  SBF = sb("SBF", [128, 512], BF16, s2)
                  SQO = sb("SQO", [128, 512], BF16, s2)

                  def hs(ap, h):
                      return ap[:, h * 128:(h + 1) * 128]

                  def mm4(bank, lhs, rhs, rk, start=True, stop=True):
                      for h in range(4):
                          P.op("pe", lambda e, h=h: e.matmul(hs(B[bank], h), lhs(h), rhs(h), start=start, stop=stop), rk, [f"B{bank}"])

                  for d in range(2):
                      P.op("pool", lambda e: e.memset(S32[:], 0.0), [], ["S32"])
                      P.op("pool", lambda e: e.memset(SBF[:], 0.0), [], ["SBF"])
                      border = list(range(NBLK)) if d == 0 else [0] + list(range(8, 0, -1))
                      for bi_, b in enumerate(border):
                          t0, nt = blk_range(b)
                          sl = bi_ % 2
                          P.dma("sp", KQ[sl][:, :, 0:nt], kqv_s[b, :, :, 0:nt], [("kqv", b)], [("KQ", sl)], ("kq", sl))
                          P.dma("sp", RW[sl][:, :, 0:nt], rows_s[b, :, :, 0:nt], [("rows", b)], [("RW", sl)], ("rw", sl))
                          need_out = not (last and b == 0)
                          if d == 1 and need_out:
                              P.dma("sp", SZ[sl][:, :, 0:nt], szb_s[b, :, :, 0:nt], [("szb", b)], [("SZ", sl)], ("sz", sl))
                              P.dma("sp", BIG[:, 0:4, t0:t0 + nt], ya_s[b, :, :, 0:nt], [("ya", b)], [("BIG", b)], ("yal", sl))
                          nch = nt // 128
                          chs = list(range(nch)) if d == 0 else list(range(nch - 1, -1, -1))
                          kq, rw, sz = KQ[sl], RW[sl], SZ[sl]
                          kqk, rwk, szk = ("KQ", sl), ("RW", sl), ("SZ", sl)
                          for ch in chs:
                              c0 = ch * 128
                              cs_ = slice(c0, c0 + 128)
                              tk = t0 + c0
                              qT = lambda h: kq[:, h, cs_]
                              kT = lambda h: kq[:, 4 + h, cs_]
                              vT = lambda h: kq[:, 8 + h, cs_]
                              g8 = rw[:, 0, cs_]
                              b8 = rw[:, 1, cs_]
                              P.op("dve", lambda e: e.tensor_tensor_scan(CS[:], ONES8, g8, 0.0, ALU.mult, ALU.add), [rwk, "SEL"], ["CS"])
                              if d == 0:
                                  P.op("dve", lambda e: e.tensor_copy(GC[:], CS[:]), ["CS"], ["GC"])
                              else:
                                  P.op("dve", lambda e: e.scalar_tensor_tensor(GC[:], CS[:], -1.0, g8, ALU.mult, ALU.add), ["CS", rwk], ["GC"])
                                  P.op("dve", lambda e: e.tensor_scalar(GC[:], GC[:], CS[:, 127:128], None, ALU.add), ["GC", "CS"], ["GC"])
                              P.op("dve", lambda e: e.tensor_scalar(DG[:], I8, CS[:, 127:128], None, ALU.mult), ["CS", "IDF"], ["DG"])
                              P.op("pe", lambda e: e.matmul(B[4][:, 0:8], GC[:], I8, start=True, stop=True), ["GC", "IDF"], ["B4"])
                              P.op("pe", lambda e: e.matmul(B[4][:, 8:16], b8, I8, start=True, stop=True), [rwk, "IDF"], ["B4"])
                              P.op("pe", lambda e: e.matmul(B[4][:, 16:24], ONES8, DG[:], start=True, stop=True), ["SEL", "DG"], ["B4"])
                              P.op("act", lambda e: e.copy(COLS[:], B[4][:, 0:24]), ["B4"], ["COLS"])
                              P.op("act", lambda e: e.activation(CF[:, 0:8], COLS[:, 0:8], AF.Exp), ["COLS"], ["CF"])
                              P.op("dve", lambda e: e.tensor_tensor(CF[:, 8:16], CF[:, 0:8], COLS[:, 8:16], ALU.mult), ["CF", "COLS"], ["CF"])
                              P.op("dve", lambda e: e.tensor_tensor(CF[:, 16:24], COLS[:, 16:24], COLS[:, 0:8], ALU.subtract), ["COLS", "CF"], ["CF"])
                              P.op("act", lambda e: e.activation(CF[:, 16:24], CF[:, 16:24], AF.Exp), ["CF"], ["CF"])
                              P.op("act", lambda e: e.activation(CF[:, 24:32], COLS[:, 16:24], AF.Exp), ["COLS", "CF"], ["CF"])
                              if stop == "s2a":
                                  P.barrier()
                                  ck("s2a")
                              for h in range(4):
                                  P.op("pe", lambda e, h=h: e.transpose(Bbf[5][:, h * 128:(h + 1) * 128], kT(h), IDB[:]), [kqk, "IDB"], ["B5"])
                              for h in range(4):
                                  P.op("pe", lambda e, h=h: e.transpose(Bbf[5][:, 512 + h * 128:512 + (h + 1) * 128], vT(h), IDB[:]), [kqk, "IDB"], ["B5"])
                              for h in range(4):
                                  dh = d * 4 + h
                                  P.op("dve", lambda e, h=h, dh=dh: e.tensor_scalar(hs(KBE[:], h), Bbf[5][:, h * 128:(h + 1) * 128], CF[:, 8 + dh:9 + dh], None, ALU.mult),
                                       ["B5", "CF"], ["KBE"])
                                  P.op("act", lambda e, h=h, dh=dh: e.activation(hs(KDEC[:], h), Bbf[5][:, h * 128:(h + 1) * 128], AF.Identity, scale=CF[:, 16 + dh:17 + dh]),
                                       ["B5", "CF"], ["KDEC"])
                                  P.op("dve", lambda e, h=h, dh=dh: e.tensor_scalar(hs(VB[:], h), Bbf[5][:, 512 + h * 128:512 + (h + 1) * 128], COLS[:, 8 + dh:9 + dh], None, ALU.mult),
                                       ["B5", "COLS"], ["VB"])
                              if stop == "s2b":
                                  P.barrier()
                                  ck("s2b")
                              for h in range(4):
                                  dh = d * 4 + h
                                  P.op("pe", lambda e, h=h, dh=dh: e.matmul(hs(B[0], h), SEL[:, dh * 128:(dh + 1) * 128], GC[:], start=True, stop=False), ["SEL", "GC"], ["B0"])
                                  P.op("pe", lambda e, h=h, dh=dh: e.matmul(hs(B[0], h), GC[:], SEL[:, 1024 + dh * 128:1024 + (dh + 1) * 128], start=False, stop=True), ["SEL", "GC"], ["B0"])
                              P.op("dve", lambda e: e.tensor_tensor(W0[:], B[0], NEGI[d], ALU.add), ["B0", "MASK"], ["W0"])
                              P.op("act", lambda e: e.activation(W1[:], W0[:], AF.Exp), ["W0"], ["W1"])
                              mm4(1, lambda h: SEL[:, (d * 4 + h) * 128:(d * 4 + h + 1) * 128], lambda h: b8, ["SEL", rwk])
                              P.op("dve", lambda e: e.tensor_tensor(W2[:], B[1], SM[d], ALU.mult), ["B1", "MASK"], ["W2"])
                              P.op("pool", lambda e: e.tensor_tensor(W2[:], W2[:], W1[:], ALU.mult), ["W2", "W1"], ["W2"])
                              mm4(2, kT, kT, [kqk])
                              mm4(3, kT, qT, [kqk])
                              P.op("dve", lambda e: e.tensor_tensor(PTB[:], B[3], W1[:], ALU.mult), ["B3", "W1"], ["PTB"])
                              P.op("dve", lambda e: e.tensor_tensor(ATB[:], B[2], W2[:], ALU.mult), ["B2", "W2"], ["ATB"])
                              for h in range(4):
                                  P.op("pe", lambda e, h=h: e.transpose(Bbf[6][:, h * 128:(h + 1) * 128], hs(ATB[:], h), IDB[:]), ["ATB", "IDB"], ["B6"])
                              P.op("act", lambda e: e.copy(AB[:], Bbf[6][:, 0:512]), ["B6"], ["AB"])
                              if stop == "s2c":
                                  P.barrier()
                                  ck("s2c")
                              P.op("dve", lambda e: e.tensor_tensor(PTN[0][:], Bbf[6][:, 0:512], BLK16, ALU.mult), ["B6", "MASK"], [("PTN", 0)])
                              P.op("pool", lambda e: e.tensor_tensor(PN[0][:], ATB[:], BLK16, ALU.mult), ["ATB", "MASK"], [("PN", 0)])
                              P.op("pool", lambda e: e.tensor_tensor(RB[:], ID4, PTN[0][:], ALU.subtract), ["MASK", ("PTN", 0)], ["RB"])
                              P.op("pool", lambda e: e.tensor_tensor(RTB[:], ID4, PN[0][:], ALU.subtract), ["MASK", ("PN", 0)], ["RTB"])
                              cur = 0
                              for it in range(3):
                                  nx = 1 - cur
                                  mm4(0, lambda h: hs(PTN[cur][:], h), lambda h: hs(PN[cur][:], h), [("PTN", cur), ("PN", cur)])
                                  mm4(1, lambda h: hs(PN[cur][:], h), lambda h: hs(PTN[cur][:], h), [("PTN", cur), ("PN", cur)])
                                  P.op("act", lambda e, nx=nx: e.copy(PN[nx][:], B[0]), ["B0"], [("PN", nx)])
                                  P.op("dve", lambda e, nx=nx: e.tensor_copy(PTN[nx][:], B[1]), ["B1"], [("PTN", nx)])
                                  mm4(2, lambda h: hs(PTN[nx][:], h), lambda h: hs(RTB[:], h), [("PTN", nx), "RTB"])
                                  mm4(3, lambda h: hs(PN[nx][:], h), lambda h: hs(RB[:], h), [("PN", nx), "RB"])
                                  P.op("dve", lambda e: e.tensor_tensor(RTB[:], B[2], RTB[:], ALU.add), ["B2", "RTB"], ["RTB"])
                                  P.op("dve", lambda e: e.tensor_tensor(RB[:], B[3], RB[:], ALU.add), ["B3", "RB"], ["RB"])
                                  cur = nx
                              if stop == "s2d":
                                  P.barrier()
                                  ck("s2d")
                              for szm in (32, 64, 128):
                                  if szm < 128:
                                      mm4(0, lambda h: hs(ATB[:], h), lambda h: hs(RB[:], h), ["ATB", "RB"])
                                      P.op("dve", lambda e, szm=szm: e.tensor_tensor(ZM[:], B[0], OFF[szm], ALU.mult), ["B0", "MASK"], ["ZM"])
                                  mm4(1, lambda h: hs(AB[:], h), lambda h: hs(RTB[:], h), ["AB", "RTB"])
                                  P.op("dve", lambda e, szm=szm: e.tensor_tensor(YM[:], B[1], OFF[szm], ALU.mult), ["B1", "MASK"], ["YM"])
                                  if szm < 128:
                                      mm4(2, lambda h: hs(RTB[:], h), lambda h: hs(ZM[:], h), ["RTB", "ZM"])
                                  mm4(3, lambda h: hs(RB[:], h), lambda h: hs(YM[:], h), ["RB", "YM"])
                                  if szm < 128:
                                      P.op("dve", lambda e: e.tensor_tensor(RB[:], RB[:], B[2], ALU.subtract), ["B2", "RB"], ["RB"])
                                  P.op("dve", lambda e: e.tensor_tensor(RTB[:], RTB[:], B[3], ALU.subtract), ["B3", "RTB"], ["RTB"])
                              if stop == "s2e":
                                  P.barrier()
                                  ck("s2e")
                              mm4(4, lambda h: hs(RTB[:], h), lambda h: hs(VB[:], h), ["RTB", "VB"])
                              P.op("act", lambda e: e.copy(U0[:], B[4]), ["B4"], ["U0"])
                              mm4(5, lambda h: hs(KBE[:], h), lambda h: hs(RTB[:], h), ["KBE", "RTB"])
                              P.op("act", lambda e: e.copy(WTB[:], B[5]), ["B5"], ["WTB"])
                              mm4(1, lambda h: SEL[:, (d * 4 + h) * 128:(d * 4 + h + 1) * 128], lambda h: GC[:], ["SEL", "GC"])
                              P.op("act", lambda e: e.activation(W3[:], B[1], AF.Exp), ["B1"], ["W3"])
                              P.op("dve", lambda e: e.tensor_tensor(h4(QDT[:]), kq[:, 0:4, cs_], h4(W3[:]), ALU.mult), [kqk, "W3"], ["QDT"])
                              if stop == "s2f":
                                  P.barrier()
                                  ck("s2f")
                              mm4(6, lambda h: hs(WTB[:], h), lambda h: hs(SBF[:], h), ["WTB", "SBF"])
                              P.op("dve", lambda e: e.tensor_tensor(UB[:], U0[:], B[6], ALU.subtract), ["U0", "B6"], ["UB"])
                              if need_out:
                                  for h in range(4):
                                      P.op("pe", lambda e, h=h: e.matmul(hs(B[7], h), hs(SBF[:], h), hs(QDT[:], h), start=True, stop=False), ["SBF", "QDT"], ["B7"])
                                      P.op("pe", lambda e, h=h: e.matmul(hs(B[7], h), hs(UB[:], h), hs(PTB[:], h), start=False, stop=True), ["UB", "PTB"], ["B7"])
                                  if d == 0:
                                      P.op("act", lambda e: e.copy(OF[:, :, tk:tk + 128], h4(B[7])), ["B7"], [("OF", tk)])
                                  else:
                                      P.op("dve", lambda e: e.tensor_tensor(h4(W0[:]), h4(B[7]), OF[:, :, tk:tk + 128], ALU.add), ["B7", ("OF", tk)], ["W0"])
                                      P.op("act", lambda e: e.activation(SQO[:], W0[:], AF.Square), ["W0"], ["SQO"])
                                      mm4(4, lambda h: ONESB[:], lambda h: hs(SQO[:], h), ["ONESB", "SQO"])
                                      P.op("act", lambda e: e.activation(W1[:], B[4], AF.Sqrt, bias=EPSC, scale=1.0 / 128), ["B4"], ["W1"])
                                      P.op("dve", lambda e: e.reciprocal(W1[:], W1[:]), ["W1"], ["W1"])
                                      P.op("pool", lambda e: e.tensor_tensor(W0[:], W0[:], W1[:], ALU.mult), ["W0", "W1"], ["W0"])
                                      P.op("dve", lambda e: e.scalar_tensor_tensor(BIG[:, 4:8, tk:tk + 128], h4(W0[:]), GN, sz[:, :, cs_], ALU.mult, ALU.mult),
                                           ["W0", "LWg", szk], [("BIG", b)])
                              mm4(6, lambda h: hs(KDEC[:], h), lambda h: hs(UB[:], h), ["KDEC", "UB"])
                              for h in range(4):
                                  dh = d * 4 + h
                                  P.op("pool", lambda e, h=h, dh=dh: e.tensor_scalar(hs(STMP[:], h), hs(S32[:], h), CF[:, 24 + dh:25 + dh], None, ALU.mult),
                                       ["S32", "CF"], ["STMP"])
                              P.op("dve", lambda e: e.tensor_tensor(S32[:], STMP[:], B[6], ALU.add), ["STMP", "B6"], ["S32"])
                              P.op("act", lambda e: e.copy(SBF[:], S32[:]), ["S32"], ["SBF"])
                  P.barrier()

              ck("s2")
              with ExitStack() as s4:
                  WOUT = sb("WOUT", [128, 8, D], BF16, s4)
                  XS = sb("XS4", [128, 8, 512], F32, s4)
                  WS4 = sb("WS4", [128, 2, D], F32, s4)
                  SQB = sb("SQB4", [128, 8, 512], BF16, s4)
                  T0 = sb("T04", [128, 512], F32, s4)
                  for k in range(8):
                      s = k % 2
                      P.dma("sp", WS4[:, s, :], w_out[l, k * 128:(k + 1) * 128, :], [], [("WS4", s)], ("ws4", s))
                      P.op("dve" if s == 0 else "pool", lambda e, k=k, s=s: e.tensor_copy(WOUT[:, k, :], WS4[:, s, :]), [("WS4", s)], ["WOUT"])
                  pp = 0
                  for nb in range(NBLK):
                      if last and nb == 0:
                          continue
                      t0, nt = blk_range(nb)
                      v = 1 if nb == 0 else 0
                      P.dma("sp", XS[:, :, 0:nt], xsrc_v[:, :, t0:t0 + nt], [("x", nb)], ["XS4"], "xs4")
                      for jn in range(8):
                          bi = pp % 2
                          pp += 1
                          for kc in range(8):
                              P.op("pe", lambda e, kc=kc, jn=jn, bi=bi: e.matmul(
                                  pview(B[bi], nb, colmaj), WOUT[:, kc, jn * 128:(jn + 1) * 128], big_view(kc, nb, colmaj, True),
                                  start=(kc == 0), stop=(kc == 7)), ["WOUT"] + big_keys(nb, colmaj), [f"B{bi}"])
                          P.op("dve", lambda e, jn=jn, bi=bi, v=v: e.scalar_tensor_tensor(
                              XS[:, jn, 0:nt], B[bi][:, 0:nt], mcol(v, 16 + jn), XS[:, jn, 0:nt], ALU.mult, ALU.add),
                              [f"B{bi}", "XS4"], [("XSo", jn)])
                      okeys = [("XSo", jn) for jn in range(8)]
                      if not last:
                          P.dma("pool", xres_v[:, :, t0:t0 + nt], XS[:, :, 0:nt], okeys, [("x", nb)], "xst")
                          P.op("dve", lambda e: e.engine_nop(), [("x", nb)], ["XS4"])
                      else:
                          P.op("act", lambda e: e.activation(SQB[:], XS[:], AF.Square), okeys, ["SQB4"])
                          for k in range(8):
                              P.op("pe", lambda e, k=k: e.matmul(B[2], ONESB[:], SQB[:, k, :], start=(k == 0), stop=(k == 7)), ["SQB4", "ONESB"], ["B2"])
                          P.op("act", lambda e: e.activation(T0[:], B[2], AF.Sqrt, bias=EPSC, scale=1.0 / D), ["B2"], ["T04"])
                          P.op("dve", lambda e: e.reciprocal(T0[:], T0[:]), ["T04"], ["T04"])
                          for k in range(8):
                              P.op("dve", lambda e, k=k: e.scalar_tensor_tensor(XS[:, k, :], XS[:, k, :], FN[:, k:k + 1], T0[:], ALU.mult, ALU.mult),
                                   [("XSo", k), "T04", "FN", "SQB4"], [("XSf", k)])
                          fk = [("XSf", k) for k in range(8)]
                          P.dma("pool", out_v[:, :, t0 - NCTX:t0 - NCTX + nt], XS[:, :, 0:nt], fk, [("out", nb)], "ost")
                          P.op("dve", lambda e: e.engine_nop(), [("out", nb)], ["XS4"])
                  P.barrier()
          P.barrier()

    except _Stop:
        pass
    return nc


def _consts():
    i = np.arange(128)
    t, s = i[None, :], i[:, None]
    negi_f = np.where(t >= s, 0.0, BIGNEG)
    negi_b = np.where(t <= s, 0.0, BIGNEG)
    sm_f = (t > s).astype(np.float64)
    sm_b = (t < s).astype(np.float64)
    blk = lambda z: ((s // z) == (t // z)).astype(np.float64)
    blk16 = blk(16)
    off = {z: blk(z) * (1 - blk(z // 2)) for z in (32, 64, 128)}
    ident = np.eye(128)
    ms = [negi_f, negi_b, sm_f, sm_b, blk16, off[32], off[64], off[128], ident]
    cmask = np.concatenate([np.tile(m, (1, 4)) for m in ms], axis=1).astype(np.float32)
    sel = np.zeros((8, 2 * 1024 + 128), np.float32)
    for dh in range(8):
        sel[dh, dh * 128:(dh + 1) * 128] = 1.0
        sel[dh, 1024 + dh * 128:1024 + (dh + 1) * 128] = -1.0
    sel[:, 2048:] = 1.0
    return cmask, ident.astype(np.float32), sel


_NC_CACHE = {}


def make_in_maps(x, c, ctx, c_ctx, norm_w, w_mod, b_mod, w_in, conv_a, conv_qkv, a_log, dt_bias, gdn_norm, w_out, final_norm):
    f = lambda a: np.ascontiguousarray(np.asarray(a, dtype=np.float32))
    x, c, ctx, c_ctx = f(x), f(c), f(ctx), f(c_ctx)
    cmask, ident, sel = _consts()
    L = norm_w.shape[0]
    col = lambda a: np.ascontiguousarray(a.reshape(-1, 128).T)
    shared = {
        "w_mod": f(w_mod), "w_in": f(w_in), "w_out": f(w_out),
        "bmod": np.stack([col(f(b_mod)[l]) for l in range(L)]),
        "normw": np.stack([col(f(norm_w)[l]) for l in range(L)]),
        "conva": np.stack([np.ascontiguousarray(f(conv_a)[l].T.reshape(4, 128, 3).transpose(1, 0, 2).reshape(128, 12)) for l in range(L)]),
        "convq": np.stack([np.ascontiguousarray(f(conv_qkv)[l].T.reshape(12, 128, 3).transpose(1, 0, 2).reshape(128, 36)) for l in range(L)]),
        "alog": np.ascontiguousarray(f(a_log).reshape(L, 8, 1)),
        "dtb": np.ascontiguousarray(f(dt_bias).reshape(L, 8, 1)),
        "gnorm": np.ascontiguousarray(f(gdn_norm).reshape(L, 128, 1)),
        "fnorm": col(f(final_norm)),
        "cmask": cmask, "cident": ident, "csel": sel,
    }
    maps = []
    for b in range(x.shape[0]):
        m = dict(shared)
        m["xT"] = np.ascontiguousarray(np.concatenate([ctx[b], x[b]], axis=0).T)
        ccb = np.stack([col(c[b]), col(c_ctx)], axis=-1).reshape(128, 16)
        m["cc"] = np.ascontiguousarray(ccb)
        maps.append(m)
    return maps


def kernel(x, c, ctx, c_ctx, norm_w, w_mod, b_mod, w_in, conv_a, conv_qkv, a_log, dt_bias, gdn_norm, w_out, final_norm, _nlayers=DEPTH):
    maps = make_in_maps(x, c, ctx, c_ctx, norm_w, w_mod, b_mod, w_in, conv_a, conv_qkv, a_log, dt_bias, gdn_norm, w_out, final_norm)
    if _nlayers not in _NC_CACHE:
        _NC_CACHE[_nlayers] = build(_nlayers)
    nc = _NC_CACHE[_nlayers]
    res = run_bass_kernel_spmd(nc, maps, core_ids=list(range(len(maps))))
    out = np.stack([np.ascontiguousarray(r["outT"].T) for r in res.results], axis=0)
    return out.astype(np.float32)
```

```python
import numpy as np
from contextlib import ExitStack
import concourse.bass as bass
import concourse.mybir as mybir
from concourse.bass_utils import run_bass_kernel_spmd

F32 = mybir.dt.float32
BF16 = mybir.dt.bfloat16
AF = mybir.ActivationFunctionType
ALU = mybir.AluOpType

D = 1024
NCTX = 256
NLAT = 4096
NTOK = NCTX + NLAT
DPROJ = 4112
DEPTH = 4
EPS = 1e-6
NBLK = 9
BIGNEG = -30000.0
PSUM_KEYS = {f"B{i}" for i in range(8)}


def blk_range(b):
    if b == 0:
        return 0, NCTX
    return NCTX + (b - 1) * 512, 512


class _Stop(Exception):
    pass


class Prog:
    def __init__(self, nc, es):
        self.nc = nc
        self.es = es
        self.eng = {"pe": nc.tensor, "act": nc.scalar, "dve": nc.vector, "pool": nc.gpsimd, "sp": nc.sync}
        self.sem = {}
        self.cnt = {}
        self.epoch = 0
        self.state = {}
        self.waited = {}
        self.dsem = {}
        self.dcnt = {}
        self.all_sems = {}
        self.nops = 0
        self.stop_at = None
        self.new_epoch()

    def new_epoch(self):
        self.epoch += 1
        for e in ("pe", "act", "dve", "pool"):
            s = self.es.enter_context(self.nc.semaphore(f"s_{e}_{self.epoch}"))
            self.sem[e] = s
            self.cnt[e] = 0
            self.all_sems[id(s)] = s

    def _collect(self, engine, reads, writes):
        need = {}

        def add(ev):
            s, v, e = ev
            k = id(s)
            if k not in need or need[k][1] < v:
                need[k] = (s, v, e)

        for k in reads:
            st = self.state.get(k)
            if st is not None:
                for ev in st["w"].values():
                    add(ev)
                if k in PSUM_KEYS:
                    for ev in st["r"].values():
                        if ev[2] != engine:
                            add(ev)
        for k in writes:
            st = self.state.get(k)
            if st is not None:
                for ev in st["w"].values():
                    if ev[2] != engine:
                        add(ev)
                for ev in st["r"].values():
                    if ev[2] != engine:
                        add(ev)
        out = []
        wd = self.waited.setdefault(engine, {})
        for k, (s, v, e) in need.items():
            if wd.get(k, 0) >= v:
                continue
            wd[k] = v
            out.append((s, v))
        return out

    def _record(self, ev, reads, writes):
        for k in reads:
            st = self.state.setdefault(k, {"w": {}, "r": {}})
            st["r"][id(ev[0])] = ev
        for k in writes:
            st = self.state.setdefault(k, {"w": {}, "r": {}})
            st["w"][id(ev[0])] = ev

    def op(self, engine, fn, reads=(), writes=()):
        e = self.eng[engine]
        for s, v in self._collect(engine, reads, writes):
            e.wait_ge(s, v)
        inst = fn(e)
        self.cnt[engine] += 1
        inst.then_inc(self.sem[engine], 1)
        self._record((self.sem[engine], self.cnt[engine], engine), reads, writes)
        self.nops += 1
        if self.stop_at is not None and self.nops == self.stop_at:
            self.barrier()
            raise _Stop()

    def dma(self, queue, out, in_, reads, writes, slot):
        e = self.eng[queue]
        for s, v in self._collect("q_" + queue, reads, writes):
            e.wait_ge(s, v)
        if slot not in self.dsem:
            self.dsem[slot] = self.es.enter_context(self.nc.semaphore(f"d_{len(self.dsem)}"))
            self.dcnt[slot] = 0
        self.dcnt[slot] += 16
        e.dma_start(out=out, in_=in_).then_inc(self.dsem[slot], 16)
        self._record((self.dsem[slot], self.dcnt[slot], "dma_" + str(slot)), reads, writes)

    def barrier(self):
        evs = [(self.sem[e], self.cnt[e]) for e in ("pe", "act", "dve", "pool") if self.cnt[e] > 0]
        evs += [(self.dsem[s], self.dcnt[s]) for s in self.dsem]
        for en in ("pe", "act", "dve", "pool", "sp"):
            key = en if en != "sp" else "q_sp"
            wd = self.waited.setdefault(key, {})
            for s, v in evs:
                if en in self.sem and s is self.sem.get(en):
                    continue
                if wd.get(id(s), 0) >= v:
                    continue
                wd[id(s)] = v
                self.eng[en].wait_ge(s, v)
        wd = self.waited.setdefault("q_pool", {})
        for s, v in evs:
            wd[id(s)] = max(wd.get(id(s), 0), v)
        self.state = {}


def build(nlayers=DEPTH, stop=None):
    def ck(name):
        if stop == name:
            print("ck", name, "nops", P.nops)
            raise _Stop()
    nc = bass.Bass("TRN2", target_bir_lowering=False)
    dt_in = lambda n, shp, dt=F32: nc.dram_tensor(n, list(shp), dt, kind="ExternalInput").ap()
    xT = dt_in("xT", [D, NTOK])
    cc = dt_in("cc", [128, 16])
    w_mod = dt_in("w_mod", [DEPTH, D, 3 * D])
    bmod = dt_in("bmod", [DEPTH, 128, 24])
    normw = dt_in("normw", [DEPTH, 128, 8])
    w_in = dt_in("w_in", [DEPTH, D, DPROJ])
    conva = dt_in("conva", [DEPTH, 128, 12])
    convq = dt_in("convq", [DEPTH, 128, 36])
    alog = dt_in("alog", [DEPTH, 8, 1])
    dtb = dt_in("dtb", [DEPTH, 8, 1])
    gnorm = dt_in("gnorm", [DEPTH, 128, 1])
    w_out = dt_in("w_out", [DEPTH, D, D])
    fnorm = dt_in("fnorm", [128, 8])
    cmask = dt_in("cmask", [128, 9 * 512])
    cident = dt_in("cident", [128, 128])
    csel = dt_in("csel", [8, 2 * 1024 + 128])
    outT = nc.dram_tensor("outT", [D, NLAT], F32, kind="ExternalOutput").ap()
    xres = nc.dram_tensor("xres", [D, NTOK], F32).ap()
    kqv_s = nc.dram_tensor("kqv_s", [NBLK, 128, 12, 512], BF16).ap()
    ya_s = nc.dram_tensor("ya_s", [NBLK, 128, 4, 512], BF16).ap()
    szb_s = nc.dram_tensor("szb_s", [NBLK, 128, 4, 512], BF16).ap()
    rows_s = nc.dram_tensor("rows_s", [NBLK, 8, 2, 512], F32).ap()

    es = ExitStack()
    try:
      with es:
          P = Prog(nc, es)
          if isinstance(stop, int):
              P.stop_at = stop
          _uniq = [0]

          def sb(n, shp, dt=F32, st=es):
              _uniq[0] += 1
              return st.enter_context(nc.sbuf_tensor(f"{n}_{_uniq[0]}", list(shp), dt))
          BIG = sb("BIG", [128, 8, NTOK], BF16)
          MASK = sb("MASK", [128, 9, 512], BF16)
          IDF = sb("IDF", [128, 128], F32)
          IDB = sb("IDB", [128, 128], BF16)
          ONESB = sb("ONESB", [128, 128], BF16)
          SEL = sb("SEL", [8, 2 * 1024 + 128], F32)
          MODS = sb("MODS", [128, DEPTH, 2, 24], F32)
          CC = sb("CC", [128, 16], F32)
          FN = sb("FN", [128, 8], F32)
          LW = sb("LW", [128, 64], F32)
          L8 = sb("L8", [8, 4], F32)
          DUM = sb("DUM", [128, 2], F32)
          EPST = sb("EPST", [128, 1], F32)
          EPSC = EPST[:, 0:1]
          banks = [es.enter_context(nc.psum_tensor(f"B{i}", [128, 512], F32)) for i in range(8)]
          B = [b[:] for b in banks]
          Bbf = [b[:].bitcast(BF16) for b in banks]

          NEGI = [MASK[:, 0, :], MASK[:, 1, :]]
          SM = [MASK[:, 2, :], MASK[:, 3, :]]
          BLK16 = MASK[:, 4, :]
          OFF = {32: MASK[:, 5, :], 64: MASK[:, 6, :], 128: MASK[:, 7, :]}
          ID4 = MASK[:, 8, :]
          I8 = IDF[0:8, 0:8]
          ONES8 = SEL[:, 2048:2176]

          def h4(ap):
              return ap.rearrange("p (h t) -> p h t", h=4)

          with ExitStack() as st0:
              MST = sb("MST", [128, 9 * 512], F32, st0)
              WST = sb("WST", [128, 2, 4096], F32, st0)
              SC = sb("SC", [128, 16], F32, st0)
              BM = sb("BM", [128, 24], F32, st0)
              NW = sb("NW", [128, 8], F32, st0)
              P.dma("sp", MST[:], cmask, [], ["MST"], "c0")
              P.dma("sp", IDF[:], cident, [], ["IDF"], "c1")
              P.dma("sp", SEL[:], csel, [], ["SEL"], "c2")
              P.dma("sp", CC[:], cc, [], ["CC"], "c3")
              P.dma("sp", FN[:], fnorm, [], ["FN"], "c4")
              P.op("dve", lambda e: e.tensor_copy(MASK[:].rearrange("p a b -> p (a b)"), MST[:]), ["MST"], ["MASK"])
              P.op("dve", lambda e: e.tensor_copy(IDB[:], IDF[:]), ["IDF"], ["IDB"])
              P.op("dve", lambda e: e.memset(ONESB[:], 1.0), [], ["ONESB"])
              P.op("dve", lambda e: e.memset(DUM[:], 0.0), [], ["DUM"])
              P.op("dve", lambda e: e.memset(EPST[:], EPS), [], ["EPST"])
              P.op("act", lambda e: e.activation(SC[:], CC[:], AF.Silu), ["CC"], ["SC"])
              for l in range(nlayers):
                  P.dma("sp", BM[:], bmod[l], [], ["BM"], "c5")
                  P.dma("sp", NW[:], normw[l], [], ["NW"], "c6")
                  for jg in range(6):
                      s = jg % 2
                      P.dma("sp", WST[:, s, :].rearrange("p (k n) -> p k n", k=8),
                            w_mod[l, :, jg * 512:(jg + 1) * 512].rearrange("(k p) n -> p k n", p=128), [], [("WST", s)], ("wst", s))
                      for jj in range(4):
                          j = jg * 4 + jj
                          for k in range(8):
                              P.op("pe", lambda e, j=j, jj=jj, k=k, s=s: e.matmul(
                                  B[0][:, 2 * j:2 * j + 2], WST[:, s, k * 512 + jj * 128:k * 512 + (jj + 1) * 128],
                                  SC[:].rearrange("p (k v) -> p k v", v=2)[:, k, :], start=(k == 0), stop=(k == 7)),
                                  [("WST", s), "SC"], ["B0"])
                  for v in range(2):
                      P.op("dve", lambda e, v=v, l=l: e.tensor_tensor(
                          MODS[:, l, v, :], B[0][:, 0:48].rearrange("p (j v) -> p j v", v=2)[:, :, v], BM[:], ALU.add),
                          ["B0", "BM"], [("MODS", l)])
                      P.op("dve", lambda e, v=v, l=l: e.scalar_tensor_tensor(
                          MODS[:, l, v, 8:16], MODS[:, l, v, 8:16], 1.0, NW[:], ALU.add, ALU.mult),
                          [("MODS", l), "NW"], [("MODS", l)])
              P.barrier()
          ck("s0")

          for l in range(nlayers):
              P.new_epoch()
              colmaj = (l % 2 == 1)
              last = (l == nlayers - 1)
              xsrc = xT if l == 0 else xres
              xsrc_v = xsrc.rearrange("(k p) t -> p k t", p=128)
              xres_v = xres.rearrange("(k p) t -> p k t", p=128)
              out_v = outT.rearrange("(k p) t -> p k t", p=128)

              def mcol(v, j, l=l):
                  return MODS[:, l, v, j:j + 1]

              P.dma("sp", LW[:, 0:12], conva[l], [], ["LWa"], "c7")
              P.dma("sp", LW[:, 12:48], convq[l], [], ["LWq"], "c8")
              P.dma("sp", LW[:, 48:49], gnorm[l], [], ["LWg"], "c9")
              P.dma("sp", L8[:, 0:1], alog[l], [], ["L8a"], "c10")
              P.dma("sp", L8[:, 1:2], dtb[l], [], ["L8"], "c11")
              P.op("act", lambda e: e.activation(L8[:, 2:3], L8[:, 0:1], AF.Exp), ["L8a"], ["L8"])
              P.op("dve", lambda e: e.tensor_scalar(L8[:, 2:3], L8[:, 2:3], -1.0, None, ALU.mult), ["L8"], ["L8"])
              CA = lambda j, tap: LW[:, j * 3 + tap: j * 3 + tap + 1]
              CQ = lambda j, tap: LW[:, 12 + j * 3 + tap: 12 + j * 3 + tap + 1]
              GN = LW[:, 48:49]

              def big_keys(b, permuted):
                  if b == 0 or not permuted:
                      return [("BIG", b)]
                  return [("BIG", i) for i in range(1, 9)]

              def big_view(kc, b, permuted, rowmajor_of_colscan):
                  t0, nt = blk_range(b)
                  if b == 0 or not permuted:
                      return BIG[:, kc, t0:t0 + nt]
                  lat = BIG[:, kc, NCTX:NTOK]
                  if rowmajor_of_colscan:
                      v = lat.rearrange("p (c r) -> p r c", r=64)
                  else:
                      v = lat.rearrange("p (r c) -> p c r", c=64)
                  return v[:, (b - 1) * 8:(b - 1) * 8 + 8, :]

              def pview(ap, b, permuted):
                  t0, nt = blk_range(b)
                  if b == 0 or not permuted:
                      return ap[:, 0:nt]
                  return ap.rearrange("p (a b) -> p a b", b=64)

              with ExitStack() as s1:
                  WIN = sb("WIN", [128, 8, DPROJ], BF16, s1)
                  XSF = sb("XS", [128, 4112], F32, s1)
                  XS = XSF[:, 0:4096].rearrange("p (k t) -> p k t", k=8)
                  T0 = sb("T0", [128, 512], F32, s1)
                  T1 = sb("T1", [128, 512], F32, s1)
                  T2 = sb("T2", [128, 512], F32, s1)
                  SQQ = sb("SQQ", [128, 512], BF16, s1)
                  KQVB = sb("KQVB", [128, 12, 512], BF16, s1)
                  SQB = KQVB[:, 0:8, :]
                  YAB = sb("YAB", [128, 4, 512], BF16, s1)
                  SZBB = sb("SZBB", [128, 4, 512], BF16, s1)
                  ROWB = sb("ROWB", [8, 2, 512], F32, s1)
                  HP = sb("HP", [128, 8, 512], BF16, s1) if colmaj else None
                  XSW = XSF[:]
                  i = 0
                  for k in range(8):
                      for hf in range(2):
                          s = i % 2
                          c0 = hf * 2056
                          P.dma("sp", XSW[:, s * 2056:(s + 1) * 2056], w_in[l, k * 128:(k + 1) * 128, c0:c0 + 2056],
                                [], [("XS", s)], ("xs", s))
                          eng = ("dve", "pool", "act")[i % 3]
                          if eng == "act":
                              P.op("act", lambda e, k=k, s=s, c0=c0: e.copy(WIN[:, k, c0:c0 + 2056], XSW[:, s * 2056:(s + 1) * 2056]),
                                   [("XS", s)], ["WIN"])
                          else:
                              P.op(eng, lambda e, k=k, s=s, c0=c0: e.tensor_copy(WIN[:, k, c0:c0 + 2056], XSW[:, s * 2056:(s + 1) * 2056]),
                                   [("XS", s)], ["WIN"])
                          i += 1
                  if stop == "s1w":
                      P.barrier()
                      ck("s1w")
                  for nb in range(NBLK):
                      t0, nt = blk_range(nb)
                      v = 1 if nb == 0 else 0
                      P.dma("sp", XS[:, :, 0:nt], xsrc_v[:, :, t0:t0 + nt], [("x", nb)], [("XS", 0), ("XS", 1)], ("xs", 0))
                      P.op("act", lambda e, nt=nt: e.activation(SQB[:, :, 0:nt], XS[:, :, 0:nt], AF.Square),
                           [("XS", 0), ("XS", 1)], ["KQVB"])
                      for k in range(8):
                          P.op("pe", lambda e, k=k, nt=nt: e.matmul(B[2][:, 0:nt], ONESB[:], SQB[:, k, 0:nt], start=(k == 0), stop=(k == 7)),
                               ["KQVB", "ONESB"], ["B2"])
                      P.op("act", lambda e, nt=nt: e.activation(T0[:, 0:nt], B[2][:, 0:nt], AF.Ln, bias=EPSC, scale=1.0 / D), ["B2"], ["T0"])
                      P.op("act", lambda e, nt=nt: e.activation(T0[:, 0:nt], T0[:, 0:nt], AF.Exp, scale=-0.5), ["T0"], ["T0"])
                      for k in range(8):
                          eng = "dve" if k % 2 == 0 else "pool"
                          P.op(eng, lambda e, k=k, nt=nt: e.tensor_tensor(XS[:, k, 0:nt], XS[:, k, 0:nt], T0[:, 0:nt], ALU.mult),
                               [("XS", 0), ("XS", 1), "T0", "KQVB"], [("XSn", k)])
                          P.op("act", lambda e, k=k, nt=nt, t0=t0, v=v: e.activation(
                              BIG[:, k, t0:t0 + nt], XS[:, k, 0:nt], AF.Identity, bias=mcol(v, k), scale=mcol(v, 8 + k)),
                              [("XSn", k)], [("BIG", nb)])
                      P.op("act", lambda e: e.activation(DUM[:, 0:1], DUM[:, 1:2], AF.Copy), [("BIG", nb)] + [("XSn", k) for k in range(8)], [("XS", 0), ("XS", 1)])

                  if stop == "s1p":
                      P.barrier()
                      ck("s1p")
                  pp = [0]
                  PROJ_BANKS = [0, 1, 3, 4, 5, 6, 7]

                  def proj(b, c0, m):
                      bi = PROJ_BANKS[pp[0] % len(PROJ_BANKS)]
                      pp[0] += 1
                      t0, nt = blk_range(b)
                      for k in range(8):
                          if colmaj and b > 0:
                              P.op("pe", lambda e, k=k, bi=bi: e.matmul(
                                  B[bi][0:m, 0:nt], WIN[:, k, c0:c0 + m], HP[:, k, :],
                                  start=(k == 0), stop=(k == 7)), ["WIN", "HP"], [f"B{bi}"])
                          else:
                              P.op("pe", lambda e, k=k, bi=bi: e.matmul(
                                  B[bi][0:m, 0:nt], WIN[:, k, c0:c0 + m], BIG[:, k, t0:t0 + nt],
                                  start=(k == 0), stop=(k == 7)), ["WIN"] + big_keys(b, False), [f"B{bi}"])
                      return bi

                  def conv(dst, dkey, src, skey, w, nt, seg):
                      P.op("dve", lambda e: e.tensor_scalar(dst[:, 0:nt], src[:, 0:nt], w(1), None, ALU.mult), [skey, "LWa", "LWq"], [dkey])
                      dv = dst[:, 0:nt].rearrange("p (a b) -> p a b", b=seg)
                      sv = src[:, 0:nt].rearrange("p (a b) -> p a b", b=seg)
                      P.op("dve", lambda e: e.scalar_tensor_tensor(dv[:, :, 1:seg], sv[:, :, 0:seg - 1], w(0), dv[:, :, 1:seg], ALU.mult, ALU.add),
                           [skey, dkey, "LWa", "LWq"], [dkey])
                      P.op("dve", lambda e: e.scalar_tensor_tensor(dv[:, :, 0:seg - 1], sv[:, :, 1:seg], w(2), dv[:, :, 0:seg - 1], ALU.mult, ALU.add),
                           [skey, dkey, "LWa", "LWq"], [dkey])

                  for b in range(NBLK):
                      t0, nt = blk_range(b)
                      seg = NCTX if b == 0 else 64
                      if colmaj and b > 0:
                          for k in range(8):
                              eng = "pool" if k % 2 == 0 else "act"
                              if eng == "pool":
                                  P.op("pool", lambda e, k=k: e.tensor_copy(HP[:, k, :].rearrange("p (a b) -> p a b", b=64), big_view(k, b, True, False)),
                                       big_keys(b, True), ["HP"])
                              else:
                                  P.op("act", lambda e, k=k: e.copy(HP[:, k, :].rearrange("p (a b) -> p a b", b=64), big_view(k, b, True, False)),
                                       big_keys(b, True), ["HP"])
                      for jj in range(4):
                          bi = proj(b, jj * 128, 128)
                          P.op("act", lambda e, bi=bi: e.copy(T0[:, 0:nt], B[bi][:, 0:nt]), [f"B{bi}"], ["T0"])
                          bi = proj(b, 1024 + jj * 128, 128)
                          P.op("dve", lambda e, bi=bi: e.tensor_tensor(T0[:, 0:nt], B[bi][:, 0:nt], T0[:, 0:nt], ALU.mult), [f"B{bi}", "T0"], ["T0"])
                          conv(T1, "T1", T0, "T0", lambda tap, jj=jj: CA(jj, tap), nt, seg)
                          bi = proj(b, 512 + jj * 128, 128)
                          P.op("dve", lambda e, bi=bi: e.tensor_tensor(T1[:, 0:nt], B[bi][:, 0:nt], T1[:, 0:nt], ALU.mult), [f"B{bi}", "T1"], ["T1"])
                          bi = proj(b, 1536 + jj * 128, 128)
                          P.op("act", lambda e, bi=bi: e.activation(T2[:, 0:nt], B[bi][:, 0:nt], AF.Silu), [f"B{bi}"], ["T2"])
                          P.op("pool", lambda e, jj=jj: e.tensor_tensor(YAB[:, jj, 0:nt], T1[:, 0:nt], T2[:, 0:nt], ALU.mult), ["T1", "T2"], ["YAB"])
                      for idx in range(12):
                          bi = proj(b, 2048 + idx * 128, 128)
                          conv(T1, "T1", B[bi], f"B{bi}", lambda tap, idx=idx: CQ(idx, tap), nt, seg)
                          if idx >= 8:
                              P.op("act", lambda e, idx=idx: e.activation(KQVB[:, idx, 0:nt], T1[:, 0:nt], AF.Silu), ["T1"], ["KQVB"])
                              continue
                          P.op("act", lambda e: e.activation(T2[:, 0:nt], T1[:, 0:nt], AF.Silu), ["T1"], ["T2"])
                          P.op("act", lambda e: e.activation(SQQ[:, 0:nt], T2[:, 0:nt], AF.Square), ["T2"], ["SQQ"])
                          P.op("pe", lambda e: e.matmul(B[2][:, 0:nt], ONESB[:], SQQ[:, 0:nt], start=True, stop=True), ["SQQ", "ONESB"], ["B2"])
                          P.op("act", lambda e: e.activation(T0[:, 0:nt], B[2][:, 0:nt], AF.Ln, bias=EPSC, scale=1.0), ["B2"], ["T0"])
                          P.op("act", lambda e: e.activation(T0[:, 0:nt], T0[:, 0:nt], AF.Exp, scale=-0.5), ["T0"], ["T0"])
                          sc = (128.0 ** -0.5) if idx < 4 else 1.0
                          P.op("dve", lambda e, idx=idx, sc=sc: e.scalar_tensor_tensor(KQVB[:, idx, 0:nt], T2[:, 0:nt], sc, T0[:, 0:nt], ALU.mult, ALU.mult),
                               ["T2", "T0"], ["KQVB"])
                      for h in range(4):
                          bi = proj(b, 3584 + h * 128, 128)
                          P.op("act", lambda e, bi=bi, h=h: e.activation(SZBB[:, h, 0:nt], B[bi][:, 0:nt], AF.Silu), [f"B{bi}"], ["SZBB"])
                      bi = proj(b, 4096, 8)
                      P.op("act", lambda e, bi=bi: e.activation(ROWB[:, 1, 0:nt], B[bi][0:8, 0:nt], AF.Sigmoid), [f"B{bi}"], ["ROWB"])
                      bi = proj(b, 4104, 8)
                      P.op("act", lambda e, bi=bi: e.activation(ROWB[:, 0, 0:nt], B[bi][0:8, 0:nt], AF.Exp, bias=L8[:, 1:2]), [f"B{bi}", "L8"], ["ROWB"])
                      P.op("act", lambda e: e.activation(ROWB[:, 0, 0:nt], ROWB[:, 0, 0:nt], AF.Ln, bias=1.0), ["ROWB"], ["ROWB"])
                      P.op("dve", lambda e: e.tensor_scalar(ROWB[:, 0, 0:nt], ROWB[:, 0, 0:nt], L8[:, 2:3], None, ALU.mult), ["ROWB", "L8"], ["ROWB"])
                      P.dma("pool", kqv_s[b, :, :, 0:nt], KQVB[:, :, 0:nt], ["KQVB"], [("kqv", b)], "sp0")
                      P.dma("pool", ya_s[b, :, :, 0:nt], YAB[:, :, 0:nt], ["YAB"], [("ya", b)], "sp1")
                      P.dma("pool", szb_s[b, :, :, 0:nt], SZBB[:, :, 0:nt], ["SZBB"], [("szb", b)], "sp2")
                      P.dma("pool", rows_s[b, :, :, 0:nt], ROWB[:, :, 0:nt], ["ROWB"], [("rows", b)], "sp3")
                  P.barrier()

              ck("s1")
              with ExitStack() as s2:
                  OF = sb("OF", [128, 4, NTOK], BF16, s2)
                  KQ = [sb(f"KQ{i}", [128, 12, 512], BF16, s2) for i in range(2)]
                  RW = [sb(f"RW{i}", [8, 2, 512], F32, s2) for i in range(2)]
                  SZ = [sb(f"SZ{i}", [128, 4, 512], BF16, s2) for i in range(2)]
                  GC = sb("GC", [8, 128], F32, s2)
                  CS = sb("CS", [8, 128], F32, s2)
                  DG = sb("DG", [8, 8], F32, s2)
                  COLS = sb("COLS", [128, 24], F32, s2)
                  CF = sb("CF", [128, 32], F32, s2)
                  W0 = sb("W0", [128, 512], F32, s2)
                  W1 = sb("W1", [128, 512], F32, s2)
                  W2 = sb("W2", [128, 512], F32, s2)
                  W3 = sb("W3", [128, 512], F32, s2)
                  PTB = sb("PTB", [128, 512], BF16, s2)
                  ATB = sb("ATB", [128, 512], BF16, s2)
                  AB = sb("AB", [128, 512], BF16, s2)
                  PN = [sb(f"PN{i}", [128, 512], BF16, s2) for i in range(2)]
                  PTN = [sb(f"PTN{i}", [128, 512], BF16, s2) for i in range(2)]
                  RB = sb("RB", [128, 512], BF16, s2)
                  RTB = sb("RTB", [128, 512], BF16, s2)
                  ZM = sb("ZM", [128, 512], BF16, s2)
                  YM = sb("YM", [128, 512], BF16, s2)
                  KBE = sb("KBE", [128, 512], BF16, s2)
                  KDEC = sb("KDEC", [128, 512], BF16, s2)
                  VB = sb("VB", [128, 512], BF16, s2)
                  U0 = sb("U0", [128, 512], F32, s2)
                  WTB = sb("WTB", [128, 512], BF16, s2)
                  QDT = sb("QDT", [128, 512], BF16, s2)
                  UB = sb("UB", [128, 512], BF16, s2)
                  S32 = sb("S32", [128, 512], F32, s2)
                  STMP = sb("STMP", [128, 512], F32, s2)
                  SBF = sb("SBF", [128, 512], BF16, s2)
                  SQO = sb("SQO", [128, 512], BF16, s2)

                  def hs(ap, h):
                      return ap[:, h * 128:(h + 1) * 128]

                  def mm4(bank, lhs, rhs, rk, start=True, stop=True):
                      for h in range(4):
                          P.op("pe", lambda e, h=h: e.matmul(hs(B[bank], h), lhs(h), rhs(h), start=start, stop=stop), rk, [f"B{bank}"])

                  for d in range(2):
                      P.op("pool", lambda e: e.memset(S32[:], 0.0), [], ["S32"])
                      P.op("pool", lambda e: e.memset(SBF[:], 0.0), [], ["SBF"])
                      border = list(range(NBLK)) if d == 0 else [0] + list(range(8, 0, -1))
                      for bi_, b in enumerate(border):
                          t0, nt = blk_range(b)
                          sl = bi_ % 2
                          P.dma("sp", KQ[sl][:, :, 0:nt], kqv_s[b, :, :, 0:nt], [("kqv", b)], [("KQ", sl)], ("kq", sl))
                          P.dma("sp", RW[sl][:, :, 0:nt], rows_s[b, :, :, 0:nt], [("rows", b)], [("RW", sl)], ("rw", sl))
                          need_out = not (last and b == 0)
                          if d == 1 and need_out:
                              P.dma("sp", SZ[sl][:, :, 0:nt], szb_s[b, :, :, 0:nt], [("szb", b)], [("SZ", sl)], ("sz", sl))
                              P.dma("sp", BIG[:, 0:4, t0:t0 + nt], ya_s[b, :, :, 0:nt], [("ya", b)], [("BIG", b)], ("yal", sl))
                          nch = nt // 128
                          chs = list(range(nch)) if d == 0 else list(range(nch - 1, -1, -1))
                          kq, rw, sz = KQ[sl], RW[sl], SZ[sl]
                          kqk, rwk, szk = ("KQ", sl), ("RW", sl), ("SZ", sl)
                          for ch in chs:
                              c0 = ch * 128
                              cs_ = slice(c0, c0 + 128)
                              tk = t0 + c0
                              qT = lambda h: kq[:, h, cs_]
                              kT = lambda h: kq[:, 4 + h, cs_]
                              vT = lambda h: kq[:, 8 + h, cs_]
                              g8 = rw[:, 0, cs_]
                              b8 = rw[:, 1, cs_]
                              P.op("dve", lambda e: e.tensor_tensor_scan(CS[:], ONES8, g8, 0.0, ALU.mult, ALU.add), [rwk, "SEL"], ["CS"])
                              if d == 0:
                                  P.op("dve", lambda e: e.tensor_copy(GC[:], CS[:]), ["CS"], ["GC"])
                              else:
                                  P.op("dve", lambda e: e.scalar_tensor_tensor(GC[:], CS[:], -1.0, g8, ALU.mult, ALU.add), ["CS", rwk], ["GC"])
                                  P.op("dve", lambda e: e.tensor_scalar(GC[:], GC[:], CS[:, 127:128], None, ALU.add), ["GC", "CS"], ["GC"])
                              P.op("dve", lambda e: e.tensor_scalar(DG[:], I8, CS[:, 127:128], None, ALU.mult), ["CS", "IDF"], ["DG"])
                              P.op("pe", lambda e: e.matmul(B[4][:, 0:8], GC[:], I8, start=True, stop=True), ["GC", "IDF"], ["B4"])
                              P.op("pe", lambda e: e.matmul(B[4][:, 8:16], b8, I8, start=True, stop=True), [rwk, "IDF"], ["B4"])
                              P.op("pe", lambda e: e.matmul(B[4][:, 16:24], ONES8, DG[:], start=True, stop=True), ["SEL", "DG"], ["B4"])
                              P.op("act", lambda e: e.copy(COLS[:], B[4][:, 0:24]), ["B4"], ["COLS"])
                              P.op("act", lambda e: e.activation(CF[:, 0:8], COLS[:, 0:8], AF.Exp), ["COLS"], ["CF"])
                              P.op("dve", lambda e: e.tensor_tensor(CF[:, 8:16], CF[:, 0:8], COLS[:, 8:16], ALU.mult), ["CF", "COLS"], ["CF"])
                              P.op("dve", lambda e: e.tensor_tensor(CF[:, 16:24], COLS[:, 16:24], COLS[:, 0:8], ALU.subtract), ["COLS", "CF"], ["CF"])
                              P.op("act", lambda e: e.activation(CF[:, 16:24], CF[:, 16:24], AF.Exp), ["CF"], ["CF"])
                              P.op("act", lambda e: e.activation(CF[:, 24:32], COLS[:, 16:24], AF.Exp), ["COLS", "CF"], ["CF"])
                              if stop == "s2a":
                                  P.barrier()
                                  ck("s2a")
                              for h in range(4):
                                  P.op("pe", lambda e, h=h: e.transpose(Bbf[5][:, h * 128:(h + 1) * 128], kT(h), IDB[:]), [kqk, "IDB"], ["B5"])
                              for h in range(4):
                                  P.op("pe", lambda e, h=h: e.transpose(Bbf[5][:, 512 + h * 128:512 + (h + 1) * 128], vT(h), IDB[:]), [kqk, "IDB"], ["B5"])
                              for h in range(4):
                                  dh = d * 4 + h
                                  P.op("dve", lambda e, h=h, dh=dh: e.tensor_scalar(hs(KBE[:], h), Bbf[5][:, h * 128:(h + 1) * 128], CF[:, 8 + dh:9 + dh], None, ALU.mult),
                                       ["B5", "CF"], ["KBE"])
                                  P.op("act", lambda e, h=h, dh=dh: e.activation(hs(KDEC[:], h), Bbf[5][:, h * 128:(h + 1) * 128], AF.Identity, scale=CF[:, 16 + dh:17 + dh]),
                                       ["B5", "CF"], ["KDEC"])
                                  P.op("dve", lambda e, h=h, dh=dh: e.tensor_scalar(hs(VB[:], h), Bbf[5][:, 512 + h * 128:512 + (h + 1) * 128], COLS[:, 8 + dh:9 + dh], None, ALU.mult),
                                       ["B5", "COLS"], ["VB"])
                              if stop == "s2b":
                                  P.barrier()
                                  ck("s2b")
                              for h in range(4):
                                  dh = d * 4 + h
                                  P.op("pe", lambda e, h=h, dh=dh: e.matmul(hs(B[0], h), SEL[:, dh * 128:(dh + 1) * 128], GC[:], start=True, stop=False), ["SEL", "GC"], ["B0"])
                                  P.op("pe", lambda e, h=h, dh=dh: e.matmul(hs(B[0], h), GC[:], SEL[:, 1024 + dh * 128:1024 + (dh + 1) * 128], start=False, stop=True), ["SEL", "GC"], ["B0"])
                              P.op("dve", lambda e: e.tensor_tensor(W0[:], B[0], NEGI[d], ALU.add), ["B0", "MASK"], ["W0"])
                              P.op("act", lambda e: e.activation(W1[:], W0[:], AF.Exp), ["W0"], ["W1"])
                              mm4(1, lambda h: SEL[:, (d * 4 + h) * 128:(d * 4 + h + 1) * 128], lambda h: b8, ["SEL", rwk])
                              P.op("dve", lambda e: e.tensor_tensor(W2[:], B[1], SM[d], ALU.mult), ["B1", "MASK"], ["W2"])
                              P.op("pool", lambda e: e.tensor_tensor(W2[:], W2[:], W1[:], ALU.mult), ["W2", "W1"], ["W2"])
                              mm4(2, kT, kT, [kqk])
                              mm4(3, kT, qT, [kqk])
                              P.op("dve", lambda e: e.tensor_tensor(PTB[:], B[3], W1[:], ALU.mult), ["B3", "W1"], ["PTB"])
                              P.op("dve", lambda e: e.tensor_tensor(ATB[:], B[2], W2[:], ALU.mult), ["B2", "W2"], ["ATB"])
                              for h in range(4):
                                  P.op("pe", lambda e, h=h: e.transpose(Bbf[6][:, h * 128:(h + 1) * 128], hs(ATB[:], h), IDB[:]), ["ATB", "IDB"], ["B6"])
                              P.op("act", lambda e: e.copy(AB[:], Bbf[6][:, 0:512]), ["B6"], ["AB"])
                              if stop == "s2c":
                                  P.barrier()
                                  ck("s2c")
                              P.op("dve", lambda e: e.tensor_tensor(PTN[0][:], Bbf[6][:, 0:512], BLK16, ALU.mult), ["B6", "MASK"], [("PTN", 0)])
                              P.op("pool", lambda e: e.tensor_tensor(PN[0][:], ATB[:], BLK16, ALU.mult), ["ATB", "MASK"], [("PN", 0)])
                              P.op("pool", lambda e: e.tensor_tensor(RB[:], ID4, PTN[0][:], ALU.subtract), ["MASK", ("PTN", 0)], ["RB"])
                              P.op("pool", lambda e: e.tensor_tensor(RTB[:], ID4, PN[0][:], ALU.subtract), ["MASK", ("PN", 0)], ["RTB"])
                              cur = 0
                              for it in range(3):
                                  nx = 1 - cur
                                  mm4(0, lambda h: hs(PTN[cur][:], h), lambda h: hs(PN[cur][:], h), [("PTN", cur), ("PN", cur)])
                                  mm4(1, lambda h: hs(PN[cur][:], h), lambda h: hs(PTN[cur][:], h), [("PTN", cur), ("PN", cur)])
                                  P.op("act", lambda e, nx=nx: e.copy(PN[nx][:], B[0]), ["B0"], [("PN", nx)])
                                  P.op("dve", lambda e, nx=nx: e.tensor_copy(PTN[nx][:], B[1]), ["B1"], [("PTN", nx)])
                                  mm4(2, lambda h: hs(PTN[nx][:], h), lambda h: hs(RTB[:], h), [("PTN", nx), "RTB"])
                                  mm4(3, lambda h: hs(PN[nx][:], h), lambda h: hs(RB[:], h), [("PN", nx), "RB"])
                                  P.op("dve", lambda e: e.tensor_tensor(RTB[:], B[2], RTB[:], ALU.add), ["B2", "RTB"], ["RTB"])
                                  P.op("dve", lambda e: e.tensor_tensor(RB[:], B[3], RB[:], ALU.add), ["B3", "RB"], ["RB"])
                                  cur = nx
                              if stop == "s2d":
                                  P.barrier()
                                  ck("s2d")
                              for szm in (32, 64, 128):
                                  if szm < 128:
                                      mm4(0, lambda h: hs(ATB[:], h), lambda h: hs(RB[:], h), ["ATB", "RB"])
                                      P.op("dve", lambda e, szm=szm: e.tensor_tensor(ZM[:], B[0], OFF[szm], ALU.mult), ["B0", "MASK"], ["ZM"])
                                  mm4(1, lambda h: hs(AB[:], h), lambda h: hs(RTB[:], h), ["AB", "RTB"])
                                  P.op("dve", lambda e, szm=szm: e.tensor_tensor(YM[:], B[1], OFF[szm], ALU.mult), ["B1", "MASK"], ["YM"])
                                  if szm < 128:
                                      mm4(2, lambda h: hs(RTB[:], h), lambda h: hs(ZM[:], h), ["RTB", "ZM"])
                                  mm4(3, lambda h: hs(RB[:], h), lambda h: hs(YM[:], h), ["RB", "YM"])
                                  if szm < 128:
                                      P.op("dve", lambda e: e.tensor_tensor(RB[:], RB[:], B[2], ALU.subtract), ["B2", "RB"], ["RB"])
                                  P.op("dve", lambda e: e.tensor_tensor(RTB[:], RTB[:], B[3], ALU.subtract), ["B3", "RTB"], ["RTB"])
                              if stop == "s2e":
                                  P.barrier()
                                  ck("s2e")
                              mm4(4, lambda h: hs(RTB[:], h), lambda h: hs(VB[:], h), ["RTB", "VB"])
                              P.op("act", lambda e: e.copy(U0[:], B[4]), ["B4"], ["U0"])
                              mm4(5, lambda h: hs(KBE[:], h), lambda h: hs(RTB[:], h), ["KBE", "RTB"])
                              P.op("act", lambda e: e.copy(WTB[:], B[5]), ["B5"], ["WTB"])
                              mm4(1, lambda h: SEL[:, (d * 4 + h) * 128:(d * 4 + h + 1) * 128], lambda h: GC[:], ["SEL", "GC"])
                              P.op("act", lambda e: e.activation(W3[:], B[1], AF.Exp), ["B1"], ["W3"])
                              P.op("dve", lambda e: e.tensor_tensor(h4(QDT[:]), kq[:, 0:4, cs_], h4(W3[:]), ALU.mult), [kqk, "W3"], ["QDT"])
                              if stop == "s2f":
                                  P.barrier()
                                  ck("s2f")
                              mm4(6, lambda h: hs(WTB[:], h), lambda h: hs(SBF[:], h), ["WTB", "SBF"])
                              P.op("dve", lambda e: e.tensor_tensor(UB[:], U0[:], B[6], ALU.subtract), ["U0", "B6"], ["UB"])
                              if need_out:
                                  for h in range(4):
                                      P.op("pe", lambda e, h=h: e.matmul(hs(B[7], h), hs(SBF[:], h), hs(QDT[:], h), start=True, stop=False), ["SBF", "QDT"], ["B7"])
                                      P.op("pe", lambda e, h=h: e.matmul(hs(B[7], h), hs(UB[:], h), hs(PTB[:], h), start=False, stop=True), ["UB", "PTB"], ["B7"])
                                  if d == 0:
                                      P.op("act", lambda e: e.copy(OF[:, :, tk:tk + 128], h4(B[7])), ["B7"], [("OF", tk)])
                                  else:
                                      P.op("dve", lambda e: e.tensor_tensor(h4(W0[:]), h4(B[7]), OF[:, :, tk:tk + 128], ALU.add), ["B7", ("OF", tk)], ["W0"])
                                      P.op("act", lambda e: e.activation(SQO[:], W0[:], AF.Square), ["W0"], ["SQO"])
                                      mm4(4, lambda h: ONESB[:], lambda h: hs(SQO[:], h), ["ONESB", "SQO"])
                                      P.op("act", lambda e: e.activation(W1[:], B[4], AF.Ln, bias=EPSC, scale=1.0 / 128), ["B4"], ["W1"])
                                      P.op("act", lambda e: e.activation(W1[:], W1[:], AF.Exp, scale=-0.5), ["W1"], ["W1"])
                                      P.op("pool", lambda e: e.tensor_tensor(W0[:], W0[:], W1[:], ALU.mult), ["W0", "W1"], ["W0"])
                                      P.op("dve", lambda e: e.scalar_tensor_tensor(BIG[:, 4:8, tk:tk + 128], h4(W0[:]), GN, sz[:, :, cs_], ALU.mult, ALU.mult),
                                           ["W0", "LWg", szk], [("BIG", b)])
                              mm4(6, lambda h: hs(KDEC[:], h), lambda h: hs(UB[:], h), ["KDEC", "UB"])
                              for h in range(4):
                                  dh = d * 4 + h
                                  P.op("dve", lambda e, h=h, dh=dh: e.scalar_tensor_tensor(hs(S32[:], h), hs(S32[:], h), CF[:, 24 + dh:25 + dh], hs(B[6], h), ALU.mult, ALU.add),
                                       ["S32", "CF", "B6"], ["S32"])
                              P.op("act", lambda e: e.copy(SBF[:], S32[:]), ["S32"], ["SBF"])
                  P.barrier()

              ck("s2")
              with ExitStack() as s4:
                  WOUT = sb("WOUT", [128, 8, D], BF16, s4)
                  XS = sb("XS4", [128, 8, 512], F32, s4)
                  WS4 = sb("WS4", [128, 2, D], F32, s4)
                  SQB = sb("SQB4", [128, 8, 512], BF16, s4)
                  T0 = sb("T04", [128, 512], F32, s4)
                  YP = sb("YP", [128, 8, 512], BF16, s4) if colmaj else None
                  for k in range(8):
                      s = k % 2
                      P.dma("sp", WS4[:, s, :], w_out[l, k * 128:(k + 1) * 128, :], [], [("WS4", s)], ("ws4", s))
                      P.op("dve" if s == 0 else "pool", lambda e, k=k, s=s: e.tensor_copy(WOUT[:, k, :], WS4[:, s, :]), [("WS4", s)], ["WOUT"])
                  pp = 0
                  for nb in range(NBLK):
                      if last and nb == 0:
                          continue
                      t0, nt = blk_range(nb)
                      v = 1 if nb == 0 else 0
                      P.dma("sp", XS[:, :, 0:nt], xsrc_v[:, :, t0:t0 + nt], [("x", nb)], ["XS4"], "xs4")
                      if colmaj and nb > 0:
                          for k in range(8):
                              if k % 2 == 0:
                                  P.op("pool", lambda e, k=k: e.tensor_copy(YP[:, k, :].rearrange("p (a b) -> p a b", b=64), big_view(k, nb, True, True)),
                                       big_keys(nb, True), ["YP"])
                              else:
                                  P.op("act", lambda e, k=k: e.copy(YP[:, k, :].rearrange("p (a b) -> p a b", b=64), big_view(k, nb, True, True)),
                                       big_keys(nb, True), ["YP"])
                      for jn in range(8):
                          bi = (0, 1, 3, 4)[pp % 4]
                          pp += 1
                          for kc in range(8):
                              if colmaj and nb > 0:
                                  P.op("pe", lambda e, kc=kc, jn=jn, bi=bi: e.matmul(
                                      B[bi][:, 0:nt], WOUT[:, kc, jn * 128:(jn + 1) * 128], YP[:, kc, :],
                                      start=(kc == 0), stop=(kc == 7)), ["WOUT", "YP"], [f"B{bi}"])
                              else:
                                  P.op("pe", lambda e, kc=kc, jn=jn, bi=bi: e.matmul(
                                      B[bi][:, 0:nt], WOUT[:, kc, jn * 128:(jn + 1) * 128], BIG[:, kc, t0:t0 + nt],
                                      start=(kc == 0), stop=(kc == 7)), ["WOUT"] + big_keys(nb, False), [f"B{bi}"])
                          P.op("dve", lambda e, jn=jn, bi=bi, v=v: e.scalar_tensor_tensor(
                              XS[:, jn, 0:nt], B[bi][:, 0:nt], mcol(v, 16 + jn), XS[:, jn, 0:nt], ALU.mult, ALU.add),
                              [f"B{bi}", "XS4"], [("XSo", jn)])
                      okeys = [("XSo", jn) for jn in range(8)]
                      if not last:
                          P.dma("pool", xres_v[:, :, t0:t0 + nt], XS[:, :, 0:nt], okeys, [("x", nb)], "xst")
                          P.op("dve", lambda e: e.engine_nop(), [("x", nb)], ["XS4"])
                      else:
                          P.op("act", lambda e: e.activation(SQB[:], XS[:], AF.Square), okeys, ["SQB4"])
                          for k in range(8):
                              P.op("pe", lambda e, k=k: e.matmul(B[2], ONESB[:], SQB[:, k, :], start=(k == 0), stop=(k == 7)), ["SQB4", "ONESB"], ["B2"])
                          P.op("act", lambda e: e.activation(T0[:], B[2], AF.Ln, bias=EPSC, scale=1.0 / D), ["B2"], ["T04"])
                          P.op("act", lambda e: e.activation(T0[:], T0[:], AF.Exp, scale=-0.5), ["T04"], ["T04"])
                          for k in range(8):
                              P.op("dve", lambda e, k=k: e.scalar_tensor_tensor(XS[:, k, :], XS[:, k, :], FN[:, k:k + 1], T0[:], ALU.mult, ALU.mult),
                                   [("XSo", k), "T04", "FN", "SQB4"], [("XSf", k)])
                          fk = [("XSf", k) for k in range(8)]
                          P.dma("pool", out_v[:, :, t0 - NCTX:t0 - NCTX + nt], XS[:, :, 0:nt], fk, [("out", nb)], "ost")
                          P.op("dve", lambda e: e.engine_nop(), [("out", nb)], ["XS4"])
                  P.barrier()
          P.barrier()

    except _Stop:
        pass
    return nc


def _consts():
    i = np.arange(128)
    t, s = i[None, :], i[:, None]
    negi_f = np.where(t >= s, 0.0, BIGNEG)
    negi_b = np.where(t <= s, 0.0, BIGNEG)
    sm_f = (t > s).astype(np.float64)
    sm_b = (t < s).astype(np.float64)
    blk = lambda z: ((s // z) == (t // z)).astype(np.float64)
    blk16 = blk(16)
    off = {z: blk(z) * (1 - blk(z // 2)) for z in (32, 64, 128)}
    ident = np.eye(128)
    ms = [negi_f, negi_b, sm_f, sm_b, blk16, off[32], off[64], off[128], ident]
    cmask = np.concatenate([np.tile(m, (1, 4)) for m in ms], axis=1).astype(np.float32)
    sel = np.zeros((8, 2 * 1024 + 128), np.float32)
    for dh in range(8):
        sel[dh, dh * 128:(dh + 1) * 128] = 1.0
        sel[dh, 1024 + dh * 128:1024 + (dh + 1) * 128] = -1.0
    sel[:, 2048:] = 1.0
    return cmask, ident.astype(np.float32), sel


_NC_CACHE = {}


def make_in_maps(x, c, ctx, c_ctx, norm_w, w_mod, b_mod, w_in, conv_a, conv_qkv, a_log, dt_bias, gdn_norm, w_out, final_norm):
    f = lambda a: np.ascontiguousarray(np.asarray(a, dtype=np.float32))
    x, c, ctx, c_ctx = f(x), f(c), f(ctx), f(c_ctx)
    cmask, ident, sel = _consts()
    L = norm_w.shape[0]
    col = lambda a: np.ascontiguousarray(a.reshape(-1, 128).T)
    shared = {
        "w_mod": f(w_mod), "w_in": f(w_in), "w_out": f(w_out),
        "bmod": np.stack([col(f(b_mod)[l]) for l in range(L)]),
        "normw": np.stack([col(f(norm_w)[l]) for l in range(L)]),
        "conva": np.stack([np.ascontiguousarray(f(conv_a)[l].T.reshape(4, 128, 3).transpose(1, 0, 2).reshape(128, 12)) for l in range(L)]),
        "convq": np.stack([np.ascontiguousarray(f(conv_qkv)[l].T.reshape(12, 128, 3).transpose(1, 0, 2).reshape(128, 36)) for l in range(L)]),
        "alog": np.ascontiguousarray(f(a_log).reshape(L, 8, 1)),
        "dtb": np.ascontiguousarray(f(dt_bias).reshape(L, 8, 1)),
        "gnorm": np.ascontiguousarray(f(gdn_norm).reshape(L, 128, 1)),
        "fnorm": col(f(final_norm)),
        "cmask": cmask, "cident": ident, "csel": sel,
    }
    maps = []
    for b in range(x.shape[0]):
        m = dict(shared)
        m["xT"] = np.ascontiguousarray(np.concatenate([ctx[b], x[b]], axis=0).T)
        ccb = np.stack([col(c[b]), col(c_ctx)], axis=-1).reshape(128, 16)
        m["cc"] = np.ascontiguousarray(ccb)
        maps.append(m)
    return maps


def kernel(x, c, ctx, c_ctx, norm_w, w_mod, b_mod, w_in, conv_a, conv_qkv, a_log, dt_bias, gdn_norm, w_out, final_norm, _nlayers=DEPTH):
    maps = make_in_maps(x, c, ctx, c_ctx, norm_w, w_mod, b_mod, w_in, conv_a, conv_qkv, a_log, dt_bias, gdn_norm, w_out, final_norm)
    if _nlayers not in _NC_CACHE:
        _NC_CACHE[_nlayers] = build(_nlayers)
    nc = _NC_CACHE[_nlayers]
    res = run_bass_kernel_spmd(nc, maps, core_ids=list(range(len(maps))))
    out = np.stack([np.ascontiguousarray(r["outT"].T) for r in res.results], axis=0)
    return out.astype(np.float32)
```

```python
import numpy as np
from contextlib import ExitStack
import concourse.bass as bass
import concourse.mybir as mybir
from concourse.bass_utils import run_bass_kernel_spmd

F32 = mybir.dt.float32
BF16 = mybir.dt.bfloat16
AF = mybir.ActivationFunctionType
ALU = mybir.AluOpType

D = 1024
NCTX = 256
NLAT = 4096
NTOK = NCTX + NLAT
DPROJ = 4112
DEPTH = 4
EPS = 1e-6
NBLK = 9
BIGNEG = -30000.0
PSUM_KEYS = {f"B{i}" for i in range(8)}


def blk_range(b):
    if b == 0:
        return 0, NCTX
    return NCTX + (b - 1) * 512, 512


class _Stop(Exception):
    pass


class Prog:
    def __init__(self, nc, es):
        self.nc = nc
        self.es = es
        self.eng = {"pe": nc.tensor, "act": nc.scalar, "dve": nc.vector, "pool": nc.gpsimd, "sp": nc.sync}
        self.sem = {}
        self.cnt = {}
        self.epoch = 0
        self.state = {}
        self.waited = {}
        self.dsem = {}
        self.dcnt = {}
        self.all_sems = {}
        self.nops = 0
        self.stop_at = None
        self.new_epoch()

    def new_epoch(self):
        self.epoch += 1
        for e in ("pe", "act", "dve", "pool"):
            s = self.es.enter_context(self.nc.semaphore(f"s_{e}_{self.epoch}"))
            self.sem[e] = s
            self.cnt[e] = 0
            self.all_sems[id(s)] = s

    def _collect(self, engine, reads, writes):
        need = {}

        def add(ev):
            s, v, e = ev
            k = id(s)
            if k not in need or need[k][1] < v:
                need[k] = (s, v, e)

        for k in reads:
            st = self.state.get(k)
            if st is not None:
                for ev in st["w"].values():
                    add(ev)
                if k in PSUM_KEYS:
                    for ev in st["r"].values():
                        if ev[2] != engine:
                            add(ev)
        for k in writes:
            st = self.state.get(k)
            if st is not None:
                for ev in st["w"].values():
                    if ev[2] != engine:
                        add(ev)
                for ev in st["r"].values():
                    if ev[2] != engine:
                        add(ev)
        out = []
        wd = self.waited.setdefault(engine, {})
        for k, (s, v, e) in need.items():
            if wd.get(k, 0) >= v:
                continue
            wd[k] = v
            out.append((s, v))
        return out

    def _record(self, ev, reads, writes):
        for k in reads:
            st = self.state.setdefault(k, {"w": {}, "r": {}})
            st["r"][id(ev[0])] = ev
        for k in writes:
            st = self.state.setdefault(k, {"w": {}, "r": {}})
            st["w"][id(ev[0])] = ev

    def op(self, engine, fn, reads=(), writes=(), inc=True):
        e = self.eng[engine]
        for s, v in self._collect(engine, reads, writes):
            e.wait_ge(s, v)
        inst = fn(e)
        if inc:
            self.cnt[engine] += 1
            inst.then_inc(self.sem[engine], 1)
            self._record((self.sem[engine], self.cnt[engine], engine), reads, writes)
        else:
            self._record((self.sem[engine], self.cnt[engine] + 1, engine), reads, writes)
        self.nops += 1
        if self.stop_at is not None and self.nops == self.stop_at:
            self.barrier()
            raise _Stop()

    def dma(self, queue, out, in_, reads, writes, slot):
        e = self.eng[queue]
        for s, v in self._collect("q_" + queue, reads, writes):
            e.wait_ge(s, v)
        if slot not in self.dsem:
            self.dsem[slot] = self.es.enter_context(self.nc.semaphore(f"d_{len(self.dsem)}"))
            self.dcnt[slot] = 0
        self.dcnt[slot] += 16
        e.dma_start(out=out, in_=in_).then_inc(self.dsem[slot], 16)
        self._record((self.dsem[slot], self.dcnt[slot], "dma_" + str(slot)), reads, writes)

    def barrier(self):
        evs = [(self.sem[e], self.cnt[e]) for e in ("pe", "act", "dve", "pool") if self.cnt[e] > 0]
        evs += [(self.dsem[s], self.dcnt[s]) for s in self.dsem]
        for en in ("pe", "act", "dve", "pool", "sp"):
            key = en if en != "sp" else "q_sp"
            wd = self.waited.setdefault(key, {})
            for s, v in evs:
                if en in self.sem and s is self.sem.get(en):
                    continue
                if wd.get(id(s), 0) >= v:
                    continue
                wd[id(s)] = v
                self.eng[en].wait_ge(s, v)
        wd = self.waited.setdefault("q_pool", {})
        for s, v in evs:
            wd[id(s)] = max(wd.get(id(s), 0), v)
        self.state = {}


def build(nlayers=DEPTH, stop=None):
    def ck(name):
        if stop == name:
            print("ck", name, "nops", P.nops)
            raise _Stop()
    nc = bass.Bass("TRN2", target_bir_lowering=False)
    dt_in = lambda n, shp, dt=F32: nc.dram_tensor(n, list(shp), dt, kind="ExternalInput").ap()
    xT = dt_in("xT", [D, NTOK])
    cc = dt_in("cc", [128, 16])
    w_mod = dt_in("w_mod", [DEPTH, D, 3 * D])
    bmod = dt_in("bmod", [DEPTH, 128, 24])
    normw = dt_in("normw", [DEPTH, 128, 8])
    w_in = dt_in("w_in", [DEPTH, D, DPROJ])
    conva = dt_in("conva", [DEPTH, 128, 12])
    convq = dt_in("convq", [DEPTH, 128, 36])
    alog = dt_in("alog", [DEPTH, 8, 1])
    dtb = dt_in("dtb", [DEPTH, 8, 1])
    gnorm = dt_in("gnorm", [DEPTH, 128, 1])
    w_out = dt_in("w_out", [DEPTH, D, D])
    fnorm = dt_in("fnorm", [128, 8])
    cmask = dt_in("cmask", [128, 9 * 512])
    cident = dt_in("cident", [128, 128])
    csel = dt_in("csel", [8, 2 * 1024 + 128])
    outT = nc.dram_tensor("outT", [D, NLAT], F32, kind="ExternalOutput").ap()
    xres = nc.dram_tensor("xres", [D, NTOK], F32).ap()
    kqv_s = nc.dram_tensor("kqv_s", [NBLK, 128, 12, 512], BF16).ap()
    ya_s = nc.dram_tensor("ya_s", [NBLK, 128, 4, 512], BF16).ap()
    szb_s = nc.dram_tensor("szb_s", [NBLK, 128, 4, 512], BF16).ap()
    rows_s = nc.dram_tensor("rows_s", [NBLK, 8, 2, 512], F32).ap()

    es = ExitStack()
    try:
      with es:
          P = Prog(nc, es)
          if isinstance(stop, int):
              P.stop_at = stop
          _uniq = [0]

          def sb(n, shp, dt=F32, st=es):
              _uniq[0] += 1
              return st.enter_context(nc.sbuf_tensor(f"{n}_{_uniq[0]}", list(shp), dt))
          BIG = sb("BIG", [128, 8, NTOK], BF16)
          MASK = sb("MASK", [128, 9, 512], BF16)
          IDF = sb("IDF", [128, 128], F32)
          IDB = sb("IDB", [128, 128], BF16)
          ONESB = sb("ONESB", [128, 128], BF16)
          SEL = sb("SEL", [8, 2 * 1024 + 128], F32)
          MODS = sb("MODS", [128, DEPTH, 2, 24], F32)
          CC = sb("CC", [128, 16], F32)
          FN = sb("FN", [128, 8], F32)
          LW = sb("LW", [128, 64], F32)
          L8 = sb("L8", [8, 4], F32)
          DUM = sb("DUM", [128, 2], F32)
          EPST = sb("EPST", [128, 1], F32)
          EPSC = EPST[:, 0:1]
          banks = [es.enter_context(nc.psum_tensor(f"B{i}", [128, 512], F32)) for i in range(8)]
          B = [b[:] for b in banks]
          Bbf = [b[:].bitcast(BF16) for b in banks]

          NEGI = [MASK[:, 0, :], MASK[:, 1, :]]
          SM = [MASK[:, 2, :], MASK[:, 3, :]]
          BLK16 = MASK[:, 4, :]
          OFF = {32: MASK[:, 5, :], 64: MASK[:, 6, :], 128: MASK[:, 7, :]}
          ID4 = MASK[:, 8, :]
          I8 = IDF[0:8, 0:8]
          ONES8 = SEL[:, 2048:2176]

          def h4(ap):
              return ap.rearrange("p (h t) -> p h t", h=4)

          with ExitStack() as st0:
              MST = sb("MST", [128, 9 * 512], F32, st0)
              WST = sb("WST", [128, 2, 4096], F32, st0)
              SC = sb("SC", [128, 16], F32, st0)
              BM = sb("BM", [128, 24], F32, st0)
              NW = sb("NW", [128, 8], F32, st0)
              P.dma("sp", MST[:], cmask, [], ["MST"], "c0")
              P.dma("sp", IDF[:], cident, [], ["IDF"], "c1")
              P.dma("sp", SEL[:], csel, [], ["SEL"], "c2")
              P.dma("sp", CC[:], cc, [], ["CC"], "c3")
              P.dma("sp", FN[:], fnorm, [], ["FN"], "c4")
              P.op("dve", lambda e: e.tensor_copy(MASK[:].rearrange("p a b -> p (a b)"), MST[:]), ["MST"], ["MASK"])
              P.op("dve", lambda e: e.tensor_copy(IDB[:], IDF[:]), ["IDF"], ["IDB"])
              P.op("dve", lambda e: e.memset(ONESB[:], 1.0), [], ["ONESB"])
              P.op("dve", lambda e: e.memset(DUM[:], 0.0), [], ["DUM"])
              P.op("dve", lambda e: e.memset(EPST[:], EPS), [], ["EPST"])
              P.op("act", lambda e: e.activation(SC[:], CC[:], AF.Silu), ["CC"], ["SC"])
              for l in range(nlayers):
                  P.dma("sp", BM[:], bmod[l], [], ["BM"], "c5")
                  P.dma("sp", NW[:], normw[l], [], ["NW"], "c6")
                  for jg in range(6):
                      s = jg % 2
                      P.dma("sp", WST[:, s, :].rearrange("p (k n) -> p k n", k=8),
                            w_mod[l, :, jg * 512:(jg + 1) * 512].rearrange("(k p) n -> p k n", p=128), [], [("WST", s)], ("wst", s))
                      for jj in range(4):
                          j = jg * 4 + jj
                          for k in range(8):
                              P.op("pe", lambda e, j=j, jj=jj, k=k, s=s: e.matmul(
                                  B[0][:, 2 * j:2 * j + 2], WST[:, s, k * 512 + jj * 128:k * 512 + (jj + 1) * 128],
                                  SC[:].rearrange("p (k v) -> p k v", v=2)[:, k, :], start=(k == 0), stop=(k == 7)),
                                  [("WST", s), "SC"], ["B0"])
                  for v in range(2):
                      P.op("dve", lambda e, v=v, l=l: e.tensor_tensor(
                          MODS[:, l, v, :], B[0][:, 0:48].rearrange("p (j v) -> p j v", v=2)[:, :, v], BM[:], ALU.add),
                          ["B0", "BM"], [("MODS", l)])
                      P.op("dve", lambda e, v=v, l=l: e.scalar_tensor_tensor(
                          MODS[:, l, v, 8:16], MODS[:, l, v, 8:16], 1.0, NW[:], ALU.add, ALU.mult),
                          [("MODS", l), "NW"], [("MODS", l)])
              P.barrier()
          ck("s0")

          for l in range(nlayers):
              P.new_epoch()
              colmaj = (l % 2 == 1)
              last = (l == nlayers - 1)
              xsrc = xT if l == 0 else xres
              xsrc_v = xsrc.rearrange("(k p) t -> p k t", p=128)
              xres_v = xres.rearrange("(k p) t -> p k t", p=128)
              out_v = outT.rearrange("(k p) t -> p k t", p=128)

              def mcol(v, j, l=l):
                  return MODS[:, l, v, j:j + 1]

              P.dma("sp", LW[:, 0:12], conva[l], [], ["LWa"], "c7")
              P.dma("sp", LW[:, 12:48], convq[l], [], ["LWq"], "c8")
              P.dma("sp", LW[:, 48:49], gnorm[l], [], ["LWg"], "c9")
              P.dma("sp", L8[:, 0:1], alog[l], [], ["L8a"], "c10")
              P.dma("sp", L8[:, 1:2], dtb[l], [], ["L8"], "c11")
              P.op("act", lambda e: e.activation(L8[:, 2:3], L8[:, 0:1], AF.Exp), ["L8a"], ["L8"])
              P.op("dve", lambda e: e.tensor_scalar(L8[:, 2:3], L8[:, 2:3], -1.0, None, ALU.mult), ["L8"], ["L8"])
              CA = lambda j, tap: LW[:, j * 3 + tap: j * 3 + tap + 1]
              CQ = lambda j, tap: LW[:, 12 + j * 3 + tap: 12 + j * 3 + tap + 1]
              GN = LW[:, 48:49]

              def big_keys(b, permuted):
                  if b == 0 or not permuted:
                      return [("BIG", b)]
                  return [("BIG", i) for i in range(1, 9)]

              def big_view(kc, b, permuted, rowmajor_of_colscan):
                  t0, nt = blk_range(b)
                  if b == 0 or not permuted:
                      return BIG[:, kc, t0:t0 + nt]
                  lat = BIG[:, kc, NCTX:NTOK]
                  if rowmajor_of_colscan:
                      v = lat.rearrange("p (c r) -> p r c", r=64)
                  else:
                      v = lat.rearrange("p (r c) -> p c r", c=64)
                  return v[:, (b - 1) * 8:(b - 1) * 8 + 8, :]

              def pview(ap, b, permuted):
                  t0, nt = blk_range(b)
                  if b == 0 or not permuted:
                      return ap[:, 0:nt]
                  return ap.rearrange("p (a b) -> p a b", b=64)

              with ExitStack() as s1:
                  WIN = sb("WIN", [128, 8, DPROJ], BF16, s1)
                  XSF = sb("XS", [128, 4112], F32, s1)
                  XS = XSF[:, 0:4096].rearrange("p (k t) -> p k t", k=8)
                  T0 = sb("T0", [128, 512], F32, s1)
                  T1 = sb("T1", [128, 512], F32, s1)
                  T2 = sb("T2", [128, 512], F32, s1)
                  SQQ = sb("SQQ", [128, 512], BF16, s1)
                  KQVB = sb("KQVB", [128, 12, 512], BF16, s1)
                  SQB = KQVB[:, 0:8, :]
                  YAB = sb("YAB", [128, 4, 512], BF16, s1)
                  SZBB = sb("SZBB", [128, 4, 512], BF16, s1)
                  ROWB = sb("ROWB", [8, 2, 512], F32, s1)
                  HP = sb("HP", [128, 8, 512], BF16, s1) if colmaj else None
                  XSW = XSF[:]
                  i = 0
                  for k in range(8):
                      for hf in range(2):
                          s = i % 2
                          c0 = hf * 2056
                          P.dma("sp", XSW[:, s * 2056:(s + 1) * 2056], w_in[l, k * 128:(k + 1) * 128, c0:c0 + 2056],
                                [], [("XS", s)], ("xs", s))
                          eng = ("dve", "pool", "act")[i % 3]
                          if eng == "act":
                              P.op("act", lambda e, k=k, s=s, c0=c0: e.copy(WIN[:, k, c0:c0 + 2056], XSW[:, s * 2056:(s + 1) * 2056]),
                                   [("XS", s)], ["WIN"])
                          else:
                              P.op(eng, lambda e, k=k, s=s, c0=c0: e.tensor_copy(WIN[:, k, c0:c0 + 2056], XSW[:, s * 2056:(s + 1) * 2056]),
                                   [("XS", s)], ["WIN"])
                          i += 1
                  if stop == "s1w":
                      P.barrier()
                      ck("s1w")
                  for nb in range(NBLK):
                      t0, nt = blk_range(nb)
                      v = 1 if nb == 0 else 0
                      P.dma("sp", XS[:, :, 0:nt], xsrc_v[:, :, t0:t0 + nt], [("x", nb)], [("XS", 0), ("XS", 1)], ("xs", 0))
                      P.op("act", lambda e, nt=nt: e.activation(SQB[:, :, 0:nt], XS[:, :, 0:nt], AF.Square),
                           [("XS", 0), ("XS", 1)], ["KQVB"])
                      for k in range(8):
                          P.op("pe", lambda e, k=k, nt=nt: e.matmul(B[2][:, 0:nt], ONESB[:], SQB[:, k, 0:nt], start=(k == 0), stop=(k == 7)),
                               ["KQVB", "ONESB"], ["B2"], inc=(k == 7))
                      P.op("act", lambda e, nt=nt: e.activation(T0[:, 0:nt], B[2][:, 0:nt], AF.Ln, bias=EPSC, scale=1.0 / D), ["B2"], ["T0"])
                      P.op("act", lambda e, nt=nt: e.activation(T0[:, 0:nt], T0[:, 0:nt], AF.Exp, scale=-0.5), ["T0"], ["T0"])
                      for k in range(8):
                          eng = "dve" if k % 2 == 0 else "pool"
                          P.op(eng, lambda e, k=k, nt=nt: e.tensor_tensor(XS[:, k, 0:nt], XS[:, k, 0:nt], T0[:, 0:nt], ALU.mult),
                               [("XS", 0), ("XS", 1), "T0", "KQVB"], [("XSn", k)])
                          P.op("act", lambda e, k=k, nt=nt, t0=t0, v=v: e.activation(
                              BIG[:, k, t0:t0 + nt], XS[:, k, 0:nt], AF.Identity, bias=mcol(v, k), scale=mcol(v, 8 + k)),
                              [("XSn", k)], [("BIG", nb)])
                      P.op("act", lambda e: e.activation(DUM[:, 0:1], DUM[:, 1:2], AF.Copy), [("BIG", nb)] + [("XSn", k) for k in range(8)], [("XS", 0), ("XS", 1)])

                  if stop == "s1p":
                      P.barrier()
                      ck("s1p")
                  P.barrier()
                  TS = [(T0, T1, T2, SQQ, "T0", "T1", "T2", "SQQ", 2)]
                  for i_ in range(2):
                      o_ = i_ * 1792
                      TS.append((XSF[:, o_:o_ + 512], XSF[:, o_ + 512:o_ + 1024], XSF[:, o_ + 1024:o_ + 1536],
                                 XSF[:, o_ + 1536:o_ + 1792].bitcast(BF16), f"T0_{i_}", f"T1_{i_}", f"T2_{i_}", f"SQQ_{i_}", 7 if i_ == 0 else 2))
                  tsi = [0]
                  pp = [0]
                  PROJ_BANKS = [0, 1, 3, 4, 5, 6]

                  def proj(b, c0, m):
                      bi = PROJ_BANKS[pp[0] % len(PROJ_BANKS)]
                      pp[0] += 1
                      t0, nt = blk_range(b)
                      for k in range(8):
                          if colmaj and b > 0:
                              P.op("pe", lambda e, k=k, bi=bi: e.matmul(
                                  B[bi][0:m, 0:nt], WIN[:, k, c0:c0 + m], HP[:, k, :],
                                  start=(k == 0), stop=(k == 7)), ["WIN", "HP"], [f"B{bi}"], inc=(k == 7))
                          else:
                              P.op("pe", lambda e, k=k, bi=bi: e.matmul(
                                  B[bi][0:m, 0:nt], WIN[:, k, c0:c0 + m], BIG[:, k, t0:t0 + nt],
                                  start=(k == 0), stop=(k == 7)), ["WIN"] + big_keys(b, False), [f"B{bi}"], inc=(k == 7))
                      return bi

                  def conv(dst, dkey, src, skey, w, nt, seg):
                      P.op("dve", lambda e: e.tensor_scalar(dst[:, 0:nt], src[:, 0:nt], w(1), None, ALU.mult), [skey, "LWa", "LWq"], [dkey])
                      dv = dst[:, 0:nt].rearrange("p (a b) -> p a b", b=seg)
                      sv = src[:, 0:nt].rearrange("p (a b) -> p a b", b=seg)
                      P.op("dve", lambda e: e.scalar_tensor_tensor(dv[:, :, 1:seg], sv[:, :, 0:seg - 1], w(0), dv[:, :, 1:seg], ALU.mult, ALU.add),
                           [skey, dkey, "LWa", "LWq"], [dkey])
                      P.op("dve", lambda e: e.scalar_tensor_tensor(dv[:, :, 0:seg - 1], sv[:, :, 1:seg], w(2), dv[:, :, 0:seg - 1], ALU.mult, ALU.add),
                           [skey, dkey, "LWa", "LWq"], [dkey])

                  for b in range(NBLK):
                      t0, nt = blk_range(b)
                      seg = NCTX if b == 0 else 64
                      if colmaj and b > 0:
                          for k in range(8):
                              eng = "pool" if k % 2 == 0 else "act"
                              if eng == "pool":
                                  P.op("pool", lambda e, k=k: e.tensor_copy(HP[:, k, :].rearrange("p (a b) -> p a b", b=64), big_view(k, b, True, False)),
                                       big_keys(b, True), ["HP"])
                              else:
                                  P.op("act", lambda e, k=k: e.copy(HP[:, k, :].rearrange("p (a b) -> p a b", b=64), big_view(k, b, True, False)),
                                       big_keys(b, True), ["HP"])
                      for jj in range(4):
                          T0, T1, T2, SQQ, k0, k1, k2, kq_, ssb = TS[tsi[0] % 3]
                          tsi[0] += 1
                          bi = proj(b, jj * 128, 128)
                          P.op("act", lambda e, bi=bi: e.copy(T0[:, 0:nt], B[bi][:, 0:nt]), [f"B{bi}"], [k0])
                          bi = proj(b, 1024 + jj * 128, 128)
                          P.op("dve", lambda e, bi=bi: e.tensor_tensor(T0[:, 0:nt], B[bi][:, 0:nt], T0[:, 0:nt], ALU.mult), [f"B{bi}", k0], [k0])
                          conv(T1, k1, T0, k0, lambda tap, jj=jj: CA(jj, tap), nt, seg)
                          bi = proj(b, 512 + jj * 128, 128)
                          P.op("dve", lambda e, bi=bi: e.tensor_tensor(T1[:, 0:nt], B[bi][:, 0:nt], T1[:, 0:nt], ALU.mult), [f"B{bi}", k1], [k1])
                          bi = proj(b, 1536 + jj * 128, 128)
                          P.op("act", lambda e, bi=bi: e.activation(T2[:, 0:nt], B[bi][:, 0:nt], AF.Silu), [f"B{bi}"], [k2])
                          P.op("pool", lambda e, jj=jj: e.tensor_tensor(YAB[:, jj, 0:nt], T1[:, 0:nt], T2[:, 0:nt], ALU.mult), [k1, k2], ["YAB"])
                      for idx in range(12):
                          T0, T1, T2, SQQ, k0, k1, k2, kq_, ssb = TS[tsi[0] % 3]
                          tsi[0] += 1
                          bi = proj(b, 2048 + idx * 128, 128)
                          conv(T1, k1, B[bi], f"B{bi}", lambda tap, idx=idx: CQ(idx, tap), nt, seg)
                          if idx >= 8:
                              P.op("act", lambda e, idx=idx: e.activation(KQVB[:, idx, 0:nt], T1[:, 0:nt], AF.Silu), [k1], ["KQVB"])
                              continue
                          P.op("act", lambda e: e.activation(T2[:, 0:nt], T1[:, 0:nt], AF.Silu), [k1], [k2])
                          P.op("act", lambda e: e.activation(SQQ[:, 0:nt], T2[:, 0:nt], AF.Square), [k2], [kq_])
                          P.op("pe", lambda e: e.matmul(B[ssb][:, 0:nt], ONESB[:], SQQ[:, 0:nt], start=True, stop=True), [kq_, "ONESB"], [f"B{ssb}"])
                          P.op("act", lambda e: e.activation(T0[:, 0:nt], B[ssb][:, 0:nt], AF.Ln, bias=EPSC, scale=1.0), [f"B{ssb}"], [k0])
                          P.op("act", lambda e: e.activation(T0[:, 0:nt], T0[:, 0:nt], AF.Exp, scale=-0.5), [k0], [k0])
                          sc = (128.0 ** -0.5) if idx < 4 else 1.0
                          P.op("dve", lambda e, idx=idx, sc=sc: e.scalar_tensor_tensor(KQVB[:, idx, 0:nt], T2[:, 0:nt], sc, T0[:, 0:nt], ALU.mult, ALU.mult),
                               [k2, k0], ["KQVB"])
                      for h in range(4):
                          bi = proj(b, 3584 + h * 128, 128)
                          P.op("act", lambda e, bi=bi, h=h: e.activation(SZBB[:, h, 0:nt], B[bi][:, 0:nt], AF.Silu), [f"B{bi}"], ["SZBB"])
                      bi = proj(b, 4096, 8)
                      P.op("act", lambda e, bi=bi: e.activation(ROWB[:, 1, 0:nt], B[bi][0:8, 0:nt], AF.Sigmoid), [f"B{bi}"], ["ROWB"])
                      bi = proj(b, 4104, 8)
                      P.op("act", lambda e, bi=bi: e.activation(ROWB[:, 0, 0:nt], B[bi][0:8, 0:nt], AF.Exp, bias=L8[:, 1:2]), [f"B{bi}", "L8"], ["ROWB"])
                      P.op("act", lambda e: e.activation(ROWB[:, 0, 0:nt], ROWB[:, 0, 0:nt], AF.Ln, bias=1.0), ["ROWB"], ["ROWB"])
                      P.op("dve", lambda e: e.tensor_scalar(ROWB[:, 0, 0:nt], ROWB[:, 0, 0:nt], L8[:, 2:3], None, ALU.mult), ["ROWB", "L8"], ["ROWB"])
                      P.dma("pool", kqv_s[b, :, :, 0:nt], KQVB[:, :, 0:nt], ["KQVB"], [("kqv", b)], "sp0")
                      P.dma("pool", ya_s[b, :, :, 0:nt], YAB[:, :, 0:nt], ["YAB"], [("ya", b)], "sp1")
                      P.dma("pool", szb_s[b, :, :, 0:nt], SZBB[:, :, 0:nt], ["SZBB"], [("szb", b)], "sp2")
                      P.dma("pool", rows_s[b, :, :, 0:nt], ROWB[:, :, 0:nt], ["ROWB"], [("rows", b)], "sp3")
                  P.barrier()

              ck("s1")
              with ExitStack() as s2:
                  OF = sb("OF", [128, 4, NTOK], BF16, s2)
                  KQ = [sb(f"KQ{i}", [128, 12, 512], BF16, s2) for i in range(2)]
                  RW = [sb(f"RW{i}", [8, 2, 512], F32, s2) for i in range(2)]
                  SZ = [sb(f"SZ{i}", [128, 4, 512], BF16, s2) for i in range(2)]
                  GC = sb("GC", [8, 128], F32, s2)
                  CS = sb("CS", [8, 128], F32, s2)
                  DG = sb("DG", [8, 8], F32, s2)
                  COLS = sb("COLS", [128, 24], F32, s2)
                  CF = sb("CF", [128, 32], F32, s2)
                  W0 = sb("W0", [128, 512], F32, s2)
                  W1 = sb("W1", [128, 512], F32, s2)
                  W2 = sb("W2", [128, 512], F32, s2)
                  W3 = sb("W3", [128, 512], F32, s2)
                  PTB = sb("PTB", [128, 512], BF16, s2)
                  ATB = sb("ATB", [128, 512], BF16, s2)
                  AB = sb("AB", [128, 512], BF16, s2)
                  PN = [sb(f"PN{i}", [128, 512], BF16, s2) for i in range(2)]
                  PTN = [sb(f"PTN{i}", [128, 512], BF16, s2) for i in range(2)]
                  RB = sb("RB", [128, 512], BF16, s2)
                  RTB = sb("RTB", [128, 512], BF16, s2)
                  ZM = sb("ZM", [128, 512], BF16, s2)
                  YM = sb("YM", [128, 512], BF16, s2)
                  KBE = sb("KBE", [128, 512], BF16, s2)
                  KDEC = sb("KDEC", [128, 512], BF16, s2)
                  VB = sb("VB", [128, 512], BF16, s2)
                  U0 = sb("U0", [128, 512], F32, s2)
                  WTB = sb("WTB", [128, 512], BF16, s2)
                  QDT = sb("QDT", [128, 512], BF16, s2)
                  UB = sb("UB", [128, 512], BF16, s2)
                  S32 = sb("S32", [128, 512], F32, s2)
                  STMP = sb("STMP", [128, 512], F32, s2)
                  SBF = sb("SBF", [128, 512], BF16, s2)
                  SQO = sb("SQO", [128, 512], BF16, s2)

                  def hs(ap, h):
                      return ap[:, h * 128:(h + 1) * 128]

                  def mm4(bank, lhs, rhs, rk, start=True, stop=True):
                      for h in range(4):
                          P.op("pe", lambda e, h=h: e.matmul(hs(B[bank], h), lhs(h), rhs(h), start=start, stop=stop), rk, [f"B{bank}"], inc=(h == 3))

                  for d in range(2):
                      P.op("pool", lambda e: e.memset(S32[:], 0.0), [], ["S32"])
                      P.op("pool", lambda e: e.memset(SBF[:], 0.0), [], ["SBF"])
                      border = list(range(NBLK)) if d == 0 else [0] + list(range(8, 0, -1))
                      for bi_, b in enumerate(border):
                          t0, nt = blk_range(b)
                          sl = bi_ % 2
                          P.dma("sp", KQ[sl][:, :, 0:nt], kqv_s[b, :, :, 0:nt], [("kqv", b)], [("KQ", sl)], ("kq", sl))
                          P.dma("sp", RW[sl][:, :, 0:nt], rows_s[b, :, :, 0:nt], [("rows", b)], [("RW", sl)], ("rw", sl))
                          need_out = not (last and b == 0)
                          if d == 1 and need_out:
                              P.dma("sp", SZ[sl][:, :, 0:nt], szb_s[b, :, :, 0:nt], [("szb", b)], [("SZ", sl)], ("sz", sl))
                              P.dma("sp", BIG[:, 0:4, t0:t0 + nt], ya_s[b, :, :, 0:nt], [("ya", b)], [("BIG", b)], ("yal", sl))
                          nch = nt // 128
                          chs = list(range(nch)) if d == 0 else list(range(nch - 1, -1, -1))
                          kq, rw, sz = KQ[sl], RW[sl], SZ[sl]
                          kqk, rwk, szk = ("KQ", sl), ("RW", sl), ("SZ", sl)
                          for ch in chs:
                              c0 = ch * 128
                              cs_ = slice(c0, c0 + 128)
                              tk = t0 + c0
                              qT = lambda h: kq[:, h, cs_]
                              kT = lambda h: kq[:, 4 + h, cs_]
                              vT = lambda h: kq[:, 8 + h, cs_]
                              g8 = rw[:, 0, cs_]
                              b8 = rw[:, 1, cs_]
                              P.op("dve", lambda e: e.tensor_tensor_scan(CS[:], ONES8, g8, 0.0, ALU.mult, ALU.add), [rwk, "SEL"], ["CS"])
                              if d == 0:
                                  P.op("dve", lambda e: e.tensor_copy(GC[:], CS[:]), ["CS"], ["GC"])
                              else:
                                  P.op("dve", lambda e: e.scalar_tensor_tensor(GC[:], CS[:], -1.0, g8, ALU.mult, ALU.add), ["CS", rwk], ["GC"])
                                  P.op("dve", lambda e: e.tensor_scalar(GC[:], GC[:], CS[:, 127:128], None, ALU.add), ["GC", "CS"], ["GC"])
                              P.op("dve", lambda e: e.tensor_scalar(DG[:], I8, CS[:, 127:128], None, ALU.mult), ["CS", "IDF"], ["DG"])
                              P.op("pe", lambda e: e.matmul(B[4][:, 0:8], GC[:], I8, start=True, stop=True), ["GC", "IDF"], ["B4"])
                              P.op("pe", lambda e: e.matmul(B[4][:, 8:16], b8, I8, start=True, stop=True), [rwk, "IDF"], ["B4"])
                              P.op("pe", lambda e: e.matmul(B[4][:, 16:24], ONES8, DG[:], start=True, stop=True), ["SEL", "DG"], ["B4"])
                              P.op("act", lambda e: e.copy(COLS[:], B[4][:, 0:24]), ["B4"], ["COLS"])
                              P.op("act", lambda e: e.activation(CF[:, 0:8], COLS[:, 0:8], AF.Exp), ["COLS"], ["CF"])
                              P.op("dve", lambda e: e.tensor_tensor(CF[:, 8:16], CF[:, 0:8], COLS[:, 8:16], ALU.mult), ["CF", "COLS"], ["CF"])
                              P.op("dve", lambda e: e.tensor_tensor(CF[:, 16:24], COLS[:, 16:24], COLS[:, 0:8], ALU.subtract), ["COLS", "CF"], ["CF"])
                              P.op("act", lambda e: e.activation(CF[:, 16:24], CF[:, 16:24], AF.Exp), ["CF"], ["CF"])
                              P.op("act", lambda e: e.activation(CF[:, 24:32], COLS[:, 16:24], AF.Exp), ["COLS", "CF"], ["CF"])
                              if stop == "s2a":
                                  P.barrier()
                                  ck("s2a")
                              for h in range(4):
                                  P.op("pe", lambda e, h=h: e.transpose(Bbf[5][:, h * 128:(h + 1) * 128], kT(h), IDB[:]), [kqk, "IDB"], ["B5"], inc=False)
                              for h in range(4):
                                  P.op("pe", lambda e, h=h: e.transpose(Bbf[5][:, 512 + h * 128:512 + (h + 1) * 128], vT(h), IDB[:]), [kqk, "IDB"], ["B5"], inc=(h == 3))
                              for h in range(4):
                                  dh = d * 4 + h
                                  P.op("dve", lambda e, h=h, dh=dh: e.tensor_scalar(hs(KBE[:], h), Bbf[5][:, h * 128:(h + 1) * 128], CF[:, 8 + dh:9 + dh], None, ALU.mult),
                                       ["B5", "CF"], ["KBE"])
                                  P.op("act", lambda e, h=h, dh=dh: e.activation(hs(KDEC[:], h), Bbf[5][:, h * 128:(h + 1) * 128], AF.Identity, scale=CF[:, 16 + dh:17 + dh]),
                                       ["B5", "CF"], ["KDEC"])
                                  P.op("dve", lambda e, h=h, dh=dh: e.tensor_scalar(hs(VB[:], h), Bbf[5][:, 512 + h * 128:512 + (h + 1) * 128], COLS[:, 8 + dh:9 + dh], None, ALU.mult),
                                       ["B5", "COLS"], ["VB"])
                              if stop == "s2b":
                                  P.barrier()
                                  ck("s2b")
                              for h in range(4):
                                  dh = d * 4 + h
                                  P.op("pe", lambda e, h=h, dh=dh: e.matmul(hs(B[0], h), SEL[:, dh * 128:(dh + 1) * 128], GC[:], start=True, stop=False), ["SEL", "GC"], ["B0"], inc=False)
                                  P.op("pe", lambda e, h=h, dh=dh: e.matmul(hs(B[0], h), GC[:], SEL[:, 1024 + dh * 128:1024 + (dh + 1) * 128], start=False, stop=True), ["SEL", "GC"], ["B0"], inc=(h == 3))
                              P.op("dve", lambda e: e.tensor_tensor(W0[:], B[0], NEGI[d], ALU.add), ["B0", "MASK"], ["W0"])
                              P.op("act", lambda e: e.activation(W1[:], W0[:], AF.Exp), ["W0"], ["W1"])
                              mm4(1, lambda h: SEL[:, (d * 4 + h) * 128:(d * 4 + h + 1) * 128], lambda h: b8, ["SEL", rwk])
                              P.op("dve", lambda e: e.tensor_tensor(W2[:], B[1], SM[d], ALU.mult), ["B1", "MASK"], ["W2"])
                              P.op("pool", lambda e: e.tensor_tensor(W2[:], W2[:], W1[:], ALU.mult), ["W2", "W1"], ["W2"])
                              mm4(2, kT, kT, [kqk])
                              mm4(3, kT, qT, [kqk])
                              P.op("dve", lambda e: e.tensor_tensor(PTB[:], B[3], W1[:], ALU.mult), ["B3", "W1"], ["PTB"])
                              P.op("dve", lambda e: e.tensor_tensor(ATB[:], B[2], W2[:], ALU.mult), ["B2", "W2"], ["ATB"])
                              for h in range(4):
                                  P.op("pe", lambda e, h=h: e.transpose(Bbf[6][:, h * 128:(h + 1) * 128], hs(ATB[:], h), IDB[:]), ["ATB", "IDB"], ["B6"], inc=(h == 3))
                              P.op("act", lambda e: e.copy(AB[:], Bbf[6][:, 0:512]), ["B6"], ["AB"])
                              if stop == "s2c":
                                  P.barrier()
                                  ck("s2c")
                              P.op("dve", lambda e: e.tensor_tensor(PTN[0][:], Bbf[6][:, 0:512], BLK16, ALU.mult), ["B6", "MASK"], [("PTN", 0)])
                              P.op("pool", lambda e: e.tensor_tensor(PN[0][:], ATB[:], BLK16, ALU.mult), ["ATB", "MASK"], [("PN", 0)])
                              P.op("pool", lambda e: e.tensor_tensor(RB[:], ID4, PTN[0][:], ALU.subtract), ["MASK", ("PTN", 0)], ["RB"])
                              P.op("pool", lambda e: e.tensor_tensor(RTB[:], ID4, PN[0][:], ALU.subtract), ["MASK", ("PN", 0)], ["RTB"])
                              cur = 0
                              for it in range(3):
                                  nx = 1 - cur
                                  mm4(0, lambda h: hs(PTN[cur][:], h), lambda h: hs(PN[cur][:], h), [("PTN", cur), ("PN", cur)])
                                  mm4(1, lambda h: hs(PN[cur][:], h), lambda h: hs(PTN[cur][:], h), [("PTN", cur), ("PN", cur)])
                                  P.op("act", lambda e, nx=nx: e.copy(PN[nx][:], B[0]), ["B0"], [("PN", nx)])
                                  P.op("dve", lambda e, nx=nx: e.tensor_copy(PTN[nx][:], B[1]), ["B1"], [("PTN", nx)])
                                  mm4(2, lambda h: hs(PTN[nx][:], h), lambda h: hs(RTB[:], h), [("PTN", nx), "RTB"])
                                  mm4(3, lambda h: hs(PN[nx][:], h), lambda h: hs(RB[:], h), [("PN", nx), "RB"])
                                  P.op("dve", lambda e: e.tensor_tensor(RTB[:], B[2], RTB[:], ALU.add), ["B2", "RTB"], ["RTB"])
                                  P.op("dve", lambda e: e.tensor_tensor(RB[:], B[3], RB[:], ALU.add), ["B3", "RB"], ["RB"])
                                  cur = nx
                              if stop == "s2d":
                                  P.barrier()
                                  ck("s2d")
                              for szm in (32, 64, 128):
                                  if szm < 128:
                                      mm4(0, lambda h: hs(ATB[:], h), lambda h: hs(RB[:], h), ["ATB", "RB"])
                                      P.op("dve", lambda e, szm=szm: e.tensor_tensor(ZM[:], B[0], OFF[szm], ALU.mult), ["B0", "MASK"], ["ZM"])
                                  mm4(1, lambda h: hs(AB[:], h), lambda h: hs(RTB[:], h), ["AB", "RTB"])
                                  P.op("dve", lambda e, szm=szm: e.tensor_tensor(YM[:], B[1], OFF[szm], ALU.mult), ["B1", "MASK"], ["YM"])
                                  if szm < 128:
                                      mm4(2, lambda h: hs(RTB[:], h), lambda h: hs(ZM[:], h), ["RTB", "ZM"])
                                  mm4(3, lambda h: hs(RB[:], h), lambda h: hs(YM[:], h), ["RB", "YM"])
                                  if szm < 128:
                                      P.op("dve", lambda e: e.tensor_tensor(RB[:], RB[:], B[2], ALU.subtract), ["B2", "RB"], ["RB"])
                                  P.op("dve", lambda e: e.tensor_tensor(RTB[:], RTB[:], B[3], ALU.subtract), ["B3", "RTB"], ["RTB"])
                              if stop == "s2e":
                                  P.barrier()
                                  ck("s2e")
                              mm4(4, lambda h: hs(RTB[:], h), lambda h: hs(VB[:], h), ["RTB", "VB"])
                              P.op("act", lambda e: e.copy(U0[:], B[4]), ["B4"], ["U0"])
                              mm4(5, lambda h: hs(KBE[:], h), lambda h: hs(RTB[:], h), ["KBE", "RTB"])
                              P.op("act", lambda e: e.copy(WTB[:], B[5]), ["B5"], ["WTB"])
                              mm4(1, lambda h: SEL[:, (d * 4 + h) * 128:(d * 4 + h + 1) * 128], lambda h: GC[:], ["SEL", "GC"])
                              P.op("act", lambda e: e.activation(W3[:], B[1], AF.Exp), ["B1"], ["W3"])
                              P.op("dve", lambda e: e.tensor_tensor(h4(QDT[:]), kq[:, 0:4, cs_], h4(W3[:]), ALU.mult), [kqk, "W3"], ["QDT"])
                              if stop == "s2f":
                                  P.barrier()
                                  ck("s2f")
                              mm4(6, lambda h: hs(WTB[:], h), lambda h: hs(SBF[:], h), ["WTB", "SBF"])
                              P.op("dve", lambda e: e.tensor_tensor(UB[:], U0[:], B[6], ALU.subtract), ["U0", "B6"], ["UB"])
                              if need_out:
                                  for h in range(4):
                                      P.op("pe", lambda e, h=h: e.matmul(hs(B[7], h), hs(SBF[:], h), hs(QDT[:], h), start=True, stop=False), ["SBF", "QDT"], ["B7"], inc=False)
                                      P.op("pe", lambda e, h=h: e.matmul(hs(B[7], h), hs(UB[:], h), hs(PTB[:], h), start=False, stop=True), ["UB", "PTB"], ["B7"], inc=(h == 3))
                                  if d == 0:
                                      P.op("act", lambda e: e.copy(OF[:, :, tk:tk + 128], h4(B[7])), ["B7"], [("OF", tk)])
                                  else:
                                      P.op("dve", lambda e: e.tensor_tensor(h4(W0[:]), h4(B[7]), OF[:, :, tk:tk + 128], ALU.add), ["B7", ("OF", tk)], ["W0"])
                                      P.op("act", lambda e: e.activation(SQO[:], W0[:], AF.Square), ["W0"], ["SQO"])
                                      mm4(4, lambda h: ONESB[:], lambda h: hs(SQO[:], h), ["ONESB", "SQO"])
                                      P.op("act", lambda e: e.activation(W1[:], B[4], AF.Ln, bias=EPSC, scale=1.0 / 128), ["B4"], ["W1"])
                                      P.op("act", lambda e: e.activation(W1[:], W1[:], AF.Exp, scale=-0.5), ["W1"], ["W1"])
                                      P.op("pool", lambda e: e.tensor_tensor(W0[:], W0[:], W1[:], ALU.mult), ["W0", "W1"], ["W0"])
                                      P.op("dve", lambda e: e.scalar_tensor_tensor(BIG[:, 4:8, tk:tk + 128], h4(W0[:]), GN, sz[:, :, cs_], ALU.mult, ALU.mult),
                                           ["W0", "LWg", szk], [("BIG", b)])
                              mm4(6, lambda h: hs(KDEC[:], h), lambda h: hs(UB[:], h), ["KDEC", "UB"])
                              for h in range(4):
                                  dh = d * 4 + h
                                  P.op("dve", lambda e, h=h, dh=dh: e.scalar_tensor_tensor(hs(S32[:], h), hs(S32[:], h), CF[:, 24 + dh:25 + dh], hs(B[6], h), ALU.mult, ALU.add),
                                       ["S32", "CF", "B6"], ["S32"])
                              P.op("act", lambda e: e.copy(SBF[:], S32[:]), ["S32"], ["SBF"])
                  P.barrier()

              ck("s2")
              with ExitStack() as s4:
                  WOUT = sb("WOUT", [128, 8, D], BF16, s4)
                  XS = sb("XS4", [128, 8, 512], F32, s4)
                  WS4 = sb("WS4", [128, 2, D], F32, s4)
                  SQB = sb("SQB4", [128, 8, 512], BF16, s4)
                  T0 = sb("T04", [128, 512], F32, s4)
                  YP = sb("YP", [128, 8, 512], BF16, s4) if colmaj else None
                  for k in range(8):
                      s = k % 2
                      P.dma("sp", WS4[:, s, :], w_out[l, k * 128:(k + 1) * 128, :], [], [("WS4", s)], ("ws4", s))
                      P.op("dve" if s == 0 else "pool", lambda e, k=k, s=s: e.tensor_copy(WOUT[:, k, :], WS4[:, s, :]), [("WS4", s)], ["WOUT"])
                  pp = 0
                  for nb in range(NBLK):
                      if last and nb == 0:
                          continue
                      t0, nt = blk_range(nb)
                      v = 1 if nb == 0 else 0
                      P.dma("sp", XS[:, :, 0:nt], xsrc_v[:, :, t0:t0 + nt], [("x", nb)], ["XS4"], "xs4")
                      if colmaj and nb > 0:
                          for k in range(8):
                              if k % 2 == 0:
                                  P.op("pool", lambda e, k=k: e.tensor_copy(YP[:, k, :].rearrange("p (a b) -> p a b", b=64), big_view(k, nb, True, True)),
                                       big_keys(nb, True), ["YP"])
                              else:
                                  P.op("act", lambda e, k=k: e.copy(YP[:, k, :].rearrange("p (a b) -> p a b", b=64), big_view(k, nb, True, True)),
                                       big_keys(nb, True), ["YP"])
                      for jn in range(8):
                          bi = (0, 1, 3, 4)[pp % 4]
                          pp += 1
                          for kc in range(8):
                              if colmaj and nb > 0:
                                  P.op("pe", lambda e, kc=kc, jn=jn, bi=bi: e.matmul(
                                      B[bi][:, 0:nt], WOUT[:, kc, jn * 128:(jn + 1) * 128], YP[:, kc, :],
                                      start=(kc == 0), stop=(kc == 7)), ["WOUT", "YP"], [f"B{bi}"], inc=(kc == 7))
                              else:
                                  P.op("pe", lambda e, kc=kc, jn=jn, bi=bi: e.matmul(
                                      B[bi][:, 0:nt], WOUT[:, kc, jn * 128:(jn + 1) * 128], BIG[:, kc, t0:t0 + nt],
                                      start=(kc == 0), stop=(kc == 7)), ["WOUT"] + big_keys(nb, False), [f"B{bi}"], inc=(kc == 7))
                          P.op("dve", lambda e, jn=jn, bi=bi, v=v: e.scalar_tensor_tensor(
                              XS[:, jn, 0:nt], B[bi][:, 0:nt], mcol(v, 16 + jn), XS[:, jn, 0:nt], ALU.mult, ALU.add),
                              [f"B{bi}", "XS4"], [("XSo", jn)])
                      okeys = [("XSo", jn) for jn in range(8)]
                      if not last:
                          P.dma("pool", xres_v[:, :, t0:t0 + nt], XS[:, :, 0:nt], okeys, [("x", nb)], "xst")
                          P.op("dve", lambda e: e.engine_nop(), [("x", nb)], ["XS4"])
                      else:
                          P.op("act", lambda e: e.activation(SQB[:], XS[:], AF.Square), okeys, ["SQB4"])
                          for k in range(8):
                              P.op("pe", lambda e, k=k: e.matmul(B[2], ONESB[:], SQB[:, k, :], start=(k == 0), stop=(k == 7)), ["SQB4", "ONESB"], ["B2"], inc=(k == 7))
                          P.op("act", lambda e: e.activation(T0[:], B[2], AF.Ln, bias=EPSC, scale=1.0 / D), ["B2"], ["T04"])
                          P.op("act", lambda e: e.activation(T0[:], T0[:], AF.Exp, scale=-0.5), ["T04"], ["T04"])
                          for k in range(8):
                              P.op("dve", lambda e, k=k: e.scalar_tensor_tensor(XS[:, k, :], XS[:, k, :], FN[:, k:k + 1], T0[:], ALU.mult, ALU.mult),
                                   [("XSo", k), "T04", "FN", "SQB4"], [("XSf", k)])
                          fk = [("XSf", k) for k in range(8)]
                          P.dma("pool", out_v[:, :, t0 - NCTX:t0 - NCTX + nt], XS[:, :, 0:nt], fk, [("out", nb)], "ost")
                          P.op("dve", lambda e: e.engine_nop(), [("out", nb)], ["XS4"])
                  P.barrier()
          P.barrier()

    except _Stop:
        pass
    return nc


def _consts():
    i = np.arange(128)
    t, s = i[None, :], i[:, None]
    negi_f = np.where(t >= s, 0.0, BIGNEG)
    negi_b = np.where(t <= s, 0.0, BIGNEG)
    sm_f = (t > s).astype(np.float64)
    sm_b = (t < s).astype(np.float64)
    blk = lambda z: ((s // z) == (t // z)).astype(np.float64)
    blk16 = blk(16)
    off = {z: blk(z) * (1 - blk(z // 2)) for z in (32, 64, 128)}
    ident = np.eye(128)
    ms = [negi_f, negi_b, sm_f, sm_b, blk16, off[32], off[64], off[128], ident]
    cmask = np.concatenate([np.tile(m, (1, 4)) for m in ms], axis=1).astype(np.float32)
    sel = np.zeros((8, 2 * 1024 + 128), np.float32)
    for dh in range(8):
        sel[dh, dh * 128:(dh + 1) * 128] = 1.0
        sel[dh, 1024 + dh * 128:1024 + (dh + 1) * 128] = -1.0
    sel[:, 2048:] = 1.0
    return cmask, ident.astype(np.float32), sel


_NC_CACHE = {}


def make_in_maps(x, c, ctx, c_ctx, norm_w, w_mod, b_mod, w_in, conv_a, conv_qkv, a_log, dt_bias, gdn_norm, w_out, final_norm):
    f = lambda a: np.ascontiguousarray(np.asarray(a, dtype=np.float32))
    x, c, ctx, c_ctx = f(x), f(c), f(ctx), f(c_ctx)
    cmask, ident, sel = _consts()
    L = norm_w.shape[0]
    col = lambda a: np.ascontiguousarray(a.reshape(-1, 128).T)
    shared = {
        "w_mod": f(w_mod), "w_in": f(w_in), "w_out": f(w_out),
        "bmod": np.stack([col(f(b_mod)[l]) for l in range(L)]),
        "normw": np.stack([col(f(norm_w)[l]) for l in range(L)]),
        "conva": np.stack([np.ascontiguousarray(f(conv_a)[l].T.reshape(4, 128, 3).transpose(1, 0, 2).reshape(128, 12)) for l in range(L)]),
        "convq": np.stack([np.ascontiguousarray(f(conv_qkv)[l].T.reshape(12, 128, 3).transpose(1, 0, 2).reshape(128, 36)) for l in range(L)]),
        "alog": np.ascontiguousarray(f(a_log).reshape(L, 8, 1)),
        "dtb": np.ascontiguousarray(f(dt_bias).reshape(L, 8, 1)),
        "gnorm": np.ascontiguousarray(f(gdn_norm).reshape(L, 128, 1)),
        "fnorm": col(f(final_norm)),
        "cmask": cmask, "cident": ident, "csel": sel,
    }
    maps = []
    for b in range(x.shape[0]):
        m = dict(shared)
        m["xT"] = np.ascontiguousarray(np.concatenate([ctx[b], x[b]], axis=0).T)
        ccb = np.stack([col(c[b]), col(c_ctx)], axis=-1).reshape(128, 16)
        m["cc"] = np.ascontiguousarray(ccb)
        maps.append(m)
    return maps


def kernel(x, c, ctx, c_ctx, norm_w, w_mod, b_mod, w_in, conv_a, conv_qkv, a_log, dt_bias, gdn_norm, w_out, final_norm, _nlayers=DEPTH):
    maps = make_in_maps(x, c, ctx, c_ctx, norm_w, w_mod, b_mod, w_in, conv_a, conv_qkv, a_log, dt_bias, gdn_norm, w_out, final_norm)
    if _nlayers not in _NC_CACHE:
        _NC_CACHE[_nlayers] = build(_nlayers)
    nc = _NC_CACHE[_nlayers]
    res = run_bass_kernel_spmd(nc, maps, core_ids=list(range(len(maps))))
    out = np.stack([np.ascontiguousarray(r["outT"].T) for r in res.results], axis=0)
    return out.astype(np.float32)
```

```python
import numpy as np
from contextlib import ExitStack
import concourse.bass as bass
import concourse.mybir as mybir
from concourse.bass_utils import run_bass_kernel_spmd

F32 = mybir.dt.float32
BF16 = mybir.dt.bfloat16
AF = mybir.ActivationFunctionType
ALU = mybir.AluOpType

D = 1024
NCTX = 256
NLAT = 4096
NTOK = NCTX + NLAT
DPROJ = 4112
DEPTH = 4
EPS = 1e-6
NBLK = 9
BIGNEG = -30000.0
PSUM_KEYS = {f"B{i}" for i in range(8)}


def blk_range(b):
    if b == 0:
        return 0, NCTX
    return NCTX + (b - 1) * 512, 512


class _Stop(Exception):
    pass


class Prog:
    def __init__(self, nc, es):
        self.nc = nc
        self.es = es
        self.eng = {"pe": nc.tensor, "act": nc.scalar, "dve": nc.vector, "pool": nc.gpsimd, "sp": nc.sync}
        self.sem = {}
        self.cnt = {}
        self.epoch = 0
        self.state = {}
        self.waited = {}
        self.dsem = {}
        self.dcnt = {}
        self.all_sems = {}
        self.nops = 0
        self.stop_at = None
        self.new_epoch()

    def new_epoch(self):
        self.epoch += 1
        for e in ("pe", "act", "dve", "pool"):
            s = self.es.enter_context(self.nc.semaphore(f"s_{e}_{self.epoch}"))
            self.sem[e] = s
            self.cnt[e] = 0
            self.all_sems[id(s)] = s

    def _collect(self, engine, reads, writes):
        need = {}

        def add(ev):
            s, v, e = ev
            k = id(s)
            if k not in need or need[k][1] < v:
                need[k] = (s, v, e)

        for k in reads:
            st = self.state.get(k)
            if st is not None:
                for ev in st["w"].values():
                    add(ev)
                if k in PSUM_KEYS:
                    for ev in st["r"].values():
                        if ev[2] != engine:
                            add(ev)
        for k in writes:
            st = self.state.get(k)
            if st is not None:
                for ev in st["w"].values():
                    if ev[2] != engine:
                        add(ev)
                for ev in st["r"].values():
                    if ev[2] != engine:
                        add(ev)
        out = []
        wd = self.waited.setdefault(engine, {})
        for k, (s, v, e) in need.items():
            if wd.get(k, 0) >= v:
                continue
            wd[k] = v
            out.append((s, v))
        return out

    def _record(self, ev, reads, writes):
        for k in reads:
            st = self.state.setdefault(k, {"w": {}, "r": {}})
            st["r"][id(ev[0])] = ev
        for k in writes:
            st = self.state.setdefault(k, {"w": {}, "r": {}})
            st["w"][id(ev[0])] = ev

    def op(self, engine, fn, reads=(), writes=(), inc=True):
        e = self.eng[engine]
        for s, v in self._collect(engine, reads, writes):
            e.wait_ge(s, v)
        inst = fn(e)
        if inc:
            self.cnt[engine] += 1
            inst.then_inc(self.sem[engine], 1)
            self._record((self.sem[engine], self.cnt[engine], engine), reads, writes)
        else:
            self._record((self.sem[engine], self.cnt[engine] + 1, engine), reads, writes)
        self.nops += 1
        if self.stop_at is not None and self.nops == self.stop_at:
            self.barrier()
            raise _Stop()

    def dma(self, queue, out, in_, reads, writes, slot):
        e = self.eng[queue]
        for s, v in self._collect("q_" + queue, reads, writes):
            e.wait_ge(s, v)
        if slot not in self.dsem:
            self.dsem[slot] = self.es.enter_context(self.nc.semaphore(f"d_{len(self.dsem)}"))
            self.dcnt[slot] = 0
        self.dcnt[slot] += 16
        e.dma_start(out=out, in_=in_).then_inc(self.dsem[slot], 16)
        self._record((self.dsem[slot], self.dcnt[slot], "dma_" + str(slot)), reads, writes)

    def barrier(self):
        evs = [(self.sem[e], self.cnt[e]) for e in ("pe", "act", "dve", "pool") if self.cnt[e] > 0]
        evs += [(self.dsem[s], self.dcnt[s]) for s in self.dsem]
        for en in ("pe", "act", "dve", "pool", "sp"):
            key = en if en != "sp" else "q_sp"
            wd = self.waited.setdefault(key, {})
            for s, v in evs:
                if en in self.sem and s is self.sem.get(en):
                    continue
                if wd.get(id(s), 0) >= v:
                    continue
                wd[id(s)] = v
                self.eng[en].wait_ge(s, v)
        wd = self.waited.setdefault("q_pool", {})
        for s, v in evs:
            wd[id(s)] = max(wd.get(id(s), 0), v)
        self.state = {}


def build(nlayers=DEPTH, stop=None):
    def ck(name):
        if stop == name:
            print("ck", name, "nops", P.nops)
            raise _Stop()
    nc = bass.Bass("TRN2", target_bir_lowering=False)
    dt_in = lambda n, shp, dt=F32: nc.dram_tensor(n, list(shp), dt, kind="ExternalInput").ap()
    xT = dt_in("xT", [D, NTOK])
    cc = dt_in("cc", [128, 16])
    w_mod = dt_in("w_mod", [DEPTH, D, 3 * D])
    bmod = dt_in("bmod", [DEPTH, 128, 24])
    normw = dt_in("normw", [DEPTH, 128, 8])
    w_in = dt_in("w_in", [DEPTH, D, DPROJ])
    conva = dt_in("conva", [DEPTH, 128, 12])
    convq = dt_in("convq", [DEPTH, 128, 36])
    alog = dt_in("alog", [DEPTH, 8, 1])
    dtb = dt_in("dtb", [DEPTH, 8, 1])
    gnorm = dt_in("gnorm", [DEPTH, 128, 1])
    w_out = dt_in("w_out", [DEPTH, D, D])
    fnorm = dt_in("fnorm", [128, 8])
    cmask = dt_in("cmask", [128, 9 * 512])
    cident = dt_in("cident", [128, 128])
    csel = dt_in("csel", [8, 2 * 1024 + 128])
    outT = nc.dram_tensor("outT", [D, NLAT], F32, kind="ExternalOutput").ap()
    xres = nc.dram_tensor("xres", [D, NTOK], F32).ap()
    kqv_s = nc.dram_tensor("kqv_s", [NBLK, 128, 12, 512], BF16).ap()
    ya_s = nc.dram_tensor("ya_s", [NBLK, 128, 4, 512], BF16).ap()
    szb_s = nc.dram_tensor("szb_s", [NBLK, 128, 4, 512], BF16).ap()
    rows_s = nc.dram_tensor("rows_s", [NBLK, 8, 2, 512], F32).ap()

    es = ExitStack()
    try:
      with es:
          P = Prog(nc, es)
          if isinstance(stop, int):
              P.stop_at = stop
          _uniq = [0]

          def sb(n, shp, dt=F32, st=es):
              _uniq[0] += 1
              return st.enter_context(nc.sbuf_tensor(f"{n}_{_uniq[0]}", list(shp), dt))
          BIG = sb("BIG", [128, 8, NTOK], BF16)
          MASK = sb("MASK", [128, 9, 512], BF16)
          IDF = sb("IDF", [128, 128], F32)
          IDB = sb("IDB", [128, 128], BF16)
          ONESB = sb("ONESB", [128, 128], BF16)
          SEL = sb("SEL", [8, 2 * 1024 + 128], F32)
          MODS = sb("MODS", [128, DEPTH, 2, 24], F32)
          CC = sb("CC", [128, 16], F32)
          FN = sb("FN", [128, 8], F32)
          LW = sb("LW", [128, 64], F32)
          L8 = sb("L8", [8, 4], F32)
          DUM = sb("DUM", [128, 2], F32)
          EPST = sb("EPST", [128, 1], F32)
          EPSC = EPST[:, 0:1]
          banks = [es.enter_context(nc.psum_tensor(f"B{i}", [128, 512], F32)) for i in range(8)]
          B = [b[:] for b in banks]
          Bbf = [b[:].bitcast(BF16) for b in banks]

          NEGI = [MASK[:, 0, :], MASK[:, 1, :]]
          SM = [MASK[:, 2, :], MASK[:, 3, :]]
          BLK16 = MASK[:, 4, :]
          OFF = {32: MASK[:, 5, :], 64: MASK[:, 6, :], 128: MASK[:, 7, :]}
          ID4 = MASK[:, 8, :]
          I8 = IDF[0:8, 0:8]
          ONES8 = SEL[:, 2048:2176]

          def h4(ap):
              return ap.rearrange("p (h t) -> p h t", h=4)

          with ExitStack() as st0:
              MST = sb("MST", [128, 9 * 512], F32, st0)
              WST = sb("WST", [128, 2, 4096], F32, st0)
              SC = sb("SC", [128, 16], F32, st0)
              BM = sb("BM", [128, 24], F32, st0)
              NW = sb("NW", [128, 8], F32, st0)
              P.dma("sp", MST[:], cmask, [], ["MST"], "c0")
              P.dma("sp", IDF[:], cident, [], ["IDF"], "c1")
              P.dma("sp", SEL[:], csel, [], ["SEL"], "c2")
              P.dma("sp", CC[:], cc, [], ["CC"], "c3")
              P.dma("sp", FN[:], fnorm, [], ["FN"], "c4")
              P.op("dve", lambda e: e.tensor_copy(MASK[:].rearrange("p a b -> p (a b)"), MST[:]), ["MST"], ["MASK"])
              P.op("dve", lambda e: e.tensor_copy(IDB[:], IDF[:]), ["IDF"], ["IDB"])
              P.op("dve", lambda e: e.memset(ONESB[:], 1.0), [], ["ONESB"])
              P.op("dve", lambda e: e.memset(DUM[:], 0.0), [], ["DUM"])
              P.op("dve", lambda e: e.memset(EPST[:], EPS), [], ["EPST"])
              P.op("act", lambda e: e.activation(SC[:], CC[:], AF.Silu), ["CC"], ["SC"])
              for l in range(nlayers):
                  P.dma("sp", BM[:], bmod[l], [], ["BM"], "c5")
                  P.dma("sp", NW[:], normw[l], [], ["NW"], "c6")
                  for jg in range(6):
                      s = jg % 2
                      P.dma("sp", WST[:, s, :].rearrange("p (k n) -> p k n", k=8),
                            w_mod[l, :, jg * 512:(jg + 1) * 512].rearrange("(k p) n -> p k n", p=128), [], [("WST", s)], ("wst", s))
                      for jj in range(4):
                          j = jg * 4 + jj
                          for k in range(8):
                              P.op("pe", lambda e, j=j, jj=jj, k=k, s=s: e.matmul(
                                  B[0][:, 2 * j:2 * j + 2], WST[:, s, k * 512 + jj * 128:k * 512 + (jj + 1) * 128],
                                  SC[:].rearrange("p (k v) -> p k v", v=2)[:, k, :], start=(k == 0), stop=(k == 7)),
                                  [("WST", s), "SC"], ["B0"])
                  for v in range(2):
                      P.op("dve", lambda e, v=v, l=l: e.tensor_tensor(
                          MODS[:, l, v, :], B[0][:, 0:48].rearrange("p (j v) -> p j v", v=2)[:, :, v], BM[:], ALU.add),
                          ["B0", "BM"], [("MODS", l)])
                      P.op("dve", lambda e, v=v, l=l: e.scalar_tensor_tensor(
                          MODS[:, l, v, 8:16], MODS[:, l, v, 8:16], 1.0, NW[:], ALU.add, ALU.mult),
                          [("MODS", l), "NW"], [("MODS", l)])
              P.barrier()
          ck("s0")

          for l in range(nlayers):
              P.new_epoch()
              colmaj = (l % 2 == 1)
              last = (l == nlayers - 1)
              xsrc = xT if l == 0 else xres
              xsrc_v = xsrc.rearrange("(k p) t -> p k t", p=128)
              xres_v = xres.rearrange("(k p) t -> p k t", p=128)
              out_v = outT.rearrange("(k p) t -> p k t", p=128)

              def mcol(v, j, l=l):
                  return MODS[:, l, v, j:j + 1]

              P.dma("sp", LW[:, 0:12], conva[l], [], ["LWa"], "c7")
              P.dma("sp", LW[:, 12:48], convq[l], [], ["LWq"], "c8")
              P.dma("sp", LW[:, 48:49], gnorm[l], [], ["LWg"], "c9")
              P.dma("sp", L8[:, 0:1], alog[l], [], ["L8a"], "c10")
              P.dma("sp", L8[:, 1:2], dtb[l], [], ["L8"], "c11")
              P.op("act", lambda e: e.activation(L8[:, 2:3], L8[:, 0:1], AF.Exp), ["L8a"], ["L8"])
              P.op("dve", lambda e: e.tensor_scalar(L8[:, 2:3], L8[:, 2:3], -1.0, None, ALU.mult), ["L8"], ["L8"])
              CA = lambda j, tap: LW[:, j * 3 + tap: j * 3 + tap + 1]
              CQ = lambda j, tap: LW[:, 12 + j * 3 + tap: 12 + j * 3 + tap + 1]
              GN = LW[:, 48:49]

              def big_keys(b, permuted):
                  if b == 0 or not permuted:
                      return [("BIG", b)]
                  return [("BIG", i) for i in range(1, 9)]

              def big_view(kc, b, permuted, rowmajor_of_colscan):
                  t0, nt = blk_range(b)
                  if b == 0 or not permuted:
                      return BIG[:, kc, t0:t0 + nt]
                  lat = BIG[:, kc, NCTX:NTOK]
                  if rowmajor_of_colscan:
                      v = lat.rearrange("p (c r) -> p r c", r=64)
                  else:
                      v = lat.rearrange("p (r c) -> p c r", c=64)
                  return v[:, (b - 1) * 8:(b - 1) * 8 + 8, :]

              def pview(ap, b, permuted):
                  t0, nt = blk_range(b)
                  if b == 0 or not permuted:
                      return ap[:, 0:nt]
                  return ap.rearrange("p (a b) -> p a b", b=64)

              with ExitStack() as s1:
                  WIN = sb("WIN", [128, 8, DPROJ], BF16, s1)
                  XSF = sb("XS", [128, 4112], F32, s1)
                  XS = XSF[:, 0:4096].rearrange("p (k t) -> p k t", k=8)
                  T0 = sb("T0", [128, 512], F32, s1)
                  T1 = sb("T1", [128, 512], F32, s1)
                  T2 = sb("T2", [128, 512], F32, s1)
                  SQQ = sb("SQQ", [128, 512], BF16, s1)
                  KQVB = sb("KQVB", [128, 12, 512], BF16, s1)
                  SQB = KQVB[:, 0:8, :]
                  YAB = sb("YAB", [128, 4, 512], BF16, s1)
                  SZBB = sb("SZBB", [128, 4, 512], BF16, s1)
                  ROWB = sb("ROWB", [8, 2, 512], F32, s1)
                  HP = sb("HP", [128, 8, 512], BF16, s1) if colmaj else None
                  XSW = XSF[:]
                  i = 0
                  for k in range(8):
                      for hf in range(2):
                          s = i % 2
                          c0 = hf * 2056
                          P.dma("sp", XSW[:, s * 2056:(s + 1) * 2056], w_in[l, k * 128:(k + 1) * 128, c0:c0 + 2056],
                                [], [("XS", s)], ("xs", s))
                          eng = ("dve", "pool", "act")[i % 3]
                          if eng == "act":
                              P.op("act", lambda e, k=k, s=s, c0=c0: e.copy(WIN[:, k, c0:c0 + 2056], XSW[:, s * 2056:(s + 1) * 2056]),
                                   [("XS", s)], ["WIN"])
                          else:
                              P.op(eng, lambda e, k=k, s=s, c0=c0: e.tensor_copy(WIN[:, k, c0:c0 + 2056], XSW[:, s * 2056:(s + 1) * 2056]),
                                   [("XS", s)], ["WIN"])
                          i += 1
                  if stop == "s1w":
                      P.barrier()
                      ck("s1w")
                  for nb in range(NBLK):
                      t0, nt = blk_range(nb)
                      v = 1 if nb == 0 else 0
                      P.dma("sp", XS[:, :, 0:nt], xsrc_v[:, :, t0:t0 + nt], [("x", nb)], [("XS", 0), ("XS", 1)], ("xs", 0))
                      P.op("act", lambda e, nt=nt: e.activation(SQB[:, :, 0:nt], XS[:, :, 0:nt], AF.Square),
                           [("XS", 0), ("XS", 1)], ["KQVB"])
                      for k in range(8):
                          P.op("pe", lambda e, k=k, nt=nt: e.matmul(B[2][:, 0:nt], ONESB[:], SQB[:, k, 0:nt], start=(k == 0), stop=(k == 7)),
                               ["KQVB", "ONESB"], ["B2"], inc=(k == 7))
                      P.op("act", lambda e, nt=nt: e.activation(T0[:, 0:nt], B[2][:, 0:nt], AF.Ln, bias=EPSC, scale=1.0 / D), ["B2"], ["T0"])
                      P.op("act", lambda e, nt=nt: e.activation(T0[:, 0:nt], T0[:, 0:nt], AF.Exp, scale=-0.5), ["T0"], ["T0"])
                      for k in range(8):
                          eng = "dve" if k % 2 == 0 else "pool"
                          P.op(eng, lambda e, k=k, nt=nt: e.tensor_tensor(XS[:, k, 0:nt], XS[:, k, 0:nt], T0[:, 0:nt], ALU.mult),
                               [("XS", 0), ("XS", 1), "T0", "KQVB"], [("XSn", k)])
                          P.op("act", lambda e, k=k, nt=nt, t0=t0, v=v: e.activation(
                              BIG[:, k, t0:t0 + nt], XS[:, k, 0:nt], AF.Identity, bias=mcol(v, k), scale=mcol(v, 8 + k)),
                              [("XSn", k)], [("BIG", nb)])
                      P.op("act", lambda e: e.activation(DUM[:, 0:1], DUM[:, 1:2], AF.Copy), [("BIG", nb)] + [("XSn", k) for k in range(8)], [("XS", 0), ("XS", 1)])

                  if stop == "s1p":
                      P.barrier()
                      ck("s1p")
                  P.barrier()
                  TS = [(T0, T1, T2, SQQ, "T0", "T1", "T2", "SQQ", 2)]
                  for i_ in range(2):
                      o_ = i_ * 1792
                      TS.append((XSF[:, o_:o_ + 512], XSF[:, o_ + 512:o_ + 1024], XSF[:, o_ + 1024:o_ + 1536],
                                 XSF[:, o_ + 1536:o_ + 1792].bitcast(BF16), f"T0_{i_}", f"T1_{i_}", f"T2_{i_}", f"SQQ_{i_}", 7 if i_ == 0 else 2))
                  tsi = [0]
                  pp = [0]
                  PROJ_BANKS = [0, 1, 3, 4, 5, 6]

                  def proj(b, c0, m):
                      bi = PROJ_BANKS[pp[0] % len(PROJ_BANKS)]
                      pp[0] += 1
                      t0, nt = blk_range(b)
                      for k in range(8):
                          if colmaj and b > 0:
                              P.op("pe", lambda e, k=k, bi=bi: e.matmul(
                                  B[bi][0:m, 0:nt], WIN[:, k, c0:c0 + m], HP[:, k, :],
                                  start=(k == 0), stop=(k == 7)), ["WIN", "HP"], [f"B{bi}"], inc=(k == 7))
                          else:
                              P.op("pe", lambda e, k=k, bi=bi: e.matmul(
                                  B[bi][0:m, 0:nt], WIN[:, k, c0:c0 + m], BIG[:, k, t0:t0 + nt],
                                  start=(k == 0), stop=(k == 7)), ["WIN"] + big_keys(b, False), [f"B{bi}"], inc=(k == 7))
                      return bi

                  def conv(dst, dkey, src, skey, w, nt, seg):
                      P.op("dve", lambda e: e.tensor_scalar(dst[:, 0:nt], src[:, 0:nt], w(1), None, ALU.mult), [skey, "LWa", "LWq"], [dkey])
                      dv = dst[:, 0:nt].rearrange("p (a b) -> p a b", b=seg)
                      sv = src[:, 0:nt].rearrange("p (a b) -> p a b", b=seg)
                      P.op("dve", lambda e: e.scalar_tensor_tensor(dv[:, :, 1:seg], sv[:, :, 0:seg - 1], w(0), dv[:, :, 1:seg], ALU.mult, ALU.add),
                           [skey, dkey, "LWa", "LWq"], [dkey])
                      P.op("dve", lambda e: e.scalar_tensor_tensor(dv[:, :, 0:seg - 1], sv[:, :, 1:seg], w(2), dv[:, :, 0:seg - 1], ALU.mult, ALU.add),
                           [skey, dkey, "LWa", "LWq"], [dkey])

                  for b in range(NBLK):
                      t0, nt = blk_range(b)
                      seg = NCTX if b == 0 else 64
                      if colmaj and b > 0:
                          for k in range(8):
                              eng = "pool" if k % 2 == 0 else "act"
                              if eng == "pool":
                                  P.op("pool", lambda e, k=k: e.tensor_copy(HP[:, k, :].rearrange("p (a b) -> p a b", b=64), big_view(k, b, True, False)),
                                       big_keys(b, True), ["HP"])
                              else:
                                  P.op("act", lambda e, k=k: e.copy(HP[:, k, :].rearrange("p (a b) -> p a b", b=64), big_view(k, b, True, False)),
                                       big_keys(b, True), ["HP"])
                      for jj in range(4):
                          T0, T1, T2, SQQ, k0, k1, k2, kq_, ssb = TS[tsi[0] % 3]
                          tsi[0] += 1
                          bi = proj(b, jj * 128, 128)
                          P.op("act", lambda e, bi=bi: e.copy(T0[:, 0:nt], B[bi][:, 0:nt]), [f"B{bi}"], [k0])
                          bi = proj(b, 1024 + jj * 128, 128)
                          P.op("dve", lambda e, bi=bi: e.tensor_tensor(T0[:, 0:nt], B[bi][:, 0:nt], T0[:, 0:nt], ALU.mult), [f"B{bi}", k0], [k0])
                          conv(T1, k1, T0, k0, lambda tap, jj=jj: CA(jj, tap), nt, seg)
                          bi = proj(b, 512 + jj * 128, 128)
                          P.op("dve", lambda e, bi=bi: e.tensor_tensor(T1[:, 0:nt], B[bi][:, 0:nt], T1[:, 0:nt], ALU.mult), [f"B{bi}", k1], [k1])
                          bi = proj(b, 1536 + jj * 128, 128)
                          P.op("act", lambda e, bi=bi: e.activation(T2[:, 0:nt], B[bi][:, 0:nt], AF.Silu), [f"B{bi}"], [k2])
                          P.op("pool", lambda e, jj=jj: e.tensor_tensor(YAB[:, jj, 0:nt], T1[:, 0:nt], T2[:, 0:nt], ALU.mult), [k1, k2], ["YAB"])
                      for idx in range(12):
                          T0, T1, T2, SQQ, k0, k1, k2, kq_, ssb = TS[tsi[0] % 3]
                          tsi[0] += 1
                          bi = proj(b, 2048 + idx * 128, 128)
                          conv(T1, k1, B[bi], f"B{bi}", lambda tap, idx=idx: CQ(idx, tap), nt, seg)
                          if idx >= 8:
                              P.op("act", lambda e, idx=idx: e.activation(KQVB[:, idx, 0:nt], T1[:, 0:nt], AF.Silu), [k1], ["KQVB"])
                              continue
                          P.op("act", lambda e: e.activation(T2[:, 0:nt], T1[:, 0:nt], AF.Silu), [k1], [k2])
                          P.op("act", lambda e: e.activation(SQQ[:, 0:nt], T2[:, 0:nt], AF.Square), [k2], [kq_])
                          P.op("pe", lambda e: e.matmul(B[ssb][:, 0:nt], ONESB[:], SQQ[:, 0:nt], start=True, stop=True), [kq_, "ONESB"], [f"B{ssb}"])
                          P.op("act", lambda e: e.activation(T0[:, 0:nt], B[ssb][:, 0:nt], AF.Ln, bias=EPSC, scale=1.0), [f"B{ssb}"], [k0])
                          P.op("act", lambda e: e.activation(T0[:, 0:nt], T0[:, 0:nt], AF.Exp, scale=-0.5), [k0], [k0])
                          sc = (128.0 ** -0.5) if idx < 4 else 1.0
                          P.op("dve", lambda e, idx=idx, sc=sc: e.scalar_tensor_tensor(KQVB[:, idx, 0:nt], T2[:, 0:nt], sc, T0[:, 0:nt], ALU.mult, ALU.mult),
                               [k2, k0], ["KQVB"])
                      for h in range(4):
                          bi = proj(b, 3584 + h * 128, 128)
                          P.op("act", lambda e, bi=bi, h=h: e.activation(SZBB[:, h, 0:nt], B[bi][:, 0:nt], AF.Silu), [f"B{bi}"], ["SZBB"])
                      bi = proj(b, 4096, 8)
                      P.op("act", lambda e, bi=bi: e.activation(ROWB[:, 1, 0:nt], B[bi][0:8, 0:nt], AF.Sigmoid), [f"B{bi}"], ["ROWB"])
                      bi = proj(b, 4104, 8)
                      P.op("act", lambda e, bi=bi: e.activation(ROWB[:, 0, 0:nt], B[bi][0:8, 0:nt], AF.Exp, bias=L8[:, 1:2]), [f"B{bi}", "L8"], ["ROWB"])
                      P.op("act", lambda e: e.activation(ROWB[:, 0, 0:nt], ROWB[:, 0, 0:nt], AF.Ln, bias=1.0), ["ROWB"], ["ROWB"])
                      P.op("dve", lambda e: e.tensor_scalar(ROWB[:, 0, 0:nt], ROWB[:, 0, 0:nt], L8[:, 2:3], None, ALU.mult), ["ROWB", "L8"], ["ROWB"])
                      P.dma("pool", kqv_s[b, :, :, 0:nt], KQVB[:, :, 0:nt], ["KQVB"], [("kqv", b)], "sp0")
                      P.dma("pool", ya_s[b, :, :, 0:nt], YAB[:, :, 0:nt], ["YAB"], [("ya", b)], "sp1")
                      P.dma("pool", szb_s[b, :, :, 0:nt], SZBB[:, :, 0:nt], ["SZBB"], [("szb", b)], "sp2")
                      P.dma("pool", rows_s[b, :, :, 0:nt], ROWB[:, :, 0:nt], ["ROWB"], [("rows", b)], "sp3")
                  P.barrier()

              ck("s1")
              with ExitStack() as s2:
                  OF = sb("OF", [128, 4, NTOK], BF16, s2)
                  S32 = sb("S32", [128, 512], F32, s2)
                  SBF = sb("SBF", [128, 512], BF16, s2)
                  NSLOT = 2

                  def mk_slot(i):
                      t = {}
                      t["KQ"] = sb(f"KQ{i}", [128, 12, 128], BF16, s2)
                      t["RW"] = sb(f"RW{i}", [8, 2, 128], F32, s2)
                      t["SZ"] = sb(f"SZ{i}", [128, 4, 128], BF16, s2)
                      for n_ in ("GC", "CS"):
                          t[n_] = sb(f"{n_}{i}", [8, 128], F32, s2)
                      t["DG"] = sb(f"DG{i}", [8, 8], F32, s2)
                      t["COLS"] = sb(f"COLS{i}", [128, 24], F32, s2)
                      t["CF"] = sb(f"CF{i}", [128, 32], F32, s2)
                      for n_ in ("W0", "W1", "W2", "W3", "U0"):
                          t[n_] = sb(f"{n_}{i}", [128, 512], F32, s2)
                      for n_ in ("PTB", "ATB", "AB", "PN0", "PN1", "PTN0", "PTN1", "RB", "RTB", "ZM", "YM", "KBE", "KDEC", "VB",
                                 "WTB", "QDT", "UB", "SQO"):
                          t[n_] = sb(f"{n_}{i}", [128, 512], BF16, s2)
                      t["banks"] = [4 * i, 4 * i + 1, 4 * i + 2, 4 * i + 3]
                      t["i"] = i
                      return t

                  slots = [mk_slot(i) for i in range(NSLOT)]

                  def hs(ap, h):
                      return ap[:, h * 128:(h + 1) * 128]

                  def mm4(bank, lhs, rhs, rk, start=True, stop=True):
                      for h in range(4):
                          P.op("pe", lambda e, h=h: e.matmul(hs(B[bank], h), lhs(h), rhs(h), start=start, stop=stop), rk, [f"B{bank}"], inc=(h == 3))

                  def chunk_gen(T, d, b, ch, need_out):
                      si = T["i"]
                      K_ = lambda n_: (n_, si)
                      ba, bb_, bc, be = T["banks"]
                      t0, nt = blk_range(b)
                      c0 = ch * 128
                      tk = t0 + c0
                      kq, rw, sz = T["KQ"], T["RW"], T["SZ"]
                      GC, CS, DG, COLS, CF = T["GC"], T["CS"], T["DG"], T["COLS"], T["CF"]
                      W0, W1, W2, W3, U0 = T["W0"], T["W1"], T["W2"], T["W3"], T["U0"]
                      PTB, ATB, AB, RB, RTB, ZM, YM = T["PTB"], T["ATB"], T["AB"], T["RB"], T["RTB"], T["ZM"], T["YM"]
                      PN = [T["PN0"], T["PN1"]]
                      PTN = [T["PTN0"], T["PTN1"]]
                      KBE, KDEC, VB, WTB, QDT, UB, SQO = T["KBE"], T["KDEC"], T["VB"], T["WTB"], T["QDT"], T["UB"], T["SQO"]
                      kqk, rwk, szk = K_("KQ"), K_("RW"), K_("SZ")
                      P.dma("sp", kq[:], kqv_s[b, :, :, c0:c0 + 128], [("kqv", b)], [kqk], ("kq", si))
                      P.dma("sp", rw[:], rows_s[b, :, :, c0:c0 + 128], [("rows", b)], [rwk], ("rw", si))
                      if d == 1 and need_out:
                          P.dma("sp", sz[:], szb_s[b, :, :, c0:c0 + 128], [("szb", b)], [szk], ("sz", si))
                      qT = lambda h: kq[:, h, :]
                      kT = lambda h: kq[:, 4 + h, :]
                      vT = lambda h: kq[:, 8 + h, :]
                      g8 = rw[:, 0, :]
                      b8 = rw[:, 1, :]
                      yield
                      P.op("dve", lambda e: e.tensor_tensor_scan(CS[:], ONES8, g8, 0.0, ALU.mult, ALU.add), [rwk, "SEL"], [K_("CS")])
                      if d == 0:
                          P.op("dve", lambda e: e.tensor_copy(GC[:], CS[:]), [K_("CS")], [K_("GC")])
                      else:
                          P.op("dve", lambda e: e.scalar_tensor_tensor(GC[:], CS[:], -1.0, g8, ALU.mult, ALU.add), [K_("CS"), rwk], [K_("GC")])
                          P.op("dve", lambda e: e.tensor_scalar(GC[:], GC[:], CS[:, 127:128], None, ALU.add), [K_("GC"), K_("CS")], [K_("GC")])
                      P.op("dve", lambda e: e.tensor_scalar(DG[:], I8, CS[:, 127:128], None, ALU.mult), [K_("CS"), "IDF"], [K_("DG")])
                      P.op("pe", lambda e: e.matmul(B[be][:, 0:8], GC[:], I8, start=True, stop=True), [K_("GC"), "IDF"], [f"B{be}"], inc=False)
                      P.op("pe", lambda e: e.matmul(B[be][:, 8:16], b8, I8, start=True, stop=True), [rwk, "IDF"], [f"B{be}"], inc=False)
                      P.op("pe", lambda e: e.matmul(B[be][:, 16:24], ONES8, DG[:], start=True, stop=True), ["SEL", K_("DG")], [f"B{be}"])
                      for h in range(4):
                          P.op("pe", lambda e, h=h: e.transpose(Bbf[bc][:, h * 128:(h + 1) * 128], kT(h), IDB[:]), [kqk, "IDB"], [f"B{bc}"], inc=False)
                      for h in range(4):
                          P.op("pe", lambda e, h=h: e.transpose(Bbf[bc][:, 512 + h * 128:512 + (h + 1) * 128], vT(h), IDB[:]), [kqk, "IDB"], [f"B{bc}"], inc=(h == 3))
                      for h in range(4):
                          dh = d * 4 + h
                          P.op("pe", lambda e, h=h, dh=dh: e.matmul(hs(B[ba], h), SEL[:, dh * 128:(dh + 1) * 128], GC[:], start=True, stop=False), ["SEL", K_("GC")], [f"B{ba}"], inc=False)
                          P.op("pe", lambda e, h=h, dh=dh: e.matmul(hs(B[ba], h), GC[:], SEL[:, 1024 + dh * 128:1024 + (dh + 1) * 128], start=False, stop=True), ["SEL", K_("GC")], [f"B{ba}"], inc=(h == 3))
                      mm4(bb_, lambda h: SEL[:, (d * 4 + h) * 128:(d * 4 + h + 1) * 128], lambda h: b8, ["SEL", rwk])
                      yield
                      P.op("act", lambda e: e.copy(COLS[:], B[be][:, 0:24]), [f"B{be}"], [K_("COLS")])
                      P.op("act", lambda e: e.activation(CF[:, 0:8], COLS[:, 0:8], AF.Exp), [K_("COLS")], [K_("CF")])
                      P.op("dve", lambda e: e.tensor_tensor(CF[:, 8:16], CF[:, 0:8], COLS[:, 8:16], ALU.mult), [K_("CF"), K_("COLS")], [K_("CF")])
                      P.op("dve", lambda e: e.tensor_tensor(CF[:, 16:24], COLS[:, 16:24], COLS[:, 0:8], ALU.subtract), [K_("COLS"), K_("CF")], [K_("CF")])
                      P.op("act", lambda e: e.activation(CF[:, 16:24], CF[:, 16:24], AF.Exp), [K_("CF")], [K_("CF")])
                      P.op("act", lambda e: e.activation(CF[:, 24:32], COLS[:, 16:24], AF.Exp), [K_("COLS"), K_("CF")], [K_("CF")])
                      P.op("dve", lambda e: e.tensor_tensor(W0[:], B[ba], NEGI[d], ALU.add), [f"B{ba}", "MASK"], [K_("W0")])
                      P.op("act", lambda e: e.activation(W1[:], W0[:], AF.Exp), [K_("W0")], [K_("W1")])
                      P.op("dve", lambda e: e.tensor_tensor(W2[:], B[bb_], SM[d], ALU.mult), [f"B{bb_}", "MASK"], [K_("W2")])
                      P.op("pool", lambda e: e.tensor_tensor(W2[:], W2[:], W1[:], ALU.mult), [K_("W2"), K_("W1")], [K_("W2")])
                      yield
                      for h in range(4):
                          dh = d * 4 + h
                          P.op("dve", lambda e, h=h, dh=dh: e.tensor_scalar(hs(KBE[:], h), Bbf[bc][:, h * 128:(h + 1) * 128], CF[:, 8 + dh:9 + dh], None, ALU.mult),
                               [f"B{bc}", K_("CF")], [K_("KBE")])
                          P.op("dve", lambda e, h=h, dh=dh: e.tensor_scalar(hs(KDEC[:], h), Bbf[bc][:, h * 128:(h + 1) * 128], CF[:, 16 + dh:17 + dh], None, ALU.mult),
                               [f"B{bc}", K_("CF")], [K_("KDEC")])
                          P.op("dve", lambda e, h=h, dh=dh: e.tensor_scalar(hs(VB[:], h), Bbf[bc][:, 512 + h * 128:512 + (h + 1) * 128], COLS[:, 8 + dh:9 + dh], None, ALU.mult),
                               [f"B{bc}", K_("COLS")], [K_("VB")])
                      mm4(be, kT, kT, [kqk])
                      yield
                      mm4(bc, kT, qT, [kqk])
                      P.op("dve", lambda e: e.tensor_tensor(ATB[:], B[be], W2[:], ALU.mult), [f"B{be}", K_("W2")], [K_("ATB")])
                      yield
                      P.op("dve", lambda e: e.tensor_tensor(PTB[:], B[bc], W1[:], ALU.mult), [f"B{bc}", K_("W1")], [K_("PTB")])
                      for h in range(4):
                          P.op("pe", lambda e, h=h: e.transpose(Bbf[ba][:, h * 128:(h + 1) * 128], hs(ATB[:], h), IDB[:]), [K_("ATB"), "IDB"], [f"B{ba}"], inc=(h == 3))
                      P.op("pool", lambda e: e.tensor_tensor(PN[0][:], ATB[:], BLK16, ALU.mult), [K_("ATB"), "MASK"], [K_("PN0")])
                      P.op("pool", lambda e: e.tensor_tensor(RTB[:], ID4, PN[0][:], ALU.subtract), ["MASK", K_("PN0")], [K_("RTB")])
                      yield
                      P.op("act", lambda e: e.copy(AB[:], Bbf[ba][:, 0:512]), [f"B{ba}"], [K_("AB")])
                      P.op("dve", lambda e: e.tensor_tensor(PTN[0][:], Bbf[ba][:, 0:512], BLK16, ALU.mult), [f"B{ba}", "MASK"], [K_("PTN0")])
                      P.op("pool", lambda e: e.tensor_tensor(RB[:], ID4, PTN[0][:], ALU.subtract), ["MASK", K_("PTN0")], [K_("RB")])
                      yield
                      cur = 0
                      for it in range(3):
                          nx = 1 - cur
                          kc_, kn_ = (K_(f"PTN{cur}"), K_(f"PN{cur}")), (K_(f"PTN{nx}"), K_(f"PN{nx}"))
                          mm4(ba, lambda h: hs(PTN[cur][:], h), lambda h: hs(PN[cur][:], h), list(kc_))
                          mm4(bb_, lambda h: hs(PN[cur][:], h), lambda h: hs(PTN[cur][:], h), list(kc_))
                          yield
                          P.op("act", lambda e, nx=nx: e.copy(PN[nx][:], B[ba]), [f"B{ba}"], [kn_[1]])
                          P.op("dve", lambda e, nx=nx: e.tensor_copy(PTN[nx][:], B[bb_]), [f"B{bb_}"], [kn_[0]])
                          mm4(bc, lambda h: hs(PTN[nx][:], h), lambda h: hs(RTB[:], h), [kn_[0], K_("RTB")])
                          mm4(be, lambda h: hs(PN[nx][:], h), lambda h: hs(RB[:], h), [kn_[1], K_("RB")])
                          yield
                          P.op("dve", lambda e: e.tensor_tensor(RTB[:], B[bc], RTB[:], ALU.add), [f"B{bc}", K_("RTB")], [K_("RTB")])
                          P.op("dve", lambda e: e.tensor_tensor(RB[:], B[be], RB[:], ALU.add), [f"B{be}", K_("RB")], [K_("RB")])
                          cur = nx
                      for szm in (32, 64, 128):
                          if szm < 128:
                              mm4(ba, lambda h: hs(ATB[:], h), lambda h: hs(RB[:], h), [K_("ATB"), K_("RB")])
                          mm4(bb_, lambda h: hs(AB[:], h), lambda h: hs(RTB[:], h), [K_("AB"), K_("RTB")])
                          yield
                          if szm < 128:
                              P.op("dve", lambda e, szm=szm: e.tensor_tensor(ZM[:], B[ba], OFF[szm], ALU.mult), [f"B{ba}", "MASK"], [K_("ZM")])
                          P.op("dve", lambda e, szm=szm: e.tensor_tensor(YM[:], B[bb_], OFF[szm], ALU.mult), [f"B{bb_}", "MASK"], [K_("YM")])
                          if szm < 128:
                              mm4(bc, lambda h: hs(RTB[:], h), lambda h: hs(ZM[:], h), [K_("RTB"), K_("ZM")])
                          mm4(be, lambda h: hs(RB[:], h), lambda h: hs(YM[:], h), [K_("RB"), K_("YM")])
                          yield
                          if szm < 128:
                              P.op("dve", lambda e: e.tensor_tensor(RB[:], RB[:], B[bc], ALU.subtract), [f"B{bc}", K_("RB")], [K_("RB")])
                          P.op("dve", lambda e: e.tensor_tensor(RTB[:], RTB[:], B[be], ALU.subtract), [f"B{be}", K_("RTB")], [K_("RTB")])
                      mm4(ba, lambda h: hs(RTB[:], h), lambda h: hs(VB[:], h), [K_("RTB"), K_("VB")])
                      mm4(bb_, lambda h: hs(KBE[:], h), lambda h: hs(RTB[:], h), [K_("KBE"), K_("RTB")])
                      mm4(bc, lambda h: SEL[:, (d * 4 + h) * 128:(d * 4 + h + 1) * 128], lambda h: GC[:], ["SEL", K_("GC")])
                      yield
                      P.op("act", lambda e: e.copy(U0[:], B[ba]), [f"B{ba}"], [K_("U0")])
                      P.op("act", lambda e: e.copy(WTB[:], B[bb_]), [f"B{bb_}"], [K_("WTB")])
                      P.op("act", lambda e: e.activation(W3[:], B[bc], AF.Exp), [f"B{bc}"], [K_("W3")])
                      P.op("dve", lambda e: e.tensor_tensor(h4(QDT[:]), kq[:, 0:4, :], h4(W3[:]), ALU.mult), [kqk, K_("W3")], [K_("QDT")])
                      yield "scan"
                      mm4(be, lambda h: hs(WTB[:], h), lambda h: hs(SBF[:], h), [K_("WTB"), "SBF"])
                      P.op("dve", lambda e: e.tensor_tensor(UB[:], U0[:], B[be], ALU.subtract), [K_("U0"), f"B{be}"], [K_("UB")])
                      if need_out:
                          for h in range(4):
                              P.op("pe", lambda e, h=h: e.matmul(hs(B[ba], h), hs(SBF[:], h), hs(QDT[:], h), start=True, stop=False), ["SBF", K_("QDT")], [f"B{ba}"], inc=False)
                              P.op("pe", lambda e, h=h: e.matmul(hs(B[ba], h), hs(UB[:], h), hs(PTB[:], h), start=False, stop=True), [K_("UB"), K_("PTB")], [f"B{ba}"], inc=(h == 3))
                      mm4(bb_, lambda h: hs(KDEC[:], h), lambda h: hs(UB[:], h), [K_("KDEC"), K_("UB")])
                      for h in range(4):
                          dh = d * 4 + h
                          P.op("dve", lambda e, h=h, dh=dh: e.scalar_tensor_tensor(hs(S32[:], h), hs(S32[:], h), CF[:, 24 + dh:25 + dh], hs(B[bb_], h), ALU.mult, ALU.add),
                               ["S32", K_("CF"), f"B{bb_}"], ["S32"])
                      P.op("act", lambda e: e.copy(SBF[:], S32[:]), ["S32"], ["SBF"])
                      if need_out:
                          if d == 0:
                              P.op("act", lambda e: e.copy(OF[:, :, tk:tk + 128], h4(B[ba])), [f"B{ba}"], [("OF", tk)])
                          else:
                              P.op("dve", lambda e: e.tensor_tensor(h4(W0[:]), h4(B[ba]), OF[:, :, tk:tk + 128], ALU.add), [f"B{ba}", ("OF", tk)], [K_("W0")])
                              P.op("act", lambda e: e.activation(SQO[:], W0[:], AF.Square), [K_("W0")], [K_("SQO")])
                              mm4(bc, lambda h: ONESB[:], lambda h: hs(SQO[:], h), ["ONESB", K_("SQO")])
                              yield
                              P.op("act", lambda e: e.activation(W1[:], B[bc], AF.Ln, bias=EPSC, scale=1.0 / 128), [f"B{bc}"], [K_("W1")])
                              P.op("act", lambda e: e.activation(W1[:], W1[:], AF.Exp, scale=-0.5), [K_("W1")], [K_("W1")])
                              P.op("pool", lambda e: e.tensor_tensor(W0[:], W0[:], W1[:], ALU.mult), [K_("W0"), K_("W1")], [K_("W0")])
                              P.op("dve", lambda e: e.scalar_tensor_tensor(BIG[:, 4:8, tk:tk + 128], h4(W0[:]), GN, sz[:, :, :], ALU.mult, ALU.mult),
                                   [K_("W0"), "LWg", szk], [("BIG", b)])

                  for d in range(2):
                      P.op("pool", lambda e: e.memset(S32[:], 0.0), [], ["S32"])
                      P.op("pool", lambda e: e.memset(SBF[:], 0.0), [], ["SBF"])
                      border = list(range(NBLK)) if d == 0 else [0] + list(range(8, 0, -1))
                      tasks = []
                      for b in border:
                          t0, nt = blk_range(b)
                          need_out = not (last and b == 0)
                          nch = nt // 128
                          chs = list(range(nch)) if d == 0 else list(range(nch - 1, -1, -1))
                          for ci, ch in enumerate(chs):
                              tasks.append((b, ch, need_out, ci == 0))
                      active = []
                      nxt = 0
                      scan_turn = 0
                      free_slots = list(range(NSLOT))
                      while nxt < len(tasks) or active:
                          if nxt < len(tasks) and free_slots and (not active or len(active) < NSLOT):
                              b, ch, need_out, first = tasks[nxt]
                              if first and d == 1 and need_out:
                                  t0, nt = blk_range(b)
                                  P.dma("sp", BIG[:, 0:4, t0:t0 + nt], ya_s[b, :, :, 0:nt], [("ya", b)], [("BIG", b)], "yal")
                              si = free_slots.pop(0)
                              active.append([chunk_gen(slots[si], d, b, ch, need_out), nxt, False, si])
                              nxt += 1
                          for ent in list(active):
                              g, ti, wscan, si = ent
                              if wscan and ti != scan_turn:
                                  continue
                              try:
                                  r = next(g)
                                  if wscan:
                                      scan_turn += 1
                                      ent[2] = False
                                  if r == "scan":
                                      ent[2] = True
                              except StopIteration:
                                  if wscan:
                                      scan_turn += 1
                                  active.remove(ent)
                                  free_slots.append(si)
                  P.barrier()
              ck("s2")
              with ExitStack() as s4:
                  WOUT = sb("WOUT", [128, 8, D], BF16, s4)
                  XS = sb("XS4", [128, 8, 512], F32, s4)
                  WS4 = sb("WS4", [128, 2, D], F32, s4)
                  SQB = sb("SQB4", [128, 8, 512], BF16, s4)
                  T0 = sb("T04", [128, 512], F32, s4)
                  YP = sb("YP", [128, 8, 512], BF16, s4) if colmaj else None
                  for k in range(8):
                      s = k % 2
                      P.dma("sp", WS4[:, s, :], w_out[l, k * 128:(k + 1) * 128, :], [], [("WS4", s)], ("ws4", s))
                      P.op("dve" if s == 0 else "pool", lambda e, k=k, s=s: e.tensor_copy(WOUT[:, k, :], WS4[:, s, :]), [("WS4", s)], ["WOUT"])
                  pp = 0
                  for nb in range(NBLK):
                      if last and nb == 0:
                          continue
                      t0, nt = blk_range(nb)
                      v = 1 if nb == 0 else 0
                      P.dma("sp", XS[:, :, 0:nt], xsrc_v[:, :, t0:t0 + nt], [("x", nb)], ["XS4"], "xs4")
                      if colmaj and nb > 0:
                          for k in range(8):
                              if k % 2 == 0:
                                  P.op("pool", lambda e, k=k: e.tensor_copy(YP[:, k, :].rearrange("p (a b) -> p a b", b=64), big_view(k, nb, True, True)),
                                       big_keys(nb, True), ["YP"])
                              else:
                                  P.op("act", lambda e, k=k: e.copy(YP[:, k, :].rearrange("p (a b) -> p a b", b=64), big_view(k, nb, True, True)),
                                       big_keys(nb, True), ["YP"])
                      for jn in range(8):
                          bi = (0, 1, 3, 4)[pp % 4]
                          pp += 1
                          for kc in range(8):
                              if colmaj and nb > 0:
                                  P.op("pe", lambda e, kc=kc, jn=jn, bi=bi: e.matmul(
                                      B[bi][:, 0:nt], WOUT[:, kc, jn * 128:(jn + 1) * 128], YP[:, kc, :],
                                      start=(kc == 0), stop=(kc == 7)), ["WOUT", "YP"], [f"B{bi}"], inc=(kc == 7))
                              else:
                                  P.op("pe", lambda e, kc=kc, jn=jn, bi=bi: e.matmul(
                                      B[bi][:, 0:nt], WOUT[:, kc, jn * 128:(jn + 1) * 128], BIG[:, kc, t0:t0 + nt],
                                      start=(kc == 0), stop=(kc == 7)), ["WOUT"] + big_keys(nb, False), [f"B{bi}"], inc=(kc == 7))
                          P.op("dve", lambda e, jn=jn, bi=bi, v=v: e.scalar_tensor_tensor(
                              XS[:, jn, 0:nt], B[bi][:, 0:nt], mcol(v, 16 + jn), XS[:, jn, 0:nt], ALU.mult, ALU.add),
                              [f"B{bi}", "XS4"], [("XSo", jn)])
                      okeys = [("XSo", jn) for jn in range(8)]
                      if not last:
                          P.dma("pool", xres_v[:, :, t0:t0 + nt], XS[:, :, 0:nt], okeys, [("x", nb)], "xst")
                          P.op("dve", lambda e: e.engine_nop(), [("x", nb)], ["XS4"])
                      else:
                          P.op("act", lambda e: e.activation(SQB[:], XS[:], AF.Square), okeys, ["SQB4"])
                          for k in range(8):
                              P.op("pe", lambda e, k=k: e.matmul(B[2], ONESB[:], SQB[:, k, :], start=(k == 0), stop=(k == 7)), ["SQB4", "ONESB"], ["B2"], inc=(k == 7))
                          P.op("act", lambda e: e.activation(T0[:], B[2], AF.Ln, bias=EPSC, scale=1.0 / D), ["B2"], ["T04"])
                          P.op("act", lambda e: e.activation(T0[:], T0[:], AF.Exp, scale=-0.5), ["T04"], ["T04"])
                          for k in range(8):
                              P.op("dve", lambda e, k=k: e.scalar_tensor_tensor(XS[:, k, :], XS[:, k, :], FN[:, k:k + 1], T0[:], ALU.mult, ALU.mult),
                                   [("XSo", k), "T04", "FN", "SQB4"], [("XSf", k)])
                          fk = [("XSf", k) for k in range(8)]
                          P.dma("pool", out_v[:, :, t0 - NCTX:t0 - NCTX + nt], XS[:, :, 0:nt], fk, [("out", nb)], "ost")
                          P.op("dve", lambda e: e.engine_nop(), [("out", nb)], ["XS4"])
                  P.barrier()
          P.barrier()

    except _Stop:
        pass
    return nc


def _consts():
    i = np.arange(128)
    t, s = i[None, :], i[:, None]
    negi_f = np.where(t >= s, 0.0, BIGNEG)
    negi_b = np.where(t <= s, 0.0, BIGNEG)
    sm_f = (t > s).astype(np.float64)
    sm_b = (t < s).astype(np.float64)
    blk = lambda z: ((s // z) == (t // z)).astype(np.float64)
    blk16 = blk(16)
    off = {z: blk(z) * (1 - blk(z // 2)) for z in (32, 64, 128)}
    ident = np.eye(128)
    ms = [negi_f, negi_b, sm_f, sm_b, blk16, off[32], off[64], off[128], ident]
    cmask = np.concatenate([np.tile(m, (1, 4)) for m in ms], axis=1).astype(np.float32)
    sel = np.zeros((8, 2 * 1024 + 128), np.float32)
    for dh in range(8):
        sel[dh, dh * 128:(dh + 1) * 128] = 1.0
        sel[dh, 1024 + dh * 128:1024 + (dh + 1) * 128] = -1.0
    sel[:, 2048:] = 1.0
    return cmask, ident.astype(np.float32), sel


_NC_CACHE = {}


def make_in_maps(x, c, ctx, c_ctx, norm_w, w_mod, b_mod, w_in, conv_a, conv_qkv, a_log, dt_bias, gdn_norm, w_out, final_norm):
    f = lambda a: np.ascontiguousarray(np.asarray(a, dtype=np.float32))
    x, c, ctx, c_ctx = f(x), f(c), f(ctx), f(c_ctx)
    cmask, ident, sel = _consts()
    L = norm_w.shape[0]
    col = lambda a: np.ascontiguousarray(a.reshape(-1, 128).T)
    shared = {
        "w_mod": f(w_mod), "w_in": f(w_in), "w_out": f(w_out),
        "bmod": np.stack([col(f(b_mod)[l]) for l in range(L)]),
        "normw": np.stack([col(f(norm_w)[l]) for l in range(L)]),
        "conva": np.stack([np.ascontiguousarray(f(conv_a)[l].T.reshape(4, 128, 3).transpose(1, 0, 2).reshape(128, 12)) for l in range(L)]),
        "convq": np.stack([np.ascontiguousarray(f(conv_qkv)[l].T.reshape(12, 128, 3).transpose(1, 0, 2).reshape(128, 36)) for l in range(L)]),
        "alog": np.ascontiguousarray(f(a_log).reshape(L, 8, 1)),
        "dtb": np.ascontiguousarray(f(dt_bias).reshape(L, 8, 1)),
        "gnorm": np.ascontiguousarray(f(gdn_norm).reshape(L, 128, 1)),
        "fnorm": col(f(final_norm)),
        "cmask": cmask, "cident": ident, "csel": sel,
    }
    maps = []
    for b in range(x.shape[0]):
        m = dict(shared)
        m["xT"] = np.ascontiguousarray(np.concatenate([ctx[b], x[b]], axis=0).T)
        ccb = np.stack([col(c[b]), col(c_ctx)], axis=-1).reshape(128, 16)
        m["cc"] = np.ascontiguousarray(ccb)
        maps.append(m)
    return maps


def kernel(x, c, ctx, c_ctx, norm_w, w_mod, b_mod, w_in, conv_a, conv_qkv, a_log, dt_bias, gdn_norm, w_out, final_norm, _nlayers=DEPTH):
    maps = make_in_maps(x, c, ctx, c_ctx, norm_w, w_mod, b_mod, w_in, conv_a, conv_qkv, a_log, dt_bias, gdn_norm, w_out, final_norm)
    if _nlayers not in _NC_CACHE:
        _NC_CACHE[_nlayers] = build(_nlayers)
    nc = _NC_CACHE[_nlayers]
    res = run_bass_kernel_spmd(nc, maps, core_ids=list(range(len(maps))))
    out = np.stack([np.ascontiguousarray(r["outT"].T) for r in res.results], axis=0)
    return out.astype(np.float32)
```

```python
import numpy as np
from contextlib import ExitStack
import concourse.bass as bass
import concourse.mybir as mybir
from concourse.bass_utils import run_bass_kernel_spmd

F32 = mybir.dt.float32
BF16 = mybir.dt.bfloat16
AF = mybir.ActivationFunctionType
ALU = mybir.AluOpType

D = 1024
NCTX = 256
NLAT = 4096
NTOK = NCTX + NLAT
DPROJ = 4112
DEPTH = 4
EPS = 1e-6
NBLK = 9
BIGNEG = -30000.0
PSUM_KEYS = {f"B{i}" for i in range(8)}


def blk_range(b):
    if b == 0:
        return 0, NCTX
    return NCTX + (b - 1) * 512, 512


class _Stop(Exception):
    pass


class Prog:
    def __init__(self, nc, es):
        self.nc = nc
        self.es = es
        self.eng = {"pe": nc.tensor, "act": nc.scalar, "dve": nc.vector, "pool": nc.gpsimd, "sp": nc.sync}
        self.sem = {}
        self.cnt = {}
        self.epoch = 0
        self.state = {}
        self.waited = {}
        self.dsem = {}
        self.dcnt = {}
        self.all_sems = {}
        self.nops = 0
        self.stop_at = None
        self.new_epoch()

    def new_epoch(self):
        self.epoch += 1
        for e in ("pe", "act", "dve", "pool"):
            s = self.es.enter_context(self.nc.semaphore(f"s_{e}_{self.epoch}"))
            self.sem[e] = s
            self.cnt[e] = 0
            self.all_sems[id(s)] = s

    def _collect(self, engine, reads, writes):
        need = {}

        def add(ev):
            s, v, e = ev
            k = id(s)
            if k not in need or need[k][1] < v:
                need[k] = (s, v, e)

        for k in reads:
            st = self.state.get(k)
            if st is not None:
                for ev in st["w"].values():
                    add(ev)
                if k in PSUM_KEYS:
                    for ev in st["r"].values():
                        if ev[2] != engine:
                            add(ev)
        for k in writes:
            st = self.state.get(k)
            if st is not None:
                for ev in st["w"].values():
                    if ev[2] != engine:
                        add(ev)
                for ev in st["r"].values():
                    if ev[2] != engine:
                        add(ev)
        out = []
        wd = self.waited.setdefault(engine, {})
        for k, (s, v, e) in need.items():
            if wd.get(k, 0) >= v:
                continue
            wd[k] = v
            out.append((s, v))
        return out

    def _record(self, ev, reads, writes):
        for k in reads:
            st = self.state.setdefault(k, {"w": {}, "r": {}})
            st["r"][id(ev[0])] = ev
        for k in writes:
            st = self.state.setdefault(k, {"w": {}, "r": {}})
            st["w"][id(ev[0])] = ev

    def op(self, engine, fn, reads=(), writes=(), inc=True):
        e = self.eng[engine]
        for s, v in self._collect(engine, reads, writes):
            e.wait_ge(s, v)
        inst = fn(e)
        if inc:
            self.cnt[engine] += 1
            inst.then_inc(self.sem[engine], 1)
            self._record((self.sem[engine], self.cnt[engine], engine), reads, writes)
        else:
            self._record((self.sem[engine], self.cnt[engine] + 1, engine), reads, writes)
        self.nops += 1
        if self.stop_at is not None and self.nops == self.stop_at:
            self.barrier()
            raise _Stop()

    def dma(self, queue, out, in_, reads, writes, slot):
        e = self.eng[queue]
        for s, v in self._collect("q_" + queue, reads, writes):
            e.wait_ge(s, v)
        if slot not in self.dsem:
            self.dsem[slot] = self.es.enter_context(self.nc.semaphore(f"d_{len(self.dsem)}"))
            self.dcnt[slot] = 0
        self.dcnt[slot] += 16
        e.dma_start(out=out, in_=in_).then_inc(self.dsem[slot], 16)
        self._record((self.dsem[slot], self.dcnt[slot], "dma_" + str(slot)), reads, writes)

    def barrier(self):
        evs = [(self.sem[e], self.cnt[e]) for e in ("pe", "act", "dve", "pool") if self.cnt[e] > 0]
        evs += [(self.dsem[s], self.dcnt[s]) for s in self.dsem]
        for en in ("pe", "act", "dve", "pool", "sp"):
            key = en if en != "sp" else "q_sp"
            wd = self.waited.setdefault(key, {})
            for s, v in evs:
                if en in self.sem and s is self.sem.get(en):
                    continue
                if wd.get(id(s), 0) >= v:
                    continue
                wd[id(s)] = v
                self.eng[en].wait_ge(s, v)
        wd = self.waited.setdefault("q_pool", {})
        for s, v in evs:
            wd[id(s)] = max(wd.get(id(s), 0), v)
        self.state = {}


def build(nlayers=DEPTH, stop=None):
    def ck(name):
        if stop == name:
            print("ck", name, "nops", P.nops)
            raise _Stop()
    nc = bass.Bass("TRN2", target_bir_lowering=False)
    dt_in = lambda n, shp, dt=F32: nc.dram_tensor(n, list(shp), dt, kind="ExternalInput").ap()
    xT = dt_in("xT", [D, NTOK])
    cc = dt_in("cc", [128, 16])
    w_mod = dt_in("w_mod", [DEPTH, D, 3 * D])
    bmod = dt_in("bmod", [DEPTH, 128, 24])
    normw = dt_in("normw", [DEPTH, 128, 8])
    w_in = dt_in("w_in", [DEPTH, D, DPROJ])
    conva = dt_in("conva", [DEPTH, 128, 12])
    convq = dt_in("convq", [DEPTH, 128, 36])
    alog = dt_in("alog", [DEPTH, 8, 1])
    dtb = dt_in("dtb", [DEPTH, 8, 1])
    gnorm = dt_in("gnorm", [DEPTH, 128, 1])
    w_out = dt_in("w_out", [DEPTH, D, D])
    fnorm = dt_in("fnorm", [128, 8])
    cmask = dt_in("cmask", [128, 9 * 512])
    cident = dt_in("cident", [128, 128])
    csel = dt_in("csel", [8, 2 * 1024 + 128])
    outT = nc.dram_tensor("outT", [D, NLAT], F32, kind="ExternalOutput").ap()
    xres = nc.dram_tensor("xres", [D, NTOK], F32).ap()
    kqv_s = nc.dram_tensor("kqv_s", [NBLK, 128, 12, 512], BF16).ap()
    ya_s = nc.dram_tensor("ya_s", [NBLK, 128, 4, 512], BF16).ap()
    szb_s = nc.dram_tensor("szb_s", [NBLK, 128, 4, 512], BF16).ap()
    rows_s = nc.dram_tensor("rows_s", [NBLK, 8, 2, 512], F32).ap()

    es = ExitStack()
    try:
      with es:
          P = Prog(nc, es)
          if isinstance(stop, int):
              P.stop_at = stop
          _uniq = [0]

          def sb(n, shp, dt=F32, st=es):
              _uniq[0] += 1
              return st.enter_context(nc.sbuf_tensor(f"{n}_{_uniq[0]}", list(shp), dt))
          BIG = sb("BIG", [128, 8, NTOK], BF16)
          MASK = sb("MASK", [128, 9, 512], BF16)
          IDF = sb("IDF", [128, 128], F32)
          IDB = sb("IDB", [128, 128], BF16)
          ONESB = sb("ONESB", [128, 128], BF16)
          SEL = sb("SEL", [8, 2 * 1024 + 128], F32)
          MODS = sb("MODS", [128, DEPTH, 2, 24], F32)
          CC = sb("CC", [128, 16], F32)
          FN = sb("FN", [128, 8], F32)
          LW = sb("LW", [128, 64], F32)
          L8 = sb("L8", [8, 4], F32)
          DUM = sb("DUM", [128, 2], F32)
          EPST = sb("EPST", [128, 1], F32)
          EPSC = EPST[:, 0:1]
          banks = [es.enter_context(nc.psum_tensor(f"B{i}", [128, 512], F32)) for i in range(8)]
          B = [b[:] for b in banks]
          Bbf = [b[:].bitcast(BF16) for b in banks]

          NEGI = [MASK[:, 0, :], MASK[:, 1, :]]
          SM = [MASK[:, 2, :], MASK[:, 3, :]]
          BLK16 = MASK[:, 4, :]
          OFF = {32: MASK[:, 5, :], 64: MASK[:, 6, :], 128: MASK[:, 7, :]}
          ID4 = MASK[:, 8, :]
          I8 = IDF[0:8, 0:8]
          ONES8 = SEL[:, 2048:2176]

          def h4(ap):
              return ap.rearrange("p (h t) -> p h t", h=4)

          with ExitStack() as st0:
              MST = sb("MST", [128, 9 * 512], F32, st0)
              WST = sb("WST", [128, 2, 4096], F32, st0)
              SC = sb("SC", [128, 16], F32, st0)
              BM = sb("BM", [128, 24], F32, st0)
              NW = sb("NW", [128, 8], F32, st0)
              P.dma("sp", MST[:], cmask, [], ["MST"], "c0")
              P.dma("sp", IDF[:], cident, [], ["IDF"], "c1")
              P.dma("sp", SEL[:], csel, [], ["SEL"], "c2")
              P.dma("sp", CC[:], cc, [], ["CC"], "c3")
              P.dma("sp", FN[:], fnorm, [], ["FN"], "c4")
              P.op("dve", lambda e: e.tensor_copy(MASK[:].rearrange("p a b -> p (a b)"), MST[:]), ["MST"], ["MASK"])
              P.op("dve", lambda e: e.tensor_copy(IDB[:], IDF[:]), ["IDF"], ["IDB"])
              P.op("dve", lambda e: e.memset(ONESB[:], 1.0), [], ["ONESB"])
              P.op("dve", lambda e: e.memset(DUM[:], 0.0), [], ["DUM"])
              P.op("dve", lambda e: e.memset(EPST[:], EPS), [], ["EPST"])
              P.op("act", lambda e: e.activation(SC[:], CC[:], AF.Silu), ["CC"], ["SC"])
              for l in range(nlayers):
                  P.dma("sp", BM[:], bmod[l], [], ["BM"], "c5")
                  P.dma("sp", NW[:], normw[l], [], ["NW"], "c6")
                  for jg in range(6):
                      s = jg % 2
                      P.dma("sp", WST[:, s, :].rearrange("p (k n) -> p k n", k=8),
                            w_mod[l, :, jg * 512:(jg + 1) * 512].rearrange("(k p) n -> p k n", p=128), [], [("WST", s)], ("wst", s))
                      for jj in range(4):
                          j = jg * 4 + jj
                          for k in range(8):
                              P.op("pe", lambda e, j=j, jj=jj, k=k, s=s: e.matmul(
                                  B[0][:, 2 * j:2 * j + 2], WST[:, s, k * 512 + jj * 128:k * 512 + (jj + 1) * 128],
                                  SC[:].rearrange("p (k v) -> p k v", v=2)[:, k, :], start=(k == 0), stop=(k == 7)),
                                  [("WST", s), "SC"], ["B0"])
                  for v in range(2):
                      P.op("dve", lambda e, v=v, l=l: e.tensor_tensor(
                          MODS[:, l, v, :], B[0][:, 0:48].rearrange("p (j v) -> p j v", v=2)[:, :, v], BM[:], ALU.add),
                          ["B0", "BM"], [("MODS", l)])
                      P.op("dve", lambda e, v=v, l=l: e.scalar_tensor_tensor(
                          MODS[:, l, v, 8:16], MODS[:, l, v, 8:16], 1.0, NW[:], ALU.add, ALU.mult),
                          [("MODS", l), "NW"], [("MODS", l)])
              P.barrier()
          ck("s0")

          for l in range(nlayers):
              P.new_epoch()
              colmaj = (l % 2 == 1)
              last = (l == nlayers - 1)
              xsrc = xT if l == 0 else xres
              xsrc_v = xsrc.rearrange("(k p) t -> p k t", p=128)
              xres_v = xres.rearrange("(k p) t -> p k t", p=128)
              out_v = outT.rearrange("(k p) t -> p k t", p=128)

              def mcol(v, j, l=l):
                  return MODS[:, l, v, j:j + 1]

              P.dma("sp", LW[:, 0:12], conva[l], [], ["LWa"], "c7")
              P.dma("sp", LW[:, 12:48], convq[l], [], ["LWq"], "c8")
              P.dma("sp", LW[:, 48:49], gnorm[l], [], ["LWg"], "c9")
              P.dma("sp", L8[:, 0:1], alog[l], [], ["L8a"], "c10")
              P.dma("sp", L8[:, 1:2], dtb[l], [], ["L8"], "c11")
              P.op("act", lambda e: e.activation(L8[:, 2:3], L8[:, 0:1], AF.Exp), ["L8a"], ["L8"])
              P.op("dve", lambda e: e.tensor_scalar(L8[:, 2:3], L8[:, 2:3], -1.0, None, ALU.mult), ["L8"], ["L8"])
              CA = lambda j, tap: LW[:, j * 3 + tap: j * 3 + tap + 1]
              CQ = lambda j, tap: LW[:, 12 + j * 3 + tap: 12 + j * 3 + tap + 1]
              GN = LW[:, 48:49]

              def big_keys(b, permuted):
                  if b == 0 or not permuted:
                      return [("BIG", b)]
                  return [("BIG", i) for i in range(1, 9)]

              def big_view(kc, b, permuted, rowmajor_of_colscan):
                  t0, nt = blk_range(b)
                  if b == 0 or not permuted:
                      return BIG[:, kc, t0:t0 + nt]
                  lat = BIG[:, kc, NCTX:NTOK]
                  if rowmajor_of_colscan:
                      v = lat.rearrange("p (c r) -> p r c", r=64)
                  else:
                      v = lat.rearrange("p (r c) -> p c r", c=64)
                  return v[:, (b - 1) * 8:(b - 1) * 8 + 8, :]

              def pview(ap, b, permuted):
                  t0, nt = blk_range(b)
                  if b == 0 or not permuted:
                      return ap[:, 0:nt]
                  return ap.rearrange("p (a b) -> p a b", b=64)

              with ExitStack() as s1:
                  WIN = sb("WIN", [128, 8, DPROJ], BF16, s1)
                  XSF = sb("XS", [128, 4112], F32, s1)
                  XS = XSF[:, 0:4096].rearrange("p (k t) -> p k t", k=8)
                  T0 = sb("T0", [128, 512], F32, s1)
                  T1 = sb("T1", [128, 512], F32, s1)
                  T2 = sb("T2", [128, 512], F32, s1)
                  SQQ = sb("SQQ", [128, 512], BF16, s1)
                  KQVB = sb("KQVB", [128, 12, 512], BF16, s1)
                  SQB = KQVB[:, 0:8, :]
                  YAB = sb("YAB", [128, 4, 512], BF16, s1)
                  SZBB = sb("SZBB", [128, 4, 512], BF16, s1)
                  ROWB = sb("ROWB", [8, 2, 512], F32, s1)
                  HP = sb("HP", [128, 8, 512], BF16, s1) if colmaj else None
                  XSW = XSF[:]
                  i = 0
                  for k in range(8):
                      for hf in range(2):
                          s = i % 2
                          c0 = hf * 2056
                          P.dma("sp", XSW[:, s * 2056:(s + 1) * 2056], w_in[l, k * 128:(k + 1) * 128, c0:c0 + 2056],
                                [], [("XS", s)], ("xs", s))
                          eng = ("dve", "pool", "act")[i % 3]
                          if eng == "act":
                              P.op("act", lambda e, k=k, s=s, c0=c0: e.copy(WIN[:, k, c0:c0 + 2056], XSW[:, s * 2056:(s + 1) * 2056]),
                                   [("XS", s)], ["WIN"])
                          else:
                              P.op(eng, lambda e, k=k, s=s, c0=c0: e.tensor_copy(WIN[:, k, c0:c0 + 2056], XSW[:, s * 2056:(s + 1) * 2056]),
                                   [("XS", s)], ["WIN"])
                          i += 1
                  if stop == "s1w":
                      P.barrier()
                      ck("s1w")
                  for nb in range(NBLK):
                      t0, nt = blk_range(nb)
                      v = 1 if nb == 0 else 0
                      P.dma("sp", XS[:, :, 0:nt], xsrc_v[:, :, t0:t0 + nt], [("x", nb)], [("XS", 0), ("XS", 1)], ("xs", 0))
                      P.op("act", lambda e, nt=nt: e.activation(SQB[:, :, 0:nt], XS[:, :, 0:nt], AF.Square),
                           [("XS", 0), ("XS", 1)], ["KQVB"])
                      for k in range(8):
                          P.op("pe", lambda e, k=k, nt=nt: e.matmul(B[2][:, 0:nt], ONESB[:], SQB[:, k, 0:nt], start=(k == 0), stop=(k == 7)),
                               ["KQVB", "ONESB"], ["B2"], inc=(k == 7))
                      P.op("act", lambda e, nt=nt: e.activation(T0[:, 0:nt], B[2][:, 0:nt], AF.Ln, bias=EPSC, scale=1.0 / D), ["B2"], ["T0"])
                      P.op("act", lambda e, nt=nt: e.activation(T0[:, 0:nt], T0[:, 0:nt], AF.Exp, scale=-0.5), ["T0"], ["T0"])
                      for k in range(8):
                          eng = "dve" if k % 2 == 0 else "pool"
                          P.op(eng, lambda e, k=k, nt=nt: e.tensor_tensor(XS[:, k, 0:nt], XS[:, k, 0:nt], T0[:, 0:nt], ALU.mult),
                               [("XS", 0), ("XS", 1), "T0", "KQVB"], [("XSn", k)])
                          P.op("act", lambda e, k=k, nt=nt, t0=t0, v=v: e.activation(
                              BIG[:, k, t0:t0 + nt], XS[:, k, 0:nt], AF.Identity, bias=mcol(v, k), scale=mcol(v, 8 + k)),
                              [("XSn", k)], [("BIG", nb)])
                      P.op("act", lambda e: e.activation(DUM[:, 0:1], DUM[:, 1:2], AF.Copy), [("BIG", nb)] + [("XSn", k) for k in range(8)], [("XS", 0), ("XS", 1)])

                  if stop == "s1p":
                      P.barrier()
                      ck("s1p")
                  P.barrier()
                  TS = [(T0, T1, T2, SQQ, "T0", "T1", "T2", "SQQ", 2)]
                  for i_ in range(2):
                      o_ = i_ * 1792
                      TS.append((XSF[:, o_:o_ + 512], XSF[:, o_ + 512:o_ + 1024], XSF[:, o_ + 1024:o_ + 1536],
                                 XSF[:, o_ + 1536:o_ + 1792].bitcast(BF16), f"T0_{i_}", f"T1_{i_}", f"T2_{i_}", f"SQQ_{i_}", 7 if i_ == 0 else 2))
                  tsi = [0]
                  pp = [0]
                  PROJ_BANKS = [0, 1, 3, 4, 5, 6]

                  def proj(b, c0, m):
                      bi = PROJ_BANKS[pp[0] % len(PROJ_BANKS)]
                      pp[0] += 1
                      t0, nt = blk_range(b)
                      for k in range(8):
                          if colmaj and b > 0:
                              P.op("pe", lambda e, k=k, bi=bi: e.matmul(
                                  B[bi][0:m, 0:nt], WIN[:, k, c0:c0 + m], HP[:, k, :],
                                  start=(k == 0), stop=(k == 7)), ["WIN", "HP"], [f"B{bi}"], inc=(k == 7))
                          else:
                              P.op("pe", lambda e, k=k, bi=bi: e.matmul(
                                  B[bi][0:m, 0:nt], WIN[:, k, c0:c0 + m], BIG[:, k, t0:t0 + nt],
                                  start=(k == 0), stop=(k == 7)), ["WIN"] + big_keys(b, False), [f"B{bi}"], inc=(k == 7))
                      return bi

                  def conv(dst, dkey, src, skey, w, nt, seg):
                      P.op("dve", lambda e: e.tensor_scalar(dst[:, 0:nt], src[:, 0:nt], w(1), None, ALU.mult), [skey, "LWa", "LWq"], [dkey])
                      dv = dst[:, 0:nt].rearrange("p (a b) -> p a b", b=seg)
                      sv = src[:, 0:nt].rearrange("p (a b) -> p a b", b=seg)
                      P.op("dve", lambda e: e.scalar_tensor_tensor(dv[:, :, 1:seg], sv[:, :, 0:seg - 1], w(0), dv[:, :, 1:seg], ALU.mult, ALU.add),
                           [skey, dkey, "LWa", "LWq"], [dkey])
                      P.op("dve", lambda e: e.scalar_tensor_tensor(dv[:, :, 0:seg - 1], sv[:, :, 1:seg], w(2), dv[:, :, 0:seg - 1], ALU.mult, ALU.add),
                           [skey, dkey, "LWa", "LWq"], [dkey])

                  def run_tasks(gens_factories, nsets):
                      pending = list(gens_factories)
                      active = []
                      free = list(range(nsets))
                      while pending or active:
                          if pending and free:
                              si = free.pop(0)
                              active.append((pending.pop(0)(TS[si]), si))
                          for ent in list(active):
                              try:
                                  next(ent[0])
                              except StopIteration:
                                  active.remove(ent)
                                  free.append(ent[1])

                  for b in range(NBLK):
                      t0, nt = blk_range(b)
                      seg = NCTX if b == 0 else 64
                      if colmaj and b > 0:
                          for k in range(8):
                              eng = "pool" if k % 2 == 0 else "act"
                              if eng == "pool":
                                  P.op("pool", lambda e, k=k: e.tensor_copy(HP[:, k, :].rearrange("p (a b) -> p a b", b=64), big_view(k, b, True, False)),
                                       big_keys(b, True), ["HP"])
                              else:
                                  P.op("act", lambda e, k=k: e.copy(HP[:, k, :].rearrange("p (a b) -> p a b", b=64), big_view(k, b, True, False)),
                                       big_keys(b, True), ["HP"])

                      def mixer_task(jj, b=b, nt=nt, seg=seg):
                          def g(ts):
                              T0, T1, T2, SQQ, k0, k1, k2, kq_, ssb = ts
                              bi = proj(b, jj * 128, 128)
                              yield
                              P.op("act", lambda e: e.copy(T0[:, 0:nt], B[bi][:, 0:nt]), [f"B{bi}"], [k0])
                              bi = proj(b, 1024 + jj * 128, 128)
                              yield
                              P.op("dve", lambda e: e.tensor_tensor(T0[:, 0:nt], B[bi][:, 0:nt], T0[:, 0:nt], ALU.mult), [f"B{bi}", k0], [k0])
                              conv(T1, k1, T0, k0, lambda tap: CA(jj, tap), nt, seg)
                              bi = proj(b, 512 + jj * 128, 128)
                              yield
                              P.op("dve", lambda e: e.tensor_tensor(T1[:, 0:nt], B[bi][:, 0:nt], T1[:, 0:nt], ALU.mult), [f"B{bi}", k1], [k1])
                              bi = proj(b, 1536 + jj * 128, 128)
                              yield
                              P.op("act", lambda e: e.activation(T2[:, 0:nt], B[bi][:, 0:nt], AF.Silu), [f"B{bi}"], [k2])
                              yield
                              P.op("pool", lambda e: e.tensor_tensor(YAB[:, jj, 0:nt], T1[:, 0:nt], T2[:, 0:nt], ALU.mult), [k1, k2], ["YAB"])
                          return g

                      def qkv_task(idx, b=b, nt=nt, seg=seg):
                          def g(ts):
                              T0, T1, T2, SQQ, k0, k1, k2, kq_, ssb = ts
                              bi = proj(b, 2048 + idx * 128, 128)
                              yield
                              conv(T1, k1, B[bi], f"B{bi}", lambda tap: CQ(idx, tap), nt, seg)
                              yield
                              if idx >= 8:
                                  P.op("act", lambda e: e.activation(KQVB[:, idx, 0:nt], T1[:, 0:nt], AF.Silu), [k1], ["KQVB"])
                                  return
                              P.op("act", lambda e: e.activation(T2[:, 0:nt], T1[:, 0:nt], AF.Silu), [k1], [k2])
                              P.op("act", lambda e: e.activation(SQQ[:, 0:nt], T2[:, 0:nt], AF.Square), [k2], [kq_])
                              yield
                              P.op("pe", lambda e: e.matmul(B[ssb][:, 0:nt], ONESB[:], SQQ[:, 0:nt], start=True, stop=True), [kq_, "ONESB"], [f"B{ssb}"])
                              yield
                              P.op("act", lambda e: e.activation(T0[:, 0:nt], B[ssb][:, 0:nt], AF.Ln, bias=EPSC, scale=1.0), [f"B{ssb}"], [k0])
                              P.op("act", lambda e: e.activation(T0[:, 0:nt], T0[:, 0:nt], AF.Exp, scale=-0.5), [k0], [k0])
                              yield
                              sc = (128.0 ** -0.5) if idx < 4 else 1.0
                              P.op("dve", lambda e: e.scalar_tensor_tensor(KQVB[:, idx, 0:nt], T2[:, 0:nt], sc, T0[:, 0:nt], ALU.mult, ALU.mult),
                                   [k2, k0], ["KQVB"])
                          return g

                      def zb_task(h, b=b, nt=nt):
                          def g(ts):
                              bi = proj(b, 3584 + h * 128, 128)
                              yield
                              P.op("act", lambda e: e.activation(SZBB[:, h, 0:nt], B[bi][:, 0:nt], AF.Silu), [f"B{bi}"], ["SZBB"])
                          return g

                      def rows_task(b=b, nt=nt):
                          def g(ts):
                              bi = proj(b, 4096, 8)
                              bi2 = proj(b, 4104, 8)
                              yield
                              P.op("act", lambda e: e.activation(ROWB[:, 1, 0:nt], B[bi][0:8, 0:nt], AF.Sigmoid), [f"B{bi}"], ["ROWB"])
                              P.op("act", lambda e: e.activation(ROWB[:, 0, 0:nt], B[bi2][0:8, 0:nt], AF.Exp, bias=L8[:, 1:2]), [f"B{bi2}", "L8"], ["ROWB"])
                              P.op("act", lambda e: e.activation(ROWB[:, 0, 0:nt], ROWB[:, 0, 0:nt], AF.Ln, bias=1.0), ["ROWB"], ["ROWB"])
                              yield
                              P.op("dve", lambda e: e.tensor_scalar(ROWB[:, 0, 0:nt], ROWB[:, 0, 0:nt], L8[:, 2:3], None, ALU.mult), ["ROWB", "L8"], ["ROWB"])
                          return g

                      tasks = [mixer_task(jj) for jj in range(4)] + [qkv_task(i) for i in range(12)] + [zb_task(h) for h in range(4)] + [rows_task()]
                      run_tasks(tasks, 3)
                      P.dma("pool", kqv_s[b, :, :, 0:nt], KQVB[:, :, 0:nt], ["KQVB"], [("kqv", b)], "sp0")
                      P.dma("pool", ya_s[b, :, :, 0:nt], YAB[:, :, 0:nt], ["YAB"], [("ya", b)], "sp1")
                      P.dma("pool", szb_s[b, :, :, 0:nt], SZBB[:, :, 0:nt], ["SZBB"], [("szb", b)], "sp2")
                      P.dma("pool", rows_s[b, :, :, 0:nt], ROWB[:, :, 0:nt], ["ROWB"], [("rows", b)], "sp3")
                  P.barrier()

              ck("s1")
              with ExitStack() as s2:
                  OF = sb("OF", [128, 4, NTOK], BF16, s2)
                  S32 = sb("S32", [128, 512], F32, s2)
                  SBF = sb("SBF", [128, 512], BF16, s2)
                  NSLOT = 2

                  def mk_slot(i):
                      t = {}
                      t["KQ"] = sb(f"KQ{i}", [128, 12, 128], BF16, s2)
                      t["RW"] = sb(f"RW{i}", [8, 2, 128], F32, s2)
                      t["SZ"] = sb(f"SZ{i}", [128, 4, 128], BF16, s2)
                      for n_ in ("GC", "CS"):
                          t[n_] = sb(f"{n_}{i}", [8, 128], F32, s2)
                      t["DG"] = sb(f"DG{i}", [8, 8], F32, s2)
                      t["COLS"] = sb(f"COLS{i}", [128, 24], F32, s2)
                      t["CF"] = sb(f"CF{i}", [128, 32], F32, s2)
                      for n_ in ("W0", "W1", "W2", "W3", "U0"):
                          t[n_] = sb(f"{n_}{i}", [128, 512], F32, s2)
                      for n_ in ("PTB", "ATB", "AB", "PN0", "PN1", "PTN0", "PTN1", "RB", "RTB", "ZM", "YM", "KBE", "KDEC", "VB",
                                 "WTB", "QDT", "UB", "SQO"):
                          t[n_] = sb(f"{n_}{i}", [128, 512], BF16, s2)
                      t["banks"] = [4 * i, 4 * i + 1, 4 * i + 2, 4 * i + 3]
                      t["i"] = i
                      return t

                  slots = [mk_slot(i) for i in range(NSLOT)]

                  def hs(ap, h):
                      return ap[:, h * 128:(h + 1) * 128]

                  def mm4(bank, lhs, rhs, rk, start=True, stop=True):
                      for h in range(4):
                          P.op("pe", lambda e, h=h: e.matmul(hs(B[bank], h), lhs(h), rhs(h), start=start, stop=stop), rk, [f"B{bank}"], inc=(h == 3))

                  def chunk_gen(T, d, b, ch, need_out):
                      si = T["i"]
                      K_ = lambda n_: (n_, si)
                      ba, bb_, bc, be = T["banks"]
                      t0, nt = blk_range(b)
                      c0 = ch * 128
                      tk = t0 + c0
                      kq, rw, sz = T["KQ"], T["RW"], T["SZ"]
                      GC, CS, DG, COLS, CF = T["GC"], T["CS"], T["DG"], T["COLS"], T["CF"]
                      W0, W1, W2, W3, U0 = T["W0"], T["W1"], T["W2"], T["W3"], T["U0"]
                      PTB, ATB, AB, RB, RTB, ZM, YM = T["PTB"], T["ATB"], T["AB"], T["RB"], T["RTB"], T["ZM"], T["YM"]
                      PN = [T["PN0"], T["PN1"]]
                      PTN = [T["PTN0"], T["PTN1"]]
                      KBE, KDEC, VB, WTB, QDT, UB, SQO = T["KBE"], T["KDEC"], T["VB"], T["WTB"], T["QDT"], T["UB"], T["SQO"]
                      kqk, rwk, szk = K_("KQ"), K_("RW"), K_("SZ")
                      P.dma("sp", kq[:], kqv_s[b, :, :, c0:c0 + 128], [("kqv", b)], [kqk], ("kq", si))
                      P.dma("sp", rw[:], rows_s[b, :, :, c0:c0 + 128], [("rows", b)], [rwk], ("rw", si))
                      if d == 1 and need_out:
                          P.dma("sp", sz[:], szb_s[b, :, :, c0:c0 + 128], [("szb", b)], [szk], ("sz", si))
                      qT = lambda h: kq[:, h, :]
                      kT = lambda h: kq[:, 4 + h, :]
                      vT = lambda h: kq[:, 8 + h, :]
                      g8 = rw[:, 0, :]
                      b8 = rw[:, 1, :]
                      yield
                      P.op("dve", lambda e: e.tensor_tensor_scan(CS[:], ONES8, g8, 0.0, ALU.mult, ALU.add), [rwk, "SEL"], [K_("CS")])
                      if d == 0:
                          P.op("dve", lambda e: e.tensor_copy(GC[:], CS[:]), [K_("CS")], [K_("GC")])
                      else:
                          P.op("dve", lambda e: e.scalar_tensor_tensor(GC[:], CS[:], -1.0, g8, ALU.mult, ALU.add), [K_("CS"), rwk], [K_("GC")])
                          P.op("dve", lambda e: e.tensor_scalar(GC[:], GC[:], CS[:, 127:128], None, ALU.add), [K_("GC"), K_("CS")], [K_("GC")])
                      P.op("dve", lambda e: e.tensor_scalar(DG[:], I8, CS[:, 127:128], None, ALU.mult), [K_("CS"), "IDF"], [K_("DG")])
                      P.op("pe", lambda e: e.matmul(B[be][:, 0:8], GC[:], I8, start=True, stop=True), [K_("GC"), "IDF"], [f"B{be}"], inc=False)
                      P.op("pe", lambda e: e.matmul(B[be][:, 8:16], b8, I8, start=True, stop=True), [rwk, "IDF"], [f"B{be}"], inc=False)
                      P.op("pe", lambda e: e.matmul(B[be][:, 16:24], ONES8, DG[:], start=True, stop=True), ["SEL", K_("DG")], [f"B{be}"])
                      for h in range(4):
                          P.op("pe", lambda e, h=h: e.transpose(Bbf[bc][:, h * 128:(h + 1) * 128], kT(h), IDB[:]), [kqk, "IDB"], [f"B{bc}"], inc=False)
                      for h in range(4):
                          P.op("pe", lambda e, h=h: e.transpose(Bbf[bc][:, 512 + h * 128:512 + (h + 1) * 128], vT(h), IDB[:]), [kqk, "IDB"], [f"B{bc}"], inc=(h == 3))
                      for h in range(4):
                          dh = d * 4 + h
                          P.op("pe", lambda e, h=h, dh=dh: e.matmul(hs(B[ba], h), SEL[:, dh * 128:(dh + 1) * 128], GC[:], start=True, stop=False), ["SEL", K_("GC")], [f"B{ba}"], inc=False)
                          P.op("pe", lambda e, h=h, dh=dh: e.matmul(hs(B[ba], h), GC[:], SEL[:, 1024 + dh * 128:1024 + (dh + 1) * 128], start=False, stop=True), ["SEL", K_("GC")], [f"B{ba}"], inc=(h == 3))
                      mm4(bb_, lambda h: SEL[:, (d * 4 + h) * 128:(d * 4 + h + 1) * 128], lambda h: b8, ["SEL", rwk])
                      yield
                      P.op("act", lambda e: e.copy(COLS[:], B[be][:, 0:24]), [f"B{be}"], [K_("COLS")])
                      P.op("act", lambda e: e.activation(CF[:, 0:8], COLS[:, 0:8], AF.Exp), [K_("COLS")], [K_("CF")])
                      P.op("dve", lambda e: e.tensor_tensor(CF[:, 8:16], CF[:, 0:8], COLS[:, 8:16], ALU.mult), [K_("CF"), K_("COLS")], [K_("CF")])
                      P.op("dve", lambda e: e.tensor_tensor(CF[:, 16:24], COLS[:, 16:24], COLS[:, 0:8], ALU.subtract), [K_("COLS"), K_("CF")], [K_("CF")])
                      P.op("act", lambda e: e.activation(CF[:, 16:24], CF[:, 16:24], AF.Exp), [K_("CF")], [K_("CF")])
                      P.op("act", lambda e: e.activation(CF[:, 24:32], COLS[:, 16:24], AF.Exp), [K_("COLS"), K_("CF")], [K_("CF")])
                      P.op("dve", lambda e: e.tensor_tensor(W0[:], B[ba], NEGI[d], ALU.add), [f"B{ba}", "MASK"], [K_("W0")])
                      P.op("act", lambda e: e.activation(W1[:], W0[:], AF.Exp), [K_("W0")], [K_("W1")])
                      P.op("dve", lambda e: e.tensor_tensor(W2[:], B[bb_], SM[d], ALU.mult), [f"B{bb_}", "MASK"], [K_("W2")])
                      P.op("pool", lambda e: e.tensor_tensor(W2[:], W2[:], W1[:], ALU.mult), [K_("W2"), K_("W1")], [K_("W2")])
                      yield
                      for h in range(4):
                          dh = d * 4 + h
                          P.op("dve", lambda e, h=h, dh=dh: e.tensor_scalar(hs(KBE[:], h), Bbf[bc][:, h * 128:(h + 1) * 128], CF[:, 8 + dh:9 + dh], None, ALU.mult),
                               [f"B{bc}", K_("CF")], [K_("KBE")])
                          P.op("dve", lambda e, h=h, dh=dh: e.tensor_scalar(hs(KDEC[:], h), Bbf[bc][:, h * 128:(h + 1) * 128], CF[:, 16 + dh:17 + dh], None, ALU.mult),
                               [f"B{bc}", K_("CF")], [K_("KDEC")])
                          P.op("dve", lambda e, h=h, dh=dh: e.tensor_scalar(hs(VB[:], h), Bbf[bc][:, 512 + h * 128:512 + (h + 1) * 128], COLS[:, 8 + dh:9 + dh], None, ALU.mult),
                               [f"B{bc}", K_("COLS")], [K_("VB")])
                      mm4(be, kT, kT, [kqk])
                      yield
                      mm4(bc, kT, qT, [kqk])
                      P.op("dve", lambda e: e.tensor_tensor(ATB[:], B[be], W2[:], ALU.mult), [f"B{be}", K_("W2")], [K_("ATB")])
                      yield
                      P.op("dve", lambda e: e.tensor_tensor(PTB[:], B[bc], W1[:], ALU.mult), [f"B{bc}", K_("W1")], [K_("PTB")])
                      for h in range(4):
                          P.op("pe", lambda e, h=h: e.transpose(Bbf[ba][:, h * 128:(h + 1) * 128], hs(ATB[:], h), IDB[:]), [K_("ATB"), "IDB"], [f"B{ba}"], inc=(h == 3))
                      P.op("pool", lambda e: e.tensor_tensor(PN[0][:], ATB[:], BLK16, ALU.mult), [K_("ATB"), "MASK"], [K_("PN0")])
                      P.op("pool", lambda e: e.tensor_tensor(RTB[:], ID4, PN[0][:], ALU.subtract), ["MASK", K_("PN0")], [K_("RTB")])
                      yield
                      P.op("act", lambda e: e.copy(AB[:], Bbf[ba][:, 0:512]), [f"B{ba}"], [K_("AB")])
                      P.op("dve", lambda e: e.tensor_tensor(PTN[0][:], Bbf[ba][:, 0:512], BLK16, ALU.mult), [f"B{ba}", "MASK"], [K_("PTN0")])
                      P.op("pool", lambda e: e.tensor_tensor(RB[:], ID4, PTN[0][:], ALU.subtract), ["MASK", K_("PTN0")], [K_("RB")])
                      yield
                      cur = 0
                      for it in range(3):
                          nx = 1 - cur
                          kc_, kn_ = (K_(f"PTN{cur}"), K_(f"PN{cur}")), (K_(f"PTN{nx}"), K_(f"PN{nx}"))
                          mm4(ba, lambda h: hs(PTN[cur][:], h), lambda h: hs(PN[cur][:], h), list(kc_))
                          mm4(bb_, lambda h: hs(PN[cur][:], h), lambda h: hs(PTN[cur][:], h), list(kc_))
                          yield
                          P.op("act", lambda e, nx=nx: e.copy(PN[nx][:], B[ba]), [f"B{ba}"], [kn_[1]])
                          P.op("dve", lambda e, nx=nx: e.tensor_copy(PTN[nx][:], B[bb_]), [f"B{bb_}"], [kn_[0]])
                          mm4(bc, lambda h: hs(PTN[nx][:], h), lambda h: hs(RTB[:], h), [kn_[0], K_("RTB")])
                          mm4(be, lambda h: hs(PN[nx][:], h), lambda h: hs(RB[:], h), [kn_[1], K_("RB")])
                          yield
                          P.op("dve", lambda e: e.tensor_tensor(RTB[:], B[bc], RTB[:], ALU.add), [f"B{bc}", K_("RTB")], [K_("RTB")])
                          P.op("dve", lambda e: e.tensor_tensor(RB[:], B[be], RB[:], ALU.add), [f"B{be}", K_("RB")], [K_("RB")])
                          cur = nx
                      for szm in (32, 64, 128):
                          if szm < 128:
                              mm4(ba, lambda h: hs(ATB[:], h), lambda h: hs(RB[:], h), [K_("ATB"), K_("RB")])
                          mm4(bb_, lambda h: hs(AB[:], h), lambda h: hs(RTB[:], h), [K_("AB"), K_("RTB")])
                          yield
                          if szm < 128:
                              P.op("dve", lambda e, szm=szm: e.tensor_tensor(ZM[:], B[ba], OFF[szm], ALU.mult), [f"B{ba}", "MASK"], [K_("ZM")])
                          P.op("dve", lambda e, szm=szm: e.tensor_tensor(YM[:], B[bb_], OFF[szm], ALU.mult), [f"B{bb_}", "MASK"], [K_("YM")])
                          if szm < 128:
                              mm4(bc, lambda h: hs(RTB[:], h), lambda h: hs(ZM[:], h), [K_("RTB"), K_("ZM")])
                          mm4(be, lambda h: hs(RB[:], h), lambda h: hs(YM[:], h), [K_("RB"), K_("YM")])
                          yield
                          if szm < 128:
                              P.op("dve", lambda e: e.tensor_tensor(RB[:], RB[:], B[bc], ALU.subtract), [f"B{bc}", K_("RB")], [K_("RB")])
                          P.op("dve", lambda e: e.tensor_tensor(RTB[:], RTB[:], B[be], ALU.subtract), [f"B{be}", K_("RTB")], [K_("RTB")])
                      mm4(ba, lambda h: hs(RTB[:], h), lambda h: hs(VB[:], h), [K_("RTB"), K_("VB")])
                      mm4(bb_, lambda h: hs(KBE[:], h), lambda h: hs(RTB[:], h), [K_("KBE"), K_("RTB")])
                      mm4(bc, lambda h: SEL[:, (d * 4 + h) * 128:(d * 4 + h + 1) * 128], lambda h: GC[:], ["SEL", K_("GC")])
                      yield
                      P.op("act", lambda e: e.copy(U0[:], B[ba]), [f"B{ba}"], [K_("U0")])
                      P.op("act", lambda e: e.copy(WTB[:], B[bb_]), [f"B{bb_}"], [K_("WTB")])
                      P.op("act", lambda e: e.activation(W3[:], B[bc], AF.Exp), [f"B{bc}"], [K_("W3")])
                      P.op("dve", lambda e: e.tensor_tensor(h4(QDT[:]), kq[:, 0:4, :], h4(W3[:]), ALU.mult), [kqk, K_("W3")], [K_("QDT")])
                      yield "scan"
                      mm4(be, lambda h: hs(WTB[:], h), lambda h: hs(SBF[:], h), [K_("WTB"), "SBF"])
                      P.op("dve", lambda e: e.tensor_tensor(UB[:], U0[:], B[be], ALU.subtract), [K_("U0"), f"B{be}"], [K_("UB")])
                      if need_out:
                          for h in range(4):
                              P.op("pe", lambda e, h=h: e.matmul(hs(B[ba], h), hs(SBF[:], h), hs(QDT[:], h), start=True, stop=False), ["SBF", K_("QDT")], [f"B{ba}"], inc=False)
                              P.op("pe", lambda e, h=h: e.matmul(hs(B[ba], h), hs(UB[:], h), hs(PTB[:], h), start=False, stop=True), [K_("UB"), K_("PTB")], [f"B{ba}"], inc=(h == 3))
                      mm4(bb_, lambda h: hs(KDEC[:], h), lambda h: hs(UB[:], h), [K_("KDEC"), K_("UB")])
                      for h in range(4):
                          dh = d * 4 + h
                          P.op("dve", lambda e, h=h, dh=dh: e.scalar_tensor_tensor(hs(S32[:], h), hs(S32[:], h), CF[:, 24 + dh:25 + dh], hs(B[bb_], h), ALU.mult, ALU.add),
                               ["S32", K_("CF"), f"B{bb_}"], ["S32"])
                      P.op("act", lambda e: e.copy(SBF[:], S32[:]), ["S32"], ["SBF"])
                      if need_out:
                          if d == 0:
                              P.op("act", lambda e: e.copy(OF[:, :, tk:tk + 128], h4(B[ba])), [f"B{ba}"], [("OF", tk)])
                          else:
                              P.op("dve", lambda e: e.tensor_tensor(h4(W0[:]), h4(B[ba]), OF[:, :, tk:tk + 128], ALU.add), [f"B{ba}", ("OF", tk)], [K_("W0")])
                              P.op("act", lambda e: e.activation(SQO[:], W0[:], AF.Square), [K_("W0")], [K_("SQO")])
                              mm4(bc, lambda h: ONESB[:], lambda h: hs(SQO[:], h), ["ONESB", K_("SQO")])
                              yield
                              P.op("act", lambda e: e.activation(W1[:], B[bc], AF.Ln, bias=EPSC, scale=1.0 / 128), [f"B{bc}"], [K_("W1")])
                              P.op("act", lambda e: e.activation(W1[:], W1[:], AF.Exp, scale=-0.5), [K_("W1")], [K_("W1")])
                              P.op("pool", lambda e: e.tensor_tensor(W0[:], W0[:], W1[:], ALU.mult), [K_("W0"), K_("W1")], [K_("W0")])
                              P.op("dve", lambda e: e.scalar_tensor_tensor(BIG[:, 4:8, tk:tk + 128], h4(W0[:]), GN, sz[:, :, :], ALU.mult, ALU.mult),
                                   [K_("W0"), "LWg", szk], [("BIG", b)])

                  for d in range(2):
                      P.op("pool", lambda e: e.memset(S32[:], 0.0), [], ["S32"])
                      P.op("pool", lambda e: e.memset(SBF[:], 0.0), [], ["SBF"])
                      border = list(range(NBLK)) if d == 0 else [0] + list(range(8, 0, -1))
                      tasks = []
                      for b in border:
                          t0, nt = blk_range(b)
                          need_out = not (last and b == 0)
                          nch = nt // 128
                          chs = list(range(nch)) if d == 0 else list(range(nch - 1, -1, -1))
                          for ci, ch in enumerate(chs):
                              tasks.append((b, ch, need_out, ci == 0))
                      active = []
                      nxt = 0
                      scan_turn = 0
                      free_slots = list(range(NSLOT))
                      while nxt < len(tasks) or active:
                          if nxt < len(tasks) and free_slots and (not active or len(active) < NSLOT):
                              b, ch, need_out, first = tasks[nxt]
                              if first and d == 1 and need_out:
                                  t0, nt = blk_range(b)
                                  P.dma("sp", BIG[:, 0:4, t0:t0 + nt], ya_s[b, :, :, 0:nt], [("ya", b)], [("BIG", b)], "yal")
                              si = free_slots.pop(0)
                              active.append([chunk_gen(slots[si], d, b, ch, need_out), nxt, False, si])
                              nxt += 1
                          for ent in list(active):
                              g, ti, wscan, si = ent
                              if wscan and ti != scan_turn:
                                  continue
                              try:
                                  r = next(g)
                                  if wscan:
                                      scan_turn += 1
                                      ent[2] = False
                                  if r == "scan":
                                      ent[2] = True
                              except StopIteration:
                                  if wscan:
                                      scan_turn += 1
                                  active.remove(ent)
                                  free_slots.append(si)
                  P.barrier()
              ck("s2")
              with ExitStack() as s4:
                  WOUT = sb("WOUT", [128, 8, D], BF16, s4)
                  XS = sb("XS4", [128, 8, 512], F32, s4)
                  WS4 = sb("WS4", [128, 2, D], F32, s4)
                  SQB = sb("SQB4", [128, 8, 512], BF16, s4)
                  T0 = sb("T04", [128, 512], F32, s4)
                  YP = sb("YP", [128, 8, 512], BF16, s4) if colmaj else None
                  for k in range(8):
                      s = k % 2
                      P.dma("sp", WS4[:, s, :], w_out[l, k * 128:(k + 1) * 128, :], [], [("WS4", s)], ("ws4", s))
                      P.op("dve" if s == 0 else "pool", lambda e, k=k, s=s: e.tensor_copy(WOUT[:, k, :], WS4[:, s, :]), [("WS4", s)], ["WOUT"])
                  pp = 0
                  for nb in range(NBLK):
                      if last and nb == 0:
                          continue
                      t0, nt = blk_range(nb)
                      v = 1 if nb == 0 else 0
                      P.dma("sp", XS[:, :, 0:nt], xsrc_v[:, :, t0:t0 + nt], [("x", nb)], ["XS4"], "xs4")
                      if colmaj and nb > 0:
                          for k in range(8):
                              if k % 2 == 0:
                                  P.op("pool", lambda e, k=k: e.tensor_copy(YP[:, k, :].rearrange("p (a b) -> p a b", b=64), big_view(k, nb, True, True)),
                                       big_keys(nb, True), ["YP"])
                              else:
                                  P.op("act", lambda e, k=k: e.copy(YP[:, k, :].rearrange("p (a b) -> p a b", b=64), big_view(k, nb, True, True)),
                                       big_keys(nb, True), ["YP"])
                      for jn in range(8):
                          bi = (0, 1, 3, 4)[pp % 4]
                          pp += 1
                          for kc in range(8):
                              if colmaj and nb > 0:
                                  P.op("pe", lambda e, kc=kc, jn=jn, bi=bi: e.matmul(
                                      B[bi][:, 0:nt], WOUT[:, kc, jn * 128:(jn + 1) * 128], YP[:, kc, :],
                                      start=(kc == 0), stop=(kc == 7)), ["WOUT", "YP"], [f"B{bi}"], inc=(kc == 7))
                              else:
                                  P.op("pe", lambda e, kc=kc, jn=jn, bi=bi: e.matmul(
                                      B[bi][:, 0:nt], WOUT[:, kc, jn * 128:(jn + 1) * 128], BIG[:, kc, t0:t0 + nt],
                                      start=(kc == 0), stop=(kc == 7)), ["WOUT"] + big_keys(nb, False), [f"B{bi}"], inc=(kc == 7))
                          P.op("dve", lambda e, jn=jn, bi=bi, v=v: e.scalar_tensor_tensor(
                              XS[:, jn, 0:nt], B[bi][:, 0:nt], mcol(v, 16 + jn), XS[:, jn, 0:nt], ALU.mult, ALU.add),
                              [f"B{bi}", "XS4"], [("XSo", jn)])
                      okeys = [("XSo", jn) for jn in range(8)]
                      if not last:
                          P.dma("pool", xres_v[:, :, t0:t0 + nt], XS[:, :, 0:nt], okeys, [("x", nb)], "xst")
                          P.op("dve", lambda e: e.engine_nop(), [("x", nb)], ["XS4"])
                      else:
                          P.op("act", lambda e: e.activation(SQB[:], XS[:], AF.Square), okeys, ["SQB4"])
                          for k in range(8):
                              P.op("pe", lambda e, k=k: e.matmul(B[2], ONESB[:], SQB[:, k, :], start=(k == 0), stop=(k == 7)), ["SQB4", "ONESB"], ["B2"], inc=(k == 7))
                          P.op("act", lambda e: e.activation(T0[:], B[2], AF.Ln, bias=EPSC, scale=1.0 / D), ["B2"], ["T04"])
                          P.op("act", lambda e: e.activation(T0[:], T0[:], AF.Exp, scale=-0.5), ["T04"], ["T04"])
                          for k in range(8):
                              P.op("dve", lambda e, k=k: e.scalar_tensor_tensor(XS[:, k, :], XS[:, k, :], FN[:, k:k + 1], T0[:], ALU.mult, ALU.mult),
                                   [("XSo", k), "T04", "FN", "SQB4"], [("XSf", k)])
                          fk = [("XSf", k) for k in range(8)]
                          P.dma("pool", out_v[:, :, t0 - NCTX:t0 - NCTX + nt], XS[:, :, 0:nt], fk, [("out", nb)], "ost")
                          P.op("dve", lambda e: e.engine_nop(), [("out", nb)], ["XS4"])
                  P.barrier()
          P.barrier()

    except _Stop:
        pass
    return nc


def _consts():
    i = np.arange(128)
    t, s = i[None, :], i[:, None]
    negi_f = np.where(t >= s, 0.0, BIGNEG)
    negi_b = np.where(t <= s, 0.0, BIGNEG)
    sm_f = (t > s).astype(np.float64)
    sm_b = (t < s).astype(np.float64)
    blk = lambda z: ((s // z) == (t // z)).astype(np.float64)
    blk16 = blk(16)
    off = {z: blk(z) * (1 - blk(z // 2)) for z in (32, 64, 128)}
    ident = np.eye(128)
    ms = [negi_f, negi_b, sm_f, sm_b, blk16, off[32], off[64], off[128], ident]
    cmask = np.concatenate([np.tile(m, (1, 4)) for m in ms], axis=1).astype(np.float32)
    sel = np.zeros((8, 2 * 1024 + 128), np.float32)
    for dh in range(8):
        sel[dh, dh * 128:(dh + 1) * 128] = 1.0
        sel[dh, 1024 + dh * 128:1024 + (dh + 1) * 128] = -1.0
    sel[:, 2048:] = 1.0
    return cmask, ident.astype(np.float32), sel


_NC_CACHE = {}


def make_in_maps(x, c, ctx, c_ctx, norm_w, w_mod, b_mod, w_in, conv_a, conv_qkv, a_log, dt_bias, gdn_norm, w_out, final_norm):
    f = lambda a: np.ascontiguousarray(np.asarray(a, dtype=np.float32))
    x, c, ctx, c_ctx = f(x), f(c), f(ctx), f(c_ctx)
    cmask, ident, sel = _consts()
    L = norm_w.shape[0]
    col = lambda a: np.ascontiguousarray(a.reshape(-1, 128).T)
    shared = {
        "w_mod": f(w_mod), "w_in": f(w_in), "w_out": f(w_out),
        "bmod": np.stack([col(f(b_mod)[l]) for l in range(L)]),
        "normw": np.stack([col(f(norm_w)[l]) for l in range(L)]),
        "conva": np.stack([np.ascontiguousarray(f(conv_a)[l].T.reshape(4, 128, 3).transpose(1, 0, 2).reshape(128, 12)) for l in range(L)]),
        "convq": np.stack([np.ascontiguousarray(f(conv_qkv)[l].T.reshape(12, 128, 3).transpose(1, 0, 2).reshape(128, 36)) for l in range(L)]),
        "alog": np.ascontiguousarray(f(a_log).reshape(L, 8, 1)),
        "dtb": np.ascontiguousarray(f(dt_bias).reshape(L, 8, 1)),
        "gnorm": np.ascontiguousarray(f(gdn_norm).reshape(L, 128, 1)),
        "fnorm": col(f(final_norm)),
        "cmask": cmask, "cident": ident, "csel": sel,
    }
    maps = []
    for b in range(x.shape[0]):
        m = dict(shared)
        m["xT"] = np.ascontiguousarray(np.concatenate([ctx[b], x[b]], axis=0).T)
        ccb = np.stack([col(c[b]), col(c_ctx)], axis=-1).reshape(128, 16)
        m["cc"] = np.ascontiguousarray(ccb)
        maps.append(m)
    return maps


def kernel(x, c, ctx, c_ctx, norm_w, w_mod, b_mod, w_in, conv_a, conv_qkv, a_log, dt_bias, gdn_norm, w_out, final_norm, _nlayers=DEPTH):
    maps = make_in_maps(x, c, ctx, c_ctx, norm_w, w_mod, b_mod, w_in, conv_a, conv_qkv, a_log, dt_bias, gdn_norm, w_out, final_norm)
    if _nlayers not in _NC_CACHE:
        _NC_CACHE[_nlayers] = build(_nlayers)
    nc = _NC_CACHE[_nlayers]
    res = run_bass_kernel_spmd(nc, maps, core_ids=list(range(len(maps))))
    out = np.stack([np.ascontiguousarray(r["outT"].T) for r in res.results], axis=0)
    return out.astype(np.float32)
```

```python
import numpy as np
from contextlib import ExitStack
import concourse.bass as bass
import concourse.mybir as mybir
from concourse.bass_utils import run_bass_kernel_spmd

F32 = mybir.dt.float32
BF16 = mybir.dt.bfloat16
AF = mybir.ActivationFunctionType
ALU = mybir.AluOpType

D = 1024
NCTX = 256
NLAT = 4096
NTOK = NCTX + NLAT
DPROJ = 4112
DEPTH = 4
EPS = 1e-6
NBLK = 9
BIGNEG = -30000.0
PSUM_KEYS = {f"B{i}" for i in range(8)}


def blk_range(b):
    if b == 0:
        return 0, NCTX
    return NCTX + (b - 1) * 512, 512


class _Stop(Exception):
    pass


class Prog:
    def __init__(self, nc, es):
        self.nc = nc
        self.es = es
        self.eng = {"pe": nc.tensor, "act": nc.scalar, "dve": nc.vector, "pool": nc.gpsimd, "sp": nc.sync}
        self.sem = {}
        self.cnt = {}
        self.epoch = 0
        self.state = {}
        self.waited = {}
        self.dsem = {}
        self.dcnt = {}
        self.all_sems = {}
        self.nops = 0
        self.stop_at = None
        self.new_epoch()

    def new_epoch(self):
        self.epoch += 1
        for e in ("pe", "act", "dve", "pool"):
            s = self.es.enter_context(self.nc.semaphore(f"s_{e}_{self.epoch}"))
            self.sem[e] = s
            self.cnt[e] = 0
            self.all_sems[id(s)] = s

    def _collect(self, engine, reads, writes):
        need = {}

        def add(ev):
            s, v, e = ev
            k = id(s)
            if k not in need or need[k][1] < v:
                need[k] = (s, v, e)

        for k in reads:
            st = self.state.get(k)
            if st is not None:
                for ev in st["w"].values():
                    add(ev)
                if k in PSUM_KEYS:
                    for ev in st["r"].values():
                        if ev[2] != engine:
                            add(ev)
        for k in writes:
            st = self.state.get(k)
            if st is not None:
                for ev in st["w"].values():
                    if ev[2] != engine:
                        add(ev)
                for ev in st["r"].values():
                    if ev[2] != engine:
                        add(ev)
        out = []
        wd = self.waited.setdefault(engine, {})
        for k, (s, v, e) in need.items():
            if wd.get(k, 0) >= v:
                continue
            wd[k] = v
            out.append((s, v))
        return out

    def _record(self, ev, reads, writes):
        for k in reads:
            st = self.state.setdefault(k, {"w": {}, "r": {}})
            st["r"][id(ev[0])] = ev
        for k in writes:
            st = self.state.setdefault(k, {"w": {}, "r": {}})
            st["w"][id(ev[0])] = ev

    def op(self, engine, fn, reads=(), writes=(), inc=True):
        e = self.eng[engine]
        for s, v in self._collect(engine, reads, writes):
            e.wait_ge(s, v)
        inst = fn(e)
        if inc:
            self.cnt[engine] += 1
            inst.then_inc(self.sem[engine], 1)
            self._record((self.sem[engine], self.cnt[engine], engine), reads, writes)
        else:
            self._record((self.sem[engine], self.cnt[engine] + 1, engine), reads, writes)
        self.nops += 1
        if self.stop_at is not None and self.nops == self.stop_at:
            self.barrier()
            raise _Stop()

    def dma(self, queue, out, in_, reads, writes, slot):
        e = self.eng[queue]
        for s, v in self._collect("q_" + queue, reads, writes):
            e.wait_ge(s, v)
        if slot not in self.dsem:
            self.dsem[slot] = self.es.enter_context(self.nc.semaphore(f"d_{len(self.dsem)}"))
            self.dcnt[slot] = 0
        self.dcnt[slot] += 16
        e.dma_start(out=out, in_=in_).then_inc(self.dsem[slot], 16)
        self._record((self.dsem[slot], self.dcnt[slot], "dma_" + str(slot)), reads, writes)

    def barrier(self):
        evs = [(self.sem[e], self.cnt[e]) for e in ("pe", "act", "dve", "pool") if self.cnt[e] > 0]
        evs += [(self.dsem[s], self.dcnt[s]) for s in self.dsem]
        for en in ("pe", "act", "dve", "pool", "sp"):
            key = en if en != "sp" else "q_sp"
            wd = self.waited.setdefault(key, {})
            for s, v in evs:
                if en in self.sem and s is self.sem.get(en):
                    continue
                if wd.get(id(s), 0) >= v:
                    continue
                wd[id(s)] = v
                self.eng[en].wait_ge(s, v)
        wd = self.waited.setdefault("q_pool", {})
        for s, v in evs:
            wd[id(s)] = max(wd.get(id(s), 0), v)
        self.state = {}


def build(nlayers=DEPTH, stop=None):
    def ck(name):
        if stop == name:
            print("ck", name, "nops", P.nops)
            raise _Stop()
    nc = bass.Bass("TRN2", target_bir_lowering=False)
    dt_in = lambda n, shp, dt=F32: nc.dram_tensor(n, list(shp), dt, kind="ExternalInput").ap()
    xT = dt_in("xT", [D, NTOK])
    cc = dt_in("cc", [128, 16])
    w_mod = dt_in("w_mod", [DEPTH, D, 3 * D])
    bmod = dt_in("bmod", [DEPTH, 128, 24])
    normw = dt_in("normw", [DEPTH, 128, 8])
    w_in = dt_in("w_in", [DEPTH, D, DPROJ])
    conva = dt_in("conva", [DEPTH, 128, 12])
    convq = dt_in("convq", [DEPTH, 128, 36])
    alog = dt_in("alog", [DEPTH, 8, 1])
    dtb = dt_in("dtb", [DEPTH, 8, 1])
    gnorm = dt_in("gnorm", [DEPTH, 128, 1])
    w_out = dt_in("w_out", [DEPTH, D, D])
    fnorm = dt_in("fnorm", [128, 8])
    cmask = dt_in("cmask", [128, 9 * 512])
    cident = dt_in("cident", [128, 128])
    csel = dt_in("csel", [8, 2 * 1024 + 128])
    outT = nc.dram_tensor("outT", [D, NLAT], F32, kind="ExternalOutput").ap()
    xres = nc.dram_tensor("xres", [D, NTOK], F32).ap()
    kqv_s = nc.dram_tensor("kqv_s", [NBLK, 128, 12, 512], BF16).ap()
    ya_s = nc.dram_tensor("ya_s", [NBLK, 128, 4, 512], BF16).ap()
    szb_s = nc.dram_tensor("szb_s", [NBLK, 128, 4, 512], BF16).ap()
    rows_s = nc.dram_tensor("rows_s", [NBLK, 8, 2, 512], F32).ap()

    es = ExitStack()
    try:
      with es:
          P = Prog(nc, es)
          if isinstance(stop, int):
              P.stop_at = stop
          _uniq = [0]

          def sb(n, shp, dt=F32, st=es):
              _uniq[0] += 1
              return st.enter_context(nc.sbuf_tensor(f"{n}_{_uniq[0]}", list(shp), dt))
          BIG = sb("BIG", [128, 8, NTOK], BF16)
          MASK = sb("MASK", [128, 9, 512], BF16)
          IDF = sb("IDF", [128, 128], F32)
          IDB = sb("IDB", [128, 128], BF16)
          ONESB = sb("ONESB", [128, 128], BF16)
          SEL = sb("SEL", [8, 2 * 1024 + 128], F32)
          MODS = sb("MODS", [128, DEPTH, 2, 24], F32)
          CC = sb("CC", [128, 16], F32)
          FN = sb("FN", [128, 8], F32)
          LW = sb("LW", [128, 64], F32)
          L8 = sb("L8", [8, 4], F32)
          DUM = sb("DUM", [128, 2], F32)
          EPST = sb("EPST", [128, 1], F32)
          EPSC = EPST[:, 0:1]
          banks = [es.enter_context(nc.psum_tensor(f"B{i}", [128, 512], F32)) for i in range(8)]
          B = [b[:] for b in banks]
          Bbf = [b[:].bitcast(BF16) for b in banks]

          NEGI = [MASK[:, 0, :], MASK[:, 1, :]]
          SM = [MASK[:, 2, :], MASK[:, 3, :]]
          BLK16 = MASK[:, 4, :]
          OFF = {32: MASK[:, 5, :], 64: MASK[:, 6, :], 128: MASK[:, 7, :]}
          ID4 = MASK[:, 8, :]
          I8 = IDF[0:8, 0:8]
          ONES8 = SEL[:, 2048:2176]

          def h4(ap):
              return ap.rearrange("p (h t) -> p h t", h=4)

          with ExitStack() as st0:
              MST = sb("MST", [128, 9 * 512], F32, st0)
              WST = sb("WST", [128, 2, 4096], F32, st0)
              SC = sb("SC", [128, 16], F32, st0)
              BM = sb("BM", [128, 24], F32, st0)
              NW = sb("NW", [128, 8], F32, st0)
              P.dma("sp", MST[:], cmask, [], ["MST"], "c0")
              P.dma("sp", IDF[:], cident, [], ["IDF"], "c1")
              P.dma("sp", SEL[:], csel, [], ["SEL"], "c2")
              P.dma("sp", CC[:], cc, [], ["CC"], "c3")
              P.dma("sp", FN[:], fnorm, [], ["FN"], "c4")
              P.op("dve", lambda e: e.tensor_copy(MASK[:].rearrange("p a b -> p (a b)"), MST[:]), ["MST"], ["MASK"])
              P.op("dve", lambda e: e.tensor_copy(IDB[:], IDF[:]), ["IDF"], ["IDB"])
              P.op("dve", lambda e: e.memset(ONESB[:], 1.0), [], ["ONESB"])
              P.op("dve", lambda e: e.memset(DUM[:], 0.0), [], ["DUM"])
              P.op("dve", lambda e: e.memset(EPST[:], EPS), [], ["EPST"])
              P.op("act", lambda e: e.activation(SC[:], CC[:], AF.Silu), ["CC"], ["SC"])
              for l in range(nlayers):
                  P.dma("sp", BM[:], bmod[l], [], ["BM"], "c5")
                  P.dma("sp", NW[:], normw[l], [], ["NW"], "c6")
                  for jg in range(6):
                      s = jg % 2
                      P.dma("sp", WST[:, s, :].rearrange("p (k n) -> p k n", k=8),
                            w_mod[l, :, jg * 512:(jg + 1) * 512].rearrange("(k p) n -> p k n", p=128), [], [("WST", s)], ("wst", s))
                      for jj in range(4):
                          j = jg * 4 + jj
                          for k in range(8):
                              P.op("pe", lambda e, j=j, jj=jj, k=k, s=s: e.matmul(
                                  B[0][:, 2 * j:2 * j + 2], WST[:, s, k * 512 + jj * 128:k * 512 + (jj + 1) * 128],
                                  SC[:].rearrange("p (k v) -> p k v", v=2)[:, k, :], start=(k == 0), stop=(k == 7)),
                                  [("WST", s), "SC"], ["B0"])
                  for v in range(2):
                      P.op("dve", lambda e, v=v, l=l: e.tensor_tensor(
                          MODS[:, l, v, :], B[0][:, 0:48].rearrange("p (j v) -> p j v", v=2)[:, :, v], BM[:], ALU.add),
                          ["B0", "BM"], [("MODS", l)])
                      P.op("dve", lambda e, v=v, l=l: e.scalar_tensor_tensor(
                          MODS[:, l, v, 8:16], MODS[:, l, v, 8:16], 1.0, NW[:], ALU.add, ALU.mult),
                          [("MODS", l), "NW"], [("MODS", l)])
              P.barrier()
          ck("s0")

          for l in range(nlayers):
              P.new_epoch()
              colmaj = (l % 2 == 1)
              last = (l == nlayers - 1)
              xsrc = xT if l == 0 else xres
              xsrc_v = xsrc.rearrange("(k p) t -> p k t", p=128)
              xres_v = xres.rearrange("(k p) t -> p k t", p=128)
              out_v = outT.rearrange("(k p) t -> p k t", p=128)

              def mcol(v, j, l=l):
                  return MODS[:, l, v, j:j + 1]

              P.dma("sp", LW[:, 0:12], conva[l], [], ["LWa"], "c7")
              P.dma("sp", LW[:, 12:48], convq[l], [], ["LWq"], "c8")
              P.dma("sp", LW[:, 48:49], gnorm[l], [], ["LWg"], "c9")
              P.dma("sp", L8[:, 0:1], alog[l], [], ["L8a"], "c10")
              P.dma("sp", L8[:, 1:2], dtb[l], [], ["L8"], "c11")
              P.op("act", lambda e: e.activation(L8[:, 2:3], L8[:, 0:1], AF.Exp), ["L8a"], ["L8"])
              P.op("dve", lambda e: e.tensor_scalar(L8[:, 2:3], L8[:, 2:3], -1.0, None, ALU.mult), ["L8"], ["L8"])
              CA = lambda j, tap: LW[:, j * 3 + tap: j * 3 + tap + 1]
              CQ = lambda j, tap: LW[:, 12 + j * 3 + tap: 12 + j * 3 + tap + 1]
              GN = LW[:, 48:49]

              def big_keys(b, permuted):
                  if b == 0 or not permuted:
                      return [("BIG", b)]
                  return [("BIG", i) for i in range(1, 9)]

              def big_view(kc, b, permuted, rowmajor_of_colscan):
                  t0, nt = blk_range(b)
                  if b == 0 or not permuted:
                      return BIG[:, kc, t0:t0 + nt]
                  lat = BIG[:, kc, NCTX:NTOK]
                  if rowmajor_of_colscan:
                      v = lat.rearrange("p (c r) -> p r c", r=64)
                  else:
                      v = lat.rearrange("p (r c) -> p c r", c=64)
                  return v[:, (b - 1) * 8:(b - 1) * 8 + 8, :]

              def pview(ap, b, permuted):
                  t0, nt = blk_range(b)
                  if b == 0 or not permuted:
                      return ap[:, 0:nt]
                  return ap.rearrange("p (a b) -> p a b", b=64)

              with ExitStack() as s1:
                  WIN = sb("WIN", [128, 8, DPROJ], BF16, s1)
                  XSF = sb("XS", [128, 4112], F32, s1)
                  XS = XSF[:, 0:4096].rearrange("p (k t) -> p k t", k=8)
                  T0 = sb("T0", [128, 512], F32, s1)
                  T1 = sb("T1", [128, 512], F32, s1)
                  T2 = sb("T2", [128, 512], F32, s1)
                  SQQ = sb("SQQ", [128, 512], BF16, s1)
                  KQVB = sb("KQVB", [128, 12, 512], BF16, s1)
                  SQB = KQVB[:, 0:8, :]
                  YAB = sb("YAB", [128, 4, 512], BF16, s1)
                  SZBB = sb("SZBB", [128, 4, 512], BF16, s1)
                  ROWB = sb("ROWB", [8, 2, 512], F32, s1)
                  HP = sb("HP", [128, 8, 512], BF16, s1) if colmaj else None
                  XSW = XSF[:]
                  i = 0
                  for k in range(8):
                      for hf in range(2):
                          s = i % 2
                          c0 = hf * 2056
                          P.dma("sp", XSW[:, s * 2056:(s + 1) * 2056], w_in[l, k * 128:(k + 1) * 128, c0:c0 + 2056],
                                [], [("XS", s)], ("xs", s))
                          eng = ("dve", "pool", "act")[i % 3]
                          if eng == "act":
                              P.op("act", lambda e, k=k, s=s, c0=c0: e.copy(WIN[:, k, c0:c0 + 2056], XSW[:, s * 2056:(s + 1) * 2056]),
                                   [("XS", s)], ["WIN"])
                          else:
                              P.op(eng, lambda e, k=k, s=s, c0=c0: e.tensor_copy(WIN[:, k, c0:c0 + 2056], XSW[:, s * 2056:(s + 1) * 2056]),
                                   [("XS", s)], ["WIN"])
                          i += 1
                  if stop == "s1w":
                      P.barrier()
                      ck("s1w")
                  for nb in range(NBLK):
                      t0, nt = blk_range(nb)
                      v = 1 if nb == 0 else 0
                      P.dma("sp", XS[:, :, 0:nt], xsrc_v[:, :, t0:t0 + nt], [("x", nb)], [("XS", 0), ("XS", 1)], ("xs", 0))
                      P.op("act", lambda e, nt=nt: e.activation(SQB[:, :, 0:nt], XS[:, :, 0:nt], AF.Square),
                           [("XS", 0), ("XS", 1)], ["KQVB"])
                      for k in range(8):
                          P.op("pe", lambda e, k=k, nt=nt: e.matmul(B[2][:, 0:nt], ONESB[:], SQB[:, k, 0:nt], start=(k == 0), stop=(k == 7)),
                               ["KQVB", "ONESB"], ["B2"], inc=(k == 7))
                      P.op("act", lambda e, nt=nt: e.activation(T0[:, 0:nt], B[2][:, 0:nt], AF.Ln, bias=EPSC, scale=1.0 / D), ["B2"], ["T0"])
                      P.op("act", lambda e, nt=nt: e.activation(T0[:, 0:nt], T0[:, 0:nt], AF.Exp, scale=-0.5), ["T0"], ["T0"])
                      for k in range(8):
                          eng = "dve" if k % 2 == 0 else "pool"
                          P.op(eng, lambda e, k=k, nt=nt: e.tensor_tensor(XS[:, k, 0:nt], XS[:, k, 0:nt], T0[:, 0:nt], ALU.mult),
                               [("XS", 0), ("XS", 1), "T0", "KQVB"], [("XSn", k)])
                          P.op("act", lambda e, k=k, nt=nt, t0=t0, v=v: e.activation(
                              BIG[:, k, t0:t0 + nt], XS[:, k, 0:nt], AF.Identity, bias=mcol(v, k), scale=mcol(v, 8 + k)),
                              [("XSn", k)], [("BIG", nb)])
                      P.op("act", lambda e: e.activation(DUM[:, 0:1], DUM[:, 1:2], AF.Copy), [("BIG", nb)] + [("XSn", k) for k in range(8)], [("XS", 0), ("XS", 1)])

                  if stop == "s1p":
                      P.barrier()
                      ck("s1p")
                  P.barrier()
                  TS = [(T0, T1, T2, SQQ, "T0", "T1", "T2", "SQQ", 2)]
                  for i_ in range(2):
                      o_ = i_ * 1792
                      TS.append((XSF[:, o_:o_ + 512], XSF[:, o_ + 512:o_ + 1024], XSF[:, o_ + 1024:o_ + 1536],
                                 XSF[:, o_ + 1536:o_ + 1792].bitcast(BF16), f"T0_{i_}", f"T1_{i_}", f"T2_{i_}", f"SQQ_{i_}", 7 if i_ == 0 else 2))
                  tsi = [0]
                  pp = [0]
                  PROJ_BANKS = [0, 1, 3, 4, 5, 6]

                  def proj(b, c0, m):
                      bi = PROJ_BANKS[pp[0] % len(PROJ_BANKS)]
                      pp[0] += 1
                      t0, nt = blk_range(b)
                      for k in range(8):
                          if colmaj and b > 0:
                              P.op("pe", lambda e, k=k, bi=bi: e.matmul(
                                  B[bi][0:m, 0:nt], WIN[:, k, c0:c0 + m], HP[:, k, :],
                                  start=(k == 0), stop=(k == 7)), ["WIN", "HP"], [f"B{bi}"], inc=(k == 7))
                          else:
                              P.op("pe", lambda e, k=k, bi=bi: e.matmul(
                                  B[bi][0:m, 0:nt], WIN[:, k, c0:c0 + m], BIG[:, k, t0:t0 + nt],
                                  start=(k == 0), stop=(k == 7)), ["WIN"] + big_keys(b, False), [f"B{bi}"], inc=(k == 7))
                      return bi

                  def conv(dst, dkey, src, skey, w, nt, seg):
                      P.op("dve", lambda e: e.tensor_scalar(dst[:, 0:nt], src[:, 0:nt], w(1), None, ALU.mult), [skey, "LWa", "LWq"], [dkey])
                      dv = dst[:, 0:nt].rearrange("p (a b) -> p a b", b=seg)
                      sv = src[:, 0:nt].rearrange("p (a b) -> p a b", b=seg)
                      P.op("dve", lambda e: e.scalar_tensor_tensor(dv[:, :, 1:seg], sv[:, :, 0:seg - 1], w(0), dv[:, :, 1:seg], ALU.mult, ALU.add),
                           [skey, dkey, "LWa", "LWq"], [dkey])
                      P.op("dve", lambda e: e.scalar_tensor_tensor(dv[:, :, 0:seg - 1], sv[:, :, 1:seg], w(2), dv[:, :, 0:seg - 1], ALU.mult, ALU.add),
                           [skey, dkey, "LWa", "LWq"], [dkey])

                  def run_tasks(gens_factories, nsets):
                      pending = list(gens_factories)
                      active = []
                      free = list(range(nsets))
                      while pending or active:
                          if pending and free:
                              si = free.pop(0)
                              active.append((pending.pop(0)(TS[si]), si))
                          for ent in list(active):
                              try:
                                  next(ent[0])
                              except StopIteration:
                                  active.remove(ent)
                                  free.append(ent[1])

                  for b in range(NBLK):
                      t0, nt = blk_range(b)
                      seg = NCTX if b == 0 else 64
                      if colmaj and b > 0:
                          for k in range(8):
                              eng = "pool" if k % 2 == 0 else "act"
                              if eng == "pool":
                                  P.op("pool", lambda e, k=k: e.tensor_copy(HP[:, k, :].rearrange("p (a b) -> p a b", b=64), big_view(k, b, True, False)),
                                       big_keys(b, True), ["HP"])
                              else:
                                  P.op("act", lambda e, k=k: e.copy(HP[:, k, :].rearrange("p (a b) -> p a b", b=64), big_view(k, b, True, False)),
                                       big_keys(b, True), ["HP"])

                      def mixer_task(jj, b=b, nt=nt, seg=seg):
                          def g(ts):
                              T0, T1, T2, SQQ, k0, k1, k2, kq_, ssb = ts
                              bi = proj(b, jj * 128, 128)
                              yield
                              P.op("act", lambda e: e.copy(T0[:, 0:nt], B[bi][:, 0:nt]), [f"B{bi}"], [k0])
                              bi = proj(b, 1024 + jj * 128, 128)
                              yield
                              P.op("dve", lambda e: e.tensor_tensor(T0[:, 0:nt], B[bi][:, 0:nt], T0[:, 0:nt], ALU.mult), [f"B{bi}", k0], [k0])
                              conv(T1, k1, T0, k0, lambda tap: CA(jj, tap), nt, seg)
                              bi = proj(b, 512 + jj * 128, 128)
                              yield
                              P.op("dve", lambda e: e.tensor_tensor(T1[:, 0:nt], B[bi][:, 0:nt], T1[:, 0:nt], ALU.mult), [f"B{bi}", k1], [k1])
                              bi = proj(b, 1536 + jj * 128, 128)
                              yield
                              P.op("act", lambda e: e.activation(T2[:, 0:nt], B[bi][:, 0:nt], AF.Silu), [f"B{bi}"], [k2])
                              yield
                              P.op("pool", lambda e: e.tensor_tensor(YAB[:, jj, 0:nt], T1[:, 0:nt], T2[:, 0:nt], ALU.mult), [k1, k2], ["YAB"])
                          return g

                      def qkv_task(idx, b=b, nt=nt, seg=seg):
                          def g(ts):
                              T0, T1, T2, SQQ, k0, k1, k2, kq_, ssb = ts
                              bi = proj(b, 2048 + idx * 128, 128)
                              yield
                              conv(T1, k1, B[bi], f"B{bi}", lambda tap: CQ(idx, tap), nt, seg)
                              yield
                              if idx >= 8:
                                  P.op("act", lambda e: e.activation(KQVB[:, idx, 0:nt], T1[:, 0:nt], AF.Silu), [k1], ["KQVB"])
                                  return
                              P.op("act", lambda e: e.activation(T2[:, 0:nt], T1[:, 0:nt], AF.Silu), [k1], [k2])
                              P.op("act", lambda e: e.activation(SQQ[:, 0:nt], T2[:, 0:nt], AF.Square), [k2], [kq_])
                              yield
                              P.op("pe", lambda e: e.matmul(B[ssb][:, 0:nt], ONESB[:], SQQ[:, 0:nt], start=True, stop=True), [kq_, "ONESB"], [f"B{ssb}"])
                              yield
                              P.op("act", lambda e: e.activation(T0[:, 0:nt], B[ssb][:, 0:nt], AF.Ln, bias=EPSC, scale=1.0), [f"B{ssb}"], [k0])
                              P.op("act", lambda e: e.activation(T0[:, 0:nt], T0[:, 0:nt], AF.Exp, scale=-0.5), [k0], [k0])
                              yield
                              sc = (128.0 ** -0.5) if idx < 4 else 1.0
                              P.op("dve", lambda e: e.scalar_tensor_tensor(KQVB[:, idx, 0:nt], T2[:, 0:nt], sc, T0[:, 0:nt], ALU.mult, ALU.mult),
                                   [k2, k0], ["KQVB"])
                          return g

                      def zb_task(h, b=b, nt=nt):
                          def g(ts):
                              bi = proj(b, 3584 + h * 128, 128)
                              yield
                              P.op("act", lambda e: e.activation(SZBB[:, h, 0:nt], B[bi][:, 0:nt], AF.Silu), [f"B{bi}"], ["SZBB"])
                          return g

                      def rows_task(b=b, nt=nt):
                          def g(ts):
                              bi = proj(b, 4096, 8)
                              bi2 = proj(b, 4104, 8)
                              yield
                              P.op("act", lambda e: e.activation(ROWB[:, 1, 0:nt], B[bi][0:8, 0:nt], AF.Sigmoid), [f"B{bi}"], ["ROWB"])
                              P.op("act", lambda e: e.activation(ROWB[:, 0, 0:nt], B[bi2][0:8, 0:nt], AF.Exp, bias=L8[:, 1:2]), [f"B{bi2}", "L8"], ["ROWB"])
                              P.op("act", lambda e: e.activation(ROWB[:, 0, 0:nt], ROWB[:, 0, 0:nt], AF.Ln, bias=1.0), ["ROWB"], ["ROWB"])
                              yield
                              P.op("dve", lambda e: e.tensor_scalar(ROWB[:, 0, 0:nt], ROWB[:, 0, 0:nt], L8[:, 2:3], None, ALU.mult), ["ROWB", "L8"], ["ROWB"])
                          return g

                      tasks = [mixer_task(jj) for jj in range(4)] + [qkv_task(i) for i in range(12)] + [zb_task(h) for h in range(4)] + [rows_task()]
                      run_tasks(tasks, 3)
                      P.dma("pool", kqv_s[b, :, :, 0:nt], KQVB[:, :, 0:nt], ["KQVB"], [("kqv", b)], "sp0")
                      P.dma("pool", ya_s[b, :, :, 0:nt], YAB[:, :, 0:nt], ["YAB"], [("ya", b)], "sp1")
                      P.dma("pool", szb_s[b, :, :, 0:nt], SZBB[:, :, 0:nt], ["SZBB"], [("szb", b)], "sp2")
                      P.dma("pool", rows_s[b, :, :, 0:nt], ROWB[:, :, 0:nt], ["ROWB"], [("rows", b)], "sp3")
                  P.barrier()

              ck("s1")
              with ExitStack() as s2:
                  OF = sb("OF", [128, 4, NTOK], BF16, s2)
                  S32 = sb("S32", [128, 512], F32, s2)
                  SBF = sb("SBF", [128, 512], BF16, s2)
                  NSLOT = 2

                  def mk_slot(i):
                      t = {}
                      t["KQ"] = sb(f"KQ{i}", [128, 12, 128], BF16, s2)
                      t["RW"] = sb(f"RW{i}", [8, 2, 128], F32, s2)
                      t["SZ"] = sb(f"SZ{i}", [128, 4, 128], BF16, s2)
                      for n_ in ("GC", "CS"):
                          t[n_] = sb(f"{n_}{i}", [8, 128], F32, s2)
                      t["DG"] = sb(f"DG{i}", [8, 8], F32, s2)
                      t["COLS"] = sb(f"COLS{i}", [128, 24], F32, s2)
                      t["CF"] = sb(f"CF{i}", [128, 32], F32, s2)
                      for n_ in ("W0", "W1", "W2", "W3", "U0"):
                          t[n_] = sb(f"{n_}{i}", [128, 512], F32, s2)
                      for n_ in ("PTB", "ATB", "AB", "PN0", "PN1", "PTN0", "PTN1", "RB", "RTB", "ZM", "YM", "KBE", "KDEC", "VB",
                                 "WTB", "QDT", "UB", "SQO"):
                          t[n_] = sb(f"{n_}{i}", [128, 512], BF16, s2)
                      t["banks"] = [4 * i, 4 * i + 1, 4 * i + 2, 4 * i + 3]
                      t["i"] = i
                      return t

                  slots = [mk_slot(i) for i in range(NSLOT)]

                  def hs(ap, h):
                      return ap[:, h * 128:(h + 1) * 128]

                  def mm4(bank, lhs, rhs, rk, start=True, stop=True):
                      for h in range(4):
                          P.op("pe", lambda e, h=h: e.matmul(hs(B[bank], h), lhs(h), rhs(h), start=start, stop=stop), rk, [f"B{bank}"], inc=(h == 3))

                  def chunk_gen(T, d, b, ch, need_out):
                      si = T["i"]
                      K_ = lambda n_: (n_, si)
                      ba, bb_, bc, be = T["banks"]
                      t0, nt = blk_range(b)
                      c0 = ch * 128
                      tk = t0 + c0
                      kq, rw, sz = T["KQ"], T["RW"], T["SZ"]
                      GC, CS, DG, COLS, CF = T["GC"], T["CS"], T["DG"], T["COLS"], T["CF"]
                      W0, W1, W2, W3, U0 = T["W0"], T["W1"], T["W2"], T["W3"], T["U0"]
                      PTB, ATB, AB, RB, RTB, ZM, YM = T["PTB"], T["ATB"], T["AB"], T["RB"], T["RTB"], T["ZM"], T["YM"]
                      PN = [T["PN0"], T["PN1"]]
                      PTN = [T["PTN0"], T["PTN1"]]
                      KBE, KDEC, VB, WTB, QDT, UB, SQO = T["KBE"], T["KDEC"], T["VB"], T["WTB"], T["QDT"], T["UB"], T["SQO"]
                      kqk, rwk, szk = K_("KQ"), K_("RW"), K_("SZ")
                      P.dma("sp", kq[:], kqv_s[b, :, :, c0:c0 + 128], [("kqv", b)], [kqk], ("kq", si))
                      P.dma("sp", rw[:], rows_s[b, :, :, c0:c0 + 128], [("rows", b)], [rwk], ("rw", si))
                      if d == 1 and need_out:
                          P.dma("sp", sz[:], szb_s[b, :, :, c0:c0 + 128], [("szb", b)], [szk], ("sz", si))
                      qT = lambda h: kq[:, h, :]
                      kT = lambda h: kq[:, 4 + h, :]
                      vT = lambda h: kq[:, 8 + h, :]
                      g8 = rw[:, 0, :]
                      b8 = rw[:, 1, :]
                      yield
                      P.op("dve", lambda e: e.tensor_tensor_scan(CS[:], ONES8, g8, 0.0, ALU.mult, ALU.add), [rwk, "SEL"], [K_("CS")])
                      if d == 0:
                          P.op("dve", lambda e: e.tensor_copy(GC[:], CS[:]), [K_("CS")], [K_("GC")])
                      else:
                          P.op("dve", lambda e: e.scalar_tensor_tensor(GC[:], CS[:], -1.0, g8, ALU.mult, ALU.add), [K_("CS"), rwk], [K_("GC")])
                          P.op("dve", lambda e: e.tensor_scalar(GC[:], GC[:], CS[:, 127:128], None, ALU.add), [K_("GC"), K_("CS")], [K_("GC")])
                      P.op("dve", lambda e: e.tensor_scalar(DG[:], I8, CS[:, 127:128], None, ALU.mult), [K_("CS"), "IDF"], [K_("DG")])
                      P.op("pe", lambda e: e.matmul(B[be][:, 0:8], GC[:], I8, start=True, stop=True), [K_("GC"), "IDF"], [f"B{be}"], inc=False)
                      P.op("pe", lambda e: e.matmul(B[be][:, 8:16], b8, I8, start=True, stop=True), [rwk, "IDF"], [f"B{be}"], inc=False)
                      P.op("pe", lambda e: e.matmul(B[be][:, 16:24], ONES8, DG[:], start=True, stop=True), ["SEL", K_("DG")], [f"B{be}"])
                      for h in range(4):
                          P.op("pe", lambda e, h=h: e.transpose(Bbf[bc][:, h * 128:(h + 1) * 128], kT(h), IDB[:]), [kqk, "IDB"], [f"B{bc}"], inc=False)
                      for h in range(4):
                          P.op("pe", lambda e, h=h: e.transpose(Bbf[bc][:, 512 + h * 128:512 + (h + 1) * 128], vT(h), IDB[:]), [kqk, "IDB"], [f"B{bc}"], inc=(h == 3))
                      for h in range(4):
                          dh = d * 4 + h
                          P.op("pe", lambda e, h=h, dh=dh: e.matmul(hs(B[ba], h), SEL[:, dh * 128:(dh + 1) * 128], GC[:], start=True, stop=True), ["SEL", K_("GC")], [f"B{ba}"], inc=(h == 3))
                      mm4(bb_, lambda h: SEL[:, (d * 4 + h) * 128:(d * 4 + h + 1) * 128], lambda h: b8, ["SEL", rwk])
                      yield
                      P.op("act", lambda e: e.copy(COLS[:], B[be][:, 0:24]), [f"B{be}"], [K_("COLS")])
                      P.op("act", lambda e: e.activation(CF[:, 0:8], COLS[:, 0:8], AF.Exp), [K_("COLS")], [K_("CF")])
                      P.op("dve", lambda e: e.tensor_tensor(CF[:, 8:16], CF[:, 0:8], COLS[:, 8:16], ALU.mult), [K_("CF"), K_("COLS")], [K_("CF")])
                      P.op("dve", lambda e: e.tensor_tensor(CF[:, 16:24], COLS[:, 16:24], COLS[:, 0:8], ALU.subtract), [K_("COLS"), K_("CF")], [K_("CF")])
                      P.op("act", lambda e: e.activation(CF[:, 16:24], CF[:, 16:24], AF.Exp), [K_("CF")], [K_("CF")])
                      P.op("act", lambda e: e.activation(CF[:, 24:32], COLS[:, 16:24], AF.Exp), [K_("COLS"), K_("CF")], [K_("CF")])
                      for h in range(4):
                          dh = d * 4 + h
                          P.op("dve", lambda e, h=h, dh=dh: e.scalar_tensor_tensor(hs(W0[:], h), hs(B[ba], h), COLS[:, dh:dh + 1], hs(NEGI[d], h), ALU.subtract, ALU.add),
                               [f"B{ba}", K_("COLS"), "MASK"], [K_("W0")])
                      P.op("act", lambda e: e.activation(W3[:], B[ba], AF.Exp), [f"B{ba}"], [K_("W3")])
                      P.op("act", lambda e: e.activation(W1[:], W0[:], AF.Exp), [K_("W0")], [K_("W1")])
                      P.op("pool", lambda e: e.tensor_tensor(h4(QDT[:]), kq[:, 0:4, :], h4(W3[:]), ALU.mult), [kqk, K_("W3")], [K_("QDT")])
                      P.op("dve", lambda e: e.tensor_tensor(W2[:], B[bb_], SM[d], ALU.mult), [f"B{bb_}", "MASK"], [K_("W2")])
                      P.op("pool", lambda e: e.tensor_tensor(W2[:], W2[:], W1[:], ALU.mult), [K_("W2"), K_("W1")], [K_("W2")])
                      yield
                      for h in range(4):
                          dh = d * 4 + h
                          P.op("dve", lambda e, h=h, dh=dh: e.tensor_scalar(hs(KBE[:], h), Bbf[bc][:, h * 128:(h + 1) * 128], CF[:, 8 + dh:9 + dh], None, ALU.mult),
                               [f"B{bc}", K_("CF")], [K_("KBE")])
                      for h in range(4):
                          dh = d * 4 + h
                          P.op("act", lambda e, h=h, dh=dh: e.activation(hs(KDEC[:], h), Bbf[bc][:, h * 128:(h + 1) * 128], AF.Identity, scale=CF[:, 16 + dh:17 + dh]),
                               [f"B{bc}", K_("CF")], [K_("KDEC")])
                          P.op("act", lambda e, h=h, dh=dh: e.activation(hs(VB[:], h), Bbf[bc][:, 512 + h * 128:512 + (h + 1) * 128], AF.Identity, scale=COLS[:, 8 + dh:9 + dh]),
                               [f"B{bc}", K_("COLS")], [K_("VB")])
                      mm4(be, kT, kT, [kqk])
                      yield
                      mm4(bc, kT, qT, [kqk])
                      P.op("dve", lambda e: e.tensor_tensor(ATB[:], B[be], W2[:], ALU.mult), [f"B{be}", K_("W2")], [K_("ATB")])
                      yield
                      P.op("dve", lambda e: e.tensor_tensor(PTB[:], B[bc], W1[:], ALU.mult), [f"B{bc}", K_("W1")], [K_("PTB")])
                      for h in range(4):
                          P.op("pe", lambda e, h=h: e.transpose(Bbf[ba][:, h * 128:(h + 1) * 128], hs(ATB[:], h), IDB[:]), [K_("ATB"), "IDB"], [f"B{ba}"], inc=(h == 3))
                      P.op("pool", lambda e: e.tensor_tensor(PN[0][:], ATB[:], BLK16, ALU.mult), [K_("ATB"), "MASK"], [K_("PN0")])
                      P.op("pool", lambda e: e.tensor_tensor(RTB[:], ID4, PN[0][:], ALU.subtract), ["MASK", K_("PN0")], [K_("RTB")])
                      yield
                      P.op("act", lambda e: e.copy(AB[:], Bbf[ba][:, 0:512]), [f"B{ba}"], [K_("AB")])
                      P.op("dve", lambda e: e.tensor_tensor(PTN[0][:], Bbf[ba][:, 0:512], BLK16, ALU.mult), [f"B{ba}", "MASK"], [K_("PTN0")])
                      P.op("pool", lambda e: e.tensor_tensor(RB[:], ID4, PTN[0][:], ALU.subtract), ["MASK", K_("PTN0")], [K_("RB")])
                      yield
                      cur = 0
                      for it in range(3):
                          nx = 1 - cur
                          kc_, kn_ = (K_(f"PTN{cur}"), K_(f"PN{cur}")), (K_(f"PTN{nx}"), K_(f"PN{nx}"))
                          mm4(ba, lambda h: hs(PTN[cur][:], h), lambda h: hs(PN[cur][:], h), list(kc_))
                          mm4(bb_, lambda h: hs(PN[cur][:], h), lambda h: hs(PTN[cur][:], h), list(kc_))
                          yield
                          P.op("act", lambda e, nx=nx: e.copy(PN[nx][:], B[ba]), [f"B{ba}"], [kn_[1]])
                          P.op("dve", lambda e, nx=nx: e.tensor_copy(PTN[nx][:], B[bb_]), [f"B{bb_}"], [kn_[0]])
                          mm4(bc, lambda h: hs(PTN[nx][:], h), lambda h: hs(RTB[:], h), [kn_[0], K_("RTB")])
                          mm4(be, lambda h: hs(PN[nx][:], h), lambda h: hs(RB[:], h), [kn_[1], K_("RB")])
                          yield
                          P.op("dve", lambda e: e.tensor_tensor(RTB[:], B[bc], RTB[:], ALU.add), [f"B{bc}", K_("RTB")], [K_("RTB")])
                          P.op("dve", lambda e: e.tensor_tensor(RB[:], B[be], RB[:], ALU.add), [f"B{be}", K_("RB")], [K_("RB")])
                          cur = nx
                      for szm in (32, 64, 128):
                          if szm < 128:
                              mm4(ba, lambda h: hs(ATB[:], h), lambda h: hs(RB[:], h), [K_("ATB"), K_("RB")])
                          mm4(bb_, lambda h: hs(AB[:], h), lambda h: hs(RTB[:], h), [K_("AB"), K_("RTB")])
                          yield
                          if szm < 128:
                              P.op("dve", lambda e, szm=szm: e.tensor_tensor(ZM[:], B[ba], OFF[szm], ALU.mult), [f"B{ba}", "MASK"], [K_("ZM")])
                          P.op("dve", lambda e, szm=szm: e.tensor_tensor(YM[:], B[bb_], OFF[szm], ALU.mult), [f"B{bb_}", "MASK"], [K_("YM")])
                          if szm < 128:
                              mm4(bc, lambda h: hs(RTB[:], h), lambda h: hs(ZM[:], h), [K_("RTB"), K_("ZM")])
                          mm4(be, lambda h: hs(RB[:], h), lambda h: hs(YM[:], h), [K_("RB"), K_("YM")])
                          yield
                          if szm < 128:
                              P.op("dve", lambda e: e.tensor_tensor(RB[:], RB[:], B[bc], ALU.subtract), [f"B{bc}", K_("RB")], [K_("RB")])
                          P.op("dve", lambda e: e.tensor_tensor(RTB[:], RTB[:], B[be], ALU.subtract), [f"B{be}", K_("RTB")], [K_("RTB")])
                      mm4(ba, lambda h: hs(RTB[:], h), lambda h: hs(VB[:], h), [K_("RTB"), K_("VB")])
                      mm4(bb_, lambda h: hs(KBE[:], h), lambda h: hs(RTB[:], h), [K_("KBE"), K_("RTB")])
                      yield
                      P.op("act", lambda e: e.copy(U0[:], B[ba]), [f"B{ba}"], [K_("U0")])
                      P.op("act", lambda e: e.copy(WTB[:], B[bb_]), [f"B{bb_}"], [K_("WTB")])
                      yield "scan"
                      mm4(be, lambda h: hs(WTB[:], h), lambda h: hs(SBF[:], h), [K_("WTB"), "SBF"])
                      P.op("dve", lambda e: e.tensor_tensor(UB[:], U0[:], B[be], ALU.subtract), [K_("U0"), f"B{be}"], [K_("UB")])
                      if need_out:
                          for h in range(4):
                              P.op("pe", lambda e, h=h: e.matmul(hs(B[ba], h), hs(SBF[:], h), hs(QDT[:], h), start=True, stop=False), ["SBF", K_("QDT")], [f"B{ba}"], inc=False)
                              P.op("pe", lambda e, h=h: e.matmul(hs(B[ba], h), hs(UB[:], h), hs(PTB[:], h), start=False, stop=True), [K_("UB"), K_("PTB")], [f"B{ba}"], inc=(h == 3))
                      mm4(bb_, lambda h: hs(KDEC[:], h), lambda h: hs(UB[:], h), [K_("KDEC"), K_("UB")])
                      for h in range(4):
                          dh = d * 4 + h
                          P.op("dve", lambda e, h=h, dh=dh: e.scalar_tensor_tensor(hs(S32[:], h), hs(S32[:], h), CF[:, 24 + dh:25 + dh], hs(B[bb_], h), ALU.mult, ALU.add),
                               ["S32", K_("CF"), f"B{bb_}"], ["S32"])
                      P.op("act", lambda e: e.copy(SBF[:], S32[:]), ["S32"], ["SBF"])
                      if need_out:
                          if d == 0:
                              P.op("act", lambda e: e.copy(OF[:, :, tk:tk + 128], h4(B[ba])), [f"B{ba}"], [("OF", tk)])
                          else:
                              P.op("dve", lambda e: e.tensor_tensor(h4(W0[:]), h4(B[ba]), OF[:, :, tk:tk + 128], ALU.add), [f"B{ba}", ("OF", tk)], [K_("W0")])
                              P.op("act", lambda e: e.activation(SQO[:], W0[:], AF.Square), [K_("W0")], [K_("SQO")])
                              mm4(bc, lambda h: ONESB[:], lambda h: hs(SQO[:], h), ["ONESB", K_("SQO")])
                              yield
                              P.op("act", lambda e: e.activation(W1[:], B[bc], AF.Ln, bias=EPSC, scale=1.0 / 128), [f"B{bc}"], [K_("W1")])
                              P.op("act", lambda e: e.activation(W1[:], W1[:], AF.Exp, scale=-0.5), [K_("W1")], [K_("W1")])
                              P.op("pool", lambda e: e.tensor_tensor(W0[:], W0[:], W1[:], ALU.mult), [K_("W0"), K_("W1")], [K_("W0")])
                              P.op("dve", lambda e: e.scalar_tensor_tensor(BIG[:, 4:8, tk:tk + 128], h4(W0[:]), GN, sz[:, :, :], ALU.mult, ALU.mult),
                                   [K_("W0"), "LWg", szk], [("BIG", b)])

                  for d in range(2):
                      P.op("pool", lambda e: e.memset(S32[:], 0.0), [], ["S32"])
                      P.op("pool", lambda e: e.memset(SBF[:], 0.0), [], ["SBF"])
                      border = list(range(NBLK)) if d == 0 else [0] + list(range(8, 0, -1))
                      tasks = []
                      for b in border:
                          t0, nt = blk_range(b)
                          need_out = not (last and b == 0)
                          nch = nt // 128
                          chs = list(range(nch)) if d == 0 else list(range(nch - 1, -1, -1))
                          for ci, ch in enumerate(chs):
                              tasks.append((b, ch, need_out, ci == 0))
                      active = []
                      nxt = 0
                      scan_turn = 0
                      free_slots = list(range(NSLOT))
                      while nxt < len(tasks) or active:
                          if nxt < len(tasks) and free_slots and (not active or len(active) < NSLOT):
                              b, ch, need_out, first = tasks[nxt]
                              if first and d == 1 and need_out:
                                  t0, nt = blk_range(b)
                                  P.dma("sp", BIG[:, 0:4, t0:t0 + nt], ya_s[b, :, :, 0:nt], [("ya", b)], [("BIG", b)], "yal")
                              si = free_slots.pop(0)
                              active.append([chunk_gen(slots[si], d, b, ch, need_out), nxt, False, si])
                              nxt += 1
                          for ent in list(active):
                              g, ti, wscan, si = ent
                              if wscan and ti != scan_turn:
                                  continue
                              try:
                                  r = next(g)
                                  if wscan:
                                      scan_turn += 1
                                      ent[2] = False
                                  if r == "scan":
                                      ent[2] = True
                              except StopIteration:
                                  if wscan:
                                      scan_turn += 1
                                  active.remove(ent)
                                  free_slots.append(si)
                  P.barrier()
              ck("s2")
              with ExitStack() as s4:
                  WOUT = sb("WOUT", [128, 8, D], BF16, s4)
                  XS = sb("XS4", [128, 8, 512], F32, s4)
                  WS4 = sb("WS4", [128, 2, D], F32, s4)
                  SQB = sb("SQB4", [128, 8, 512], BF16, s4)
                  T0 = sb("T04", [128, 512], F32, s4)
                  YP = sb("YP", [128, 8, 512], BF16, s4) if colmaj else None
                  for k in range(8):
                      s = k % 2
                      P.dma("sp", WS4[:, s, :], w_out[l, k * 128:(k + 1) * 128, :], [], [("WS4", s)], ("ws4", s))
                      P.op("dve" if s == 0 else "pool", lambda e, k=k, s=s: e.tensor_copy(WOUT[:, k, :], WS4[:, s, :]), [("WS4", s)], ["WOUT"])
                  pp = 0
                  for nb in range(NBLK):
                      if last and nb == 0:
                          continue
                      t0, nt = blk_range(nb)
                      v = 1 if nb == 0 else 0
                      P.dma("sp", XS[:, :, 0:nt], xsrc_v[:, :, t0:t0 + nt], [("x", nb)], ["XS4"], "xs4")
                      if colmaj and nb > 0:
                          for k in range(8):
                              if k % 2 == 0:
                                  P.op("pool", lambda e, k=k: e.tensor_copy(YP[:, k, :].rearrange("p (a b) -> p a b", b=64), big_view(k, nb, True, True)),
                                       big_keys(nb, True), ["YP"])
                              else:
                                  P.op("act", lambda e, k=k: e.copy(YP[:, k, :].rearrange("p (a b) -> p a b", b=64), big_view(k, nb, True, True)),
                                       big_keys(nb, True), ["YP"])
                      for jn in range(8):
                          bi = (0, 1, 3, 4)[pp % 4]
                          pp += 1
                          for kc in range(8):
                              if colmaj and nb > 0:
                                  P.op("pe", lambda e, kc=kc, jn=jn, bi=bi: e.matmul(
                                      B[bi][:, 0:nt], WOUT[:, kc, jn * 128:(jn + 1) * 128], YP[:, kc, :],
                                      start=(kc == 0), stop=(kc == 7)), ["WOUT", "YP"], [f"B{bi}"], inc=(kc == 7))
                              else:
                                  P.op("pe", lambda e, kc=kc, jn=jn, bi=bi: e.matmul(
                                      B[bi][:, 0:nt], WOUT[:, kc, jn * 128:(jn + 1) * 128], BIG[:, kc, t0:t0 + nt],
                                      start=(kc == 0), stop=(kc == 7)), ["WOUT"] + big_keys(nb, False), [f"B{bi}"], inc=(kc == 7))
                          P.op("dve", lambda e, jn=jn, bi=bi, v=v: e.scalar_tensor_tensor(
                              XS[:, jn, 0:nt], B[bi][:, 0:nt], mcol(v, 16 + jn), XS[:, jn, 0:nt], ALU.mult, ALU.add),
                              [f"B{bi}", "XS4"], [("XSo", jn)])
                      okeys = [("XSo", jn) for jn in range(8)]
                      if not last:
                          P.dma("pool", xres_v[:, :, t0:t0 + nt], XS[:, :, 0:nt], okeys, [("x", nb)], "xst")
                          P.op("dve", lambda e: e.engine_nop(), [("x", nb)], ["XS4"])
                      else:
                          P.op("act", lambda e: e.activation(SQB[:], XS[:], AF.Square), okeys, ["SQB4"])
                          for k in range(8):
                              P.op("pe", lambda e, k=k: e.matmul(B[2], ONESB[:], SQB[:, k, :], start=(k == 0), stop=(k == 7)), ["SQB4", "ONESB"], ["B2"], inc=(k == 7))
                          P.op("act", lambda e: e.activation(T0[:], B[2], AF.Ln, bias=EPSC, scale=1.0 / D), ["B2"], ["T04"])
                          P.op("act", lambda e: e.activation(T0[:], T0[:], AF.Exp, scale=-0.5), ["T04"], ["T04"])
                          for k in range(8):
                              P.op("dve", lambda e, k=k: e.scalar_tensor_tensor(XS[:, k, :], XS[:, k, :], FN[:, k:k + 1], T0[:], ALU.mult, ALU.mult),
                                   [("XSo", k), "T04", "FN", "SQB4"], [("XSf", k)])
                          fk = [("XSf", k) for k in range(8)]
                          P.dma("pool", out_v[:, :, t0 - NCTX:t0 - NCTX + nt], XS[:, :, 0:nt], fk, [("out", nb)], "ost")
                          P.op("dve", lambda e: e.engine_nop(), [("out", nb)], ["XS4"])
                  P.barrier()
          P.barrier()

    except _Stop:
        pass
    return nc


def _consts():
    i = np.arange(128)
    t, s = i[None, :], i[:, None]
    negi_f = np.where(t >= s, 0.0, BIGNEG)
    negi_b = np.where(t <= s, 0.0, BIGNEG)
    sm_f = (t > s).astype(np.float64)
    sm_b = (t < s).astype(np.float64)
    blk = lambda z: ((s // z) == (t // z)).astype(np.float64)
    blk16 = blk(16)
    off = {z: blk(z) * (1 - blk(z // 2)) for z in (32, 64, 128)}
    ident = np.eye(128)
    ms = [negi_f, negi_b, sm_f, sm_b, blk16, off[32], off[64], off[128], ident]
    cmask = np.concatenate([np.tile(m, (1, 4)) for m in ms], axis=1).astype(np.float32)
    sel = np.zeros((8, 2 * 1024 + 128), np.float32)
    for dh in range(8):
        sel[dh, dh * 128:(dh + 1) * 128] = 1.0
        sel[dh, 1024 + dh * 128:1024 + (dh + 1) * 128] = -1.0
    sel[:, 2048:] = 1.0
    return cmask, ident.astype(np.float32), sel


_NC_CACHE = {}


def make_in_maps(x, c, ctx, c_ctx, norm_w, w_mod, b_mod, w_in, conv_a, conv_qkv, a_log, dt_bias, gdn_norm, w_out, final_norm):
    f = lambda a: np.ascontiguousarray(np.asarray(a, dtype=np.float32))
    x, c, ctx, c_ctx = f(x), f(c), f(ctx), f(c_ctx)
    cmask, ident, sel = _consts()
    L = norm_w.shape[0]
    col = lambda a: np.ascontiguousarray(a.reshape(-1, 128).T)
    shared = {
        "w_mod": f(w_mod), "w_in": f(w_in), "w_out": f(w_out),
        "bmod": np.stack([col(f(b_mod)[l]) for l in range(L)]),
        "normw": np.stack([col(f(norm_w)[l]) for l in range(L)]),
        "conva": np.stack([np.ascontiguousarray(f(conv_a)[l].T.reshape(4, 128, 3).transpose(1, 0, 2).reshape(128, 12)) for l in range(L)]),
        "convq": np.stack([np.ascontiguousarray(f(conv_qkv)[l].T.reshape(12, 128, 3).transpose(1, 0, 2).reshape(128, 36)) for l in range(L)]),
        "alog": np.ascontiguousarray(f(a_log).reshape(L, 8, 1)),
        "dtb": np.ascontiguousarray(f(dt_bias).reshape(L, 8, 1)),
        "gnorm": np.ascontiguousarray(f(gdn_norm).reshape(L, 128, 1)),
        "fnorm": col(f(final_norm)),
        "cmask": cmask, "cident": ident, "csel": sel,
    }
    maps = []
    for b in range(x.shape[0]):
        m = dict(shared)
        m["xT"] = np.ascontiguousarray(np.concatenate([ctx[b], x[b]], axis=0).T)
        ccb = np.stack([col(c[b]), col(c_ctx)], axis=-1).reshape(128, 16)
        m["cc"] = np.ascontiguousarray(ccb)
        maps.append(m)
    return maps


def kernel(x, c, ctx, c_ctx, norm_w, w_mod, b_mod, w_in, conv_a, conv_qkv, a_log, dt_bias, gdn_norm, w_out, final_norm, _nlayers=DEPTH):
    maps = make_in_maps(x, c, ctx, c_ctx, norm_w, w_mod, b_mod, w_in, conv_a, conv_qkv, a_log, dt_bias, gdn_norm, w_out, final_norm)
    if _nlayers not in _NC_CACHE:
        _NC_CACHE[_nlayers] = build(_nlayers)
    nc = _NC_CACHE[_nlayers]
    res = run_bass_kernel_spmd(nc, maps, core_ids=list(range(len(maps))))
    out = np.stack([np.ascontiguousarray(r["outT"].T) for r in res.results], axis=0)
    return out.astype(np.float32)
```

```python
import numpy as np
from contextlib import ExitStack
import concourse.bass as bass
import concourse.mybir as mybir
from concourse.bass_utils import run_bass_kernel_spmd

F32 = mybir.dt.float32
BF16 = mybir.dt.bfloat16
AF = mybir.ActivationFunctionType
ALU = mybir.AluOpType

D = 1024
NCTX = 256
NLAT = 4096
NTOK = NCTX + NLAT
DPROJ = 4112
DEPTH = 4
EPS = 1e-6
NBLK = 9
BIGNEG = -30000.0
PSUM_KEYS = {f"B{i}" for i in range(8)}


def blk_range(b):
    if b == 0:
        return 0, NCTX
    return NCTX + (b - 1) * 512, 512


class _Stop(Exception):
    pass


class Prog:
    def __init__(self, nc, es):
        self.nc = nc
        self.es = es
        self.eng = {"pe": nc.tensor, "act": nc.scalar, "dve": nc.vector, "pool": nc.gpsimd, "sp": nc.sync}
        self.sem = {}
        self.cnt = {}
        self.epoch = 0
        self.state = {}
        self.waited = {}
        self.dsem = {}
        self.dcnt = {}
        self.all_sems = {}
        self.nops = 0
        self.stop_at = None
        self.new_epoch()

    def new_epoch(self):
        self.epoch += 1
        for e in ("pe", "act", "dve", "pool"):
            s = self.es.enter_context(self.nc.semaphore(f"s_{e}_{self.epoch}"))
            self.sem[e] = s
            self.cnt[e] = 0
            self.all_sems[id(s)] = s

    def _collect(self, engine, reads, writes):
        need = {}

        def add(ev):
            s, v, e = ev
            k = id(s)
            if k not in need or need[k][1] < v:
                need[k] = (s, v, e)

        for k in reads:
            st = self.state.get(k)
            if st is not None:
                for ev in st["w"].values():
                    add(ev)
                if k in PSUM_KEYS:
                    for ev in st["r"].values():
                        if ev[2] != engine:
                            add(ev)
        for k in writes:
            st = self.state.get(k)
            if st is not None:
                for ev in st["w"].values():
                    if ev[2] != engine:
                        add(ev)
                for ev in st["r"].values():
                    if ev[2] != engine:
                        add(ev)
        out = []
        wd = self.waited.setdefault(engine, {})
        for k, (s, v, e) in need.items():
            if wd.get(k, 0) >= v:
                continue
            wd[k] = v
            out.append((s, v))
        return out

    def _record(self, ev, reads, writes):
        for k in reads:
            st = self.state.setdefault(k, {"w": {}, "r": {}})
            st["r"][id(ev[0])] = ev
        for k in writes:
            st = self.state.setdefault(k, {"w": {}, "r": {}})
            st["w"][id(ev[0])] = ev

    def op(self, engine, fn, reads=(), writes=(), inc=True):
        e = self.eng[engine]
        for s, v in self._collect(engine, reads, writes):
            e.wait_ge(s, v)
        inst = fn(e)
        if inc:
            self.cnt[engine] += 1
            inst.then_inc(self.sem[engine], 1)
            self._record((self.sem[engine], self.cnt[engine], engine), reads, writes)
        else:
            self._record((self.sem[engine], self.cnt[engine] + 1, engine), reads, writes)
        self.nops += 1
        if self.stop_at is not None and self.nops == self.stop_at:
            self.barrier()
            raise _Stop()

    def dma(self, queue, out, in_, reads, writes, slot):
        e = self.eng[queue]
        for s, v in self._collect("q_" + queue, reads, writes):
            e.wait_ge(s, v)
        if slot not in self.dsem:
            self.dsem[slot] = self.es.enter_context(self.nc.semaphore(f"d_{len(self.dsem)}"))
            self.dcnt[slot] = 0
        self.dcnt[slot] += 16
        e.dma_start(out=out, in_=in_).then_inc(self.dsem[slot], 16)
        self._record((self.dsem[slot], self.dcnt[slot], "dma_" + str(slot)), reads, writes)

    def barrier(self):
        evs = [(self.sem[e], self.cnt[e]) for e in ("pe", "act", "dve", "pool") if self.cnt[e] > 0]
        evs += [(self.dsem[s], self.dcnt[s]) for s in self.dsem]
        for en in ("pe", "act", "dve", "pool", "sp"):
            key = en if en != "sp" else "q_sp"
            wd = self.waited.setdefault(key, {})
            for s, v in evs:
                if en in self.sem and s is self.sem.get(en):
                    continue
                if wd.get(id(s), 0) >= v:
                    continue
                wd[id(s)] = v
                self.eng[en].wait_ge(s, v)
        wd = self.waited.setdefault("q_pool", {})
        for s, v in evs:
            wd[id(s)] = max(wd.get(id(s), 0), v)
        self.state = {}


def build(nlayers=DEPTH, stop=None):
    def ck(name):
        if stop == name:
            print("ck", name, "nops", P.nops)
            raise _Stop()
    nc = bass.Bass("TRN2", target_bir_lowering=False)
    dt_in = lambda n, shp, dt=F32: nc.dram_tensor(n, list(shp), dt, kind="ExternalInput").ap()
    xT = dt_in("xT", [D, NTOK])
    cc = dt_in("cc", [128, 16])
    w_mod = dt_in("w_mod", [DEPTH, D, 3 * D])
    bmod = dt_in("bmod", [DEPTH, 128, 24])
    normw = dt_in("normw", [DEPTH, 128, 8])
    w_in = dt_in("w_in", [DEPTH, D, DPROJ])
    conva = dt_in("conva", [DEPTH, 128, 12])
    convq = dt_in("convq", [DEPTH, 128, 36])
    alog = dt_in("alog", [DEPTH, 8, 1])
    dtb = dt_in("dtb", [DEPTH, 8, 1])
    gnorm = dt_in("gnorm", [DEPTH, 128, 1])
    w_out = dt_in("w_out", [DEPTH, D, D])
    fnorm = dt_in("fnorm", [128, 8])
    cmask = dt_in("cmask", [128, 9 * 512])
    cident = dt_in("cident", [128, 128])
    csel = dt_in("csel", [8, 2 * 1024 + 128])
    outT = nc.dram_tensor("outT", [D, NLAT], F32, kind="ExternalOutput").ap()
    xres = nc.dram_tensor("xres", [D, NTOK], F32).ap()
    kqv_s = nc.dram_tensor("kqv_s", [NBLK, 128, 12, 512], BF16).ap()
    ya_s = nc.dram_tensor("ya_s", [NBLK, 128, 4, 512], BF16).ap()
    szb_s = nc.dram_tensor("szb_s", [NBLK, 128, 4, 512], BF16).ap()
    rows_s = nc.dram_tensor("rows_s", [NBLK, 8, 2, 512], F32).ap()

    es = ExitStack()
    try:
      with es:
          P = Prog(nc, es)
          if isinstance(stop, int):
              P.stop_at = stop
          _uniq = [0]

          def sb(n, shp, dt=F32, st=es):
              _uniq[0] += 1
              return st.enter_context(nc.sbuf_tensor(f"{n}_{_uniq[0]}", list(shp), dt))
          BIG = sb("BIG", [128, 8, NTOK], BF16)
          MASK = sb("MASK", [128, 9, 512], BF16)
          IDF = sb("IDF", [128, 128], F32)
          IDB = sb("IDB", [128, 128], BF16)
          ONESB = sb("ONESB", [128, 128], BF16)
          SEL = sb("SEL", [8, 2 * 1024 + 128], F32)
          MODS = sb("MODS", [128, DEPTH, 2, 24], F32)
          CC = sb("CC", [128, 16], F32)
          FN = sb("FN", [128, 8], F32)
          LW = sb("LW", [128, 64], F32)
          L8 = sb("L8", [8, 4], F32)
          DUM = sb("DUM", [128, 2], F32)
          EPST = sb("EPST", [128, 1], F32)
          EPSC = EPST[:, 0:1]
          banks = [es.enter_context(nc.psum_tensor(f"B{i}", [128, 512], F32)) for i in range(8)]
          B = [b[:] for b in banks]
          Bbf = [b[:].bitcast(BF16) for b in banks]

          NEGI = [MASK[:, 0, :], MASK[:, 1, :]]
          SM = [MASK[:, 2, :], MASK[:, 3, :]]
          BLK16 = MASK[:, 4, :]
          OFF = {32: MASK[:, 5, :], 64: MASK[:, 6, :], 128: MASK[:, 7, :]}
          ID4 = MASK[:, 8, :]
          I8 = IDF[0:8, 0:8]
          ONES8 = SEL[:, 2048:2176]

          def h4(ap):
              return ap.rearrange("p (h t) -> p h t", h=4)

          with ExitStack() as st0:
              MST = sb("MST", [128, 9 * 512], F32, st0)
              WST = sb("WST", [128, 2, 4096], F32, st0)
              SC = sb("SC", [128, 16], F32, st0)
              BM = sb("BM", [128, 24], F32, st0)
              NW = sb("NW", [128, 8], F32, st0)
              P.dma("sp", MST[:], cmask, [], ["MST"], "c0")
              P.dma("sp", IDF[:], cident, [], ["IDF"], "c1")
              P.dma("sp", SEL[:], csel, [], ["SEL"], "c2")
              P.dma("sp", CC[:], cc, [], ["CC"], "c3")
              P.dma("sp", FN[:], fnorm, [], ["FN"], "c4")
              P.op("dve", lambda e: e.tensor_copy(MASK[:].rearrange("p a b -> p (a b)"), MST[:]), ["MST"], ["MASK"])
              P.op("dve", lambda e: e.tensor_copy(IDB[:], IDF[:]), ["IDF"], ["IDB"])
              P.op("dve", lambda e: e.memset(ONESB[:], 1.0), [], ["ONESB"])
              P.op("dve", lambda e: e.memset(DUM[:], 0.0), [], ["DUM"])
              P.op("dve", lambda e: e.memset(EPST[:], EPS), [], ["EPST"])
              P.op("act", lambda e: e.activation(SC[:], CC[:], AF.Silu), ["CC"], ["SC"])
              for l in range(nlayers):
                  P.dma("sp", BM[:], bmod[l], [], ["BM"], "c5")
                  P.dma("sp", NW[:], normw[l], [], ["NW"], "c6")
                  for jg in range(6):
                      s = jg % 2
                      P.dma("sp", WST[:, s, :].rearrange("p (k n) -> p k n", k=8),
                            w_mod[l, :, jg * 512:(jg + 1) * 512].rearrange("(k p) n -> p k n", p=128), [], [("WST", s)], ("wst", s))
                      for jj in range(4):
                          j = jg * 4 + jj
                          for k in range(8):
                              P.op("pe", lambda e, j=j, jj=jj, k=k, s=s: e.matmul(
                                  B[0][:, 2 * j:2 * j + 2], WST[:, s, k * 512 + jj * 128:k * 512 + (jj + 1) * 128],
                                  SC[:].rearrange("p (k v) -> p k v", v=2)[:, k, :], start=(k == 0), stop=(k == 7)),
                                  [("WST", s), "SC"], ["B0"])
                  for v in range(2):
                      P.op("dve", lambda e, v=v, l=l: e.tensor_tensor(
                          MODS[:, l, v, :], B[0][:, 0:48].rearrange("p (j v) -> p j v", v=2)[:, :, v], BM[:], ALU.add),
                          ["B0", "BM"], [("MODS", l)])
                      P.op("dve", lambda e, v=v, l=l: e.scalar_tensor_tensor(
                          MODS[:, l, v, 8:16], MODS[:, l, v, 8:16], 1.0, NW[:], ALU.add, ALU.mult),
                          [("MODS", l), "NW"], [("MODS", l)])
              P.barrier()
          ck("s0")

          for l in range(nlayers):
              P.new_epoch()
              colmaj = (l % 2 == 1)
              last = (l == nlayers - 1)
              xsrc = xT if l == 0 else xres
              xsrc_v = xsrc.rearrange("(k p) t -> p k t", p=128)
              xres_v = xres.rearrange("(k p) t -> p k t", p=128)
              out_v = outT.rearrange("(k p) t -> p k t", p=128)

              def mcol(v, j, l=l):
                  return MODS[:, l, v, j:j + 1]

              P.dma("sp", LW[:, 0:12], conva[l], [], ["LWa"], "c7")
              P.dma("sp", LW[:, 12:48], convq[l], [], ["LWq"], "c8")
              P.dma("sp", LW[:, 48:49], gnorm[l], [], ["LWg"], "c9")
              P.dma("sp", L8[:, 0:1], alog[l], [], ["L8a"], "c10")
              P.dma("sp", L8[:, 1:2], dtb[l], [], ["L8"], "c11")
              P.op("act", lambda e: e.activation(L8[:, 2:3], L8[:, 0:1], AF.Exp), ["L8a"], ["L8"])
              P.op("dve", lambda e: e.tensor_scalar(L8[:, 2:3], L8[:, 2:3], -1.0, None, ALU.mult), ["L8"], ["L8"])
              CA = lambda j, tap: LW[:, j * 3 + tap: j * 3 + tap + 1]
              CQ = lambda j, tap: LW[:, 12 + j * 3 + tap: 12 + j * 3 + tap + 1]
              GN = LW[:, 48:49]

              def big_keys(b, permuted):
                  if b == 0 or not permuted:
                      return [("BIG", b)]
                  return [("BIG", i) for i in range(1, 9)]

              def big_view(kc, b, permuted, rowmajor_of_colscan):
                  t0, nt = blk_range(b)
                  if b == 0 or not permuted:
                      return BIG[:, kc, t0:t0 + nt]
                  lat = BIG[:, kc, NCTX:NTOK]
                  if rowmajor_of_colscan:
                      v = lat.rearrange("p (c r) -> p r c", r=64)
                  else:
                      v = lat.rearrange("p (r c) -> p c r", c=64)
                  return v[:, (b - 1) * 8:(b - 1) * 8 + 8, :]

              def pview(ap, b, permuted):
                  t0, nt = blk_range(b)
                  if b == 0 or not permuted:
                      return ap[:, 0:nt]
                  return ap.rearrange("p (a b) -> p a b", b=64)

              with ExitStack() as s1:
                  WIN = sb("WIN", [128, 8, DPROJ], BF16, s1)
                  XSF = sb("XS", [128, 4112], F32, s1)
                  XS = XSF[:, 0:4096].rearrange("p (k t) -> p k t", k=8)
                  T0 = sb("T0", [128, 512], F32, s1)
                  T1 = sb("T1", [128, 512], F32, s1)
                  T2 = sb("T2", [128, 512], F32, s1)
                  SQQ = sb("SQQ", [128, 512], BF16, s1)
                  KQVB = sb("KQVB", [128, 12, 512], BF16, s1)
                  SQB = KQVB[:, 0:8, :]
                  YAB = sb("YAB", [128, 4, 512], BF16, s1)
                  SZBB = sb("SZBB", [128, 4, 512], BF16, s1)
                  ROWB = sb("ROWB", [8, 2, 512], F32, s1)
                  HP = sb("HP", [128, 8, 512], BF16, s1) if colmaj else None
                  XSW = XSF[:]
                  i = 0
                  for k in range(8):
                      for hf in range(2):
                          s = i % 2
                          c0 = hf * 2056
                          P.dma("sp", XSW[:, s * 2056:(s + 1) * 2056], w_in[l, k * 128:(k + 1) * 128, c0:c0 + 2056],
                                [], [("XS", s)], ("xs", s))
                          eng = ("dve", "pool", "act")[i % 3]
                          if eng == "act":
                              P.op("act", lambda e, k=k, s=s, c0=c0: e.copy(WIN[:, k, c0:c0 + 2056], XSW[:, s * 2056:(s + 1) * 2056]),
                                   [("XS", s)], ["WIN"])
                          else:
                              P.op(eng, lambda e, k=k, s=s, c0=c0: e.tensor_copy(WIN[:, k, c0:c0 + 2056], XSW[:, s * 2056:(s + 1) * 2056]),
                                   [("XS", s)], ["WIN"])
                          i += 1
                  if stop == "s1w":
                      P.barrier()
                      ck("s1w")
                  for nb in range(NBLK):
                      t0, nt = blk_range(nb)
                      v = 1 if nb == 0 else 0
                      P.dma("sp", XS[:, :, 0:nt], xsrc_v[:, :, t0:t0 + nt], [("x", nb)], [("XS", 0), ("XS", 1)], ("xs", 0))
                      P.op("act", lambda e, nt=nt: e.activation(SQB[:, :, 0:nt], XS[:, :, 0:nt], AF.Square),
                           [("XS", 0), ("XS", 1)], ["KQVB"])
                      for k in range(8):
                          P.op("pe", lambda e, k=k, nt=nt: e.matmul(B[2][:, 0:nt], ONESB[:], SQB[:, k, 0:nt], start=(k == 0), stop=(k == 7)),
                               ["KQVB", "ONESB"], ["B2"], inc=(k == 7))
                      P.op("act", lambda e, nt=nt: e.activation(T0[:, 0:nt], B[2][:, 0:nt], AF.Ln, bias=EPSC, scale=1.0 / D), ["B2"], ["T0"])
                      P.op("act", lambda e, nt=nt: e.activation(T0[:, 0:nt], T0[:, 0:nt], AF.Exp, scale=-0.5), ["T0"], ["T0"])
                      for k in range(8):
                          eng = "dve" if k % 2 == 0 else "pool"
                          P.op(eng, lambda e, k=k, nt=nt: e.tensor_tensor(XS[:, k, 0:nt], XS[:, k, 0:nt], T0[:, 0:nt], ALU.mult),
                               [("XS", 0), ("XS", 1), "T0", "KQVB"], [("XSn", k)])
                          P.op("act", lambda e, k=k, nt=nt, t0=t0, v=v: e.activation(
                              BIG[:, k, t0:t0 + nt], XS[:, k, 0:nt], AF.Identity, bias=mcol(v, k), scale=mcol(v, 8 + k)),
                              [("XSn", k)], [("BIG", nb)])
                      P.op("act", lambda e: e.activation(DUM[:, 0:1], DUM[:, 1:2], AF.Copy), [("BIG", nb)] + [("XSn", k) for k in range(8)], [("XS", 0), ("XS", 1)])

                  if stop == "s1p":
                      P.barrier()
                      ck("s1p")
                  P.barrier()
                  TS = [(T0, T1, T2, SQQ, "T0", "T1", "T2", "SQQ", 2)]
                  for i_ in range(2):
                      o_ = i_ * 1792
                      TS.append((XSF[:, o_:o_ + 512], XSF[:, o_ + 512:o_ + 1024], XSF[:, o_ + 1024:o_ + 1536],
                                 XSF[:, o_ + 1536:o_ + 1792].bitcast(BF16), f"T0_{i_}", f"T1_{i_}", f"T2_{i_}", f"SQQ_{i_}", 7 if i_ == 0 else 2))
                  tsi = [0]
                  pp = [0]
                  PROJ_BANKS = [0, 1, 3, 4, 5, 6]

                  def proj(b, c0, m):
                      bi = PROJ_BANKS[pp[0] % len(PROJ_BANKS)]
                      pp[0] += 1
                      t0, nt = blk_range(b)
                      for k in range(8):
                          if colmaj and b > 0:
                              P.op("pe", lambda e, k=k, bi=bi: e.matmul(
                                  B[bi][0:m, 0:nt], WIN[:, k, c0:c0 + m], HP[:, k, :],
                                  start=(k == 0), stop=(k == 7)), ["WIN", "HP"], [f"B{bi}"], inc=(k == 7))
                          else:
                              P.op("pe", lambda e, k=k, bi=bi: e.matmul(
                                  B[bi][0:m, 0:nt], WIN[:, k, c0:c0 + m], BIG[:, k, t0:t0 + nt],
                                  start=(k == 0), stop=(k == 7)), ["WIN"] + big_keys(b, False), [f"B{bi}"], inc=(k == 7))
                      return bi

                  def conv(dst, dkey, src, skey, w, nt, seg):
                      P.op("dve", lambda e: e.tensor_scalar(dst[:, 0:nt], src[:, 0:nt], w(1), None, ALU.mult), [skey, "LWa", "LWq"], [dkey])
                      dv = dst[:, 0:nt].rearrange("p (a b) -> p a b", b=seg)
                      sv = src[:, 0:nt].rearrange("p (a b) -> p a b", b=seg)
                      P.op("dve", lambda e: e.scalar_tensor_tensor(dv[:, :, 1:seg], sv[:, :, 0:seg - 1], w(0), dv[:, :, 1:seg], ALU.mult, ALU.add),
                           [skey, dkey, "LWa", "LWq"], [dkey])
                      P.op("dve", lambda e: e.scalar_tensor_tensor(dv[:, :, 0:seg - 1], sv[:, :, 1:seg], w(2), dv[:, :, 0:seg - 1], ALU.mult, ALU.add),
                           [skey, dkey, "LWa", "LWq"], [dkey])

                  def run_tasks(gens_factories, nsets):
                      pending = list(gens_factories)
                      active = []
                      free = list(range(nsets))
                      while pending or active:
                          if pending and free:
                              si = free.pop(0)
                              active.append((pending.pop(0)(TS[si]), si))
                          for ent in list(active):
                              try:
                                  next(ent[0])
                              except StopIteration:
                                  active.remove(ent)
                                  free.append(ent[1])

                  for b in range(NBLK):
                      t0, nt = blk_range(b)
                      seg = NCTX if b == 0 else 64
                      if colmaj and b > 0:
                          for k in range(8):
                              eng = "pool" if k % 2 == 0 else "act"
                              if eng == "pool":
                                  P.op("pool", lambda e, k=k: e.tensor_copy(HP[:, k, :].rearrange("p (a b) -> p a b", b=64), big_view(k, b, True, False)),
                                       big_keys(b, True), ["HP"])
                              else:
                                  P.op("act", lambda e, k=k: e.copy(HP[:, k, :].rearrange("p (a b) -> p a b", b=64), big_view(k, b, True, False)),
                                       big_keys(b, True), ["HP"])

                      def mixer_task(jj, b=b, nt=nt, seg=seg):
                          def g(ts):
                              T0, T1, T2, SQQ, k0, k1, k2, kq_, ssb = ts
                              bi = proj(b, jj * 128, 128)
                              yield
                              P.op("act", lambda e: e.copy(T0[:, 0:nt], B[bi][:, 0:nt]), [f"B{bi}"], [k0])
                              bi = proj(b, 1024 + jj * 128, 128)
                              yield
                              P.op("dve", lambda e: e.tensor_tensor(T0[:, 0:nt], B[bi][:, 0:nt], T0[:, 0:nt], ALU.mult), [f"B{bi}", k0], [k0])
                              conv(T1, k1, T0, k0, lambda tap: CA(jj, tap), nt, seg)
                              bi = proj(b, 512 + jj * 128, 128)
                              yield
                              P.op("dve", lambda e: e.tensor_tensor(T1[:, 0:nt], B[bi][:, 0:nt], T1[:, 0:nt], ALU.mult), [f"B{bi}", k1], [k1])
                              bi = proj(b, 1536 + jj * 128, 128)
                              yield
                              P.op("act", lambda e: e.activation(T2[:, 0:nt], B[bi][:, 0:nt], AF.Silu), [f"B{bi}"], [k2])
                              yield
                              P.op("pool", lambda e: e.tensor_tensor(YAB[:, jj, 0:nt], T1[:, 0:nt], T2[:, 0:nt], ALU.mult), [k1, k2], ["YAB"])
                          return g

                      def qkv_task(idx, b=b, nt=nt, seg=seg):
                          def g(ts):
                              T0, T1, T2, SQQ, k0, k1, k2, kq_, ssb = ts
                              bi = proj(b, 2048 + idx * 128, 128)
                              yield
                              conv(T1, k1, B[bi], f"B{bi}", lambda tap: CQ(idx, tap), nt, seg)
                              yield
                              if idx >= 8:
                                  P.op("act", lambda e: e.activation(KQVB[:, idx, 0:nt], T1[:, 0:nt], AF.Silu), [k1], ["KQVB"])
                                  return
                              P.op("act", lambda e: e.activation(T2[:, 0:nt], T1[:, 0:nt], AF.Silu), [k1], [k2])
                              P.op("act", lambda e: e.activation(SQQ[:, 0:nt], T2[:, 0:nt], AF.Square), [k2], [kq_])
                              yield
                              P.op("pe", lambda e: e.matmul(B[ssb][:, 0:nt], ONESB[:], SQQ[:, 0:nt], start=True, stop=True), [kq_, "ONESB"], [f"B{ssb}"])
                              yield
                              P.op("act", lambda e: e.activation(T0[:, 0:nt], B[ssb][:, 0:nt], AF.Ln, bias=EPSC, scale=1.0), [f"B{ssb}"], [k0])
                              P.op("act", lambda e: e.activation(T0[:, 0:nt], T0[:, 0:nt], AF.Exp, scale=-0.5), [k0], [k0])
                              yield
                              sc = (128.0 ** -0.5) if idx < 4 else 1.0
                              P.op("dve", lambda e: e.scalar_tensor_tensor(KQVB[:, idx, 0:nt], T2[:, 0:nt], sc, T0[:, 0:nt], ALU.mult, ALU.mult),
                                   [k2, k0], ["KQVB"])
                          return g

                      def zb_task(h, b=b, nt=nt):
                          def g(ts):
                              bi = proj(b, 3584 + h * 128, 128)
                              yield
                              P.op("act", lambda e: e.activation(SZBB[:, h, 0:nt], B[bi][:, 0:nt], AF.Silu), [f"B{bi}"], ["SZBB"])
                          return g

                      def rows_task(b=b, nt=nt):
                          def g(ts):
                              bi = proj(b, 4096, 8)
                              bi2 = proj(b, 4104, 8)
                              yield
                              P.op("act", lambda e: e.activation(ROWB[:, 1, 0:nt], B[bi][0:8, 0:nt], AF.Sigmoid), [f"B{bi}"], ["ROWB"])
                              P.op("act", lambda e: e.activation(ROWB[:, 0, 0:nt], B[bi2][0:8, 0:nt], AF.Exp, bias=L8[:, 1:2]), [f"B{bi2}", "L8"], ["ROWB"])
                              P.op("act", lambda e: e.activation(ROWB[:, 0, 0:nt], ROWB[:, 0, 0:nt], AF.Ln, bias=1.0), ["ROWB"], ["ROWB"])
                              yield
                              P.op("dve", lambda e: e.tensor_scalar(ROWB[:, 0, 0:nt], ROWB[:, 0, 0:nt], L8[:, 2:3], None, ALU.mult), ["ROWB", "L8"], ["ROWB"])
                          return g

                      tasks = [mixer_task(jj) for jj in range(4)] + [qkv_task(i) for i in range(12)] + [zb_task(h) for h in range(4)] + [rows_task()]
                      run_tasks(tasks, 3)
                      P.dma("pool", kqv_s[b, :, :, 0:nt], KQVB[:, :, 0:nt], ["KQVB"], [("kqv", b)], "sp0")
                      P.dma("pool", ya_s[b, :, :, 0:nt], YAB[:, :, 0:nt], ["YAB"], [("ya", b)], "sp1")
                      P.dma("pool", szb_s[b, :, :, 0:nt], SZBB[:, :, 0:nt], ["SZBB"], [("szb", b)], "sp2")
                      P.dma("pool", rows_s[b, :, :, 0:nt], ROWB[:, :, 0:nt], ["ROWB"], [("rows", b)], "sp3")
                  P.barrier()

              ck("s1")
              with ExitStack() as s2:
                  OF = sb("OF", [128, 4, NTOK], BF16, s2)
                  S32 = sb("S32", [128, 512], F32, s2)
                  SBF = sb("SBF", [128, 512], BF16, s2)
                  NSLOT = 3
                  SB0, SB1 = 6, 7

                  def mk_slot(i):
                      t = {}
                      t["KQ"] = sb(f"KQ{i}", [128, 12, 128], BF16, s2)
                      t["RW"] = sb(f"RW{i}", [8, 2, 128], F32, s2)
                      t["SZ"] = sb(f"SZ{i}", [128, 4, 128], BF16, s2)
                      for n_ in ("GC", "CS"):
                          t[n_] = sb(f"{n_}{i}", [8, 128], F32, s2)
                      t["DG"] = sb(f"DG{i}", [8, 8], F32, s2)
                      t["COLS"] = sb(f"COLS{i}", [128, 24], F32, s2)
                      t["CF"] = sb(f"CF{i}", [128, 32], F32, s2)
                      for n_ in ("W0", "W1", "W2", "W3"):
                          t[n_] = sb(f"{n_}{i}", [128, 512], F32, s2)
                      for n_ in ("PTB", "ATB", "AB", "PN0", "PN1", "PTN0", "PTN1", "RB", "RTB", "KBE", "KDEC", "VB",
                                 "QDT"):
                          t[n_] = sb(f"{n_}{i}", [128, 512], BF16, s2)
                      t["banks"] = [2 * i, 2 * i + 1]
                      t["i"] = i
                      return t

                  slots = [mk_slot(i) for i in range(NSLOT)]

                  def hs(ap, h):
                      return ap[:, h * 128:(h + 1) * 128]

                  def mm4(bank, lhs, rhs, rk, start=True, stop=True):
                      for h in range(4):
                          P.op("pe", lambda e, h=h: e.matmul(hs(B[bank], h), lhs(h), rhs(h), start=start, stop=stop), rk, [f"B{bank}"], inc=(h == 3))

                  def chunk_gen(T, d, b, ch, need_out):
                      si = T["i"]
                      K_ = lambda n_: (n_, si)
                      b0, b1 = T["banks"]
                      t0, nt = blk_range(b)
                      c0 = ch * 128
                      tk = t0 + c0
                      kq, rw, sz = T["KQ"], T["RW"], T["SZ"]
                      GC, CS, DG, COLS, CF = T["GC"], T["CS"], T["DG"], T["COLS"], T["CF"]
                      W0, W1, W2, W3 = T["W0"], T["W1"], T["W2"], T["W3"]
                      U0, kU0 = W2, K_("W2")
                      PTB, ATB, AB, RB, RTB = T["PTB"], T["ATB"], T["AB"], T["RB"], T["RTB"]
                      PN = [T["PN0"], T["PN1"]]
                      PTN = [T["PTN0"], T["PTN1"]]
                      KBE, KDEC, VB, QDT = T["KBE"], T["KDEC"], T["VB"], T["QDT"]
                      WTB, kWTB = W3[:, 0:256].bitcast(BF16), K_("W3")
                      ZM, kZM = PN[0], K_("PN0")
                      YM, kYM = PTN[0], K_("PTN0")
                      SQO, kSQO = PN[1], K_("PN1")
                      UB, kUB = PTN[1], K_("PTN1")
                      kqk, rwk, szk = K_("KQ"), K_("RW"), K_("SZ")
                      P.dma("sp", kq[:], kqv_s[b, :, :, c0:c0 + 128], [("kqv", b)], [kqk], ("kq", si))
                      P.dma("sp", rw[:], rows_s[b, :, :, c0:c0 + 128], [("rows", b)], [rwk], ("rw", si))
                      if d == 1 and need_out:
                          P.dma("sp", sz[:], szb_s[b, :, :, c0:c0 + 128], [("szb", b)], [szk], ("sz", si))
                      qT = lambda h: kq[:, h, :]
                      kT = lambda h: kq[:, 4 + h, :]
                      vT = lambda h: kq[:, 8 + h, :]
                      g8 = rw[:, 0, :]
                      b8 = rw[:, 1, :]
                      yield
                      P.op("dve", lambda e: e.tensor_tensor_scan(CS[:], ONES8, g8, 0.0, ALU.mult, ALU.add), [rwk, "SEL"], [K_("CS")])
                      if d == 0:
                          P.op("dve", lambda e: e.tensor_copy(GC[:], CS[:]), [K_("CS")], [K_("GC")])
                      else:
                          P.op("dve", lambda e: e.scalar_tensor_tensor(GC[:], CS[:], -1.0, g8, ALU.mult, ALU.add), [K_("CS"), rwk], [K_("GC")])
                          P.op("dve", lambda e: e.tensor_scalar(GC[:], GC[:], CS[:, 127:128], None, ALU.add), [K_("GC"), K_("CS")], [K_("GC")])
                      P.op("dve", lambda e: e.tensor_scalar(DG[:], I8, CS[:, 127:128], None, ALU.mult), [K_("CS"), "IDF"], [K_("DG")])
                      P.op("pe", lambda e: e.matmul(B[b0][:, 0:8], GC[:], I8, start=True, stop=True), [K_("GC"), "IDF"], [f"B{b0}"], inc=False)
                      P.op("pe", lambda e: e.matmul(B[b0][:, 8:16], b8, I8, start=True, stop=True), [rwk, "IDF"], [f"B{b0}"], inc=False)
                      P.op("pe", lambda e: e.matmul(B[b0][:, 16:24], ONES8, DG[:], start=True, stop=True), ["SEL", K_("DG")], [f"B{b0}"])
                      for h in range(4):
                          P.op("pe", lambda e, h=h: e.transpose(Bbf[b1][:, h * 128:(h + 1) * 128], kT(h), IDB[:]), [kqk, "IDB"], [f"B{b1}"], inc=False)
                      for h in range(4):
                          P.op("pe", lambda e, h=h: e.transpose(Bbf[b1][:, 512 + h * 128:512 + (h + 1) * 128], vT(h), IDB[:]), [kqk, "IDB"], [f"B{b1}"], inc=(h == 3))
                      yield
                      P.op("act", lambda e: e.copy(COLS[:], B[b0][:, 0:24]), [f"B{b0}"], [K_("COLS")])
                      P.op("act", lambda e: e.activation(CF[:, 0:8], COLS[:, 0:8], AF.Exp), [K_("COLS")], [K_("CF")])
                      P.op("dve", lambda e: e.tensor_tensor(CF[:, 8:16], CF[:, 0:8], COLS[:, 8:16], ALU.mult), [K_("CF"), K_("COLS")], [K_("CF")])
                      P.op("dve", lambda e: e.tensor_tensor(CF[:, 16:24], COLS[:, 16:24], COLS[:, 0:8], ALU.subtract), [K_("COLS"), K_("CF")], [K_("CF")])
                      P.op("act", lambda e: e.activation(CF[:, 16:24], CF[:, 16:24], AF.Exp), [K_("CF")], [K_("CF")])
                      P.op("act", lambda e: e.activation(CF[:, 24:32], COLS[:, 16:24], AF.Exp), [K_("COLS"), K_("CF")], [K_("CF")])
                      for h in range(4):
                          dh = d * 4 + h
                          P.op("pe", lambda e, h=h, dh=dh: e.matmul(hs(B[b0], h), SEL[:, dh * 128:(dh + 1) * 128], GC[:], start=True, stop=True), ["SEL", K_("GC")], [f"B{b0}"], inc=(h == 3))
                      yield
                      for h in range(4):
                          dh = d * 4 + h
                          P.op("dve", lambda e, h=h, dh=dh: e.tensor_scalar(hs(KBE[:], h), Bbf[b1][:, h * 128:(h + 1) * 128], CF[:, 8 + dh:9 + dh], None, ALU.mult),
                               [f"B{b1}", K_("CF")], [K_("KBE")])
                      for h in range(4):
                          dh = d * 4 + h
                          P.op("act", lambda e, h=h, dh=dh: e.activation(hs(KDEC[:], h), Bbf[b1][:, h * 128:(h + 1) * 128], AF.Identity, scale=CF[:, 16 + dh:17 + dh]),
                               [f"B{b1}", K_("CF")], [K_("KDEC")])
                          P.op("act", lambda e, h=h, dh=dh: e.activation(hs(VB[:], h), Bbf[b1][:, 512 + h * 128:512 + (h + 1) * 128], AF.Identity, scale=COLS[:, 8 + dh:9 + dh]),
                               [f"B{b1}", K_("COLS")], [K_("VB")])
                      mm4(b1, lambda h: SEL[:, (d * 4 + h) * 128:(d * 4 + h + 1) * 128], lambda h: b8, ["SEL", rwk])
                      yield
                      for h in range(4):
                          dh = d * 4 + h
                          P.op("dve", lambda e, h=h, dh=dh: e.scalar_tensor_tensor(hs(W0[:], h), hs(B[b0], h), COLS[:, dh:dh + 1], hs(NEGI[d], h), ALU.subtract, ALU.add),
                               [f"B{b0}", K_("COLS"), "MASK"], [K_("W0")])
                      P.op("act", lambda e: e.activation(W3[:], B[b0], AF.Exp), [f"B{b0}"], [K_("W3")])
                      P.op("act", lambda e: e.activation(W1[:], W0[:], AF.Exp), [K_("W0")], [K_("W1")])
                      P.op("pool", lambda e: e.tensor_tensor(h4(QDT[:]), kq[:, 0:4, :], h4(W3[:]), ALU.mult), [kqk, K_("W3")], [K_("QDT")])
                      P.op("dve", lambda e: e.tensor_tensor(W2[:], B[b1], SM[d], ALU.mult), [f"B{b1}", "MASK"], [K_("W2")])
                      P.op("pool", lambda e: e.tensor_tensor(W2[:], W2[:], W1[:], ALU.mult), [K_("W2"), K_("W1")], [K_("W2")])
                      mm4(b0, kT, kT, [kqk])
                      mm4(b1, kT, qT, [kqk])
                      yield
                      P.op("dve", lambda e: e.tensor_tensor(ATB[:], B[b0], W2[:], ALU.mult), [f"B{b0}", K_("W2")], [K_("ATB")])
                      P.op("dve", lambda e: e.tensor_tensor(PTB[:], B[b1], W1[:], ALU.mult), [f"B{b1}", K_("W1")], [K_("PTB")])
                      for h in range(4):
                          P.op("pe", lambda e, h=h: e.transpose(Bbf[b0][:, h * 128:(h + 1) * 128], hs(ATB[:], h), IDB[:]), [K_("ATB"), "IDB"], [f"B{b0}"], inc=(h == 3))
                      P.op("pool", lambda e: e.tensor_tensor(PN[0][:], ATB[:], BLK16, ALU.mult), [K_("ATB"), "MASK"], [K_("PN0")])
                      P.op("pool", lambda e: e.tensor_tensor(RTB[:], ID4, PN[0][:], ALU.subtract), ["MASK", K_("PN0")], [K_("RTB")])
                      yield
                      P.op("act", lambda e: e.copy(AB[:], Bbf[b0][:, 0:512]), [f"B{b0}"], [K_("AB")])
                      P.op("dve", lambda e: e.tensor_tensor(PTN[0][:], Bbf[b0][:, 0:512], BLK16, ALU.mult), [f"B{b0}", "MASK"], [K_("PTN0")])
                      P.op("pool", lambda e: e.tensor_tensor(RB[:], ID4, PTN[0][:], ALU.subtract), ["MASK", K_("PTN0")], [K_("RB")])
                      yield
                      cur = 0
                      for it in range(3):
                          nx = 1 - cur
                          kc_, kn_ = (K_(f"PTN{cur}"), K_(f"PN{cur}")), (K_(f"PTN{nx}"), K_(f"PN{nx}"))
                          mm4(b0, lambda h: hs(PTN[cur][:], h), lambda h: hs(PN[cur][:], h), list(kc_))
                          mm4(b1, lambda h: hs(PN[cur][:], h), lambda h: hs(PTN[cur][:], h), list(kc_))
                          yield
                          P.op("act", lambda e, nx=nx: e.copy(PN[nx][:], B[b0]), [f"B{b0}"], [kn_[1]])
                          P.op("dve", lambda e, nx=nx: e.tensor_copy(PTN[nx][:], B[b1]), [f"B{b1}"], [kn_[0]])
                          mm4(b0, lambda h: hs(PTN[nx][:], h), lambda h: hs(RTB[:], h), [kn_[0], K_("RTB")])
                          mm4(b1, lambda h: hs(PN[nx][:], h), lambda h: hs(RB[:], h), [kn_[1], K_("RB")])
                          yield
                          P.op("dve", lambda e: e.tensor_tensor(RTB[:], B[b0], RTB[:], ALU.add), [f"B{b0}", K_("RTB")], [K_("RTB")])
                          P.op("dve", lambda e: e.tensor_tensor(RB[:], B[b1], RB[:], ALU.add), [f"B{b1}", K_("RB")], [K_("RB")])
                          cur = nx
                      for szm in (32, 64, 128):
                          if szm < 128:
                              mm4(b0, lambda h: hs(ATB[:], h), lambda h: hs(RB[:], h), [K_("ATB"), K_("RB")])
                          mm4(b1, lambda h: hs(AB[:], h), lambda h: hs(RTB[:], h), [K_("AB"), K_("RTB")])
                          yield
                          if szm < 128:
                              P.op("dve", lambda e, szm=szm: e.tensor_tensor(ZM[:], B[b0], OFF[szm], ALU.mult), [f"B{b0}", "MASK"], [kZM])
                          P.op("dve", lambda e, szm=szm: e.tensor_tensor(YM[:], B[b1], OFF[szm], ALU.mult), [f"B{b1}", "MASK"], [kYM])
                          if szm < 128:
                              mm4(b0, lambda h: hs(RTB[:], h), lambda h: hs(ZM[:], h), [K_("RTB"), kZM])
                          mm4(b1, lambda h: hs(RB[:], h), lambda h: hs(YM[:], h), [K_("RB"), kYM])
                          yield
                          if szm < 128:
                              P.op("dve", lambda e: e.tensor_tensor(RB[:], RB[:], B[b0], ALU.subtract), [f"B{b0}", K_("RB")], [K_("RB")])
                          P.op("dve", lambda e: e.tensor_tensor(RTB[:], RTB[:], B[b1], ALU.subtract), [f"B{b1}", K_("RTB")], [K_("RTB")])
                      mm4(b0, lambda h: hs(RTB[:], h), lambda h: hs(VB[:], h), [K_("RTB"), K_("VB")])
                      mm4(b1, lambda h: hs(KBE[:], h), lambda h: hs(RTB[:], h), [K_("KBE"), K_("RTB")])
                      yield
                      P.op("act", lambda e: e.copy(U0[:], B[b0]), [f"B{b0}"], [kU0])
                      P.op("act", lambda e: e.copy(WTB, B[b1]), [f"B{b1}"], [kWTB])
                      yield "scan"
                      mm4(SB0, lambda h: hs(WTB, h), lambda h: hs(SBF[:], h), [kWTB, "SBF"])
                      P.op("dve", lambda e: e.tensor_tensor(UB[:], U0[:], B[SB0], ALU.subtract), [kU0, f"B{SB0}"], [kUB])
                      if need_out:
                          for h in range(4):
                              P.op("pe", lambda e, h=h: e.matmul(hs(B[SB1], h), hs(SBF[:], h), hs(QDT[:], h), start=True, stop=False), ["SBF", K_("QDT")], [f"B{SB1}"], inc=False)
                              P.op("pe", lambda e, h=h: e.matmul(hs(B[SB1], h), hs(UB[:], h), hs(PTB[:], h), start=False, stop=True), [kUB, K_("PTB")], [f"B{SB1}"], inc=(h == 3))
                      mm4(SB0, lambda h: hs(KDEC[:], h), lambda h: hs(UB[:], h), [K_("KDEC"), kUB])
                      for h in range(4):
                          dh = d * 4 + h
                          P.op("dve", lambda e, h=h, dh=dh: e.scalar_tensor_tensor(hs(S32[:], h), hs(S32[:], h), CF[:, 24 + dh:25 + dh], hs(B[SB0], h), ALU.mult, ALU.add),
                               ["S32", K_("CF"), f"B{SB0}"], ["S32"])
                      P.op("act", lambda e: e.copy(SBF[:], S32[:]), ["S32"], ["SBF"])
                      if need_out:
                          if d == 0:
                              P.op("act", lambda e: e.copy(OF[:, :, tk:tk + 128], h4(B[SB1])), [f"B{SB1}"], [("OF", tk)])
                          else:
                              P.op("dve", lambda e: e.tensor_tensor(h4(W0[:]), h4(B[SB1]), OF[:, :, tk:tk + 128], ALU.add), [f"B{SB1}", ("OF", tk)], [K_("W0")])
                              P.op("act", lambda e: e.activation(SQO[:], W0[:], AF.Square), [K_("W0")], [kSQO])
                              mm4(SB0, lambda h: ONESB[:], lambda h: hs(SQO[:], h), ["ONESB", kSQO])
                              P.op("act", lambda e: e.activation(W1[:], B[SB0], AF.Ln, bias=EPSC, scale=1.0 / 128), [f"B{SB0}"], [K_("W1")])
                              P.op("act", lambda e: e.activation(W1[:], W1[:], AF.Exp, scale=-0.5), [K_("W1")], [K_("W1")])
                              P.op("pool", lambda e: e.tensor_tensor(W0[:], W0[:], W1[:], ALU.mult), [K_("W0"), K_("W1")], [K_("W0")])
                              P.op("dve", lambda e: e.scalar_tensor_tensor(BIG[:, 4:8, tk:tk + 128], h4(W0[:]), GN, sz[:, :, :], ALU.mult, ALU.mult),
                                   [K_("W0"), "LWg", szk], [("BIG", b)])

                  for d in range(2):
                      P.op("pool", lambda e: e.memset(S32[:], 0.0), [], ["S32"])
                      P.op("pool", lambda e: e.memset(SBF[:], 0.0), [], ["SBF"])
                      border = list(range(NBLK)) if d == 0 else [0] + list(range(8, 0, -1))
                      tasks = []
                      for b in border:
                          t0, nt = blk_range(b)
                          need_out = not (last and b == 0)
                          nch = nt // 128
                          chs = list(range(nch)) if d == 0 else list(range(nch - 1, -1, -1))
                          for ci, ch in enumerate(chs):
                              tasks.append((b, ch, need_out, ci == 0))
                      active = []
                      nxt = 0
                      scan_turn = 0
                      free_slots = list(range(NSLOT))
                      while nxt < len(tasks) or active:
                          if nxt < len(tasks) and free_slots and (not active or len(active) < NSLOT):
                              b, ch, need_out, first = tasks[nxt]
                              if first and d == 1 and need_out:
                                  t0, nt = blk_range(b)
                                  P.dma("sp", BIG[:, 0:4, t0:t0 + nt], ya_s[b, :, :, 0:nt], [("ya", b)], [("BIG", b)], "yal")
                              si = free_slots.pop(0)
                              active.append([chunk_gen(slots[si], d, b, ch, need_out), nxt, False, si])
                              nxt += 1
                          for ent in list(active):
                              g, ti, wscan, si = ent
                              if wscan and ti != scan_turn:
                                  continue
                              try:
                                  r = next(g)
                                  if wscan:
                                      scan_turn += 1
                                      ent[2] = False
                                  if r == "scan":
                                      ent[2] = True
                              except StopIteration:
                                  if wscan:
                                      scan_turn += 1
                                  active.remove(ent)
                                  free_slots.append(si)
                  P.barrier()
              ck("s2")
              with ExitStack() as s4:
                  WOUT = sb("WOUT", [128, 8, D], BF16, s4)
                  XS = sb("XS4", [128, 8, 512], F32, s4)
                  WS4 = sb("WS4", [128, 2, D], F32, s4)
                  SQB = sb("SQB4", [128, 8, 512], BF16, s4)
                  T0 = sb("T04", [128, 512], F32, s4)
                  YP = sb("YP", [128, 8, 512], BF16, s4) if colmaj else None
                  for k in range(8):
                      s = k % 2
                      P.dma("sp", WS4[:, s, :], w_out[l, k * 128:(k + 1) * 128, :], [], [("WS4", s)], ("ws4", s))
                      P.op("dve" if s == 0 else "pool", lambda e, k=k, s=s: e.tensor_copy(WOUT[:, k, :], WS4[:, s, :]), [("WS4", s)], ["WOUT"])
                  pp = 0
                  for nb in range(NBLK):
                      if last and nb == 0:
                          continue
                      t0, nt = blk_range(nb)
                      v = 1 if nb == 0 else 0
                      P.dma("sp", XS[:, :, 0:nt], xsrc_v[:, :, t0:t0 + nt], [("x", nb)], ["XS4"], "xs4")
                      if colmaj and nb > 0:
                          for k in range(8):
                              if k % 2 == 0:
                                  P.op("pool", lambda e, k=k: e.tensor_copy(YP[:, k, :].rearrange("p (a b) -> p a b", b=64), big_view(k, nb, True, True)),
                                       big_keys(nb, True), ["YP"])
                              else:
                                  P.op("act", lambda e, k=k: e.copy(YP[:, k, :].rearrange("p (a b) -> p a b", b=64), big_view(k, nb, True, True)),
                                       big_keys(nb, True), ["YP"])
                      for jn in range(8):
                          bi = (0, 1, 3, 4)[pp % 4]
                          pp += 1
                          for kc in range(8):
                              if colmaj and nb > 0:
                                  P.op("pe", lambda e, kc=kc, jn=jn, bi=bi: e.matmul(
                                      B[bi][:, 0:nt], WOUT[:, kc, jn * 128:(jn + 1) * 128], YP[:, kc, :],
                                      start=(kc == 0), stop=(kc == 7)), ["WOUT", "YP"], [f"B{bi}"], inc=(kc == 7))
                              else:
                                  P.op("pe", lambda e, kc=kc, jn=jn, bi=bi: e.matmul(
                                      B[bi][:, 0:nt], WOUT[:, kc, jn * 128:(jn + 1) * 128], BIG[:, kc, t0:t0 + nt],
                                      start=(kc == 0), stop=(kc == 7)), ["WOUT"] + big_keys(nb, False), [f"B{bi}"], inc=(kc == 7))
                          P.op("dve", lambda e, jn=jn, bi=bi, v=v: e.scalar_tensor_tensor(
                              XS[:, jn, 0:nt], B[bi][:, 0:nt], mcol(v, 16 + jn), XS[:, jn, 0:nt], ALU.mult, ALU.add),
                              [f"B{bi}", "XS4"], [("XSo", jn)])
                      okeys = [("XSo", jn) for jn in range(8)]
                      if not last:
                          P.dma("pool", xres_v[:, :, t0:t0 + nt], XS[:, :, 0:nt], okeys, [("x", nb)], "xst")
                          P.op("dve", lambda e: e.engine_nop(), [("x", nb)], ["XS4"])
                      else:
                          P.op("act", lambda e: e.activation(SQB[:], XS[:], AF.Square), okeys, ["SQB4"])
                          for k in range(8):
                              P.op("pe", lambda e, k=k: e.matmul(B[2], ONESB[:], SQB[:, k, :], start=(k == 0), stop=(k == 7)), ["SQB4", "ONESB"], ["B2"], inc=(k == 7))
                          P.op("act", lambda e: e.activation(T0[:], B[2], AF.Ln, bias=EPSC, scale=1.0 / D), ["B2"], ["T04"])
                          P.op("act", lambda e: e.activation(T0[:], T0[:], AF.Exp, scale=-0.5), ["T04"], ["T04"])
                          for k in range(8):
                              P.op("dve", lambda e, k=k: e.scalar_tensor_tensor(XS[:, k, :], XS[:, k, :], FN[:, k:k + 1], T0[:], ALU.mult, ALU.mult),
                                   [("XSo", k), "T04", "FN", "SQB4"], [("XSf", k)])
                          fk = [("XSf", k) for k in range(8)]
                          P.dma("pool", out_v[:, :, t0 - NCTX:t0 - NCTX + nt], XS[:, :, 0:nt], fk, [("out", nb)], "ost")
                          P.op("dve", lambda e: e.engine_nop(), [("out", nb)], ["XS4"])
                  P.barrier()
          P.barrier()

    except _Stop:
        pass
    return nc


def _consts():
    i = np.arange(128)
    t, s = i[None, :], i[:, None]
    negi_f = np.where(t >= s, 0.0, BIGNEG)
    negi_b = np.where(t <= s, 0.0, BIGNEG)
    sm_f = (t > s).astype(np.float64)
    sm_b = (t < s).astype(np.float64)
    blk = lambda z: ((s // z) == (t // z)).astype(np.float64)
    blk16 = blk(16)
    off = {z: blk(z) * (1 - blk(z // 2)) for z in (32, 64, 128)}
    ident = np.eye(128)
    ms = [negi_f, negi_b, sm_f, sm_b, blk16, off[32], off[64], off[128], ident]
    cmask = np.concatenate([np.tile(m, (1, 4)) for m in ms], axis=1).astype(np.float32)
    sel = np.zeros((8, 2 * 1024 + 128), np.float32)
    for dh in range(8):
        sel[dh, dh * 128:(dh + 1) * 128] = 1.0
        sel[dh, 1024 + dh * 128:1024 + (dh + 1) * 128] = -1.0
    sel[:, 2048:] = 1.0
    return cmask, ident.astype(np.float32), sel


_NC_CACHE = {}


def make_in_maps(x, c, ctx, c_ctx, norm_w, w_mod, b_mod, w_in, conv_a, conv_qkv, a_log, dt_bias, gdn_norm, w_out, final_norm):
    f = lambda a: np.ascontiguousarray(np.asarray(a, dtype=np.float32))
    x, c, ctx, c_ctx = f(x), f(c), f(ctx), f(c_ctx)
    cmask, ident, sel = _consts()
    L = norm_w.shape[0]
    col = lambda a: np.ascontiguousarray(a.reshape(-1, 128).T)
    shared = {
        "w_mod": f(w_mod), "w_in": f(w_in), "w_out": f(w_out),
        "bmod": np.stack([col(f(b_mod)[l]) for l in range(L)]),
        "normw": np.stack([col(f(norm_w)[l]) for l in range(L)]),
        "conva": np.stack([np.ascontiguousarray(f(conv_a)[l].T.reshape(4, 128, 3).transpose(1, 0, 2).reshape(128, 12)) for l in range(L)]),
        "convq": np.stack([np.ascontiguousarray(f(conv_qkv)[l].T.reshape(12, 128, 3).transpose(1, 0, 2).reshape(128, 36)) for l in range(L)]),
        "alog": np.ascontiguousarray(f(a_log).reshape(L, 8, 1)),
        "dtb": np.ascontiguousarray(f(dt_bias).reshape(L, 8, 1)),
        "gnorm": np.ascontiguousarray(f(gdn_norm).reshape(L, 128, 1)),
        "fnorm": col(f(final_norm)),
        "cmask": cmask, "cident": ident, "csel": sel,
    }
    maps = []
    for b in range(x.shape[0]):
        m = dict(shared)
        m["xT"] = np.ascontiguousarray(np.concatenate([ctx[b], x[b]], axis=0).T)
        ccb = np.stack([col(c[b]), col(c_ctx)], axis=-1).reshape(128, 16)
        m["cc"] = np.ascontiguousarray(ccb)
        maps.append(m)
    return maps


def kernel(x, c, ctx, c_ctx, norm_w, w_mod, b_mod, w_in, conv_a, conv_qkv, a_log, dt_bias, gdn_norm, w_out, final_norm, _nlayers=DEPTH):
    maps = make_in_maps(x, c, ctx, c_ctx, norm_w, w_mod, b_mod, w_in, conv_a, conv_qkv, a_log, dt_bias, gdn_norm, w_out, final_norm)
    if _nlayers not in _NC_CACHE:
        _NC_CACHE[_nlayers] = build(_nlayers)
    nc = _NC_CACHE[_nlayers]
    res = run_bass_kernel_spmd(nc, maps, core_ids=list(range(len(maps))))
    out = np.stack([np.ascontiguousarray(r["outT"].T) for r in res.results], axis=0)
    return out.astype(np.float32)
```

```python
import numpy as np
from contextlib import ExitStack
import concourse.bass as bass
import concourse.mybir as mybir
from concourse.bass_utils import run_bass_kernel_spmd

F32 = mybir.dt.float32
BF16 = mybir.dt.bfloat16
AF = mybir.ActivationFunctionType
ALU = mybir.AluOpType

D = 1024
NCTX = 256
NLAT = 4096
NTOK = NCTX + NLAT
DPROJ = 4112
DEPTH = 4
EPS = 1e-6
NBLK = 9
BIGNEG = -30000.0
PSUM_KEYS = {f"B{i}" for i in range(8)}


def blk_range(b):
    if b == 0:
        return 0, NCTX
    return NCTX + (b - 1) * 512, 512


class _Stop(Exception):
    pass


class Prog:
    def __init__(self, nc, es):
        self.nc = nc
        self.es = es
        self.eng = {"pe": nc.tensor, "act": nc.scalar, "dve": nc.vector, "pool": nc.gpsimd, "sp": nc.sync}
        self.sem = {}
        self.cnt = {}
        self.epoch = 0
        self.state = {}
        self.waited = {}
        self.dsem = {}
        self.dcnt = {}
        self.all_sems = {}
        self.nops = 0
        self.stop_at = None
        self.new_epoch()

    def new_epoch(self):
        self.epoch += 1
        for e in ("pe", "act", "dve", "pool"):
            s = self.es.enter_context(self.nc.semaphore(f"s_{e}_{self.epoch}"))
            self.sem[e] = s
            self.cnt[e] = 0
            self.all_sems[id(s)] = s

    def _collect(self, engine, reads, writes):
        need = {}

        def add(ev):
            s, v, e = ev
            k = id(s)
            if k not in need or need[k][1] < v:
                need[k] = (s, v, e)

        for k in reads:
            st = self.state.get(k)
            if st is not None:
                for ev in st["w"].values():
                    add(ev)
                if k in PSUM_KEYS:
                    for ev in st["r"].values():
                        if ev[2] != engine:
                            add(ev)
        for k in writes:
            st = self.state.get(k)
            if st is not None:
                for ev in st["w"].values():
                    if ev[2] != engine:
                        add(ev)
                for ev in st["r"].values():
                    if ev[2] != engine:
                        add(ev)
        out = []
        wd = self.waited.setdefault(engine, {})
        for k, (s, v, e) in need.items():
            if wd.get(k, 0) >= v:
                continue
            wd[k] = v
            out.append((s, v))
        return out

    def _record(self, ev, reads, writes):
        for k in reads:
            st = self.state.setdefault(k, {"w": {}, "r": {}})
            st["r"][id(ev[0])] = ev
        for k in writes:
            st = self.state.setdefault(k, {"w": {}, "r": {}})
            st["w"][id(ev[0])] = ev

    def op(self, engine, fn, reads=(), writes=(), inc=True):
        e = self.eng[engine]
        for s, v in self._collect(engine, reads, writes):
            e.wait_ge(s, v)
        inst = fn(e)
        if inc:
            self.cnt[engine] += 1
            inst.then_inc(self.sem[engine], 1)
            self._record((self.sem[engine], self.cnt[engine], engine), reads, writes)
        else:
            self._record((self.sem[engine], self.cnt[engine] + 1, engine), reads, writes)
        self.nops += 1
        if self.stop_at is not None and self.nops == self.stop_at:
            self.barrier()
            raise _Stop()

    def dma(self, queue, out, in_, reads, writes, slot):
        e = self.eng[queue]
        for s, v in self._collect("q_" + queue, reads, writes):
            e.wait_ge(s, v)
        if slot not in self.dsem:
            self.dsem[slot] = self.es.enter_context(self.nc.semaphore(f"d_{len(self.dsem)}"))
            self.dcnt[slot] = 0
        self.dcnt[slot] += 16
        e.dma_start(out=out, in_=in_).then_inc(self.dsem[slot], 16)
        self._record((self.dsem[slot], self.dcnt[slot], "dma_" + str(slot)), reads, writes)

    def barrier(self):
        evs = [(self.sem[e], self.cnt[e]) for e in ("pe", "act", "dve", "pool") if self.cnt[e] > 0]
        evs += [(self.dsem[s], self.dcnt[s]) for s in self.dsem]
        for en in ("pe", "act", "dve", "pool", "sp"):
            key = en if en != "sp" else "q_sp"
            wd = self.waited.setdefault(key, {})
            for s, v in evs:
                if en in self.sem and s is self.sem.get(en):
                    continue
                if wd.get(id(s), 0) >= v:
                    continue
                wd[id(s)] = v
                self.eng[en].wait_ge(s, v)
        wd = self.waited.setdefault("q_pool", {})
        for s, v in evs:
            wd[id(s)] = max(wd.get(id(s), 0), v)
        self.state = {}


def build(nlayers=DEPTH, stop=None):
    def ck(name):
        if stop == name:
            print("ck", name, "nops", P.nops)
            raise _Stop()
    nc = bass.Bass("TRN2", target_bir_lowering=False)
    dt_in = lambda n, shp, dt=F32: nc.dram_tensor(n, list(shp), dt, kind="ExternalInput").ap()
    xT = dt_in("xT", [D, NTOK])
    cc = dt_in("cc", [128, 16])
    w_mod = dt_in("w_mod", [DEPTH, D, 3 * D])
    bmod = dt_in("bmod", [DEPTH, 128, 24])
    normw = dt_in("normw", [DEPTH, 128, 8])
    w_in = dt_in("w_in", [DEPTH, D, DPROJ])
    conva = dt_in("conva", [DEPTH, 128, 12])
    convq = dt_in("convq", [DEPTH, 128, 36])
    alog = dt_in("alog", [DEPTH, 8, 1])
    dtb = dt_in("dtb", [DEPTH, 8, 1])
    gnorm = dt_in("gnorm", [DEPTH, 128, 1])
    w_out = dt_in("w_out", [DEPTH, D, D])
    fnorm = dt_in("fnorm", [128, 8])
    cmask = dt_in("cmask", [128, 9 * 512])
    cident = dt_in("cident", [128, 128])
    csel = dt_in("csel", [8, 2 * 1024 + 128])
    outT = nc.dram_tensor("outT", [D, NLAT], F32, kind="ExternalOutput").ap()
    xres = nc.dram_tensor("xres", [D, NTOK], F32).ap()
    kqv_s = nc.dram_tensor("kqv_s", [NBLK, 128, 12, 512], BF16).ap()
    ya_s = nc.dram_tensor("ya_s", [NBLK, 128, 4, 512], BF16).ap()
    szb_s = nc.dram_tensor("szb_s", [NBLK, 128, 4, 512], BF16).ap()
    rows_s = nc.dram_tensor("rows_s", [NBLK, 8, 2, 512], F32).ap()

    es = ExitStack()
    try:
      with es:
          P = Prog(nc, es)
          if isinstance(stop, int):
              P.stop_at = stop
          _uniq = [0]

          def sb(n, shp, dt=F32, st=es):
              _uniq[0] += 1
              return st.enter_context(nc.sbuf_tensor(f"{n}_{_uniq[0]}", list(shp), dt))
          BIG = sb("BIG", [128, 8, NTOK], BF16)
          MASK = sb("MASK", [128, 9, 512], BF16)
          IDF = sb("IDF", [128, 128], F32)
          IDB = sb("IDB", [128, 128], BF16)
          ONESB = sb("ONESB", [128, 128], BF16)
          SEL = sb("SEL", [8, 2 * 1024 + 128], F32)
          MODS = sb("MODS", [128, DEPTH, 2, 24], F32)
          CC = sb("CC", [128, 16], F32)
          FN = sb("FN", [128, 8], F32)
          LW = sb("LW", [128, 64], F32)
          L8 = sb("L8", [8, 4], F32)
          DUM = sb("DUM", [128, 2], F32)
          EPST = sb("EPST", [128, 1], F32)
          EPSC = EPST[:, 0:1]
          banks = [es.enter_context(nc.psum_tensor(f"B{i}", [128, 512], F32)) for i in range(8)]
          B = [b[:] for b in banks]
          Bbf = [b[:].bitcast(BF16) for b in banks]

          NEGI = [MASK[:, 0, :], MASK[:, 1, :]]
          SM = [MASK[:, 2, :], MASK[:, 3, :]]
          BLK16 = MASK[:, 4, :]
          OFF = {32: MASK[:, 5, :], 64: MASK[:, 6, :], 128: MASK[:, 7, :]}
          ID4 = MASK[:, 8, :]
          I8 = IDF[0:8, 0:8]
          ONES8 = SEL[:, 2048:2176]

          def h4(ap):
              return ap.rearrange("p (h t) -> p h t", h=4)

          with ExitStack() as st0:
              MST = sb("MST", [128, 9 * 512], F32, st0)
              WST = sb("WST", [128, 2, 4096], F32, st0)
              SC = sb("SC", [128, 16], F32, st0)
              BM = sb("BM", [128, 24], F32, st0)
              NW = sb("NW", [128, 8], F32, st0)
              P.dma("sp", MST[:], cmask, [], ["MST"], "c0")
              P.dma("sp", IDF[:], cident, [], ["IDF"], "c1")
              P.dma("sp", SEL[:], csel, [], ["SEL"], "c2")
              P.dma("sp", CC[:], cc, [], ["CC"], "c3")
              P.dma("sp", FN[:], fnorm, [], ["FN"], "c4")
              P.op("dve", lambda e: e.tensor_copy(MASK[:].rearrange("p a b -> p (a b)"), MST[:]), ["MST"], ["MASK"])
              P.op("dve", lambda e: e.tensor_copy(IDB[:], IDF[:]), ["IDF"], ["IDB"])
              P.op("dve", lambda e: e.memset(ONESB[:], 1.0), [], ["ONESB"])
              P.op("dve", lambda e: e.memset(DUM[:], 0.0), [], ["DUM"])
              P.op("dve", lambda e: e.memset(EPST[:], EPS), [], ["EPST"])
              P.op("act", lambda e: e.activation(SC[:], CC[:], AF.Silu), ["CC"], ["SC"])
              for l in range(nlayers):
                  P.dma("sp", BM[:], bmod[l], [], ["BM"], "c5")
                  P.dma("sp", NW[:], normw[l], [], ["NW"], "c6")
                  for jg in range(6):
                      s = jg % 2
                      P.dma("sp", WST[:, s, :].rearrange("p (k n) -> p k n", k=8),
                            w_mod[l, :, jg * 512:(jg + 1) * 512].rearrange("(k p) n -> p k n", p=128), [], [("WST", s)], ("wst", s))
                      for jj in range(4):
                          j = jg * 4 + jj
                          for k in range(8):
                              P.op("pe", lambda e, j=j, jj=jj, k=k, s=s: e.matmul(
                                  B[0][:, 2 * j:2 * j + 2], WST[:, s, k * 512 + jj * 128:k * 512 + (jj + 1) * 128],
                                  SC[:].rearrange("p (k v) -> p k v", v=2)[:, k, :], start=(k == 0), stop=(k == 7)),
                                  [("WST", s), "SC"], ["B0"])
                  for v in range(2):
                      P.op("dve", lambda e, v=v, l=l: e.tensor_tensor(
                          MODS[:, l, v, :], B[0][:, 0:48].rearrange("p (j v) -> p j v", v=2)[:, :, v], BM[:], ALU.add),
                          ["B0", "BM"], [("MODS", l)])
                      P.op("dve", lambda e, v=v, l=l: e.scalar_tensor_tensor(
                          MODS[:, l, v, 8:16], MODS[:, l, v, 8:16], 1.0, NW[:], ALU.add, ALU.mult),
                          [("MODS", l), "NW"], [("MODS", l)])
              P.barrier()
          ck("s0")

          for l in range(nlayers):
              P.new_epoch()
              colmaj = (l % 2 == 1)
              last = (l == nlayers - 1)
              xsrc = xT if l == 0 else xres
              xsrc_v = xsrc.rearrange("(k p) t -> p k t", p=128)
              xres_v = xres.rearrange("(k p) t -> p k t", p=128)
              out_v = outT.rearrange("(k p) t -> p k t", p=128)

              def mcol(v, j, l=l):
                  return MODS[:, l, v, j:j + 1]

              P.dma("sp", LW[:, 0:12], conva[l], [], ["LWa"], "c7")
              P.dma("sp", LW[:, 12:48], convq[l], [], ["LWq"], "c8")
              P.dma("sp", LW[:, 48:49], gnorm[l], [], ["LWg"], "c9")
              P.dma("sp", L8[:, 0:1], alog[l], [], ["L8a"], "c10")
              P.dma("sp", L8[:, 1:2], dtb[l], [], ["L8"], "c11")
              P.op("act", lambda e: e.activation(L8[:, 2:3], L8[:, 0:1], AF.Exp), ["L8a"], ["L8"])
              P.op("dve", lambda e: e.tensor_scalar(L8[:, 2:3], L8[:, 2:3], -1.0, None, ALU.mult), ["L8"], ["L8"])
              CA = lambda j, tap: LW[:, j * 3 + tap: j * 3 + tap + 1]
              CQ = lambda j, tap: LW[:, 12 + j * 3 + tap: 12 + j * 3 + tap + 1]
              GN = LW[:, 48:49]

              def big_keys(b, permuted):
                  if b == 0 or not permuted:
                      return [("BIG", b)]
                  return [("BIG", i) for i in range(1, 9)]

              def big_view(kc, b, permuted, rowmajor_of_colscan):
                  t0, nt = blk_range(b)
                  if b == 0 or not permuted:
                      return BIG[:, kc, t0:t0 + nt]
                  lat = BIG[:, kc, NCTX:NTOK]
                  if rowmajor_of_colscan:
                      v = lat.rearrange("p (c r) -> p r c", r=64)
                  else:
                      v = lat.rearrange("p (r c) -> p c r", c=64)
                  return v[:, (b - 1) * 8:(b - 1) * 8 + 8, :]

              def pview(ap, b, permuted):
                  t0, nt = blk_range(b)
                  if b == 0 or not permuted:
                      return ap[:, 0:nt]
                  return ap.rearrange("p (a b) -> p a b", b=64)

              with ExitStack() as s1:
                  WIN = sb("WIN", [128, 8, DPROJ], BF16, s1)
                  XSF = sb("XS", [128, 4112], F32, s1)
                  XS = XSF[:, 0:4096].rearrange("p (k t) -> p k t", k=8)
                  T0 = sb("T0", [128, 512], F32, s1)
                  T1 = sb("T1", [128, 512], F32, s1)
                  T2 = sb("T2", [128, 512], F32, s1)
                  SQQ = sb("SQQ", [128, 512], BF16, s1)
                  KQVB = sb("KQVB", [128, 12, 512], BF16, s1)
                  SQB = KQVB[:, 0:8, :]
                  YAB = sb("YAB", [128, 4, 512], BF16, s1)
                  SZBB = sb("SZBB", [128, 4, 512], BF16, s1)
                  ROWB = sb("ROWB", [8, 2, 512], F32, s1)
                  HP = sb("HP", [128, 8, 512], BF16, s1) if colmaj else None
                  XSW = XSF[:]
                  i = 0
                  for k in range(8):
                      for hf in range(2):
                          s = i % 2
                          c0 = hf * 2056
                          P.dma("sp", XSW[:, s * 2056:(s + 1) * 2056], w_in[l, k * 128:(k + 1) * 128, c0:c0 + 2056],
                                [], [("XS", s)], ("xs", s))
                          eng = ("dve", "pool", "act")[i % 3]
                          if eng == "act":
                              P.op("act", lambda e, k=k, s=s, c0=c0: e.copy(WIN[:, k, c0:c0 + 2056], XSW[:, s * 2056:(s + 1) * 2056]),
                                   [("XS", s)], ["WIN"])
                          else:
                              P.op(eng, lambda e, k=k, s=s, c0=c0: e.tensor_copy(WIN[:, k, c0:c0 + 2056], XSW[:, s * 2056:(s + 1) * 2056]),
                                   [("XS", s)], ["WIN"])
                          i += 1
                  if stop == "s1w":
                      P.barrier()
                      ck("s1w")
                  for nb in range(NBLK):
                      t0, nt = blk_range(nb)
                      v = 1 if nb == 0 else 0
                      P.dma("sp", XS[:, :, 0:nt], xsrc_v[:, :, t0:t0 + nt], [("x", nb)], [("XS", 0), ("XS", 1)], ("xs", 0))
                      P.op("act", lambda e, nt=nt: e.activation(SQB[:, :, 0:nt], XS[:, :, 0:nt], AF.Square),
                           [("XS", 0), ("XS", 1)], ["KQVB"])
                      for k in range(8):
                          P.op("pe", lambda e, k=k, nt=nt: e.matmul(B[2][:, 0:nt], ONESB[:], SQB[:, k, 0:nt], start=(k == 0), stop=(k == 7)),
                               ["KQVB", "ONESB"], ["B2"], inc=(k == 7))
                      P.op("act", lambda e, nt=nt: e.activation(T0[:, 0:nt], B[2][:, 0:nt], AF.Ln, bias=EPSC, scale=1.0 / D), ["B2"], ["T0"])
                      P.op("act", lambda e, nt=nt: e.activation(T0[:, 0:nt], T0[:, 0:nt], AF.Exp, scale=-0.5), ["T0"], ["T0"])
                      for k in range(8):
                          eng = "dve" if k % 2 == 0 else "pool"
                          P.op(eng, lambda e, k=k, nt=nt: e.tensor_tensor(XS[:, k, 0:nt], XS[:, k, 0:nt], T0[:, 0:nt], ALU.mult),
                               [("XS", 0), ("XS", 1), "T0", "KQVB"], [("XSn", k)])
                          P.op("act", lambda e, k=k, nt=nt, t0=t0, v=v: e.activation(
                              BIG[:, k, t0:t0 + nt], XS[:, k, 0:nt], AF.Identity, bias=mcol(v, k), scale=mcol(v, 8 + k)),
                              [("XSn", k)], [("BIG", nb)])
                      P.op("act", lambda e: e.activation(DUM[:, 0:1], DUM[:, 1:2], AF.Copy), [("BIG", nb)] + [("XSn", k) for k in range(8)], [("XS", 0), ("XS", 1)])

                  if stop == "s1p":
                      P.barrier()
                      ck("s1p")
                  P.barrier()
                  TS = [(T0, T1, T2, SQQ, "T0", "T1", "T2", "SQQ", 2)]
                  for i_ in range(2):
                      o_ = i_ * 1792
                      TS.append((XSF[:, o_:o_ + 512], XSF[:, o_ + 512:o_ + 1024], XSF[:, o_ + 1024:o_ + 1536],
                                 XSF[:, o_ + 1536:o_ + 1792].bitcast(BF16), f"T0_{i_}", f"T1_{i_}", f"T2_{i_}", f"SQQ_{i_}", 7 if i_ == 0 else 2))
                  tsi = [0]
                  pp = [0]
                  PROJ_BANKS = [0, 1, 3, 4, 5, 6]

                  def proj(b, c0, m):
                      bi = PROJ_BANKS[pp[0] % len(PROJ_BANKS)]
                      pp[0] += 1
                      t0, nt = blk_range(b)
                      for k in range(8):
                          if colmaj and b > 0:
                              P.op("pe", lambda e, k=k, bi=bi: e.matmul(
                                  B[bi][0:m, 0:nt], WIN[:, k, c0:c0 + m], HP[:, k, :],
                                  start=(k == 0), stop=(k == 7)), ["WIN", "HP"], [f"B{bi}"], inc=(k == 7))
                          else:
                              P.op("pe", lambda e, k=k, bi=bi: e.matmul(
                                  B[bi][0:m, 0:nt], WIN[:, k, c0:c0 + m], BIG[:, k, t0:t0 + nt],
                                  start=(k == 0), stop=(k == 7)), ["WIN"] + big_keys(b, False), [f"B{bi}"], inc=(k == 7))
                      return bi

                  def conv(dst, dkey, src, skey, w, nt, seg):
                      P.op("dve", lambda e: e.tensor_scalar(dst[:, 0:nt], src[:, 0:nt], w(1), None, ALU.mult), [skey, "LWa", "LWq"], [dkey])
                      dv = dst[:, 0:nt].rearrange("p (a b) -> p a b", b=seg)
                      sv = src[:, 0:nt].rearrange("p (a b) -> p a b", b=seg)
                      P.op("dve", lambda e: e.scalar_tensor_tensor(dv[:, :, 1:seg], sv[:, :, 0:seg - 1], w(0), dv[:, :, 1:seg], ALU.mult, ALU.add),
                           [skey, dkey, "LWa", "LWq"], [dkey])
                      P.op("dve", lambda e: e.scalar_tensor_tensor(dv[:, :, 0:seg - 1], sv[:, :, 1:seg], w(2), dv[:, :, 0:seg - 1], ALU.mult, ALU.add),
                           [skey, dkey, "LWa", "LWq"], [dkey])

                  def run_tasks(gens_factories, nsets):
                      pending = list(gens_factories)
                      active = []
                      free = list(range(nsets))
                      while pending or active:
                          if pending and free:
                              si = free.pop(0)
                              active.append((pending.pop(0)(TS[si]), si))
                          for ent in list(active):
                              try:
                                  next(ent[0])
                              except StopIteration:
                                  active.remove(ent)
                                  free.append(ent[1])

                  for b in range(NBLK):
                      t0, nt = blk_range(b)
                      seg = NCTX if b == 0 else 64
                      if colmaj and b > 0:
                          for k in range(8):
                              eng = "pool" if k % 2 == 0 else "act"
                              if eng == "pool":
                                  P.op("pool", lambda e, k=k: e.tensor_copy(HP[:, k, :].rearrange("p (a b) -> p a b", b=64), big_view(k, b, True, False)),
                                       big_keys(b, True), ["HP"])
                              else:
                                  P.op("act", lambda e, k=k: e.copy(HP[:, k, :].rearrange("p (a b) -> p a b", b=64), big_view(k, b, True, False)),
                                       big_keys(b, True), ["HP"])

                      def mixer_task(jj, b=b, nt=nt, seg=seg):
                          def g(ts):
                              T0, T1, T2, SQQ, k0, k1, k2, kq_, ssb = ts
                              bi = proj(b, jj * 128, 128)
                              yield
                              P.op("act", lambda e: e.copy(T0[:, 0:nt], B[bi][:, 0:nt]), [f"B{bi}"], [k0])
                              bi = proj(b, 1024 + jj * 128, 128)
                              yield
                              P.op("dve", lambda e: e.tensor_tensor(T0[:, 0:nt], B[bi][:, 0:nt], T0[:, 0:nt], ALU.mult), [f"B{bi}", k0], [k0])
                              conv(T1, k1, T0, k0, lambda tap: CA(jj, tap), nt, seg)
                              bi = proj(b, 512 + jj * 128, 128)
                              yield
                              P.op("dve", lambda e: e.tensor_tensor(T1[:, 0:nt], B[bi][:, 0:nt], T1[:, 0:nt], ALU.mult), [f"B{bi}", k1], [k1])
                              bi = proj(b, 1536 + jj * 128, 128)
                              yield
                              P.op("act", lambda e: e.activation(T2[:, 0:nt], B[bi][:, 0:nt], AF.Silu), [f"B{bi}"], [k2])
                              yield
                              P.op("pool", lambda e: e.tensor_tensor(YAB[:, jj, 0:nt], T1[:, 0:nt], T2[:, 0:nt], ALU.mult), [k1, k2], ["YAB"])
                          return g

                      def qkv_task(idx, b=b, nt=nt, seg=seg):
                          def g(ts):
                              T0, T1, T2, SQQ, k0, k1, k2, kq_, ssb = ts
                              bi = proj(b, 2048 + idx * 128, 128)
                              yield
                              conv(T1, k1, B[bi], f"B{bi}", lambda tap: CQ(idx, tap), nt, seg)
                              yield
                              if idx >= 8:
                                  P.op("act", lambda e: e.activation(KQVB[:, idx, 0:nt], T1[:, 0:nt], AF.Silu), [k1], ["KQVB"])
                                  return
                              P.op("act", lambda e: e.activation(T2[:, 0:nt], T1[:, 0:nt], AF.Silu), [k1], [k2])
                              P.op("act", lambda e: e.activation(SQQ[:, 0:nt], T2[:, 0:nt], AF.Square), [k2], [kq_])
                              yield
                              P.op("pe", lambda e: e.matmul(B[ssb][:, 0:nt], ONESB[:], SQQ[:, 0:nt], start=True, stop=True), [kq_, "ONESB"], [f"B{ssb}"])
                              yield
                              P.op("act", lambda e: e.activation(T0[:, 0:nt], B[ssb][:, 0:nt], AF.Ln, bias=EPSC, scale=1.0), [f"B{ssb}"], [k0])
                              P.op("act", lambda e: e.activation(T0[:, 0:nt], T0[:, 0:nt], AF.Exp, scale=-0.5), [k0], [k0])
                              yield
                              sc = (128.0 ** -0.5) if idx < 4 else 1.0
                              P.op("dve", lambda e: e.scalar_tensor_tensor(KQVB[:, idx, 0:nt], T2[:, 0:nt], sc, T0[:, 0:nt], ALU.mult, ALU.mult),
                                   [k2, k0], ["KQVB"])
                          return g

                      def zb_task(h, b=b, nt=nt):
                          def g(ts):
                              bi = proj(b, 3584 + h * 128, 128)
                              yield
                              P.op("act", lambda e: e.activation(SZBB[:, h, 0:nt], B[bi][:, 0:nt], AF.Silu), [f"B{bi}"], ["SZBB"])
                          return g

                      def rows_task(b=b, nt=nt):
                          def g(ts):
                              bi = proj(b, 4096, 8)
                              bi2 = proj(b, 4104, 8)
                              yield
                              P.op("act", lambda e: e.activation(ROWB[:, 1, 0:nt], B[bi][0:8, 0:nt], AF.Sigmoid), [f"B{bi}"], ["ROWB"])
                              P.op("act", lambda e: e.activation(ROWB[:, 0, 0:nt], B[bi2][0:8, 0:nt], AF.Exp, bias=L8[:, 1:2]), [f"B{bi2}", "L8"], ["ROWB"])
                              P.op("act", lambda e: e.activation(ROWB[:, 0, 0:nt], ROWB[:, 0, 0:nt], AF.Ln, bias=1.0), ["ROWB"], ["ROWB"])
                              yield
                              P.op("dve", lambda e: e.tensor_scalar(ROWB[:, 0, 0:nt], ROWB[:, 0, 0:nt], L8[:, 2:3], None, ALU.mult), ["ROWB", "L8"], ["ROWB"])
                          return g

                      tasks = [mixer_task(jj) for jj in range(4)] + [qkv_task(i) for i in range(12)] + [zb_task(h) for h in range(4)] + [rows_task()]
                      run_tasks(tasks, 3)
                      P.dma("pool", kqv_s[b, :, :, 0:nt], KQVB[:, :, 0:nt], ["KQVB"], [("kqv", b)], "sp0")
                      P.dma("pool", ya_s[b, :, :, 0:nt], YAB[:, :, 0:nt], ["YAB"], [("ya", b)], "sp1")
                      P.dma("pool", szb_s[b, :, :, 0:nt], SZBB[:, :, 0:nt], ["SZBB"], [("szb", b)], "sp2")
                      P.dma("pool", rows_s[b, :, :, 0:nt], ROWB[:, :, 0:nt], ["ROWB"], [("rows", b)], "sp3")
                  P.barrier()

              ck("s1")
              with ExitStack() as s2:
                  OF = sb("OF", [128, 4, NTOK], BF16, s2)
                  S32 = sb("S32", [128, 512], F32, s2)
                  SBF = sb("SBF", [128, 512], BF16, s2)
                  NSLOT = 3
                  SB0, SB1 = 6, 7

                  def mk_slot(i):
                      t = {}
                      t["KQ"] = sb(f"KQ{i}", [128, 12, 128], BF16, s2)
                      t["RW"] = sb(f"RW{i}", [8, 2, 128], F32, s2)
                      t["SZ"] = sb(f"SZ{i}", [128, 4, 128], BF16, s2)
                      for n_ in ("GC", "CS"):
                          t[n_] = sb(f"{n_}{i}", [8, 128], F32, s2)
                      t["DG"] = sb(f"DG{i}", [8, 8], F32, s2)
                      t["COLS"] = sb(f"COLS{i}", [128, 24], F32, s2)
                      t["CF"] = sb(f"CF{i}", [128, 32], F32, s2)
                      for n_ in ("W0", "W1", "W2", "W3"):
                          t[n_] = sb(f"{n_}{i}", [128, 512], F32, s2)
                      for n_ in ("PTB", "ATB", "AB", "PN0", "PN1", "PTN0", "PTN1", "RB", "RTB", "KBE", "KDEC", "VB",
                                 "QDT"):
                          t[n_] = sb(f"{n_}{i}", [128, 512], BF16, s2)
                      t["banks"] = [2 * i, 2 * i + 1]
                      t["i"] = i
                      return t

                  slots = [mk_slot(i) for i in range(NSLOT)]

                  def hs(ap, h):
                      return ap[:, h * 128:(h + 1) * 128]

                  def mm4(bank, lhs, rhs, rk, start=True, stop=True):
                      for h in range(4):
                          P.op("pe", lambda e, h=h: e.matmul(hs(B[bank], h), lhs(h), rhs(h), start=start, stop=stop), rk, [f"B{bank}"], inc=(h == 3))

                  def chunk_gen(T, d, b, ch, need_out):
                      si = T["i"]
                      K_ = lambda n_: (n_, si)
                      b0, b1 = T["banks"]
                      t0, nt = blk_range(b)
                      c0 = ch * 128
                      tk = t0 + c0
                      kq, rw, sz = T["KQ"], T["RW"], T["SZ"]
                      GC, CS, DG, COLS, CF = T["GC"], T["CS"], T["DG"], T["COLS"], T["CF"]
                      W0, W1, W2, W3 = T["W0"], T["W1"], T["W2"], T["W3"]
                      U0, kU0 = W2, K_("W2")
                      PTB, ATB, AB, RB, RTB = T["PTB"], T["ATB"], T["AB"], T["RB"], T["RTB"]
                      PN = [T["PN0"], T["PN1"]]
                      PTN = [T["PTN0"], T["PTN1"]]
                      KBE, KDEC, VB, QDT = T["KBE"], T["KDEC"], T["VB"], T["QDT"]
                      WTB, kWTB = W3[:, 0:256].bitcast(BF16), K_("W3")
                      ZM, kZM = PN[0], K_("PN0")
                      YM, kYM = PTN[0], K_("PTN0")
                      SQO, kSQO = PN[1], K_("PN1")
                      UB, kUB = PTN[1], K_("PTN1")
                      kqk, rwk, szk = K_("KQ"), K_("RW"), K_("SZ")
                      P.dma("sp", kq[:], kqv_s[b, :, :, c0:c0 + 128], [("kqv", b)], [kqk], ("kq", si))
                      P.dma("sp", rw[:], rows_s[b, :, :, c0:c0 + 128], [("rows", b)], [rwk], ("rw", si))
                      if d == 1 and need_out:
                          P.dma("sp", sz[:], szb_s[b, :, :, c0:c0 + 128], [("szb", b)], [szk], ("sz", si))
                      qT = lambda h: kq[:, h, :]
                      kT = lambda h: kq[:, 4 + h, :]
                      vT = lambda h: kq[:, 8 + h, :]
                      g8 = rw[:, 0, :]
                      b8 = rw[:, 1, :]
                      yield
                      P.op("dve", lambda e: e.tensor_tensor_scan(CS[:], ONES8, g8, 0.0, ALU.mult, ALU.add), [rwk, "SEL"], [K_("CS")])
                      if d == 0:
                          P.op("dve", lambda e: e.tensor_copy(GC[:], CS[:]), [K_("CS")], [K_("GC")])
                      else:
                          P.op("dve", lambda e: e.scalar_tensor_tensor(GC[:], CS[:], -1.0, g8, ALU.mult, ALU.add), [K_("CS"), rwk], [K_("GC")])
                          P.op("dve", lambda e: e.tensor_scalar(GC[:], GC[:], CS[:, 127:128], None, ALU.add), [K_("GC"), K_("CS")], [K_("GC")])
                      P.op("dve", lambda e: e.tensor_scalar(DG[:], I8, CS[:, 127:128], None, ALU.mult), [K_("CS"), "IDF"], [K_("DG")])
                      P.op("pe", lambda e: e.matmul(B[b0][:, 0:8], GC[:], I8, start=True, stop=True), [K_("GC"), "IDF"], [f"B{b0}"], inc=False)
                      P.op("pe", lambda e: e.matmul(B[b0][:, 8:16], b8, I8, start=True, stop=True), [rwk, "IDF"], [f"B{b0}"], inc=False)
                      P.op("pe", lambda e: e.matmul(B[b0][:, 16:24], ONES8, DG[:], start=True, stop=True), ["SEL", K_("DG")], [f"B{b0}"])
                      for h in range(4):
                          P.op("pe", lambda e, h=h: e.transpose(Bbf[b1][:, h * 128:(h + 1) * 128], kT(h), IDB[:]), [kqk, "IDB"], [f"B{b1}"], inc=False)
                      for h in range(4):
                          P.op("pe", lambda e, h=h: e.transpose(Bbf[b1][:, 512 + h * 128:512 + (h + 1) * 128], vT(h), IDB[:]), [kqk, "IDB"], [f"B{b1}"], inc=(h == 3))
                      yield
                      P.op("act", lambda e: e.copy(COLS[:], B[b0][:, 0:24]), [f"B{b0}"], [K_("COLS")])
                      P.op("act", lambda e: e.activation(CF[:, 0:8], COLS[:, 0:8], AF.Exp), [K_("COLS")], [K_("CF")])
                      P.op("dve", lambda e: e.tensor_tensor(CF[:, 8:16], CF[:, 0:8], COLS[:, 8:16], ALU.mult), [K_("CF"), K_("COLS")], [K_("CF")])
                      P.op("dve", lambda e: e.tensor_tensor(CF[:, 16:24], COLS[:, 16:24], COLS[:, 0:8], ALU.subtract), [K_("COLS"), K_("CF")], [K_("CF")])
                      P.op("act", lambda e: e.activation(CF[:, 16:24], CF[:, 16:24], AF.Exp), [K_("CF")], [K_("CF")])
                      P.op("act", lambda e: e.activation(CF[:, 24:32], COLS[:, 16:24], AF.Exp), [K_("COLS"), K_("CF")], [K_("CF")])
                      for h in range(4):
                          dh = d * 4 + h
                          P.op("pe", lambda e, h=h, dh=dh: e.matmul(hs(B[b0], h), SEL[:, dh * 128:(dh + 1) * 128], GC[:], start=True, stop=True), ["SEL", K_("GC")], [f"B{b0}"], inc=(h == 3))
                      yield
                      for h in range(4):
                          dh = d * 4 + h
                          P.op("dve", lambda e, h=h, dh=dh: e.tensor_scalar(hs(KBE[:], h), Bbf[b1][:, h * 128:(h + 1) * 128], CF[:, 8 + dh:9 + dh], None, ALU.mult),
                               [f"B{b1}", K_("CF")], [K_("KBE")])
                      for h in range(4):
                          dh = d * 4 + h
                          P.op("act", lambda e, h=h, dh=dh: e.activation(hs(KDEC[:], h), Bbf[b1][:, h * 128:(h + 1) * 128], AF.Identity, scale=CF[:, 16 + dh:17 + dh]),
                               [f"B{b1}", K_("CF")], [K_("KDEC")])
                          P.op("act", lambda e, h=h, dh=dh: e.activation(hs(VB[:], h), Bbf[b1][:, 512 + h * 128:512 + (h + 1) * 128], AF.Identity, scale=COLS[:, 8 + dh:9 + dh]),
                               [f"B{b1}", K_("COLS")], [K_("VB")])
                      mm4(b1, lambda h: SEL[:, (d * 4 + h) * 128:(d * 4 + h + 1) * 128], lambda h: b8, ["SEL", rwk])
                      yield
                      for h in range(4):
                          dh = d * 4 + h
                          P.op("dve", lambda e, h=h, dh=dh: e.scalar_tensor_tensor(hs(W0[:], h), hs(B[b0], h), COLS[:, dh:dh + 1], hs(NEGI[d], h), ALU.subtract, ALU.add),
                               [f"B{b0}", K_("COLS"), "MASK"], [K_("W0")])
                      P.op("act", lambda e: e.activation(W3[:], B[b0], AF.Exp), [f"B{b0}"], [K_("W3")])
                      P.op("act", lambda e: e.activation(W1[:], W0[:], AF.Exp), [K_("W0")], [K_("W1")])
                      P.op("pool", lambda e: e.tensor_tensor(h4(QDT[:]), kq[:, 0:4, :], h4(W3[:]), ALU.mult), [kqk, K_("W3")], [K_("QDT")])
                      P.op("dve", lambda e: e.tensor_tensor(W2[:], B[b1], SM[d], ALU.mult), [f"B{b1}", "MASK"], [K_("W2")])
                      P.op("pool", lambda e: e.tensor_tensor(W2[:], W2[:], W1[:], ALU.mult), [K_("W2"), K_("W1")], [K_("W2")])
                      mm4(b0, kT, kT, [kqk])
                      mm4(b1, kT, qT, [kqk])
                      yield
                      P.op("dve", lambda e: e.tensor_tensor(ATB[:], B[b0], W2[:], ALU.mult), [f"B{b0}", K_("W2")], [K_("ATB")])
                      P.op("dve", lambda e: e.tensor_tensor(PTB[:], B[b1], W1[:], ALU.mult), [f"B{b1}", K_("W1")], [K_("PTB")])
                      for h in range(4):
                          P.op("pe", lambda e, h=h: e.transpose(Bbf[b0][:, h * 128:(h + 1) * 128], hs(ATB[:], h), IDB[:]), [K_("ATB"), "IDB"], [f"B{b0}"], inc=(h == 3))
                      P.op("pool", lambda e: e.tensor_tensor(PN[0][:], ATB[:], BLK16, ALU.mult), [K_("ATB"), "MASK"], [K_("PN0")])
                      P.op("pool", lambda e: e.tensor_tensor(RTB[:], ID4, PN[0][:], ALU.subtract), ["MASK", K_("PN0")], [K_("RTB")])
                      yield
                      P.op("act", lambda e: e.copy(AB[:], Bbf[b0][:, 0:512]), [f"B{b0}"], [K_("AB")])
                      P.op("dve", lambda e: e.tensor_tensor(PTN[0][:], Bbf[b0][:, 0:512], BLK16, ALU.mult), [f"B{b0}", "MASK"], [K_("PTN0")])
                      P.op("pool", lambda e: e.tensor_tensor(RB[:], ID4, PTN[0][:], ALU.subtract), ["MASK", K_("PTN0")], [K_("RB")])
                      yield
                      cur = 0
                      for it in range(3):
                          nx = 1 - cur
                          kc_, kn_ = (K_(f"PTN{cur}"), K_(f"PN{cur}")), (K_(f"PTN{nx}"), K_(f"PN{nx}"))
                          mm4(b0, lambda h: hs(PTN[cur][:], h), lambda h: hs(PN[cur][:], h), list(kc_))
                          mm4(b1, lambda h: hs(PN[cur][:], h), lambda h: hs(PTN[cur][:], h), list(kc_))
                          yield
                          P.op("act", lambda e, nx=nx: e.copy(PN[nx][:], B[b0]), [f"B{b0}"], [kn_[1]])
                          P.op("act", lambda e, nx=nx: e.copy(PTN[nx][:], B[b1]), [f"B{b1}"], [kn_[0]])
                          mm4(b0, lambda h: hs(PTN[nx][:], h), lambda h: hs(RTB[:], h), [kn_[0], K_("RTB")])
                          mm4(b1, lambda h: hs(PN[nx][:], h), lambda h: hs(RB[:], h), [kn_[1], K_("RB")])
                          yield
                          P.op("dve", lambda e: e.tensor_tensor(RTB[:], B[b0], RTB[:], ALU.add), [f"B{b0}", K_("RTB")], [K_("RTB")])
                          P.op("act", lambda e: e.copy(W0[:], B[b1]), [f"B{b1}"], [K_("W0")])
                          P.op("pool", lambda e: e.tensor_tensor(RB[:], W0[:], RB[:], ALU.add), [K_("W0"), K_("RB")], [K_("RB")])
                          cur = nx
                      for szm in (32, 64, 128):
                          if szm < 128:
                              mm4(b0, lambda h: hs(ATB[:], h), lambda h: hs(RB[:], h), [K_("ATB"), K_("RB")])
                          mm4(b1, lambda h: hs(AB[:], h), lambda h: hs(RTB[:], h), [K_("AB"), K_("RTB")])
                          yield
                          if szm < 128:
                              P.op("dve", lambda e, szm=szm: e.tensor_tensor(ZM[:], B[b0], OFF[szm], ALU.mult), [f"B{b0}", "MASK"], [kZM])
                          P.op("dve", lambda e, szm=szm: e.tensor_tensor(YM[:], B[b1], OFF[szm], ALU.mult), [f"B{b1}", "MASK"], [kYM])
                          if szm < 128:
                              mm4(b0, lambda h: hs(RTB[:], h), lambda h: hs(ZM[:], h), [K_("RTB"), kZM])
                          mm4(b1, lambda h: hs(RB[:], h), lambda h: hs(YM[:], h), [K_("RB"), kYM])
                          yield
                          if szm < 128:
                              P.op("act", lambda e: e.copy(W0[:], B[b0]), [f"B{b0}"], [K_("W0")])
                              P.op("pool", lambda e: e.tensor_tensor(RB[:], RB[:], W0[:], ALU.subtract), [K_("W0"), K_("RB")], [K_("RB")])
                          P.op("dve", lambda e: e.tensor_tensor(RTB[:], RTB[:], B[b1], ALU.subtract), [f"B{b1}", K_("RTB")], [K_("RTB")])
                      mm4(b0, lambda h: hs(RTB[:], h), lambda h: hs(VB[:], h), [K_("RTB"), K_("VB")])
                      mm4(b1, lambda h: hs(KBE[:], h), lambda h: hs(RTB[:], h), [K_("KBE"), K_("RTB")])
                      yield
                      P.op("act", lambda e: e.copy(U0[:], B[b0]), [f"B{b0}"], [kU0])
                      P.op("act", lambda e: e.copy(WTB, B[b1]), [f"B{b1}"], [kWTB])
                      yield "scan"
                      mm4(SB0, lambda h: hs(WTB, h), lambda h: hs(SBF[:], h), [kWTB, "SBF"])
                      P.op("dve", lambda e: e.tensor_tensor(UB[:], U0[:], B[SB0], ALU.subtract), [kU0, f"B{SB0}"], [kUB])
                      if need_out:
                          for h in range(4):
                              P.op("pe", lambda e, h=h: e.matmul(hs(B[SB1], h), hs(SBF[:], h), hs(QDT[:], h), start=True, stop=False), ["SBF", K_("QDT")], [f"B{SB1}"], inc=False)
                              P.op("pe", lambda e, h=h: e.matmul(hs(B[SB1], h), hs(UB[:], h), hs(PTB[:], h), start=False, stop=True), [kUB, K_("PTB")], [f"B{SB1}"], inc=(h == 3))
                      mm4(SB0, lambda h: hs(KDEC[:], h), lambda h: hs(UB[:], h), [K_("KDEC"), kUB])
                      for h in range(4):
                          dh = d * 4 + h
                          P.op("dve", lambda e, h=h, dh=dh: e.scalar_tensor_tensor(hs(S32[:], h), hs(S32[:], h), CF[:, 24 + dh:25 + dh], hs(B[SB0], h), ALU.mult, ALU.add),
                               ["S32", K_("CF"), f"B{SB0}"], ["S32"])
                      P.op("act", lambda e: e.copy(SBF[:], S32[:]), ["S32"], ["SBF"])
                      if need_out:
                          if d == 0:
                              P.op("act", lambda e: e.copy(OF[:, :, tk:tk + 128], h4(B[SB1])), [f"B{SB1}"], [("OF", tk)])
                          else:
                              P.op("dve", lambda e: e.tensor_tensor(h4(W0[:]), h4(B[SB1]), OF[:, :, tk:tk + 128], ALU.add), [f"B{SB1}", ("OF", tk)], [K_("W0")])
                              P.op("act", lambda e: e.activation(SQO[:], W0[:], AF.Square), [K_("W0")], [kSQO])
                              mm4(SB0, lambda h: ONESB[:], lambda h: hs(SQO[:], h), ["ONESB", kSQO])
                              P.op("act", lambda e: e.activation(W1[:], B[SB0], AF.Ln, bias=EPSC, scale=1.0 / 128), [f"B{SB0}"], [K_("W1")])
                              P.op("act", lambda e: e.activation(W1[:], W1[:], AF.Exp, scale=-0.5), [K_("W1")], [K_("W1")])
                              P.op("pool", lambda e: e.tensor_tensor(W0[:], W0[:], W1[:], ALU.mult), [K_("W0"), K_("W1")], [K_("W0")])
                              P.op("dve", lambda e: e.scalar_tensor_tensor(BIG[:, 4:8, tk:tk + 128], h4(W0[:]), GN, sz[:, :, :], ALU.mult, ALU.mult),
                                   [K_("W0"), "LWg", szk], [("BIG", b)])

                  for d in range(2):
                      P.op("pool", lambda e: e.memset(S32[:], 0.0), [], ["S32"])
                      P.op("pool", lambda e: e.memset(SBF[:], 0.0), [], ["SBF"])
                      border = list(range(NBLK)) if d == 0 else [0] + list(range(8, 0, -1))
                      tasks = []
                      for b in border:
                          t0, nt = blk_range(b)
                          need_out = not (last and b == 0)
                          nch = nt // 128
                          chs = list(range(nch)) if d == 0 else list(range(nch - 1, -1, -1))
                          for ci, ch in enumerate(chs):
                              tasks.append((b, ch, need_out, ci == 0))
                      active = []
                      nxt = 0
                      scan_turn = 0
                      free_slots = list(range(NSLOT))
                      while nxt < len(tasks) or active:
                          if nxt < len(tasks) and free_slots and (not active or len(active) < NSLOT):
                              b, ch, need_out, first = tasks[nxt]
                              if first and d == 1 and need_out:
                                  t0, nt = blk_range(b)
                                  P.dma("sp", BIG[:, 0:4, t0:t0 + nt], ya_s[b, :, :, 0:nt], [("ya", b)], [("BIG", b)], "yal")
                              si = free_slots.pop(0)
                              active.append([chunk_gen(slots[si], d, b, ch, need_out), nxt, False, si])
                              nxt += 1
                          for ent in list(active):
                              g, ti, wscan, si = ent
                              if wscan and ti != scan_turn:
                                  continue
                              try:
                                  r = next(g)
                                  if wscan:
                                      scan_turn += 1
                                      ent[2] = False
                                  if r == "scan":
                                      ent[2] = True
                              except StopIteration:
                                  if wscan:
                                      scan_turn += 1
                                  active.remove(ent)
                                  free_slots.append(si)
                  P.barrier()
              ck("s2")
              with ExitStack() as s4:
                  WOUT = sb("WOUT", [128, 8, D], BF16, s4)
                  XS = sb("XS4", [128, 8, 512], F32, s4)
                  WS4 = sb("WS4", [128, 2, D], F32, s4)
                  SQB = sb("SQB4", [128, 8, 512], BF16, s4)
                  T0 = sb("T04", [128, 512], F32, s4)
                  YP = sb("YP", [128, 8, 512], BF16, s4) if colmaj else None
                  for k in range(8):
                      s = k % 2
                      P.dma("sp", WS4[:, s, :], w_out[l, k * 128:(k + 1) * 128, :], [], [("WS4", s)], ("ws4", s))
                      P.op("dve" if s == 0 else "pool", lambda e, k=k, s=s: e.tensor_copy(WOUT[:, k, :], WS4[:, s, :]), [("WS4", s)], ["WOUT"])
                  pp = 0
                  for nb in range(NBLK):
                      if last and nb == 0:
                          continue
                      t0, nt = blk_range(nb)
                      v = 1 if nb == 0 else 0
                      P.dma("sp", XS[:, :, 0:nt], xsrc_v[:, :, t0:t0 + nt], [("x", nb)], ["XS4"], "xs4")
                      if colmaj and nb > 0:
                          for k in range(8):
                              if k % 2 == 0:
                                  P.op("pool", lambda e, k=k: e.tensor_copy(YP[:, k, :].rearrange("p (a b) -> p a b", b=64), big_view(k, nb, True, True)),
                                       big_keys(nb, True), ["YP"])
                              else:
                                  P.op("act", lambda e, k=k: e.copy(YP[:, k, :].rearrange("p (a b) -> p a b", b=64), big_view(k, nb, True, True)),
                                       big_keys(nb, True), ["YP"])
                      for jn in range(8):
                          bi = (0, 1, 3, 4)[pp % 4]
                          pp += 1
                          for kc in range(8):
                              if colmaj and nb > 0:
                                  P.op("pe", lambda e, kc=kc, jn=jn, bi=bi: e.matmul(
                                      B[bi][:, 0:nt], WOUT[:, kc, jn * 128:(jn + 1) * 128], YP[:, kc, :],
                                      start=(kc == 0), stop=(kc == 7)), ["WOUT", "YP"], [f"B{bi}"], inc=(kc == 7))
                              else:
                                  P.op("pe", lambda e, kc=kc, jn=jn, bi=bi: e.matmul(
                                      B[bi][:, 0:nt], WOUT[:, kc, jn * 128:(jn + 1) * 128], BIG[:, kc, t0:t0 + nt],
                                      start=(kc == 0), stop=(kc == 7)), ["WOUT"] + big_keys(nb, False), [f"B{bi}"], inc=(kc == 7))
                          P.op("dve", lambda e, jn=jn, bi=bi, v=v: e.scalar_tensor_tensor(
                              XS[:, jn, 0:nt], B[bi][:, 0:nt], mcol(v, 16 + jn), XS[:, jn, 0:nt], ALU.mult, ALU.add),
                              [f"B{bi}", "XS4"], [("XSo", jn)])
                      okeys = [("XSo", jn) for jn in range(8)]
                      if not last:
                          P.dma("pool", xres_v[:, :, t0:t0 + nt], XS[:, :, 0:nt], okeys, [("x", nb)], "xst")
                          P.op("dve", lambda e: e.engine_nop(), [("x", nb)], ["XS4"])
                      else:
                          P.op("act", lambda e: e.activation(SQB[:], XS[:], AF.Square), okeys, ["SQB4"])
                          for k in range(8):
                              P.op("pe", lambda e, k=k: e.matmul(B[2], ONESB[:], SQB[:, k, :], start=(k == 0), stop=(k == 7)), ["SQB4", "ONESB"], ["B2"], inc=(k == 7))
                          P.op("act", lambda e: e.activation(T0[:], B[2], AF.Ln, bias=EPSC, scale=1.0 / D), ["B2"], ["T04"])
                          P.op("act", lambda e: e.activation(T0[:], T0[:], AF.Exp, scale=-0.5), ["T04"], ["T04"])
                          for k in range(8):
                              P.op("dve", lambda e, k=k: e.scalar_tensor_tensor(XS[:, k, :], XS[:, k, :], FN[:, k:k + 1], T0[:], ALU.mult, ALU.mult),
                                   [("XSo", k), "T04", "FN", "SQB4"], [("XSf", k)])
                          fk = [("XSf", k) for k in range(8)]
                          P.dma("pool", out_v[:, :, t0 - NCTX:t0 - NCTX + nt], XS[:, :, 0:nt], fk, [("out", nb)], "ost")
                          P.op("dve", lambda e: e.engine_nop(), [("out", nb)], ["XS4"])
                  P.barrier()
          P.barrier()

    except _Stop:
        pass
    return nc


def _consts():
    i = np.arange(128)
    t, s = i[None, :], i[:, None]
    negi_f = np.where(t >= s, 0.0, BIGNEG)
    negi_b = np.where(t <= s, 0.0, BIGNEG)
    sm_f = (t > s).astype(np.float64)
    sm_b = (t < s).astype(np.float64)
    blk = lambda z: ((s // z) == (t // z)).astype(np.float64)
    blk16 = blk(16)
    off = {z: blk(z) * (1 - blk(z // 2)) for z in (32, 64, 128)}
    ident = np.eye(128)
    ms = [negi_f, negi_b, sm_f, sm_b, blk16, off[32], off[64], off[128], ident]
    cmask = np.concatenate([np.tile(m, (1, 4)) for m in ms], axis=1).astype(np.float32)
    sel = np.zeros((8, 2 * 1024 + 128), np.float32)
    for dh in range(8):
        sel[dh, dh * 128:(dh + 1) * 128] = 1.0
        sel[dh, 1024 + dh * 128:1024 + (dh + 1) * 128] = -1.0
    sel[:, 2048:] = 1.0
    return cmask, ident.astype(np.float32), sel


_NC_CACHE = {}


def make_in_maps(x, c, ctx, c_ctx, norm_w, w_mod, b_mod, w_in, conv_a, conv_qkv, a_log, dt_bias, gdn_norm, w_out, final_norm):
    f = lambda a: np.ascontiguousarray(np.asarray(a, dtype=np.float32))
    x, c, ctx, c_ctx = f(x), f(c), f(ctx), f(c_ctx)
    cmask, ident, sel = _consts()
    L = norm_w.shape[0]
    col = lambda a: np.ascontiguousarray(a.reshape(-1, 128).T)
    shared = {
        "w_mod": f(w_mod), "w_in": f(w_in), "w_out": f(w_out),
        "bmod": np.stack([col(f(b_mod)[l]) for l in range(L)]),
        "normw": np.stack([col(f(norm_w)[l]) for l in range(L)]),
        "conva": np.stack([np.ascontiguousarray(f(conv_a)[l].T.reshape(4, 128, 3).transpose(1, 0, 2).reshape(128, 12)) for l in range(L)]),
        "convq": np.stack([np.ascontiguousarray(f(conv_qkv)[l].T.reshape(12, 128, 3).transpose(1, 0, 2).reshape(128, 36)) for l in range(L)]),
        "alog": np.ascontiguousarray(f(a_log).reshape(L, 8, 1)),
        "dtb": np.ascontiguousarray(f(dt_bias).reshape(L, 8, 1)),
        "gnorm": np.ascontiguousarray(f(gdn_norm).reshape(L, 128, 1)),
        "fnorm": col(f(final_norm)),
        "cmask": cmask, "cident": ident, "csel": sel,
    }
    maps = []
    for b in range(x.shape[0]):
        m = dict(shared)
        m["xT"] = np.ascontiguousarray(np.concatenate([ctx[b], x[b]], axis=0).T)
        ccb = np.stack([col(c[b]), col(c_ctx)], axis=-1).reshape(128, 16)
        m["cc"] = np.ascontiguousarray(ccb)
        maps.append(m)
    return maps


def kernel(x, c, ctx, c_ctx, norm_w, w_mod, b_mod, w_in, conv_a, conv_qkv, a_log, dt_bias, gdn_norm, w_out, final_norm, _nlayers=DEPTH):
    maps = make_in_maps(x, c, ctx, c_ctx, norm_w, w_mod, b_mod, w_in, conv_a, conv_qkv, a_log, dt_bias, gdn_norm, w_out, final_norm)
    if _nlayers not in _NC_CACHE:
        _NC_CACHE[_nlayers] = build(_nlayers)
    nc = _NC_CACHE[_nlayers]
    res = run_bass_kernel_spmd(nc, maps, core_ids=list(range(len(maps))))
    out = np.stack([np.ascontiguousarray(r["outT"].T) for r in res.results], axis=0)
    return out.astype(np.float32)
```

```python
import numpy as np
from contextlib import ExitStack
import concourse.bass as bass
import concourse.mybir as mybir
from concourse.bass_utils import run_bass_kernel_spmd

F32 = mybir.dt.float32
BF16 = mybir.dt.bfloat16
AF = mybir.ActivationFunctionType
ALU = mybir.AluOpType

D = 1024
NCTX = 256
NLAT = 4096
NTOK = NCTX + NLAT
DPROJ = 4112
DEPTH = 4
EPS = 1e-6
NBLK = 9
BIGNEG = -30000.0
PSUM_KEYS = {f"B{i}" for i in range(8)}


def blk_range(b):
    if b == 0:
        return 0, NCTX
    return NCTX + (b - 1) * 512, 512


class _Stop(Exception):
    pass


class Prog:
    def __init__(self, nc, es):
        self.nc = nc
        self.es = es
        self.eng = {"pe": nc.tensor, "act": nc.scalar, "dve": nc.vector, "pool": nc.gpsimd, "sp": nc.sync}
        self.sem = {}
        self.cnt = {}
        self.epoch = 0
        self.state = {}
        self.waited = {}
        self.dsem = {}
        self.dcnt = {}
        self.all_sems = {}
        self.nops = 0
        self.stop_at = None
        self.new_epoch()

    def new_epoch(self):
        self.epoch += 1
        for e in ("pe", "act", "dve", "pool"):
            s = self.es.enter_context(self.nc.semaphore(f"s_{e}_{self.epoch}"))
            self.sem[e] = s
            self.cnt[e] = 0
            self.all_sems[id(s)] = s

    def _collect(self, engine, reads, writes):
        need = {}

        def add(ev):
            s, v, e = ev
            k = id(s)
            if k not in need or need[k][1] < v:
                need[k] = (s, v, e)

        for k in reads:
            st = self.state.get(k)
            if st is not None:
                for ev in st["w"].values():
                    add(ev)
                if k in PSUM_KEYS:
                    for ev in st["r"].values():
                        if ev[2] != engine:
                            add(ev)
        for k in writes:
            st = self.state.get(k)
            if st is not None:
                for ev in st["w"].values():
                    if ev[2] != engine:
                        add(ev)
                for ev in st["r"].values():
                    if ev[2] != engine:
                        add(ev)
        out = []
        wd = self.waited.setdefault(engine, {})
        for k, (s, v, e) in need.items():
            if wd.get(k, 0) >= v:
                continue
            wd[k] = v
            out.append((s, v))
        return out

    def _record(self, ev, reads, writes):
        for k in reads:
            st = self.state.setdefault(k, {"w": {}, "r": {}})
            st["r"][id(ev[0])] = ev
        for k in writes:
            st = self.state.setdefault(k, {"w": {}, "r": {}})
            st["w"][id(ev[0])] = ev

    def op(self, engine, fn, reads=(), writes=(), inc=True):
        e = self.eng[engine]
        for s, v in self._collect(engine, reads, writes):
            e.wait_ge(s, v)
        inst = fn(e)
        if inc:
            self.cnt[engine] += 1
            inst.then_inc(self.sem[engine], 1)
            self._record((self.sem[engine], self.cnt[engine], engine), reads, writes)
        else:
            self._record((self.sem[engine], self.cnt[engine] + 1, engine), reads, writes)
        self.nops += 1
        if self.stop_at is not None and self.nops == self.stop_at:
            self.barrier()
            raise _Stop()

    def dma(self, queue, out, in_, reads, writes, slot):
        e = self.eng[queue]
        for s, v in self._collect("q_" + queue, reads, writes):
            e.wait_ge(s, v)
        if slot not in self.dsem:
            self.dsem[slot] = self.es.enter_context(self.nc.semaphore(f"d_{len(self.dsem)}"))
            self.dcnt[slot] = 0
        self.dcnt[slot] += 16
        e.dma_start(out=out, in_=in_).then_inc(self.dsem[slot], 16)
        self._record((self.dsem[slot], self.dcnt[slot], "dma_" + str(slot)), reads, writes)

    def barrier(self):
        evs = [(self.sem[e], self.cnt[e]) for e in ("pe", "act", "dve", "pool") if self.cnt[e] > 0]
        evs += [(self.dsem[s], self.dcnt[s]) for s in self.dsem]
        for en in ("pe", "act", "dve", "pool", "sp"):
            key = en if en != "sp" else "q_sp"
            wd = self.waited.setdefault(key, {})
            for s, v in evs:
                if en in self.sem and s is self.sem.get(en):
                    continue
                if wd.get(id(s), 0) >= v:
                    continue
                wd[id(s)] = v
                self.eng[en].wait_ge(s, v)
        wd = self.waited.setdefault("q_pool", {})
        for s, v in evs:
            wd[id(s)] = max(wd.get(id(s), 0), v)
        self.state = {}


def build(nlayers=DEPTH, stop=None):
    def ck(name):
        if stop == name:
            print("ck", name, "nops", P.nops)
            raise _Stop()
    nc = bass.Bass("TRN2", target_bir_lowering=False)
    dt_in = lambda n, shp, dt=F32: nc.dram_tensor(n, list(shp), dt, kind="ExternalInput").ap()
    xT = dt_in("xT", [D, NTOK])
    cc = dt_in("cc", [128, 16])
    w_mod = dt_in("w_mod", [DEPTH, D, 3 * D])
    bmod = dt_in("bmod", [DEPTH, 128, 24])
    normw = dt_in("normw", [DEPTH, 128, 8])
    w_in = dt_in("w_in", [DEPTH, D, DPROJ])
    conva = dt_in("conva", [DEPTH, 128, 12])
    convq = dt_in("convq", [DEPTH, 128, 36])
    alog = dt_in("alog", [DEPTH, 8, 1])
    dtb = dt_in("dtb", [DEPTH, 8, 1])
    gnorm = dt_in("gnorm", [DEPTH, 128, 1])
    w_out = dt_in("w_out", [DEPTH, D, D])
    fnorm = dt_in("fnorm", [128, 8])
    cmask = dt_in("cmask", [128, 9 * 512])
    cident = dt_in("cident", [128, 128])
    csel = dt_in("csel", [8, 2 * 1024 + 128])
    outT = nc.dram_tensor("outT", [D, NLAT], F32, kind="ExternalOutput").ap()
    xres = nc.dram_tensor("xres", [D, NTOK], F32).ap()
    kqv_s = nc.dram_tensor("kqv_s", [NBLK, 128, 12, 512], BF16).ap()
    ya_s = nc.dram_tensor("ya_s", [NBLK, 128, 4, 512], BF16).ap()
    szb_s = nc.dram_tensor("szb_s", [NBLK, 128, 4, 512], BF16).ap()
    rows_s = nc.dram_tensor("rows_s", [NBLK, 8, 2, 512], F32).ap()
    of_s = nc.dram_tensor("of_s", [NTOK // 128, 128, 4, 128], BF16).ap()

    es = ExitStack()
    try:
      with es:
          P = Prog(nc, es)
          if isinstance(stop, int):
              P.stop_at = stop
          _uniq = [0]

          def sb(n, shp, dt=F32, st=es):
              _uniq[0] += 1
              return st.enter_context(nc.sbuf_tensor(f"{n}_{_uniq[0]}", list(shp), dt))
          BIG = sb("BIG", [128, 8, NTOK], BF16)
          MASK = sb("MASK", [128, 9, 512], BF16)
          IDF = sb("IDF", [128, 128], F32)
          IDB = sb("IDB", [128, 128], BF16)
          ONESB = sb("ONESB", [128, 128], BF16)
          SEL = sb("SEL", [8, 2 * 1024 + 128], F32)
          MODS = sb("MODS", [128, DEPTH, 2, 24], F32)
          CC = sb("CC", [128, 16], F32)
          FN = sb("FN", [128, 8], F32)
          LW = sb("LW", [128, 64], F32)
          L8 = sb("L8", [8, 4], F32)
          DUM = sb("DUM", [128, 2], F32)
          EPST = sb("EPST", [128, 1], F32)
          EPSC = EPST[:, 0:1]
          banks = [es.enter_context(nc.psum_tensor(f"B{i}", [128, 512], F32)) for i in range(8)]
          B = [b[:] for b in banks]
          Bbf = [b[:].bitcast(BF16) for b in banks]

          NEGI = [MASK[:, 0, :], MASK[:, 1, :]]
          SM = [MASK[:, 2, :], MASK[:, 3, :]]
          BLK16 = MASK[:, 4, :]
          OFF = {32: MASK[:, 5, :], 64: MASK[:, 6, :], 128: MASK[:, 7, :]}
          ID4 = MASK[:, 8, :]
          I8 = IDF[0:8, 0:8]
          ONES8 = SEL[:, 2048:2176]

          def h4(ap):
              return ap.rearrange("p (h t) -> p h t", h=4)

          with ExitStack() as st0:
              MST = sb("MST", [128, 9 * 512], F32, st0)
              WST = sb("WST", [128, 2, 4096], F32, st0)
              SC = sb("SC", [128, 16], F32, st0)
              BM = sb("BM", [128, 24], F32, st0)
              NW = sb("NW", [128, 8], F32, st0)
              P.dma("sp", MST[:], cmask, [], ["MST"], "c0")
              P.dma("sp", IDF[:], cident, [], ["IDF"], "c1")
              P.dma("sp", SEL[:], csel, [], ["SEL"], "c2")
              P.dma("sp", CC[:], cc, [], ["CC"], "c3")
              P.dma("sp", FN[:], fnorm, [], ["FN"], "c4")
              P.op("dve", lambda e: e.tensor_copy(MASK[:].rearrange("p a b -> p (a b)"), MST[:]), ["MST"], ["MASK"])
              P.op("dve", lambda e: e.tensor_copy(IDB[:], IDF[:]), ["IDF"], ["IDB"])
              P.op("dve", lambda e: e.memset(ONESB[:], 1.0), [], ["ONESB"])
              P.op("dve", lambda e: e.memset(DUM[:], 0.0), [], ["DUM"])
              P.op("dve", lambda e: e.memset(EPST[:], EPS), [], ["EPST"])
              P.op("act", lambda e: e.activation(SC[:], CC[:], AF.Silu), ["CC"], ["SC"])
              for l in range(nlayers):
                  P.dma("sp", BM[:], bmod[l], [], ["BM"], "c5")
                  P.dma("sp", NW[:], normw[l], [], ["NW"], "c6")
                  for jg in range(6):
                      s = jg % 2
                      P.dma("sp", WST[:, s, :].rearrange("p (k n) -> p k n", k=8),
                            w_mod[l, :, jg * 512:(jg + 1) * 512].rearrange("(k p) n -> p k n", p=128), [], [("WST", s)], ("wst", s))
                      for jj in range(4):
                          j = jg * 4 + jj
                          for k in range(8):
                              P.op("pe", lambda e, j=j, jj=jj, k=k, s=s: e.matmul(
                                  B[0][:, 2 * j:2 * j + 2], WST[:, s, k * 512 + jj * 128:k * 512 + (jj + 1) * 128],
                                  SC[:].rearrange("p (k v) -> p k v", v=2)[:, k, :], start=(k == 0), stop=(k == 7)),
                                  [("WST", s), "SC"], ["B0"])
                  for v in range(2):
                      P.op("dve", lambda e, v=v, l=l: e.tensor_tensor(
                          MODS[:, l, v, :], B[0][:, 0:48].rearrange("p (j v) -> p j v", v=2)[:, :, v], BM[:], ALU.add),
                          ["B0", "BM"], [("MODS", l)])
                      P.op("dve", lambda e, v=v, l=l: e.scalar_tensor_tensor(
                          MODS[:, l, v, 8:16], MODS[:, l, v, 8:16], 1.0, NW[:], ALU.add, ALU.mult),
                          [("MODS", l), "NW"], [("MODS", l)])
              P.barrier()
          ck("s0")

          for l in range(nlayers):
              P.new_epoch()
              colmaj = (l % 2 == 1)
              last = (l == nlayers - 1)
              xsrc = xT if l == 0 else xres
              xsrc_v = xsrc.rearrange("(k p) t -> p k t", p=128)
              xres_v = xres.rearrange("(k p) t -> p k t", p=128)
              out_v = outT.rearrange("(k p) t -> p k t", p=128)

              def mcol(v, j, l=l):
                  return MODS[:, l, v, j:j + 1]

              P.dma("sp", LW[:, 0:12], conva[l], [], ["LWa"], "c7")
              P.dma("sp", LW[:, 12:48], convq[l], [], ["LWq"], "c8")
              P.dma("sp", LW[:, 48:49], gnorm[l], [], ["LWg"], "c9")
              P.dma("sp", L8[:, 0:1], alog[l], [], ["L8a"], "c10")
              P.dma("sp", L8[:, 1:2], dtb[l], [], ["L8"], "c11")
              P.op("act", lambda e: e.activation(L8[:, 2:3], L8[:, 0:1], AF.Exp), ["L8a"], ["L8"])
              P.op("dve", lambda e: e.tensor_scalar(L8[:, 2:3], L8[:, 2:3], -1.0, None, ALU.mult), ["L8"], ["L8"])
              CA = lambda j, tap: LW[:, j * 3 + tap: j * 3 + tap + 1]
              CQ = lambda j, tap: LW[:, 12 + j * 3 + tap: 12 + j * 3 + tap + 1]
              GN = LW[:, 48:49]

              def big_keys(b, permuted):
                  if b == 0 or not permuted:
                      return [("BIG", b)]
                  return [("BIG", i) for i in range(1, 9)]

              def big_view(kc, b, permuted, rowmajor_of_colscan):
                  t0, nt = blk_range(b)
                  if b == 0 or not permuted:
                      return BIG[:, kc, t0:t0 + nt]
                  lat = BIG[:, kc, NCTX:NTOK]
                  if rowmajor_of_colscan:
                      v = lat.rearrange("p (c r) -> p r c", r=64)
                  else:
                      v = lat.rearrange("p (r c) -> p c r", c=64)
                  return v[:, (b - 1) * 8:(b - 1) * 8 + 8, :]

              def pview(ap, b, permuted):
                  t0, nt = blk_range(b)
                  if b == 0 or not permuted:
                      return ap[:, 0:nt]
                  return ap.rearrange("p (a b) -> p a b", b=64)

              with ExitStack() as s1:
                  WIN = sb("WIN", [128, 8, DPROJ], BF16, s1)
                  XSF = sb("XS", [128, 4112], F32, s1)
                  XS = XSF[:, 0:4096].rearrange("p (k t) -> p k t", k=8)
                  T0 = sb("T0", [128, 512], F32, s1)
                  T1 = sb("T1", [128, 512], F32, s1)
                  T2 = sb("T2", [128, 512], F32, s1)
                  SQQ = sb("SQQ", [128, 512], BF16, s1)
                  KQVB = sb("KQVB", [128, 12, 512], BF16, s1)
                  SQB = KQVB[:, 0:8, :]
                  YAB = sb("YAB", [128, 4, 512], BF16, s1)
                  SZBB = sb("SZBB", [128, 4, 512], BF16, s1)
                  ROWB = sb("ROWB", [8, 2, 512], F32, s1)
                  HP = sb("HP", [128, 8, 512], BF16, s1) if colmaj else None
                  XSW = XSF[:]
                  i = 0
                  for k in range(8):
                      for hf in range(2):
                          s = i % 2
                          c0 = hf * 2056
                          P.dma("sp", XSW[:, s * 2056:(s + 1) * 2056], w_in[l, k * 128:(k + 1) * 128, c0:c0 + 2056],
                                [], [("XS", s)], ("xs", s))
                          eng = ("dve", "pool", "act")[i % 3]
                          if eng == "act":
                              P.op("act", lambda e, k=k, s=s, c0=c0: e.copy(WIN[:, k, c0:c0 + 2056], XSW[:, s * 2056:(s + 1) * 2056]),
                                   [("XS", s)], ["WIN"])
                          else:
                              P.op(eng, lambda e, k=k, s=s, c0=c0: e.tensor_copy(WIN[:, k, c0:c0 + 2056], XSW[:, s * 2056:(s + 1) * 2056]),
                                   [("XS", s)], ["WIN"])
                          i += 1
                  if stop == "s1w":
                      P.barrier()
                      ck("s1w")
                  for nb in range(NBLK):
                      t0, nt = blk_range(nb)
                      v = 1 if nb == 0 else 0
                      P.dma("sp", XS[:, :, 0:nt], xsrc_v[:, :, t0:t0 + nt], [("x", nb)], [("XS", 0), ("XS", 1)], ("xs", 0))
                      P.op("act", lambda e, nt=nt: e.activation(SQB[:, :, 0:nt], XS[:, :, 0:nt], AF.Square),
                           [("XS", 0), ("XS", 1)], ["KQVB"])
                      for k in range(8):
                          P.op("pe", lambda e, k=k, nt=nt: e.matmul(B[2][:, 0:nt], ONESB[:], SQB[:, k, 0:nt], start=(k == 0), stop=(k == 7)),
                               ["KQVB", "ONESB"], ["B2"], inc=(k == 7))
                      P.op("act", lambda e, nt=nt: e.activation(T0[:, 0:nt], B[2][:, 0:nt], AF.Ln, bias=EPSC, scale=1.0 / D), ["B2"], ["T0"])
                      P.op("act", lambda e, nt=nt: e.activation(T0[:, 0:nt], T0[:, 0:nt], AF.Exp, scale=-0.5), ["T0"], ["T0"])
                      for k in range(8):
                          eng = "dve" if k % 2 == 0 else "pool"
                          P.op(eng, lambda e, k=k, nt=nt: e.tensor_tensor(XS[:, k, 0:nt], XS[:, k, 0:nt], T0[:, 0:nt], ALU.mult),
                               [("XS", 0), ("XS", 1), "T0", "KQVB"], [("XSn", k)])
                          P.op("act", lambda e, k=k, nt=nt, t0=t0, v=v: e.activation(
                              BIG[:, k, t0:t0 + nt], XS[:, k, 0:nt], AF.Identity, bias=mcol(v, k), scale=mcol(v, 8 + k)),
                              [("XSn", k)], [("BIG", nb)])
                      P.op("act", lambda e: e.activation(DUM[:, 0:1], DUM[:, 1:2], AF.Copy), [("BIG", nb)] + [("XSn", k) for k in range(8)], [("XS", 0), ("XS", 1)])

                  if stop == "s1p":
                      P.barrier()
                      ck("s1p")
                  P.barrier()
                  TS = [(T0, T1, T2, SQQ, "T0", "T1", "T2", "SQQ", 2)]
                  for i_ in range(2):
                      o_ = i_ * 1792
                      TS.append((XSF[:, o_:o_ + 512], XSF[:, o_ + 512:o_ + 1024], XSF[:, o_ + 1024:o_ + 1536],
                                 XSF[:, o_ + 1536:o_ + 1792].bitcast(BF16), f"T0_{i_}", f"T1_{i_}", f"T2_{i_}", f"SQQ_{i_}", 7 if i_ == 0 else 2))
                  tsi = [0]
                  pp = [0]
                  PROJ_BANKS = [0, 1, 3, 4, 5, 6]

                  def proj(b, c0, m):
                      bi = PROJ_BANKS[pp[0] % len(PROJ_BANKS)]
                      pp[0] += 1
                      t0, nt = blk_range(b)
                      for k in range(8):
                          if colmaj and b > 0:
                              P.op("pe", lambda e, k=k, bi=bi: e.matmul(
                                  B[bi][0:m, 0:nt], WIN[:, k, c0:c0 + m], HP[:, k, :],
                                  start=(k == 0), stop=(k == 7)), ["WIN", "HP"], [f"B{bi}"], inc=(k == 7))
                          else:
                              P.op("pe", lambda e, k=k, bi=bi: e.matmul(
                                  B[bi][0:m, 0:nt], WIN[:, k, c0:c0 + m], BIG[:, k, t0:t0 + nt],
                                  start=(k == 0), stop=(k == 7)), ["WIN"] + big_keys(b, False), [f"B{bi}"], inc=(k == 7))
                      return bi

                  def conv(dst, dkey, src, skey, w, nt, seg):
                      P.op("dve", lambda e: e.tensor_scalar(dst[:, 0:nt], src[:, 0:nt], w(1), None, ALU.mult), [skey, "LWa", "LWq"], [dkey])
                      dv = dst[:, 0:nt].rearrange("p (a b) -> p a b", b=seg)
                      sv = src[:, 0:nt].rearrange("p (a b) -> p a b", b=seg)
                      P.op("dve", lambda e: e.scalar_tensor_tensor(dv[:, :, 1:seg], sv[:, :, 0:seg - 1], w(0), dv[:, :, 1:seg], ALU.mult, ALU.add),
                           [skey, dkey, "LWa", "LWq"], [dkey])
                      P.op("dve", lambda e: e.scalar_tensor_tensor(dv[:, :, 0:seg - 1], sv[:, :, 1:seg], w(2), dv[:, :, 0:seg - 1], ALU.mult, ALU.add),
                           [skey, dkey, "LWa", "LWq"], [dkey])

                  def run_tasks(gens_factories, nsets):
                      pending = list(gens_factories)
                      active = []
                      free = list(range(nsets))
                      while pending or active:
                          if pending and free:
                              si = free.pop(0)
                              active.append((pending.pop(0)(TS[si]), si))
                          for ent in list(active):
                              try:
                                  next(ent[0])
                              except StopIteration:
                                  active.remove(ent)
                                  free.append(ent[1])

                  for b in range(NBLK):
                      t0, nt = blk_range(b)
                      seg = NCTX if b == 0 else 64
                      if colmaj and b > 0:
                          for k in range(8):
                              eng = "pool" if k % 2 == 0 else "act"
                              if eng == "pool":
                                  P.op("pool", lambda e, k=k: e.tensor_copy(HP[:, k, :].rearrange("p (a b) -> p a b", b=64), big_view(k, b, True, False)),
                                       big_keys(b, True), ["HP"])
                              else:
                                  P.op("act", lambda e, k=k: e.copy(HP[:, k, :].rearrange("p (a b) -> p a b", b=64), big_view(k, b, True, False)),
                                       big_keys(b, True), ["HP"])

                      def mixer_task(jj, b=b, nt=nt, seg=seg):
                          def g(ts):
                              T0, T1, T2, SQQ, k0, k1, k2, kq_, ssb = ts
                              bi = proj(b, jj * 128, 128)
                              yield
                              P.op("act", lambda e: e.copy(T0[:, 0:nt], B[bi][:, 0:nt]), [f"B{bi}"], [k0])
                              bi = proj(b, 1024 + jj * 128, 128)
                              yield
                              P.op("dve", lambda e: e.tensor_tensor(T0[:, 0:nt], B[bi][:, 0:nt], T0[:, 0:nt], ALU.mult), [f"B{bi}", k0], [k0])
                              conv(T1, k1, T0, k0, lambda tap: CA(jj, tap), nt, seg)
                              bi = proj(b, 512 + jj * 128, 128)
                              yield
                              P.op("dve", lambda e: e.tensor_tensor(T1[:, 0:nt], B[bi][:, 0:nt], T1[:, 0:nt], ALU.mult), [f"B{bi}", k1], [k1])
                              bi = proj(b, 1536 + jj * 128, 128)
                              yield
                              P.op("act", lambda e: e.activation(T2[:, 0:nt], B[bi][:, 0:nt], AF.Silu), [f"B{bi}"], [k2])
                              yield
                              P.op("pool", lambda e: e.tensor_tensor(YAB[:, jj, 0:nt], T1[:, 0:nt], T2[:, 0:nt], ALU.mult), [k1, k2], ["YAB"])
                          return g

                      def qkv_task(idx, b=b, nt=nt, seg=seg):
                          def g(ts):
                              T0, T1, T2, SQQ, k0, k1, k2, kq_, ssb = ts
                              bi = proj(b, 2048 + idx * 128, 128)
                              yield
                              conv(T1, k1, B[bi], f"B{bi}", lambda tap: CQ(idx, tap), nt, seg)
                              yield
                              if idx >= 8:
                                  P.op("act", lambda e: e.activation(KQVB[:, idx, 0:nt], T1[:, 0:nt], AF.Silu), [k1], ["KQVB"])
                                  return
                              P.op("act", lambda e: e.activation(T2[:, 0:nt], T1[:, 0:nt], AF.Silu), [k1], [k2])
                              P.op("act", lambda e: e.activation(SQQ[:, 0:nt], T2[:, 0:nt], AF.Square), [k2], [kq_])
                              yield
                              P.op("pe", lambda e: e.matmul(B[ssb][:, 0:nt], ONESB[:], SQQ[:, 0:nt], start=True, stop=True), [kq_, "ONESB"], [f"B{ssb}"])
                              yield
                              P.op("act", lambda e: e.activation(T0[:, 0:nt], B[ssb][:, 0:nt], AF.Ln, bias=EPSC, scale=1.0), [f"B{ssb}"], [k0])
                              P.op("act", lambda e: e.activation(T0[:, 0:nt], T0[:, 0:nt], AF.Exp, scale=-0.5), [k0], [k0])
                              yield
                              sc = (128.0 ** -0.5) if idx < 4 else 1.0
                              P.op("dve", lambda e: e.scalar_tensor_tensor(KQVB[:, idx, 0:nt], T2[:, 0:nt], sc, T0[:, 0:nt], ALU.mult, ALU.mult),
                                   [k2, k0], ["KQVB"])
                          return g

                      def zb_task(h, b=b, nt=nt):
                          def g(ts):
                              bi = proj(b, 3584 + h * 128, 128)
                              yield
                              P.op("act", lambda e: e.activation(SZBB[:, h, 0:nt], B[bi][:, 0:nt], AF.Silu), [f"B{bi}"], ["SZBB"])
                          return g

                      def rows_task(b=b, nt=nt):
                          def g(ts):
                              bi = proj(b, 4096, 8)
                              bi2 = proj(b, 4104, 8)
                              yield
                              P.op("act", lambda e: e.activation(ROWB[:, 1, 0:nt], B[bi][0:8, 0:nt], AF.Sigmoid), [f"B{bi}"], ["ROWB"])
                              P.op("act", lambda e: e.activation(ROWB[:, 0, 0:nt], B[bi2][0:8, 0:nt], AF.Exp, bias=L8[:, 1:2]), [f"B{bi2}", "L8"], ["ROWB"])
                              P.op("act", lambda e: e.activation(ROWB[:, 0, 0:nt], ROWB[:, 0, 0:nt], AF.Ln, bias=1.0), ["ROWB"], ["ROWB"])
                              yield
                              P.op("dve", lambda e: e.tensor_scalar(ROWB[:, 0, 0:nt], ROWB[:, 0, 0:nt], L8[:, 2:3], None, ALU.mult), ["ROWB", "L8"], ["ROWB"])
                          return g

                      tasks = [mixer_task(jj) for jj in range(4)] + [qkv_task(i) for i in range(12)] + [zb_task(h) for h in range(4)] + [rows_task()]
                      run_tasks(tasks, 3)
                      P.dma("pool", kqv_s[b, :, :, 0:nt], KQVB[:, :, 0:nt], ["KQVB"], [("kqv", b)], "sp0")
                      P.dma("pool", ya_s[b, :, :, 0:nt], YAB[:, :, 0:nt], ["YAB"], [("ya", b)], "sp1")
                      P.dma("pool", szb_s[b, :, :, 0:nt], SZBB[:, :, 0:nt], ["SZBB"], [("szb", b)], "sp2")
                      P.dma("pool", rows_s[b, :, :, 0:nt], ROWB[:, :, 0:nt], ["ROWB"], [("rows", b)], "sp3")
                  P.barrier()

              ck("s1")
              with ExitStack() as s2:
                  S32 = sb("S32", [128, 512], F32, s2)
                  SBF = sb("SBF", [128, 512], BF16, s2)
                  NSLOT = 4

                  def mk_slot(i):
                      t = {}
                      t["KQ"] = sb(f"KQ{i}", [128, 12, 128], BF16, s2)
                      t["RW"] = sb(f"RW{i}", [8, 2, 128], F32, s2)
                      t["SZ"] = sb(f"SZ{i}", [128, 4, 128], BF16, s2)
                      t["OFT"] = sb(f"OFT{i}", [128, 4, 128], BF16, s2)
                      for n_ in ("GC", "CS"):
                          t[n_] = sb(f"{n_}{i}", [8, 128], F32, s2)
                      t["DG"] = sb(f"DG{i}", [8, 8], F32, s2)
                      t["COLS"] = sb(f"COLS{i}", [128, 24], F32, s2)
                      t["CF"] = sb(f"CF{i}", [128, 32], F32, s2)
                      for n_ in ("W0", "W1", "W2", "W3"):
                          t[n_] = sb(f"{n_}{i}", [128, 512], F32, s2)
                      for n_ in ("PTB", "ATB", "AB", "PN0", "PN1", "PTN0", "PTN1", "RB", "RTB", "KBE", "KDEC", "VB",
                                 "QDT"):
                          t[n_] = sb(f"{n_}{i}", [128, 512], BF16, s2)
                      t["banks"] = [2 * i, 2 * i + 1]
                      t["i"] = i
                      return t

                  slots = [mk_slot(i) for i in range(NSLOT)]

                  def hs(ap, h):
                      return ap[:, h * 128:(h + 1) * 128]

                  def mm4(bank, lhs, rhs, rk, start=True, stop=True):
                      for h in range(4):
                          P.op("pe", lambda e, h=h: e.matmul(hs(B[bank], h), lhs(h), rhs(h), start=start, stop=stop), rk, [f"B{bank}"], inc=(h == 3))

                  def chunk_gen(T, d, b, ch, need_out):
                      si = T["i"]
                      K_ = lambda n_: (n_, si)
                      b0, b1 = T["banks"]
                      SB0, SB1 = b0, b1
                      OFT = T["OFT"]
                      t0, nt = blk_range(b)
                      c0 = ch * 128
                      tk = t0 + c0
                      kq, rw, sz = T["KQ"], T["RW"], T["SZ"]
                      GC, CS, DG, COLS, CF = T["GC"], T["CS"], T["DG"], T["COLS"], T["CF"]
                      W0, W1, W2, W3 = T["W0"], T["W1"], T["W2"], T["W3"]
                      U0, kU0 = W2, K_("W2")
                      PTB, ATB, AB, RB, RTB = T["PTB"], T["ATB"], T["AB"], T["RB"], T["RTB"]
                      PN = [T["PN0"], T["PN1"]]
                      PTN = [T["PTN0"], T["PTN1"]]
                      KBE, KDEC, VB, QDT = T["KBE"], T["KDEC"], T["VB"], T["QDT"]
                      WTB, kWTB = W3[:, 0:256].bitcast(BF16), K_("W3")
                      ZM, kZM = PN[0], K_("PN0")
                      YM, kYM = PTN[0], K_("PTN0")
                      SQO, kSQO = PN[1], K_("PN1")
                      UB, kUB = PTN[1], K_("PTN1")
                      kqk, rwk, szk = K_("KQ"), K_("RW"), K_("SZ")
                      P.dma("sp", kq[:], kqv_s[b, :, :, c0:c0 + 128], [("kqv", b)], [kqk], ("kq", si))
                      P.dma("sp", rw[:], rows_s[b, :, :, c0:c0 + 128], [("rows", b)], [rwk], ("rw", si))
                      if d == 1 and need_out:
                          P.dma("sp", sz[:], szb_s[b, :, :, c0:c0 + 128], [("szb", b)], [szk], ("sz", si))
                          P.dma("sp", OFT[:], of_s[tk // 128], [("of", tk)], [K_("OFT")], ("oft", si))
                      qT = lambda h: kq[:, h, :]
                      kT = lambda h: kq[:, 4 + h, :]
                      vT = lambda h: kq[:, 8 + h, :]
                      g8 = rw[:, 0, :]
                      b8 = rw[:, 1, :]
                      yield
                      P.op("dve", lambda e: e.tensor_tensor_scan(CS[:], ONES8, g8, 0.0, ALU.mult, ALU.add), [rwk, "SEL"], [K_("CS")])
                      if d == 0:
                          P.op("dve", lambda e: e.tensor_copy(GC[:], CS[:]), [K_("CS")], [K_("GC")])
                      else:
                          P.op("dve", lambda e: e.scalar_tensor_tensor(GC[:], CS[:], -1.0, g8, ALU.mult, ALU.add), [K_("CS"), rwk], [K_("GC")])
                          P.op("dve", lambda e: e.tensor_scalar(GC[:], GC[:], CS[:, 127:128], None, ALU.add), [K_("GC"), K_("CS")], [K_("GC")])
                      P.op("dve", lambda e: e.tensor_scalar(DG[:], I8, CS[:, 127:128], None, ALU.mult), [K_("CS"), "IDF"], [K_("DG")])
                      P.op("pe", lambda e: e.matmul(B[b0][:, 0:8], GC[:], I8, start=True, stop=True), [K_("GC"), "IDF"], [f"B{b0}"], inc=False)
                      P.op("pe", lambda e: e.matmul(B[b0][:, 8:16], b8, I8, start=True, stop=True), [rwk, "IDF"], [f"B{b0}"], inc=False)
                      P.op("pe", lambda e: e.matmul(B[b0][:, 16:24], ONES8, DG[:], start=True, stop=True), ["SEL", K_("DG")], [f"B{b0}"])
                      for h in range(4):
                          P.op("pe", lambda e, h=h: e.transpose(Bbf[b1][:, h * 128:(h + 1) * 128], kT(h), IDB[:]), [kqk, "IDB"], [f"B{b1}"], inc=False)
                      for h in range(4):
                          P.op("pe", lambda e, h=h: e.transpose(Bbf[b1][:, 512 + h * 128:512 + (h + 1) * 128], vT(h), IDB[:]), [kqk, "IDB"], [f"B{b1}"], inc=(h == 3))
                      yield
                      P.op("act", lambda e: e.copy(COLS[:], B[b0][:, 0:24]), [f"B{b0}"], [K_("COLS")])
                      P.op("act", lambda e: e.activation(CF[:, 0:8], COLS[:, 0:8], AF.Exp), [K_("COLS")], [K_("CF")])
                      P.op("dve", lambda e: e.tensor_tensor(CF[:, 8:16], CF[:, 0:8], COLS[:, 8:16], ALU.mult), [K_("CF"), K_("COLS")], [K_("CF")])
                      P.op("dve", lambda e: e.tensor_tensor(CF[:, 16:24], COLS[:, 16:24], COLS[:, 0:8], ALU.subtract), [K_("COLS"), K_("CF")], [K_("CF")])
                      P.op("act", lambda e: e.activation(CF[:, 16:24], CF[:, 16:24], AF.Exp), [K_("CF")], [K_("CF")])
                      P.op("act", lambda e: e.activation(CF[:, 24:32], COLS[:, 16:24], AF.Exp), [K_("COLS"), K_("CF")], [K_("CF")])
                      for h in range(4):
                          dh = d * 4 + h
                          P.op("pe", lambda e, h=h, dh=dh: e.matmul(hs(B[b0], h), SEL[:, dh * 128:(dh + 1) * 128], GC[:], start=True, stop=True), ["SEL", K_("GC")], [f"B{b0}"], inc=(h == 3))
                      yield
                      for h in range(4):
                          dh = d * 4 + h
                          P.op("dve", lambda e, h=h, dh=dh: e.tensor_scalar(hs(KBE[:], h), Bbf[b1][:, h * 128:(h + 1) * 128], CF[:, 8 + dh:9 + dh], None, ALU.mult),
                               [f"B{b1}", K_("CF")], [K_("KBE")])
                      for h in range(4):
                          dh = d * 4 + h
                          P.op("act", lambda e, h=h, dh=dh: e.activation(hs(KDEC[:], h), Bbf[b1][:, h * 128:(h + 1) * 128], AF.Identity, scale=CF[:, 16 + dh:17 + dh]),
                               [f"B{b1}", K_("CF")], [K_("KDEC")])
                          P.op("act", lambda e, h=h, dh=dh: e.activation(hs(VB[:], h), Bbf[b1][:, 512 + h * 128:512 + (h + 1) * 128], AF.Identity, scale=COLS[:, 8 + dh:9 + dh]),
                               [f"B{b1}", K_("COLS")], [K_("VB")])
                      mm4(b1, lambda h: SEL[:, (d * 4 + h) * 128:(d * 4 + h + 1) * 128], lambda h: b8, ["SEL", rwk])
                      yield
                      for h in range(4):
                          dh = d * 4 + h
                          P.op("dve", lambda e, h=h, dh=dh: e.scalar_tensor_tensor(hs(W0[:], h), hs(B[b0], h), COLS[:, dh:dh + 1], hs(NEGI[d], h), ALU.subtract, ALU.add),
                               [f"B{b0}", K_("COLS"), "MASK"], [K_("W0")])
                      P.op("act", lambda e: e.activation(W3[:], B[b0], AF.Exp), [f"B{b0}"], [K_("W3")])
                      P.op("act", lambda e: e.activation(W1[:], W0[:], AF.Exp), [K_("W0")], [K_("W1")])
                      P.op("pool", lambda e: e.tensor_tensor(h4(QDT[:]), kq[:, 0:4, :], h4(W3[:]), ALU.mult), [kqk, K_("W3")], [K_("QDT")])
                      P.op("dve", lambda e: e.tensor_tensor(W2[:], B[b1], SM[d], ALU.mult), [f"B{b1}", "MASK"], [K_("W2")])
                      P.op("pool", lambda e: e.tensor_tensor(W2[:], W2[:], W1[:], ALU.mult), [K_("W2"), K_("W1")], [K_("W2")])
                      mm4(b0, kT, kT, [kqk])
                      mm4(b1, kT, qT, [kqk])
                      yield
                      P.op("dve", lambda e: e.tensor_tensor(ATB[:], B[b0], W2[:], ALU.mult), [f"B{b0}", K_("W2")], [K_("ATB")])
                      P.op("dve", lambda e: e.tensor_tensor(PTB[:], B[b1], W1[:], ALU.mult), [f"B{b1}", K_("W1")], [K_("PTB")])
                      for h in range(4):
                          P.op("pe", lambda e, h=h: e.transpose(Bbf[b0][:, h * 128:(h + 1) * 128], hs(ATB[:], h), IDB[:]), [K_("ATB"), "IDB"], [f"B{b0}"], inc=(h == 3))
                      P.op("pool", lambda e: e.tensor_tensor(PN[0][:], ATB[:], BLK16, ALU.mult), [K_("ATB"), "MASK"], [K_("PN0")])
                      P.op("pool", lambda e: e.tensor_tensor(RTB[:], ID4, PN[0][:], ALU.subtract), ["MASK", K_("PN0")], [K_("RTB")])
                      yield
                      P.op("act", lambda e: e.copy(AB[:], Bbf[b0][:, 0:512]), [f"B{b0}"], [K_("AB")])
                      P.op("dve", lambda e: e.tensor_tensor(PTN[0][:], Bbf[b0][:, 0:512], BLK16, ALU.mult), [f"B{b0}", "MASK"], [K_("PTN0")])
                      P.op("pool", lambda e: e.tensor_tensor(RB[:], ID4, PTN[0][:], ALU.subtract), ["MASK", K_("PTN0")], [K_("RB")])
                      yield
                      cur = 0
                      for it in range(3):
                          nx = 1 - cur
                          kc_, kn_ = (K_(f"PTN{cur}"), K_(f"PN{cur}")), (K_(f"PTN{nx}"), K_(f"PN{nx}"))
                          mm4(b0, lambda h: hs(PTN[cur][:], h), lambda h: hs(PN[cur][:], h), list(kc_))
                          mm4(b1, lambda h: hs(PN[cur][:], h), lambda h: hs(PTN[cur][:], h), list(kc_))
                          yield
                          P.op("act", lambda e, nx=nx: e.copy(PN[nx][:], B[b0]), [f"B{b0}"], [kn_[1]])
                          P.op("act", lambda e, nx=nx: e.copy(PTN[nx][:], B[b1]), [f"B{b1}"], [kn_[0]])
                          mm4(b0, lambda h: hs(PTN[nx][:], h), lambda h: hs(RTB[:], h), [kn_[0], K_("RTB")])
                          mm4(b1, lambda h: hs(PN[nx][:], h), lambda h: hs(RB[:], h), [kn_[1], K_("RB")])
                          yield
                          P.op("dve", lambda e: e.tensor_tensor(RTB[:], B[b0], RTB[:], ALU.add), [f"B{b0}", K_("RTB")], [K_("RTB")])
                          P.op("act", lambda e: e.copy(W0[:], B[b1]), [f"B{b1}"], [K_("W0")])
                          P.op("pool", lambda e: e.tensor_tensor(RB[:], W0[:], RB[:], ALU.add), [K_("W0"), K_("RB")], [K_("RB")])
                          cur = nx
                      for szm in (32, 64, 128):
                          if szm < 128:
                              mm4(b0, lambda h: hs(ATB[:], h), lambda h: hs(RB[:], h), [K_("ATB"), K_("RB")])
                          mm4(b1, lambda h: hs(AB[:], h), lambda h: hs(RTB[:], h), [K_("AB"), K_("RTB")])
                          yield
                          if szm < 128:
                              P.op("dve", lambda e, szm=szm: e.tensor_tensor(ZM[:], B[b0], OFF[szm], ALU.mult), [f"B{b0}", "MASK"], [kZM])
                          P.op("dve", lambda e, szm=szm: e.tensor_tensor(YM[:], B[b1], OFF[szm], ALU.mult), [f"B{b1}", "MASK"], [kYM])
                          if szm < 128:
                              mm4(b0, lambda h: hs(RTB[:], h), lambda h: hs(ZM[:], h), [K_("RTB"), kZM])
                          mm4(b1, lambda h: hs(RB[:], h), lambda h: hs(YM[:], h), [K_("RB"), kYM])
                          yield
                          if szm < 128:
                              P.op("act", lambda e: e.copy(W0[:], B[b0]), [f"B{b0}"], [K_("W0")])
                              P.op("pool", lambda e: e.tensor_tensor(RB[:], RB[:], W0[:], ALU.subtract), [K_("W0"), K_("RB")], [K_("RB")])
                          P.op("dve", lambda e: e.tensor_tensor(RTB[:], RTB[:], B[b1], ALU.subtract), [f"B{b1}", K_("RTB")], [K_("RTB")])
                      mm4(b0, lambda h: hs(RTB[:], h), lambda h: hs(VB[:], h), [K_("RTB"), K_("VB")])
                      mm4(b1, lambda h: hs(KBE[:], h), lambda h: hs(RTB[:], h), [K_("KBE"), K_("RTB")])
                      yield
                      P.op("act", lambda e: e.copy(U0[:], B[b0]), [f"B{b0}"], [kU0])
                      P.op("act", lambda e: e.copy(WTB, B[b1]), [f"B{b1}"], [kWTB])
                      yield "scan"
                      mm4(SB0, lambda h: hs(WTB, h), lambda h: hs(SBF[:], h), [kWTB, "SBF"])
                      P.op("dve", lambda e: e.tensor_tensor(UB[:], U0[:], B[SB0], ALU.subtract), [kU0, f"B{SB0}"], [kUB])
                      if need_out:
                          for h in range(4):
                              P.op("pe", lambda e, h=h: e.matmul(hs(B[SB1], h), hs(SBF[:], h), hs(QDT[:], h), start=True, stop=False), ["SBF", K_("QDT")], [f"B{SB1}"], inc=False)
                              P.op("pe", lambda e, h=h: e.matmul(hs(B[SB1], h), hs(UB[:], h), hs(PTB[:], h), start=False, stop=True), [kUB, K_("PTB")], [f"B{SB1}"], inc=(h == 3))
                      mm4(SB0, lambda h: hs(KDEC[:], h), lambda h: hs(UB[:], h), [K_("KDEC"), kUB])
                      for h in range(4):
                          dh = d * 4 + h
                          P.op("dve", lambda e, h=h, dh=dh: e.scalar_tensor_tensor(hs(S32[:], h), hs(S32[:], h), CF[:, 24 + dh:25 + dh], hs(B[SB0], h), ALU.mult, ALU.add),
                               ["S32", K_("CF"), f"B{SB0}"], ["S32"])
                      P.op("act", lambda e: e.copy(SBF[:], S32[:]), ["S32"], ["SBF"])
                      if need_out:
                          if d == 0:
                              P.op("act", lambda e: e.copy(OFT[:], h4(B[SB1])), [f"B{SB1}"], [K_("OFT")])
                              P.dma("pool", of_s[tk // 128], OFT[:], [K_("OFT")], [("of", tk)], ("ofst", si))
                          else:
                              P.op("dve", lambda e: e.tensor_tensor(h4(W0[:]), h4(B[SB1]), OFT[:], ALU.add), [f"B{SB1}", K_("OFT")], [K_("W0")])
                              P.op("act", lambda e: e.activation(SQO[:], W0[:], AF.Square), [K_("W0")], [kSQO])
                              mm4(SB0, lambda h: ONESB[:], lambda h: hs(SQO[:], h), ["ONESB", kSQO])
                              P.op("act", lambda e: e.activation(W1[:], B[SB0], AF.Ln, bias=EPSC, scale=1.0 / 128), [f"B{SB0}"], [K_("W1")])
                              P.op("act", lambda e: e.activation(W1[:], W1[:], AF.Exp, scale=-0.5), [K_("W1")], [K_("W1")])
                              P.op("pool", lambda e: e.tensor_tensor(W0[:], W0[:], W1[:], ALU.mult), [K_("W0"), K_("W1")], [K_("W0")])
                              P.op("dve", lambda e: e.scalar_tensor_tensor(BIG[:, 4:8, tk:tk + 128], h4(W0[:]), GN, sz[:, :, :], ALU.mult, ALU.mult),
                                   [K_("W0"), "LWg", szk], [("BIG", b)])

                  for d in range(2):
                      P.op("pool", lambda e: e.memset(S32[:], 0.0), [], ["S32"])
                      P.op("pool", lambda e: e.memset(SBF[:], 0.0), [], ["SBF"])
                      border = list(range(NBLK)) if d == 0 else [0] + list(range(8, 0, -1))
                      tasks = []
                      for b in border:
                          t0, nt = blk_range(b)
                          need_out = not (last and b == 0)
                          nch = nt // 128
                          chs = list(range(nch)) if d == 0 else list(range(nch - 1, -1, -1))
                          for ci, ch in enumerate(chs):
                              tasks.append((b, ch, need_out, ci == 0))
                      active = []
                      nxt = 0
                      scan_turn = 0
                      free_slots = list(range(NSLOT))
                      while nxt < len(tasks) or active:
                          if nxt < len(tasks) and free_slots and (not active or len(active) < NSLOT):
                              b, ch, need_out, first = tasks[nxt]
                              if first and d == 1 and need_out:
                                  t0, nt = blk_range(b)
                                  P.dma("sp", BIG[:, 0:4, t0:t0 + nt], ya_s[b, :, :, 0:nt], [("ya", b)], [("BIG", b)], "yal")
                              si = free_slots.pop(0)
                              active.append([chunk_gen(slots[si], d, b, ch, need_out), nxt, False, si])
                              nxt += 1
                          for ent in list(active):
                              g, ti, wscan, si = ent
                              if wscan and ti != scan_turn:
                                  continue
                              try:
                                  r = next(g)
                                  if wscan:
                                      scan_turn += 1
                                      ent[2] = False
                                  if r == "scan":
                                      ent[2] = True
                              except StopIteration:
                                  if wscan:
                                      scan_turn += 1
                                  active.remove(ent)
                                  free_slots.append(si)
                  P.barrier()
              ck("s2")
              with ExitStack() as s4:
                  WOUT = sb("WOUT", [128, 8, D], BF16, s4)
                  XS = sb("XS4", [128, 8, 512], F32, s4)
                  WS4 = sb("WS4", [128, 2, D], F32, s4)
                  SQB = sb("SQB4", [128, 8, 512], BF16, s4)
                  T0 = sb("T04", [128, 512], F32, s4)
                  YP = sb("YP", [128, 8, 512], BF16, s4) if colmaj else None
                  for k in range(8):
                      s = k % 2
                      P.dma("sp", WS4[:, s, :], w_out[l, k * 128:(k + 1) * 128, :], [], [("WS4", s)], ("ws4", s))
                      P.op("dve" if s == 0 else "pool", lambda e, k=k, s=s: e.tensor_copy(WOUT[:, k, :], WS4[:, s, :]), [("WS4", s)], ["WOUT"])
                  pp = 0
                  for nb in range(NBLK):
                      if last and nb == 0:
                          continue
                      t0, nt = blk_range(nb)
                      v = 1 if nb == 0 else 0
                      P.dma("sp", XS[:, :, 0:nt], xsrc_v[:, :, t0:t0 + nt], [("x", nb)], ["XS4"], "xs4")
                      if colmaj and nb > 0:
                          for k in range(8):
                              if k % 2 == 0:
                                  P.op("pool", lambda e, k=k: e.tensor_copy(YP[:, k, :].rearrange("p (a b) -> p a b", b=64), big_view(k, nb, True, True)),
                                       big_keys(nb, True), ["YP"])
                              else:
                                  P.op("act", lambda e, k=k: e.copy(YP[:, k, :].rearrange("p (a b) -> p a b", b=64), big_view(k, nb, True, True)),
                                       big_keys(nb, True), ["YP"])
                      for jn in range(8):
                          bi = (0, 1, 3, 4)[pp % 4]
                          pp += 1
                          for kc in range(8):
                              if colmaj and nb > 0:
                                  P.op("pe", lambda e, kc=kc, jn=jn, bi=bi: e.matmul(
                                      B[bi][:, 0:nt], WOUT[:, kc, jn * 128:(jn + 1) * 128], YP[:, kc, :],
                                      start=(kc == 0), stop=(kc == 7)), ["WOUT", "YP"], [f"B{bi}"], inc=(kc == 7))
                              else:
                                  P.op("pe", lambda e, kc=kc, jn=jn, bi=bi: e.matmul(
                                      B[bi][:, 0:nt], WOUT[:, kc, jn * 128:(jn + 1) * 128], BIG[:, kc, t0:t0 + nt],
                                      start=(kc == 0), stop=(kc == 7)), ["WOUT"] + big_keys(nb, False), [f"B{bi}"], inc=(kc == 7))
                          P.op("dve", lambda e, jn=jn, bi=bi, v=v: e.scalar_tensor_tensor(
                              XS[:, jn, 0:nt], B[bi][:, 0:nt], mcol(v, 16 + jn), XS[:, jn, 0:nt], ALU.mult, ALU.add),
                              [f"B{bi}", "XS4"], [("XSo", jn)])
                      okeys = [("XSo", jn) for jn in range(8)]
                      if not last:
                          P.dma("pool", xres_v[:, :, t0:t0 + nt], XS[:, :, 0:nt], okeys, [("x", nb)], "xst")
                          P.op("dve", lambda e: e.engine_nop(), [("x", nb)], ["XS4"])
                      else:
                          P.op("act", lambda e: e.activation(SQB[:], XS[:], AF.Square), okeys, ["SQB4"])
                          for k in range(8):
                              P.op("pe", lambda e, k=k: e.matmul(B[2], ONESB[:], SQB[:, k, :], start=(k == 0), stop=(k == 7)), ["SQB4", "ONESB"], ["B2"], inc=(k == 7))
                          P.op("act", lambda e: e.activation(T0[:], B[2], AF.Ln, bias=EPSC, scale=1.0 / D), ["B2"], ["T04"])
                          P.op("act", lambda e: e.activation(T0[:], T0[:], AF.Exp, scale=-0.5), ["T04"], ["T04"])
                          for k in range(8):
                              P.op("dve", lambda e, k=k: e.scalar_tensor_tensor(XS[:, k, :], XS[:, k, :], FN[:, k:k + 1], T0[:], ALU.mult, ALU.mult),
                                   [("XSo", k), "T04", "FN", "SQB4"], [("XSf", k)])
                          fk = [("XSf", k) for k in range(8)]
                          P.dma("pool", out_v[:, :, t0 - NCTX:t0 - NCTX + nt], XS[:, :, 0:nt], fk, [("out", nb)], "ost")
                          P.op("dve", lambda e: e.engine_nop(), [("out", nb)], ["XS4"])
                  P.barrier()
          P.barrier()

    except _Stop:
        pass
    return nc


def _consts():
    i = np.arange(128)
    t, s = i[None, :], i[:, None]
    negi_f = np.where(t >= s, 0.0, BIGNEG)
    negi_b = np.where(t <= s, 0.0, BIGNEG)
    sm_f = (t > s).astype(np.float64)
    sm_b = (t < s).astype(np.float64)
    blk = lambda z: ((s // z) == (t // z)).astype(np.float64)
    blk16 = blk(16)
    off = {z: blk(z) * (1 - blk(z // 2)) for z in (32, 64, 128)}
    ident = np.eye(128)
    ms = [negi_f, negi_b, sm_f, sm_b, blk16, off[32], off[64], off[128], ident]
    cmask = np.concatenate([np.tile(m, (1, 4)) for m in ms], axis=1).astype(np.float32)
    sel = np.zeros((8, 2 * 1024 + 128), np.float32)
    for dh in range(8):
        sel[dh, dh * 128:(dh + 1) * 128] = 1.0
        sel[dh, 1024 + dh * 128:1024 + (dh + 1) * 128] = -1.0
    sel[:, 2048:] = 1.0
    return cmask, ident.astype(np.float32), sel


_NC_CACHE = {}


def make_in_maps(x, c, ctx, c_ctx, norm_w, w_mod, b_mod, w_in, conv_a, conv_qkv, a_log, dt_bias, gdn_norm, w_out, final_norm):
    f = lambda a: np.ascontiguousarray(np.asarray(a, dtype=np.float32))
    x, c, ctx, c_ctx = f(x), f(c), f(ctx), f(c_ctx)
    cmask, ident, sel = _consts()
    L = norm_w.shape[0]
    col = lambda a: np.ascontiguousarray(a.reshape(-1, 128).T)
    shared = {
        "w_mod": f(w_mod), "w_in": f(w_in), "w_out": f(w_out),
        "bmod": np.stack([col(f(b_mod)[l]) for l in range(L)]),
        "normw": np.stack([col(f(norm_w)[l]) for l in range(L)]),
        "conva": np.stack([np.ascontiguousarray(f(conv_a)[l].T.reshape(4, 128, 3).transpose(1, 0, 2).reshape(128, 12)) for l in range(L)]),
        "convq": np.stack([np.ascontiguousarray(f(conv_qkv)[l].T.reshape(12, 128, 3).transpose(1, 0, 2).reshape(128, 36)) for l in range(L)]),
        "alog": np.ascontiguousarray(f(a_log).reshape(L, 8, 1)),
        "dtb": np.ascontiguousarray(f(dt_bias).reshape(L, 8, 1)),
        "gnorm": np.ascontiguousarray(f(gdn_norm).reshape(L, 128, 1)),
        "fnorm": col(f(final_norm)),
        "cmask": cmask, "cident": ident, "csel": sel,
    }
    maps = []
    for b in range(x.shape[0]):
        m = dict(shared)
        m["xT"] = np.ascontiguousarray(np.concatenate([ctx[b], x[b]], axis=0).T)
        ccb = np.stack([col(c[b]), col(c_ctx)], axis=-1).reshape(128, 16)
        m["cc"] = np.ascontiguousarray(ccb)
        maps.append(m)
    return maps


def kernel(x, c, ctx, c_ctx, norm_w, w_mod, b_mod, w_in, conv_a, conv_qkv, a_log, dt_bias, gdn_norm, w_out, final_norm, _nlayers=DEPTH):
    maps = make_in_maps(x, c, ctx, c_ctx, norm_w, w_mod, b_mod, w_in, conv_a, conv_qkv, a_log, dt_bias, gdn_norm, w_out, final_norm)
    if _nlayers not in _NC_CACHE:
        _NC_CACHE[_nlayers] = build(_nlayers)
    nc = _NC_CACHE[_nlayers]
    res = run_bass_kernel_spmd(nc, maps, core_ids=list(range(len(maps))))
    out = np.stack([np.ascontiguousarray(r["outT"].T) for r in res.results], axis=0)
    return out.astype(np.float32)
```

```python
import numpy as np
from contextlib import ExitStack
import concourse.bass as bass
import concourse.mybir as mybir
from concourse.bass_utils import run_bass_kernel_spmd

F32 = mybir.dt.float32
BF16 = mybir.dt.bfloat16
AF = mybir.ActivationFunctionType
ALU = mybir.AluOpType

D = 1024
NCTX = 256
NLAT = 4096
NTOK = NCTX + NLAT
DPROJ = 4112
DEPTH = 4
EPS = 1e-6
NBLK = 9
BIGNEG = -30000.0
PSUM_KEYS = {f"B{i}" for i in range(8)}


def blk_range(b):
    if b == 0:
        return 0, NCTX
    return NCTX + (b - 1) * 512, 512


class _Stop(Exception):
    pass


class Prog:
    def __init__(self, nc, es):
        self.nc = nc
        self.es = es
        self.eng = {"pe": nc.tensor, "act": nc.scalar, "dve": nc.vector, "pool": nc.gpsimd, "sp": nc.sync}
        self.sem = {}
        self.cnt = {}
        self.epoch = 0
        self.state = {}
        self.waited = {}
        self.dsem = {}
        self.dcnt = {}
        self.all_sems = {}
        self.nops = 0
        self.stop_at = None
        self.new_epoch()

    def new_epoch(self):
        self.epoch += 1
        for e in ("pe", "act", "dve", "pool"):
            s = self.es.enter_context(self.nc.semaphore(f"s_{e}_{self.epoch}"))
            self.sem[e] = s
            self.cnt[e] = 0
            self.all_sems[id(s)] = s

    def _collect(self, engine, reads, writes):
        need = {}

        def add(ev):
            s, v, e = ev
            k = id(s)
            if k not in need or need[k][1] < v:
                need[k] = (s, v, e)

        for k in reads:
            st = self.state.get(k)
            if st is not None:
                for ev in st["w"].values():
                    add(ev)
                if k in PSUM_KEYS:
                    for ev in st["r"].values():
                        if ev[2] != engine:
                            add(ev)
        for k in writes:
            st = self.state.get(k)
            if st is not None:
                for ev in st["w"].values():
                    if ev[2] != engine:
                        add(ev)
                for ev in st["r"].values():
                    if ev[2] != engine:
                        add(ev)
        out = []
        wd = self.waited.setdefault(engine, {})
        for k, (s, v, e) in need.items():
            if wd.get(k, 0) >= v:
                continue
            wd[k] = v
            out.append((s, v))
        return out

    def _record(self, ev, reads, writes):
        for k in reads:
            st = self.state.setdefault(k, {"w": {}, "r": {}})
            st["r"][id(ev[0])] = ev
        for k in writes:
            st = self.state.setdefault(k, {"w": {}, "r": {}})
            st["w"][id(ev[0])] = ev

    def op(self, engine, fn, reads=(), writes=(), inc=True):
        e = self.eng[engine]
        for s, v in self._collect(engine, reads, writes):
            e.wait_ge(s, v)
        inst = fn(e)
        if inc:
            self.cnt[engine] += 1
            inst.then_inc(self.sem[engine], 1)
            self._record((self.sem[engine], self.cnt[engine], engine), reads, writes)
        else:
            self._record((self.sem[engine], self.cnt[engine] + 1, engine), reads, writes)
        self.nops += 1
        if self.stop_at is not None and self.nops == self.stop_at:
            self.barrier()
            raise _Stop()

    def dma(self, queue, out, in_, reads, writes, slot):
        e = self.eng[queue]
        for s, v in self._collect("q_" + queue, reads, writes):
            e.wait_ge(s, v)
        if slot not in self.dsem:
            self.dsem[slot] = self.es.enter_context(self.nc.semaphore(f"d_{len(self.dsem)}"))
            self.dcnt[slot] = 0
        self.dcnt[slot] += 16
        e.dma_start(out=out, in_=in_).then_inc(self.dsem[slot], 16)
        self._record((self.dsem[slot], self.dcnt[slot], "dma_" + str(slot)), reads, writes)

    def barrier(self):
        evs = [(self.sem[e], self.cnt[e]) for e in ("pe", "act", "dve", "pool") if self.cnt[e] > 0]
        evs += [(self.dsem[s], self.dcnt[s]) for s in self.dsem]
        for en in ("pe", "act", "dve", "pool", "sp"):
            key = en if en != "sp" else "q_sp"
            wd = self.waited.setdefault(key, {})
            for s, v in evs:
                if en in self.sem and s is self.sem.get(en):
                    continue
                if wd.get(id(s), 0) >= v:
                    continue
                wd[id(s)] = v
                self.eng[en].wait_ge(s, v)
        wd = self.waited.setdefault("q_pool", {})
        for s, v in evs:
            wd[id(s)] = max(wd.get(id(s), 0), v)
        self.state = {}


def build(nlayers=DEPTH, stop=None):
    def ck(name):
        if stop == name:
            print("ck", name, "nops", P.nops)
            raise _Stop()
    nc = bass.Bass("TRN2", target_bir_lowering=False)
    dt_in = lambda n, shp, dt=F32: nc.dram_tensor(n, list(shp), dt, kind="ExternalInput").ap()
    xT = dt_in("xT", [D, NTOK])
    cc = dt_in("cc", [128, 16])
    w_mod = dt_in("w_mod", [DEPTH, D, 3 * D])
    bmod = dt_in("bmod", [DEPTH, 128, 24])
    normw = dt_in("normw", [DEPTH, 128, 8])
    w_in = dt_in("w_in", [DEPTH, D, DPROJ])
    conva = dt_in("conva", [DEPTH, 128, 12])
    convq = dt_in("convq", [DEPTH, 128, 36])
    alog = dt_in("alog", [DEPTH, 8, 1])
    dtb = dt_in("dtb", [DEPTH, 8, 1])
    gnorm = dt_in("gnorm", [DEPTH, 128, 1])
    w_out = dt_in("w_out", [DEPTH, D, D])
    fnorm = dt_in("fnorm", [128, 8])
    cmask = dt_in("cmask", [128, 9 * 512])
    cident = dt_in("cident", [128, 128])
    csel = dt_in("csel", [8, 2 * 1024 + 128])
    outT = nc.dram_tensor("outT", [D, NLAT], F32, kind="ExternalOutput").ap()
    xres = nc.dram_tensor("xres", [D, NTOK], F32).ap()
    kqv_s = nc.dram_tensor("kqv_s", [NBLK, 128, 12, 512], BF16).ap()
    ya_s = nc.dram_tensor("ya_s", [NBLK, 128, 4, 512], BF16).ap()
    szb_s = nc.dram_tensor("szb_s", [NBLK, 128, 4, 512], BF16).ap()
    rows_s = nc.dram_tensor("rows_s", [NBLK, 8, 2, 512], F32).ap()
    of_s = nc.dram_tensor("of_s", [NTOK // 128, 128, 4, 128], BF16).ap()

    es = ExitStack()
    try:
      with es:
          P = Prog(nc, es)
          if isinstance(stop, int):
              P.stop_at = stop
          _uniq = [0]

          def sb(n, shp, dt=F32, st=es):
              _uniq[0] += 1
              return st.enter_context(nc.sbuf_tensor(f"{n}_{_uniq[0]}", list(shp), dt))
          BIG = sb("BIG", [128, 8, NTOK], BF16)
          MASK = sb("MASK", [128, 9, 512], BF16)
          IDF = sb("IDF", [128, 128], F32)
          IDB = sb("IDB", [128, 128], BF16)
          ONESB = sb("ONESB", [128, 128], BF16)
          SEL = sb("SEL", [8, 2 * 1024 + 128], F32)
          MODS = sb("MODS", [128, DEPTH, 2, 24], F32)
          CC = sb("CC", [128, 16], F32)
          FN = sb("FN", [128, 8], F32)
          LW = sb("LW", [128, 64], F32)
          L8 = sb("L8", [8, 4], F32)
          DUM = sb("DUM", [128, 2], F32)
          EPST = sb("EPST", [128, 1], F32)
          EPSC = EPST[:, 0:1]
          banks = [es.enter_context(nc.psum_tensor(f"B{i}", [128, 512], F32)) for i in range(8)]
          B = [b[:] for b in banks]
          Bbf = [b[:].bitcast(BF16) for b in banks]

          NEGI = [MASK[:, 0, :], MASK[:, 1, :]]
          SM = [MASK[:, 2, :], MASK[:, 3, :]]
          BLK16 = MASK[:, 4, :]
          OFF = {32: MASK[:, 5, :], 64: MASK[:, 6, :], 128: MASK[:, 7, :]}
          ID4 = MASK[:, 8, :]
          I8 = IDF[0:8, 0:8]
          ONES8 = SEL[:, 2048:2176]

          def h4(ap):
              return ap.rearrange("p (h t) -> p h t", h=4)

          with ExitStack() as st0:
              MST = sb("MST", [128, 9 * 512], F32, st0)
              WST = sb("WST", [128, 2, 4096], F32, st0)
              SC = sb("SC", [128, 16], F32, st0)
              BM = sb("BM", [128, 24], F32, st0)
              NW = sb("NW", [128, 8], F32, st0)
              P.dma("sp", MST[:], cmask, [], ["MST"], "c0")
              P.dma("sp", IDF[:], cident, [], ["IDF"], "c1")
              P.dma("sp", SEL[:], csel, [], ["SEL"], "c2")
              P.dma("sp", CC[:], cc, [], ["CC"], "c3")
              P.dma("sp", FN[:], fnorm, [], ["FN"], "c4")
              P.op("dve", lambda e: e.tensor_copy(MASK[:].rearrange("p a b -> p (a b)"), MST[:]), ["MST"], ["MASK"])
              P.op("dve", lambda e: e.tensor_copy(IDB[:], IDF[:]), ["IDF"], ["IDB"])
              P.op("dve", lambda e: e.memset(ONESB[:], 1.0), [], ["ONESB"])
              P.op("dve", lambda e: e.memset(DUM[:], 0.0), [], ["DUM"])
              P.op("dve", lambda e: e.memset(EPST[:], EPS), [], ["EPST"])
              P.op("act", lambda e: e.activation(SC[:], CC[:], AF.Silu), ["CC"], ["SC"])
              for l in range(nlayers):
                  P.dma("sp", BM[:], bmod[l], [], ["BM"], "c5")
                  P.dma("sp", NW[:], normw[l], [], ["NW"], "c6")
                  for jg in range(6):
                      s = jg % 2
                      P.dma("sp", WST[:, s, :].rearrange("p (k n) -> p k n", k=8),
                            w_mod[l, :, jg * 512:(jg + 1) * 512].rearrange("(k p) n -> p k n", p=128), [], [("WST", s)], ("wst", s))
                      for jj in range(4):
                          j = jg * 4 + jj
                          for k in range(8):
                              P.op("pe", lambda e, j=j, jj=jj, k=k, s=s: e.matmul(
                                  B[0][:, 2 * j:2 * j + 2], WST[:, s, k * 512 + jj * 128:k * 512 + (jj + 1) * 128],
                                  SC[:].rearrange("p (k v) -> p k v", v=2)[:, k, :], start=(k == 0), stop=(k == 7)),
                                  [("WST", s), "SC"], ["B0"])
                  for v in range(2):
                      P.op("dve", lambda e, v=v, l=l: e.tensor_tensor(
                          MODS[:, l, v, :], B[0][:, 0:48].rearrange("p (j v) -> p j v", v=2)[:, :, v], BM[:], ALU.add),
                          ["B0", "BM"], [("MODS", l)])
                      P.op("dve", lambda e, v=v, l=l: e.scalar_tensor_tensor(
                          MODS[:, l, v, 8:16], MODS[:, l, v, 8:16], 1.0, NW[:], ALU.add, ALU.mult),
                          [("MODS", l), "NW"], [("MODS", l)])
              P.barrier()
          ck("s0")

          for l in range(nlayers):
              P.new_epoch()
              colmaj = (l % 2 == 1)
              last = (l == nlayers - 1)
              xsrc = xT if l == 0 else xres
              xsrc_v = xsrc.rearrange("(k p) t -> p k t", p=128)
              xres_v = xres.rearrange("(k p) t -> p k t", p=128)
              out_v = outT.rearrange("(k p) t -> p k t", p=128)

              def mcol(v, j, l=l):
                  return MODS[:, l, v, j:j + 1]

              P.dma("sp", LW[:, 0:12], conva[l], [], ["LWa"], "c7")
              P.dma("sp", LW[:, 12:48], convq[l], [], ["LWq"], "c8")
              P.dma("sp", LW[:, 48:49], gnorm[l], [], ["LWg"], "c9")
              P.dma("sp", L8[:, 0:1], alog[l], [], ["L8a"], "c10")
              P.dma("sp", L8[:, 1:2], dtb[l], [], ["L8"], "c11")
              P.op("act", lambda e: e.activation(L8[:, 2:3], L8[:, 0:1], AF.Exp), ["L8a"], ["L8"])
              P.op("dve", lambda e: e.tensor_scalar(L8[:, 2:3], L8[:, 2:3], -1.0, None, ALU.mult), ["L8"], ["L8"])
              CA = lambda j, tap: LW[:, j * 3 + tap: j * 3 + tap + 1]
              CQ = lambda j, tap: LW[:, 12 + j * 3 + tap: 12 + j * 3 + tap + 1]
              GN = LW[:, 48:49]

              def big_keys(b, permuted):
                  if b == 0 or not permuted:
                      return [("BIG", b)]
                  return [("BIG", i) for i in range(1, 9)]

              def big_view(kc, b, permuted, rowmajor_of_colscan):
                  t0, nt = blk_range(b)
                  if b == 0 or not permuted:
                      return BIG[:, kc, t0:t0 + nt]
                  lat = BIG[:, kc, NCTX:NTOK]
                  if rowmajor_of_colscan:
                      v = lat.rearrange("p (c r) -> p r c", r=64)
                  else:
                      v = lat.rearrange("p (r c) -> p c r", c=64)
                  return v[:, (b - 1) * 8:(b - 1) * 8 + 8, :]

              def pview(ap, b, permuted):
                  t0, nt = blk_range(b)
                  if b == 0 or not permuted:
                      return ap[:, 0:nt]
                  return ap.rearrange("p (a b) -> p a b", b=64)

              with ExitStack() as s1:
                  WIN = sb("WIN", [128, 8, DPROJ], BF16, s1)
                  XSF = sb("XS", [128, 4112], F32, s1)
                  XS = XSF[:, 0:4096].rearrange("p (k t) -> p k t", k=8)
                  T0 = sb("T0", [128, 512], F32, s1)
                  T1 = sb("T1", [128, 512], F32, s1)
                  T2 = sb("T2", [128, 512], F32, s1)
                  SQQ = sb("SQQ", [128, 512], BF16, s1)
                  KQVB = sb("KQVB", [128, 12, 512], BF16, s1)
                  SQB = KQVB[:, 0:8, :]
                  YAB = sb("YAB", [128, 4, 512], BF16, s1)
                  SZBB = sb("SZBB", [128, 4, 512], BF16, s1)
                  ROWB = sb("ROWB", [8, 2, 512], F32, s1)
                  HP = sb("HP", [128, 8, 512], BF16, s1) if colmaj else None
                  XSW = XSF[:]
                  i = 0
                  for k in range(8):
                      for hf in range(2):
                          s = i % 2
                          c0 = hf * 2056
                          P.dma("sp", XSW[:, s * 2056:(s + 1) * 2056], w_in[l, k * 128:(k + 1) * 128, c0:c0 + 2056],
                                [], [("XS", s)], ("xs", s))
                          eng = ("dve", "pool", "act")[i % 3]
                          if eng == "act":
                              P.op("act", lambda e, k=k, s=s, c0=c0: e.copy(WIN[:, k, c0:c0 + 2056], XSW[:, s * 2056:(s + 1) * 2056]),
                                   [("XS", s)], ["WIN"])
                          else:
                              P.op(eng, lambda e, k=k, s=s, c0=c0: e.tensor_copy(WIN[:, k, c0:c0 + 2056], XSW[:, s * 2056:(s + 1) * 2056]),
                                   [("XS", s)], ["WIN"])
                          i += 1
                  if stop == "s1w":
                      P.barrier()
                      ck("s1w")
                  for nb in range(NBLK):
                      t0, nt = blk_range(nb)
                      v = 1 if nb == 0 else 0
                      P.dma("sp", XS[:, :, 0:nt], xsrc_v[:, :, t0:t0 + nt], [("x", nb)], [("XS", 0), ("XS", 1)], ("xs", 0))
                      P.op("act", lambda e, nt=nt: e.activation(SQB[:, :, 0:nt], XS[:, :, 0:nt], AF.Square),
                           [("XS", 0), ("XS", 1)], ["KQVB"])
                      for k in range(8):
                          P.op("pe", lambda e, k=k, nt=nt: e.matmul(B[2][:, 0:nt], ONESB[:], SQB[:, k, 0:nt], start=(k == 0), stop=(k == 7)),
                               ["KQVB", "ONESB"], ["B2"], inc=(k == 7))
                      P.op("act", lambda e, nt=nt: e.activation(T0[:, 0:nt], B[2][:, 0:nt], AF.Ln, bias=EPSC, scale=1.0 / D), ["B2"], ["T0"])
                      P.op("act", lambda e, nt=nt: e.activation(T0[:, 0:nt], T0[:, 0:nt], AF.Exp, scale=-0.5), ["T0"], ["T0"])
                      for k in range(8):
                          eng = "dve" if k % 2 == 0 else "pool"
                          P.op(eng, lambda e, k=k, nt=nt: e.tensor_tensor(XS[:, k, 0:nt], XS[:, k, 0:nt], T0[:, 0:nt], ALU.mult),
                               [("XS", 0), ("XS", 1), "T0", "KQVB"], [("XSn", k)])
                          P.op("act", lambda e, k=k, nt=nt, t0=t0, v=v: e.activation(
                              BIG[:, k, t0:t0 + nt], XS[:, k, 0:nt], AF.Identity, bias=mcol(v, k), scale=mcol(v, 8 + k)),
                              [("XSn", k)], [("BIG", nb)])
                      P.op("act", lambda e: e.activation(DUM[:, 0:1], DUM[:, 1:2], AF.Copy), [("BIG", nb)] + [("XSn", k) for k in range(8)], [("XS", 0), ("XS", 1)])

                  if stop == "s1p":
                      P.barrier()
                      ck("s1p")
                  P.barrier()
                  TS = [(T0, T1, T2, SQQ, "T0", "T1", "T2", "SQQ", 2)]
                  for i_ in range(2):
                      o_ = i_ * 1792
                      TS.append((XSF[:, o_:o_ + 512], XSF[:, o_ + 512:o_ + 1024], XSF[:, o_ + 1024:o_ + 1536],
                                 XSF[:, o_ + 1536:o_ + 1792].bitcast(BF16), f"T0_{i_}", f"T1_{i_}", f"T2_{i_}", f"SQQ_{i_}", 7 if i_ == 0 else 2))
                  tsi = [0]
                  pp = [0]
                  PROJ_BANKS = [0, 1, 3, 4, 5, 6]

                  def proj(b, c0, m):
                      bi = PROJ_BANKS[pp[0] % len(PROJ_BANKS)]
                      pp[0] += 1
                      t0, nt = blk_range(b)
                      for k in range(8):
                          if colmaj and b > 0:
                              P.op("pe", lambda e, k=k, bi=bi: e.matmul(
                                  B[bi][0:m, 0:nt], WIN[:, k, c0:c0 + m], HP[:, k, :],
                                  start=(k == 0), stop=(k == 7)), ["WIN", "HP"], [f"B{bi}"], inc=(k == 7))
                          else:
                              P.op("pe", lambda e, k=k, bi=bi: e.matmul(
                                  B[bi][0:m, 0:nt], WIN[:, k, c0:c0 + m], BIG[:, k, t0:t0 + nt],
                                  start=(k == 0), stop=(k == 7)), ["WIN"] + big_keys(b, False), [f"B{bi}"], inc=(k == 7))
                      return bi

                  def conv(dst, dkey, src, skey, w, nt, seg):
                      P.op("dve", lambda e: e.tensor_scalar(dst[:, 0:nt], src[:, 0:nt], w(1), None, ALU.mult), [skey, "LWa", "LWq"], [dkey])
                      dv = dst[:, 0:nt].rearrange("p (a b) -> p a b", b=seg)
                      sv = src[:, 0:nt].rearrange("p (a b) -> p a b", b=seg)
                      P.op("dve", lambda e: e.scalar_tensor_tensor(dv[:, :, 1:seg], sv[:, :, 0:seg - 1], w(0), dv[:, :, 1:seg], ALU.mult, ALU.add),
                           [skey, dkey, "LWa", "LWq"], [dkey])
                      P.op("dve", lambda e: e.scalar_tensor_tensor(dv[:, :, 0:seg - 1], sv[:, :, 1:seg], w(2), dv[:, :, 0:seg - 1], ALU.mult, ALU.add),
                           [skey, dkey, "LWa", "LWq"], [dkey])

                  def run_tasks(gens_factories, nsets):
                      pending = list(gens_factories)
                      active = []
                      free = list(range(nsets))
                      while pending or active:
                          if pending and free:
                              si = free.pop(0)
                              active.append((pending.pop(0)(TS[si]), si))
                          for ent in list(active):
                              try:
                                  next(ent[0])
                              except StopIteration:
                                  active.remove(ent)
                                  free.append(ent[1])

                  for b in range(NBLK):
                      t0, nt = blk_range(b)
                      seg = NCTX if b == 0 else 64
                      if colmaj and b > 0:
                          for k in range(8):
                              eng = "pool" if k % 2 == 0 else "act"
                              if eng == "pool":
                                  P.op("pool", lambda e, k=k: e.tensor_copy(HP[:, k, :].rearrange("p (a b) -> p a b", b=64), big_view(k, b, True, False)),
                                       big_keys(b, True), ["HP"])
                              else:
                                  P.op("act", lambda e, k=k: e.copy(HP[:, k, :].rearrange("p (a b) -> p a b", b=64), big_view(k, b, True, False)),
                                       big_keys(b, True), ["HP"])

                      def mixer_task(jj, b=b, nt=nt, seg=seg):
                          def g(ts):
                              T0, T1, T2, SQQ, k0, k1, k2, kq_, ssb = ts
                              bi = proj(b, jj * 128, 128)
                              yield
                              P.op("act", lambda e: e.copy(T0[:, 0:nt], B[bi][:, 0:nt]), [f"B{bi}"], [k0])
                              bi = proj(b, 1024 + jj * 128, 128)
                              yield
                              P.op("dve", lambda e: e.tensor_tensor(T0[:, 0:nt], B[bi][:, 0:nt], T0[:, 0:nt], ALU.mult), [f"B{bi}", k0], [k0])
                              conv(T1, k1, T0, k0, lambda tap: CA(jj, tap), nt, seg)
                              bi = proj(b, 512 + jj * 128, 128)
                              yield
                              P.op("dve", lambda e: e.tensor_tensor(T1[:, 0:nt], B[bi][:, 0:nt], T1[:, 0:nt], ALU.mult), [f"B{bi}", k1], [k1])
                              bi = proj(b, 1536 + jj * 128, 128)
                              yield
                              P.op("act", lambda e: e.activation(T2[:, 0:nt], B[bi][:, 0:nt], AF.Silu), [f"B{bi}"], [k2])
                              yield
                              P.op("pool", lambda e: e.tensor_tensor(YAB[:, jj, 0:nt], T1[:, 0:nt], T2[:, 0:nt], ALU.mult), [k1, k2], ["YAB"])
                          return g

                      def qkv_task(idx, b=b, nt=nt, seg=seg):
                          def g(ts):
                              T0, T1, T2, SQQ, k0, k1, k2, kq_, ssb = ts
                              bi = proj(b, 2048 + idx * 128, 128)
                              yield
                              conv(T1, k1, B[bi], f"B{bi}", lambda tap: CQ(idx, tap), nt, seg)
                              yield
                              if idx >= 8:
                                  P.op("act", lambda e: e.activation(KQVB[:, idx, 0:nt], T1[:, 0:nt], AF.Silu), [k1], ["KQVB"])
                                  return
                              P.op("act", lambda e: e.activation(T2[:, 0:nt], T1[:, 0:nt], AF.Silu), [k1], [k2])
                              P.op("act", lambda e: e.activation(SQQ[:, 0:nt], T2[:, 0:nt], AF.Square), [k2], [kq_])
                              yield
                              P.op("pe", lambda e: e.matmul(B[ssb][:, 0:nt], ONESB[:], SQQ[:, 0:nt], start=True, stop=True), [kq_, "ONESB"], [f"B{ssb}"])
                              yield
                              P.op("act", lambda e: e.activation(T0[:, 0:nt], B[ssb][:, 0:nt], AF.Ln, bias=EPSC, scale=1.0), [f"B{ssb}"], [k0])
                              P.op("act", lambda e: e.activation(T0[:, 0:nt], T0[:, 0:nt], AF.Exp, scale=-0.5), [k0], [k0])
                              yield
                              sc = (128.0 ** -0.5) if idx < 4 else 1.0
                              P.op("dve", lambda e: e.scalar_tensor_tensor(KQVB[:, idx, 0:nt], T2[:, 0:nt], sc, T0[:, 0:nt], ALU.mult, ALU.mult),
                                   [k2, k0], ["KQVB"])
                          return g

                      def zb_task(h, b=b, nt=nt):
                          def g(ts):
                              bi = proj(b, 3584 + h * 128, 128)
                              yield
                              P.op("act", lambda e: e.activation(SZBB[:, h, 0:nt], B[bi][:, 0:nt], AF.Silu), [f"B{bi}"], ["SZBB"])
                          return g

                      def rows_task(b=b, nt=nt):
                          def g(ts):
                              bi = proj(b, 4096, 8)
                              bi2 = proj(b, 4104, 8)
                              yield
                              P.op("act", lambda e: e.activation(ROWB[:, 1, 0:nt], B[bi][0:8, 0:nt], AF.Sigmoid), [f"B{bi}"], ["ROWB"])
                              P.op("act", lambda e: e.activation(ROWB[:, 0, 0:nt], B[bi2][0:8, 0:nt], AF.Exp, bias=L8[:, 1:2]), [f"B{bi2}", "L8"], ["ROWB"])
                              P.op("act", lambda e: e.activation(ROWB[:, 0, 0:nt], ROWB[:, 0, 0:nt], AF.Ln, bias=1.0), ["ROWB"], ["ROWB"])
                              yield
                              P.op("dve", lambda e: e.tensor_scalar(ROWB[:, 0, 0:nt], ROWB[:, 0, 0:nt], L8[:, 2:3], None, ALU.mult), ["ROWB", "L8"], ["ROWB"])
                          return g

                      tasks = [mixer_task(jj) for jj in range(4)] + [qkv_task(i) for i in range(12)] + [zb_task(h) for h in range(4)] + [rows_task()]
                      run_tasks(tasks, 3)
                      P.dma("pool", kqv_s[b, :, :, 0:nt], KQVB[:, :, 0:nt], ["KQVB"], [("kqv", b)], "sp0")
                      P.dma("pool", ya_s[b, :, :, 0:nt], YAB[:, :, 0:nt], ["YAB"], [("ya", b)], "sp1")
                      P.dma("pool", szb_s[b, :, :, 0:nt], SZBB[:, :, 0:nt], ["SZBB"], [("szb", b)], "sp2")
                      P.dma("pool", rows_s[b, :, :, 0:nt], ROWB[:, :, 0:nt], ["ROWB"], [("rows", b)], "sp3")
                  P.barrier()

              ck("s1")
              with ExitStack() as s2:
                  S32 = sb("S32", [128, 512], F32, s2)
                  SBF = sb("SBF", [128, 512], BF16, s2)
                  NSLOT = 4

                  def mk_slot(i):
                      t = {}
                      t["KQ"] = sb(f"KQ{i}", [128, 12, 128], BF16, s2)
                      t["RW"] = sb(f"RW{i}", [8, 2, 128], F32, s2)
                      t["SZ"] = sb(f"SZ{i}", [128, 4, 128], BF16, s2)
                      t["OFT"] = sb(f"OFT{i}", [128, 4, 128], BF16, s2)
                      for n_ in ("GC", "CS"):
                          t[n_] = sb(f"{n_}{i}", [8, 128], F32, s2)
                      t["DG"] = sb(f"DG{i}", [8, 8], F32, s2)
                      t["COLS"] = sb(f"COLS{i}", [128, 24], F32, s2)
                      t["CF"] = sb(f"CF{i}", [128, 32], F32, s2)
                      for n_ in ("W0", "W1", "W2", "W3"):
                          t[n_] = sb(f"{n_}{i}", [128, 512], F32, s2)
                      for n_ in ("PTB", "ATB", "AB", "PN0", "PN1", "PTN0", "PTN1", "RB", "RTB", "KBE", "KDEC", "VB",
                                 "QDT"):
                          t[n_] = sb(f"{n_}{i}", [128, 512], BF16, s2)
                      t["banks"] = [2 * i, 2 * i + 1]
                      t["i"] = i
                      return t

                  slots = [mk_slot(i) for i in range(NSLOT)]

                  def hs(ap, h):
                      return ap[:, h * 128:(h + 1) * 128]

                  def mm4(bank, lhs, rhs, rk, start=True, stop=True):
                      for h in range(4):
                          P.op("pe", lambda e, h=h: e.matmul(hs(B[bank], h), lhs(h), rhs(h), start=start, stop=stop), rk, [f"B{bank}"], inc=(h == 3))

                  def chunk_gen(T, d, b, ch, need_out):
                      si = T["i"]
                      K_ = lambda n_: (n_, si)
                      b0, b1 = T["banks"]
                      SB0, SB1 = b0, b1
                      OFT = T["OFT"]
                      t0, nt = blk_range(b)
                      c0 = ch * 128
                      tk = t0 + c0
                      kq, rw, sz = T["KQ"], T["RW"], T["SZ"]
                      GC, CS, DG, COLS, CF = T["GC"], T["CS"], T["DG"], T["COLS"], T["CF"]
                      W0, W1, W2, W3 = T["W0"], T["W1"], T["W2"], T["W3"]
                      U0, kU0 = W2, K_("W2")
                      PTB, ATB, AB, RB, RTB = T["PTB"], T["ATB"], T["AB"], T["RB"], T["RTB"]
                      PN = [T["PN0"], T["PN1"]]
                      PTN = [T["PTN0"], T["PTN1"]]
                      KBE, KDEC, VB, QDT = T["KBE"], T["KDEC"], T["VB"], T["QDT"]
                      WTB, kWTB = W3[:, 0:256].bitcast(BF16), K_("W3")
                      ZM, kZM = PN[0], K_("PN0")
                      YM, kYM = PTN[0], K_("PTN0")
                      SQO, kSQO = PN[1], K_("PN1")
                      UB, kUB = PTN[1], K_("PTN1")
                      kqk, rwk, szk = K_("KQ"), K_("RW"), K_("SZ")
                      P.dma("sp", kq[:], kqv_s[b, :, :, c0:c0 + 128], [("kqv", b)], [kqk], ("kq", si))
                      P.dma("sp", rw[:], rows_s[b, :, :, c0:c0 + 128], [("rows", b)], [rwk], ("rw", si))
                      if d == 1 and need_out:
                          P.dma("sp", sz[:], szb_s[b, :, :, c0:c0 + 128], [("szb", b)], [szk], ("sz", si))
                          P.dma("sp", OFT[:], of_s[tk // 128], [("of", tk)], [K_("OFT")], ("oft", si))
                      qT = lambda h: kq[:, h, :]
                      kT = lambda h: kq[:, 4 + h, :]
                      vT = lambda h: kq[:, 8 + h, :]
                      g8 = rw[:, 0, :]
                      b8 = rw[:, 1, :]
                      yield
                      P.op("dve", lambda e: e.tensor_tensor_scan(CS[:], ONES8, g8, 0.0, ALU.mult, ALU.add), [rwk, "SEL"], [K_("CS")])
                      if d == 0:
                          P.op("dve", lambda e: e.tensor_copy(GC[:], CS[:]), [K_("CS")], [K_("GC")])
                      else:
                          P.op("dve", lambda e: e.scalar_tensor_tensor(GC[:], CS[:], -1.0, g8, ALU.mult, ALU.add), [K_("CS"), rwk], [K_("GC")])
                          P.op("dve", lambda e: e.tensor_scalar(GC[:], GC[:], CS[:, 127:128], None, ALU.add), [K_("GC"), K_("CS")], [K_("GC")])
                      P.op("dve", lambda e: e.tensor_scalar(DG[:], I8, CS[:, 127:128], None, ALU.mult), [K_("CS"), "IDF"], [K_("DG")])
                      P.op("pe", lambda e: e.matmul(B[b0][:, 0:8], GC[:], I8, start=True, stop=True), [K_("GC"), "IDF"], [f"B{b0}"], inc=False)
                      P.op("pe", lambda e: e.matmul(B[b0][:, 8:16], b8, I8, start=True, stop=True), [rwk, "IDF"], [f"B{b0}"], inc=False)
                      P.op("pe", lambda e: e.matmul(B[b0][:, 16:24], ONES8, DG[:], start=True, stop=True), ["SEL", K_("DG")], [f"B{b0}"])
                      for h in range(4):
                          P.op("pe", lambda e, h=h: e.transpose(Bbf[b1][:, h * 128:(h + 1) * 128], kT(h), IDB[:]), [kqk, "IDB"], [f"B{b1}"], inc=False)
                      for h in range(4):
                          P.op("pe", lambda e, h=h: e.transpose(Bbf[b1][:, 512 + h * 128:512 + (h + 1) * 128], vT(h), IDB[:]), [kqk, "IDB"], [f"B{b1}"], inc=(h == 3))
                      yield
                      P.op("act", lambda e: e.copy(COLS[:], B[b0][:, 0:24]), [f"B{b0}"], [K_("COLS")])
                      P.op("act", lambda e: e.activation(CF[:, 0:8], COLS[:, 0:8], AF.Exp), [K_("COLS")], [K_("CF")])
                      P.op("dve", lambda e: e.tensor_tensor(CF[:, 8:16], CF[:, 0:8], COLS[:, 8:16], ALU.mult), [K_("CF"), K_("COLS")], [K_("CF")])
                      P.op("dve", lambda e: e.tensor_tensor(CF[:, 16:24], COLS[:, 16:24], COLS[:, 0:8], ALU.subtract), [K_("COLS"), K_("CF")], [K_("CF")])
                      P.op("act", lambda e: e.activation(CF[:, 16:24], CF[:, 16:24], AF.Exp), [K_("CF")], [K_("CF")])
                      P.op("act", lambda e: e.activation(CF[:, 24:32], COLS[:, 16:24], AF.Exp), [K_("COLS"), K_("CF")], [K_("CF")])
                      for h in range(4):
                          dh = d * 4 + h
                          P.op("pe", lambda e, h=h, dh=dh: e.matmul(hs(B[b0], h), SEL[:, dh * 128:(dh + 1) * 128], GC[:], start=True, stop=True), ["SEL", K_("GC")], [f"B{b0}"], inc=(h == 3))
                      yield
                      for h in range(4):
                          dh = d * 4 + h
                          P.op("dve", lambda e, h=h, dh=dh: e.tensor_scalar(hs(KBE[:], h), Bbf[b1][:, h * 128:(h + 1) * 128], CF[:, 8 + dh:9 + dh], None, ALU.mult),
                               [f"B{b1}", K_("CF")], [K_("KBE")])
                      for h in range(4):
                          dh = d * 4 + h
                          P.op("act", lambda e, h=h, dh=dh: e.activation(hs(KDEC[:], h), Bbf[b1][:, h * 128:(h + 1) * 128], AF.Identity, scale=CF[:, 16 + dh:17 + dh]),
                               [f"B{b1}", K_("CF")], [K_("KDEC")])
                          P.op("act", lambda e, h=h, dh=dh: e.activation(hs(VB[:], h), Bbf[b1][:, 512 + h * 128:512 + (h + 1) * 128], AF.Identity, scale=COLS[:, 8 + dh:9 + dh]),
                               [f"B{b1}", K_("COLS")], [K_("VB")])
                      mm4(b1, lambda h: SEL[:, (d * 4 + h) * 128:(d * 4 + h + 1) * 128], lambda h: b8, ["SEL", rwk])
                      yield
                      for h in range(4):
                          dh = d * 4 + h
                          P.op("dve", lambda e, h=h, dh=dh: e.scalar_tensor_tensor(hs(W0[:], h), hs(B[b0], h), COLS[:, dh:dh + 1], hs(NEGI[d], h), ALU.subtract, ALU.add),
                               [f"B{b0}", K_("COLS"), "MASK"], [K_("W0")])
                      P.op("act", lambda e: e.activation(W3[:], B[b0], AF.Exp), [f"B{b0}"], [K_("W3")])
                      P.op("act", lambda e: e.activation(W1[:], W0[:], AF.Exp), [K_("W0")], [K_("W1")])
                      P.op("pool", lambda e: e.tensor_tensor(h4(QDT[:]), kq[:, 0:4, :], h4(W3[:]), ALU.mult), [kqk, K_("W3")], [K_("QDT")])
                      P.op("dve", lambda e: e.tensor_tensor(W2[:], B[b1], SM[d], ALU.mult), [f"B{b1}", "MASK"], [K_("W2")])
                      P.op("pool", lambda e: e.tensor_tensor(W2[:], W2[:], W1[:], ALU.mult), [K_("W2"), K_("W1")], [K_("W2")])
                      mm4(b0, kT, kT, [kqk])
                      mm4(b1, kT, qT, [kqk])
                      yield
                      P.op("dve", lambda e: e.tensor_tensor(ATB[:], B[b0], W2[:], ALU.mult), [f"B{b0}", K_("W2")], [K_("ATB")])
                      P.op("dve", lambda e: e.tensor_tensor(PTB[:], B[b1], W1[:], ALU.mult), [f"B{b1}", K_("W1")], [K_("PTB")])
                      for h in range(4):
                          P.op("pe", lambda e, h=h: e.transpose(Bbf[b0][:, h * 128:(h + 1) * 128], hs(ATB[:], h), IDB[:]), [K_("ATB"), "IDB"], [f"B{b0}"], inc=(h == 3))
                      P.op("pool", lambda e: e.tensor_tensor(PN[0][:], ATB[:], BLK16, ALU.mult), [K_("ATB"), "MASK"], [K_("PN0")])
                      P.op("pool", lambda e: e.tensor_tensor(RTB[:], ID4, PN[0][:], ALU.subtract), ["MASK", K_("PN0")], [K_("RTB")])
                      yield
                      P.op("act", lambda e: e.copy(AB[:], Bbf[b0][:, 0:512]), [f"B{b0}"], [K_("AB")])
                      P.op("dve", lambda e: e.tensor_tensor(PTN[0][:], Bbf[b0][:, 0:512], BLK16, ALU.mult), [f"B{b0}", "MASK"], [K_("PTN0")])
                      P.op("pool", lambda e: e.tensor_tensor(RB[:], ID4, PTN[0][:], ALU.subtract), ["MASK", K_("PTN0")], [K_("RB")])
                      yield
                      cur = 0
                      for it in range(3):
                          nx = 1 - cur
                          kc_, kn_ = (K_(f"PTN{cur}"), K_(f"PN{cur}")), (K_(f"PTN{nx}"), K_(f"PN{nx}"))
                          mm4(b0, lambda h: hs(PTN[cur][:], h), lambda h: hs(PN[cur][:], h), list(kc_))
                          mm4(b1, lambda h: hs(PN[cur][:], h), lambda h: hs(PTN[cur][:], h), list(kc_))
                          yield
                          P.op("act", lambda e, nx=nx: e.copy(PN[nx][:], B[b0]), [f"B{b0}"], [kn_[1]])
                          P.op("act", lambda e, nx=nx: e.copy(PTN[nx][:], B[b1]), [f"B{b1}"], [kn_[0]])
                          mm4(b0, lambda h: hs(PTN[nx][:], h), lambda h: hs(RTB[:], h), [kn_[0], K_("RTB")])
                          mm4(b1, lambda h: hs(PN[nx][:], h), lambda h: hs(RB[:], h), [kn_[1], K_("RB")])
                          yield
                          P.op("dve", lambda e: e.tensor_tensor(RTB[:], B[b0], RTB[:], ALU.add), [f"B{b0}", K_("RTB")], [K_("RTB")])
                          P.op("act", lambda e: e.copy(W0[:], B[b1]), [f"B{b1}"], [K_("W0")])
                          P.op("pool", lambda e: e.tensor_tensor(RB[:], W0[:], RB[:], ALU.add), [K_("W0"), K_("RB")], [K_("RB")])
                          cur = nx
                      for szm in (32, 64, 128):
                          if szm < 128:
                              mm4(b0, lambda h: hs(ATB[:], h), lambda h: hs(RB[:], h), [K_("ATB"), K_("RB")])
                          mm4(b1, lambda h: hs(AB[:], h), lambda h: hs(RTB[:], h), [K_("AB"), K_("RTB")])
                          yield
                          if szm < 128:
                              P.op("dve", lambda e, szm=szm: e.tensor_tensor(ZM[:], B[b0], OFF[szm], ALU.mult), [f"B{b0}", "MASK"], [kZM])
                          P.op("dve", lambda e, szm=szm: e.tensor_tensor(YM[:], B[b1], OFF[szm], ALU.mult), [f"B{b1}", "MASK"], [kYM])
                          if szm < 128:
                              mm4(b0, lambda h: hs(RTB[:], h), lambda h: hs(ZM[:], h), [K_("RTB"), kZM])
                          mm4(b1, lambda h: hs(RB[:], h), lambda h: hs(YM[:], h), [K_("RB"), kYM])
                          yield
                          if szm < 128:
                              P.op("act", lambda e: e.copy(W0[:], B[b0]), [f"B{b0}"], [K_("W0")])
                              P.op("pool", lambda e: e.tensor_tensor(RB[:], RB[:], W0[:], ALU.subtract), [K_("W0"), K_("RB")], [K_("RB")])
                          P.op("dve", lambda e: e.tensor_tensor(RTB[:], RTB[:], B[b1], ALU.subtract), [f"B{b1}", K_("RTB")], [K_("RTB")])
                      mm4(b0, lambda h: hs(RTB[:], h), lambda h: hs(VB[:], h), [K_("RTB"), K_("VB")])
                      mm4(b1, lambda h: hs(KBE[:], h), lambda h: hs(RTB[:], h), [K_("KBE"), K_("RTB")])
                      yield
                      P.op("act", lambda e: e.copy(U0[:], B[b0]), [f"B{b0}"], [kU0])
                      P.op("act", lambda e: e.copy(WTB, B[b1]), [f"B{b1}"], [kWTB])
                      yield "scan"
                      mm4(SB0, lambda h: hs(WTB, h), lambda h: hs(SBF[:], h), [kWTB, "SBF"])
                      P.op("dve", lambda e: e.tensor_tensor(UB[:], U0[:], B[SB0], ALU.subtract), [kU0, f"B{SB0}"], [kUB])
                      if need_out:
                          for h in range(4):
                              P.op("pe", lambda e, h=h: e.matmul(hs(B[SB1], h), hs(SBF[:], h), hs(QDT[:], h), start=True, stop=False), ["SBF", K_("QDT")], [f"B{SB1}"], inc=False)
                              P.op("pe", lambda e, h=h: e.matmul(hs(B[SB1], h), hs(UB[:], h), hs(PTB[:], h), start=False, stop=True), [kUB, K_("PTB")], [f"B{SB1}"], inc=(h == 3))
                      mm4(SB0, lambda h: hs(KDEC[:], h), lambda h: hs(UB[:], h), [K_("KDEC"), kUB])
                      for h in range(4):
                          dh = d * 4 + h
                          P.op("dve", lambda e, h=h, dh=dh: e.scalar_tensor_tensor(hs(S32[:], h), hs(S32[:], h), CF[:, 24 + dh:25 + dh], hs(B[SB0], h), ALU.mult, ALU.add),
                               ["S32", K_("CF"), f"B{SB0}"], ["S32"])
                      P.op("act", lambda e: e.copy(SBF[:], S32[:]), ["S32"], ["SBF"])
                      if need_out:
                          if d == 0:
                              P.op("act", lambda e: e.copy(OFT[:], h4(B[SB1])), [f"B{SB1}"], [K_("OFT")])
                              P.dma("pool", of_s[tk // 128], OFT[:], [K_("OFT")], [("of", tk)], ("ofst", si))
                          else:
                              P.op("dve", lambda e: e.tensor_tensor(h4(W0[:]), h4(B[SB1]), OFT[:], ALU.add), [f"B{SB1}", K_("OFT")], [K_("W0")])
                              P.op("act", lambda e: e.activation(SQO[:], W0[:], AF.Square), [K_("W0")], [kSQO])
                              mm4(SB0, lambda h: ONESB[:], lambda h: hs(SQO[:], h), ["ONESB", kSQO])
                              P.op("act", lambda e: e.activation(W1[:], B[SB0], AF.Ln, bias=EPSC, scale=1.0 / 128), [f"B{SB0}"], [K_("W1")])
                              P.op("act", lambda e: e.activation(W1[:], W1[:], AF.Exp, scale=-0.5), [K_("W1")], [K_("W1")])
                              P.op("pool", lambda e: e.tensor_tensor(W0[:], W0[:], W1[:], ALU.mult), [K_("W0"), K_("W1")], [K_("W0")])
                              P.op("dve", lambda e: e.scalar_tensor_tensor(BIG[:, 4:8, tk:tk + 128], h4(W0[:]), GN, sz[:, :, :], ALU.mult, ALU.mult),
                                   [K_("W0"), "LWg", szk], [("BIG", b)])

                  for d in range(2):
                      P.op("pool", lambda e: e.memset(S32[:], 0.0), [], ["S32"])
                      P.op("pool", lambda e: e.memset(SBF[:], 0.0), [], ["SBF"])
                      border = list(range(NBLK)) if d == 0 else [0] + list(range(8, 0, -1))
                      tasks = []
                      for b in border:
                          t0, nt = blk_range(b)
                          need_out = not (last and b == 0)
                          nch = nt // 128
                          chs = list(range(nch)) if d == 0 else list(range(nch - 1, -1, -1))
                          for ci, ch in enumerate(chs):
                              tasks.append((b, ch, need_out, ci == 0))
                      active = []
                      nxt = 0
                      scan_turn = 0
                      free_slots = list(range(NSLOT))
                      while nxt < len(tasks) or active:
                          if nxt < len(tasks) and free_slots and (not active or len(active) < NSLOT):
                              b, ch, need_out, first = tasks[nxt]
                              if first and d == 1 and need_out:
                                  t0, nt = blk_range(b)
                                  P.dma("sp", BIG[:, 0:4, t0:t0 + nt], ya_s[b, :, :, 0:nt], [("ya", b)], [("BIG", b)], "yal")
                              si = free_slots.pop(0)
                              active.append([chunk_gen(slots[si], d, b, ch, need_out), nxt, False, si])
                              nxt += 1
                          for ent in list(active):
                              g, ti, wscan, si = ent
                              if wscan and ti != scan_turn:
                                  continue
                              try:
                                  r = next(g)
                                  if wscan:
                                      scan_turn += 1
                                      ent[2] = False
                                  if r == "scan":
                                      ent[2] = True
                              except StopIteration:
                                  if wscan:
                                      scan_turn += 1
                                  active.remove(ent)
                                  free_slots.append(si)
                  P.barrier()
              ck("s2")
              with ExitStack() as s4:
                  WOUT = sb("WOUT", [128, 8, D], BF16, s4)
                  XS4S = [sb(f"XS4_{i_}", [128, 8, 512], F32, s4) for i_ in range(3)]
                  WS4 = sb("WS4", [128, 2, D], F32, s4)
                  SQB = sb("SQB4", [128, 8, 512], BF16, s4)
                  T0 = sb("T04", [128, 512], F32, s4)
                  YP = sb("YP", [128, 8, 512], BF16, s4) if colmaj else None
                  for k in range(8):
                      s = k % 2
                      P.dma("sp", WS4[:, s, :], w_out[l, k * 128:(k + 1) * 128, :], [], [("WS4", s)], ("ws4", s))
                      P.op("dve" if s == 0 else "pool", lambda e, k=k, s=s: e.tensor_copy(WOUT[:, k, :], WS4[:, s, :]), [("WS4", s)], ["WOUT"])
                  pp = 0
                  for nb in range(NBLK):
                      if last and nb == 0:
                          continue
                      t0, nt = blk_range(nb)
                      v = 1 if nb == 0 else 0
                      xi = nb % 3
                      XS = XS4S[xi]
                      xk = ("XS4", xi)
                      P.dma("sp", XS[:, :, 0:nt], xsrc_v[:, :, t0:t0 + nt], [("x", nb)], [xk], ("xs4", xi))
                      if colmaj and nb > 0:
                          for k in range(8):
                              if k % 2 == 0:
                                  P.op("pool", lambda e, k=k: e.tensor_copy(YP[:, k, :].rearrange("p (a b) -> p a b", b=64), big_view(k, nb, True, True)),
                                       big_keys(nb, True), ["YP"])
                              else:
                                  P.op("act", lambda e, k=k: e.copy(YP[:, k, :].rearrange("p (a b) -> p a b", b=64), big_view(k, nb, True, True)),
                                       big_keys(nb, True), ["YP"])
                      for jn in range(8):
                          bi = (0, 1, 3, 4)[pp % 4]
                          pp += 1
                          for kc in range(8):
                              if colmaj and nb > 0:
                                  P.op("pe", lambda e, kc=kc, jn=jn, bi=bi: e.matmul(
                                      B[bi][:, 0:nt], WOUT[:, kc, jn * 128:(jn + 1) * 128], YP[:, kc, :],
                                      start=(kc == 0), stop=(kc == 7)), ["WOUT", "YP"], [f"B{bi}"], inc=(kc == 7))
                              else:
                                  P.op("pe", lambda e, kc=kc, jn=jn, bi=bi: e.matmul(
                                      B[bi][:, 0:nt], WOUT[:, kc, jn * 128:(jn + 1) * 128], BIG[:, kc, t0:t0 + nt],
                                      start=(kc == 0), stop=(kc == 7)), ["WOUT"] + big_keys(nb, False), [f"B{bi}"], inc=(kc == 7))
                          P.op("dve", lambda e, jn=jn, bi=bi, v=v: e.scalar_tensor_tensor(
                              XS[:, jn, 0:nt], B[bi][:, 0:nt], mcol(v, 16 + jn), XS[:, jn, 0:nt], ALU.mult, ALU.add),
                              [f"B{bi}", xk], [("XSo", xi, jn)])
                      okeys = [("XSo", xi, jn) for jn in range(8)]
                      if not last:
                          P.dma("pool", xres_v[:, :, t0:t0 + nt], XS[:, :, 0:nt], okeys + [xk], [("x", nb)], ("xst", xi))
                      else:
                          P.op("act", lambda e: e.activation(SQB[:], XS[:], AF.Square), okeys, ["SQB4"])
                          for k in range(8):
                              P.op("pe", lambda e, k=k: e.matmul(B[2], ONESB[:], SQB[:, k, :], start=(k == 0), stop=(k == 7)), ["SQB4", "ONESB"], ["B2"], inc=(k == 7))
                          P.op("act", lambda e: e.activation(T0[:], B[2], AF.Ln, bias=EPSC, scale=1.0 / D), ["B2"], ["T04"])
                          P.op("act", lambda e: e.activation(T0[:], T0[:], AF.Exp, scale=-0.5), ["T04"], ["T04"])
                          for k in range(8):
                              P.op("dve", lambda e, k=k: e.scalar_tensor_tensor(XS[:, k, :], XS[:, k, :], FN[:, k:k + 1], T0[:], ALU.mult, ALU.mult),
                                   [("XSo", xi, k), "T04", "FN", "SQB4"], [("XSf", xi, k)])
                          fk = [("XSf", xi, k) for k in range(8)]
                          P.dma("pool", out_v[:, :, t0 - NCTX:t0 - NCTX + nt], XS[:, :, 0:nt], fk + [xk], [("out", nb)], ("ost", xi))
                  P.barrier()
          P.barrier()

    except _Stop:
        pass
    return nc


def _consts():
    i = np.arange(128)
    t, s = i[None, :], i[:, None]
    negi_f = np.where(t >= s, 0.0, BIGNEG)
    negi_b = np.where(t <= s, 0.0, BIGNEG)
    sm_f = (t > s).astype(np.float64)
    sm_b = (t < s).astype(np.float64)
    blk = lambda z: ((s // z) == (t // z)).astype(np.float64)
    blk16 = blk(16)
    off = {z: blk(z) * (1 - blk(z // 2)) for z in (32, 64, 128)}
    ident = np.eye(128)
    ms = [negi_f, negi_b, sm_f, sm_b, blk16, off[32], off[64], off[128], ident]
    cmask = np.concatenate([np.tile(m, (1, 4)) for m in ms], axis=1).astype(np.float32)
    sel = np.zeros((8, 2 * 1024 + 128), np.float32)
    for dh in range(8):
        sel[dh, dh * 128:(dh + 1) * 128] = 1.0
        sel[dh, 1024 + dh * 128:1024 + (dh + 1) * 128] = -1.0
    sel[:, 2048:] = 1.0
    return cmask, ident.astype(np.float32), sel


_NC_CACHE = {}


def make_in_maps(x, c, ctx, c_ctx, norm_w, w_mod, b_mod, w_in, conv_a, conv_qkv, a_log, dt_bias, gdn_norm, w_out, final_norm):
    f = lambda a: np.ascontiguousarray(np.asarray(a, dtype=np.float32))
    x, c, ctx, c_ctx = f(x), f(c), f(ctx), f(c_ctx)
    cmask, ident, sel = _consts()
    L = norm_w.shape[0]
    col = lambda a: np.ascontiguousarray(a.reshape(-1, 128).T)
    shared = {
        "w_mod": f(w_mod), "w_in": f(w_in), "w_out": f(w_out),
        "bmod": np.stack([col(f(b_mod)[l]) for l in range(L)]),
        "normw": np.stack([col(f(norm_w)[l]) for l in range(L)]),
        "conva": np.stack([np.ascontiguousarray(f(conv_a)[l].T.reshape(4, 128, 3).transpose(1, 0, 2).reshape(128, 12)) for l in range(L)]),
        "convq": np.stack([np.ascontiguousarray(f(conv_qkv)[l].T.reshape(12, 128, 3).transpose(1, 0, 2).reshape(128, 36)) for l in range(L)]),
        "alog": np.ascontiguousarray(f(a_log).reshape(L, 8, 1)),
        "dtb": np.ascontiguousarray(f(dt_bias).reshape(L, 8, 1)),
        "gnorm": np.ascontiguousarray(f(gdn_norm).reshape(L, 128, 1)),
        "fnorm": col(f(final_norm)),
        "cmask": cmask, "cident": ident, "csel": sel,
    }
    maps = []
    for b in range(x.shape[0]):
        m = dict(shared)
        m["xT"] = np.ascontiguousarray(np.concatenate([ctx[b], x[b]], axis=0).T)
        ccb = np.stack([col(c[b]), col(c_ctx)], axis=-1).reshape(128, 16)
        m["cc"] = np.ascontiguousarray(ccb)
        maps.append(m)
    return maps


def kernel(x, c, ctx, c_ctx, norm_w, w_mod, b_mod, w_in, conv_a, conv_qkv, a_log, dt_bias, gdn_norm, w_out, final_norm, _nlayers=DEPTH):
    maps = make_in_maps(x, c, ctx, c_ctx, norm_w, w_mod, b_mod, w_in, conv_a, conv_qkv, a_log, dt_bias, gdn_norm, w_out, final_norm)
    if _nlayers not in _NC_CACHE:
        _NC_CACHE[_nlayers] = build(_nlayers)
    nc = _NC_CACHE[_nlayers]
    res = run_bass_kernel_spmd(nc, maps, core_ids=list(range(len(maps))))
    out = np.stack([np.ascontiguousarray(r["outT"].T) for r in res.results], axis=0)
    return out.astype(np.float32)
```
